# Optimizing a Trainium2 kernel written in Bass

```python
import math
import jax, jax.numpy as jnp
from jax import lax
import numpy as np

D_MODEL = 1024
BATCH = 2
SEQ = 8192
DEPTH = 4

BR_WIDTH = 512
N_BRANCH = 3
CONV_WIDTH = 3
HG_HEADS = 4
HG_DK = BR_WIDTH // HG_HEADS
HG_DV = BR_WIDTH // HG_HEADS
HG_CHUNK = 64
F_FLOOR = 1e-30
NSA_QHEADS = 8
NSA_KVHEADS = 2
NSA_GROUP = NSA_QHEADS // NSA_KVHEADS
NSA_HD = BR_WIDTH // NSA_QHEADS
CMP_BLOCK = 32
CMP_STRIDE = 16
CMP_HIDDEN = 256
SLC_BLOCK = 64
SLC_TOPK = 16
WINDOW = 512
Q_BLOCK = 128
ROPE_THETA = 10000.0
D_FF = 2816
N_FFN = 2
N_NORMS = 6
EPS = 1e-6
SEL_FORCE = 1e6
MASK_VALUE = -1e30
KV_COLS = NSA_KVHEADS * NSA_HD
IN_SIZES = (BR_WIDTH,) * 3 + (BR_WIDTH,) * 4 + (BR_WIDTH,) + (KV_COLS,) * 6 + (3 * NSA_QHEADS, N_BRANCH * D_MODEL)
IN_COLS = 3 * BR_WIDTH + 4 * BR_WIDTH + BR_WIDTH + 6 * KV_COLS + 3 * NSA_QHEADS + N_BRANCH * D_MODEL

kernel_name = 'hybrid_conv_hgrn2_nsa_macaron'


def rmsnorm(x, g):
    x32 = x.astype(jnp.float32)
    y = x32 * lax.rsqrt(jnp.mean(x32 * x32, axis=-1, keepdims=True) + EPS)
    return (y * g.astype(jnp.float32)).astype(x.dtype)


def rope(x, positions):
    hd = x.shape[-1]
    half = hd // 2
    inv = ROPE_THETA ** (-jnp.arange(half, dtype=jnp.float32) * 2.0 / hd)
    ang = positions.astype(jnp.float32)[..., None] * inv
    cos = jnp.cos(ang)[:, :, None, :]
    sin = jnp.sin(ang)[:, :, None, :]
    x32 = x.astype(jnp.float32)
    x1, x2 = x32[..., :half], x32[..., half:]
    return jnp.concatenate([x1 * cos - x2 * sin, x2 * cos + x1 * sin], axis=-1).astype(x.dtype)


def masked_softmax(s, mask):
    s = jnp.where(mask, s.astype(jnp.float32), MASK_VALUE)
    m = jnp.max(s, axis=-1, keepdims=True)
    e = jnp.where(mask, jnp.exp(s - m), 0.0)
    d = jnp.sum(e, axis=-1, keepdims=True)
    return e / jnp.where(d > 0, d, 1.0)


def swiglu(h, wg, wu, wd):
    return (jax.nn.silu(h @ wg) * (h @ wu)) @ wd


def short_conv_mixer(b, c, xt, w):
    v = c * xt
    y = lax.conv_general_dilated(v, w[:, None, :].astype(v.dtype), window_strides=(1,),
                                 padding=[(CONV_WIDTH - 1, 0)],
                                 dimension_numbers=('NWC', 'WIO', 'NWC'),
                                 feature_group_count=BR_WIDTH)
    return b * y


def hgrn2_mixer(q, f_logit, i, g, lb, gnorm):
    B, S, _ = q.shape
    dt = q.dtype
    f32 = jnp.float32
    q = jax.nn.silu(q.astype(f32))
    z = f_logit.astype(f32)
    f = lb + (1.0 - lb) * jax.nn.sigmoid(z)
    log_f = jnp.log(jnp.maximum(f, F_FLOOR))
    k = (1.0 - lb) * jax.nn.sigmoid(-z)
    v = i.astype(f32)
    n = S // HG_CHUNK

    def chunks(a, d):
        return a.reshape(B, n, HG_CHUNK, HG_HEADS, d).transpose(1, 0, 3, 2, 4)

    tri = jnp.tril(jnp.ones((HG_CHUNK, HG_CHUNK), dtype=bool))[:, :, None]
    tri_f = tri.astype(f32)

    def step(state, xs):
        qc, kc, vc, lc = xs
        bcum = jnp.cumsum(lc, axis=2)
        diff = bcum[:, :, :, None, :] - bcum[:, :, None, :, :]
        decay = jnp.exp(jnp.where(tri, diff, 0.0)) * tri_f
        attn = jnp.einsum('bhtd,bhsd,bhtsd->bhts', qc, kc, decay)
        o = (jnp.einsum('bhts,bhsv->bhtv', attn, vc)
             + jnp.einsum('bhtd,bhdv->bhtv', qc * jnp.exp(bcum), state))
        b_last = bcum[:, :, -1]
        new_state = (jnp.exp(b_last)[..., None] * state
                     + jnp.einsum('bhsd,bhsv->bhdv', kc * jnp.exp(b_last[:, :, None] - bcum), vc))
        return new_state, o

    state0 = jnp.zeros((B, HG_HEADS, HG_DK, HG_DV), f32)
    _, o = lax.scan(step, state0, (chunks(q, HG_DK), chunks(k, HG_DK), chunks(v, HG_DV), chunks(log_f, HG_DK)))
    o = o.transpose(1, 0, 3, 2, 4).reshape(B, S, HG_HEADS, HG_DV)
    o = rmsnorm(o, gnorm) * jax.nn.silu(g.astype(f32).reshape(B, S, HG_HEADS, HG_DV))
    return o.reshape(B, S, BR_WIDTH).astype(dt)


def compress_blocks(k, pe, w1, w2):
    B, S, H, hd = k.shape
    ch = k.reshape(B, S // CMP_STRIDE, CMP_STRIDE, H, hd)
    blk = jnp.concatenate([ch[:, :-1], ch[:, 1:]], axis=2)
    blk = blk + pe[None, None, :, None, :].astype(blk.dtype)
    flat = blk.transpose(0, 1, 3, 2, 4).reshape(B, blk.shape[1], H, CMP_BLOCK * hd)
    return jax.nn.gelu(flat @ w1) @ w2


def nsa_mixer(q, kc, vc, ks, vs, kw, vw, gate_logits, positions, pe, w1, w2):
    B, S, _ = q.shape
    scale = NSA_HD ** -0.5
    q = rope(q.reshape(B, S, NSA_QHEADS, NSA_HD), positions).reshape(B, S, NSA_KVHEADS, NSA_GROUP, NSA_HD)
    kvr = lambda a: a.reshape(B, S, NSA_KVHEADS, NSA_HD)
    kc, ks, kw = rope(kvr(kc), positions), rope(kvr(ks), positions), rope(kvr(kw), positions)
    vc, vs, vw = kvr(vc), kvr(vs), kvr(vw)
    gates = jax.nn.sigmoid(gate_logits.reshape(B, S, NSA_KVHEADS, NSA_GROUP, 3))

    k_cmp = compress_blocks(kc, pe[0], w1[0], w2[0])
    v_cmp = compress_blocks(vc, pe[1], w1[1], w2[1])
    n_cmp = k_cmp.shape[1]
    cmp_start = jnp.arange(n_cmp) * CMP_STRIDE
    cmp_end = cmp_start + CMP_BLOCK - 1
    n_sel = S // SLC_BLOCK
    top_k = min(SLC_TOPK, n_sel)
    sel_start = jnp.arange(n_sel) * SLC_BLOCK
    overlap = ((cmp_start[:, None] < sel_start[None, :] + SLC_BLOCK)
               & (cmp_start[:, None] + CMP_BLOCK > sel_start[None, :])).astype(jnp.float32)
    ks_blk = ks.transpose(0, 2, 1, 3).reshape(B, NSA_KVHEADS, n_sel, SLC_BLOCK, NSA_HD)
    vs_blk = vs.transpose(0, 2, 1, 3).reshape(B, NSA_KVHEADS, n_sel, SLC_BLOCK, NSA_HD)
    gather = jax.vmap(jax.vmap(lambda blk, ix: blk[ix]))
    pad = ((0, 0), (WINDOW, 0), (0, 0), (0, 0))
    kw_pad, vw_pad = jnp.pad(kw, pad), jnp.pad(vw, pad)
    jsel = jnp.arange(n_sel)

    def block(qb_i):
        t0 = qb_i * Q_BLOCK
        qb = lax.dynamic_slice_in_dim(q, t0, Q_BLOCK, axis=1)
        gb = lax.dynamic_slice_in_dim(gates, t0, Q_BLOCK, axis=1)
        t = t0 + jnp.arange(Q_BLOCK)
        s = jnp.einsum('btkgd,bnkd->bkgtn', qb, k_cmp) * scale
        p_cmp = masked_softmax(s, cmp_end[None, :] <= t[:, None])
        o_cmp = jnp.einsum('bkgtn,bnkd->btkgd', p_cmp.astype(v_cmp.dtype), v_cmp)
        imp = jnp.einsum('bkgtn,nj->bktj', p_cmp, overlap)
        cur = t // SLC_BLOCK
        forced = (jsel[None] == 0) | (jsel[None] == cur[:, None]) | (jsel[None] == cur[:, None] - 1)
        score = jnp.where(forced, SEL_FORCE, jnp.where(sel_start[None] <= t[:, None], imp, -SEL_FORCE))
        _, idx = lax.top_k(score, top_k)
        k_sel = gather(ks_blk, idx)
        v_sel = gather(vs_blk, idx).reshape(B, NSA_KVHEADS, Q_BLOCK, top_k * SLC_BLOCK, NSA_HD)
        s = jnp.einsum('btkgd,bktnld->bkgtnl', qb, k_sel) * scale
        pos = idx[..., None] * SLC_BLOCK + jnp.arange(SLC_BLOCK)
        smask = (pos <= t[:, None, None])[:, :, None]
        p = masked_softmax(s.reshape(B, NSA_KVHEADS, NSA_GROUP, Q_BLOCK, top_k * SLC_BLOCK),
                           smask.reshape(B, NSA_KVHEADS, 1, Q_BLOCK, top_k * SLC_BLOCK))
        o_slc = jnp.einsum('bkgtm,bktmd->btkgd', p.astype(v_sel.dtype), v_sel)
        kwb = lax.dynamic_slice_in_dim(kw_pad, t0, Q_BLOCK + WINDOW, axis=1)
        vwb = lax.dynamic_slice_in_dim(vw_pad, t0, Q_BLOCK + WINDOW, axis=1)
        s_pos = t0 - WINDOW + jnp.arange(Q_BLOCK + WINDOW)
        wmask = (s_pos[None] <= t[:, None]) & (s_pos[None] > t[:, None] - WINDOW) & (s_pos[None] >= 0)
        s = jnp.einsum('btkgd,bskd->bkgts', qb, kwb) * scale
        p = masked_softmax(s, wmask)
        o_win = jnp.einsum('bkgts,bskd->btkgd', p.astype(vwb.dtype), vwb)
        o = gb[..., 0:1] * o_cmp + gb[..., 1:2] * o_slc + gb[..., 2:3] * o_win
        return o.reshape(B, Q_BLOCK, NSA_QHEADS * NSA_HD)

    out = lax.map(block, jnp.arange(S // Q_BLOCK))
    return out.transpose(1, 0, 2, 3).reshape(B, S, BR_WIDTH)


def hybrid_mixer(h, positions, w_in, conv_w, lb, gnorm, cmp_pe, cmp_w1, cmp_w2, w_branch, w_out):
    B, S, _ = h.shape
    u = h @ w_in
    split_pts = [int(v) for v in np.cumsum(IN_SIZES)[:-1]]
    (a_b, a_c, a_x, hq, hf, hi, hg, nq, nkc, nvc, nks, nvs, nkw, nvw, ngate, mgate) = jnp.split(u, split_pts, axis=-1)
    y_a = short_conv_mixer(a_b, a_c, a_x, conv_w)
    y_b = hgrn2_mixer(hq, hf, hi, hg, lb, gnorm)
    y_c = nsa_mixer(nq, nkc, nvc, nks, nvs, nkw, nvw, ngate, positions, cmp_pe, cmp_w1, cmp_w2)
    br = jnp.stack([y_a, y_b.astype(y_a.dtype), y_c.astype(y_a.dtype)], axis=2)
    proj = jnp.einsum('bsnw,nwd->bsnd', br, w_branch)
    g = jax.nn.sigmoid(mgate.reshape(B, S, N_BRANCH, D_MODEL))
    return jnp.sum(g * proj, axis=2) @ w_out


def setup_inputs(seed: int = 0) -> dict:
    key = jax.random.key(seed)
    ks = jax.random.split(key, 16)
    nrm = lambda k, shape, sc: jax.random.normal(k, shape, jnp.float32) * sc
    x = nrm(ks[0], (BATCH, SEQ, D_MODEL), 1.0)
    offset = jax.random.randint(ks[1], (BATCH, 1), 0, 1024, dtype=jnp.int32)
    positions = offset + jnp.arange(SEQ, dtype=jnp.int32)[None, :]
    hgrn_lb_logits = nrm(ks[2], (DEPTH, BR_WIDTH), 0.1)
    norm_gains = 1.0 + nrm(ks[3], (DEPTH, N_NORMS, D_MODEL), 0.05)
    w_ffn_gate = nrm(ks[4], (DEPTH, N_FFN, D_MODEL, D_FF), D_MODEL ** -0.5)
    w_ffn_up = nrm(ks[5], (DEPTH, N_FFN, D_MODEL, D_FF), D_MODEL ** -0.5)
    w_ffn_down = nrm(ks[6], (DEPTH, N_FFN, D_FF, D_MODEL), D_FF ** -0.5)
    w_in = nrm(ks[7], (DEPTH, D_MODEL, IN_COLS), D_MODEL ** -0.5)
    conv_w = nrm(ks[8], (DEPTH, CONV_WIDTH, BR_WIDTH), CONV_WIDTH ** -0.5)
    hgrn_gnorm = 1.0 + nrm(ks[9], (DEPTH, HG_DV), 0.05)
    cmp_pe = nrm(ks[10], (DEPTH, 2, CMP_BLOCK, NSA_HD), 0.1)
    cmp_w1 = nrm(ks[11], (DEPTH, 2, CMP_BLOCK * NSA_HD, CMP_HIDDEN), (CMP_BLOCK * NSA_HD) ** -0.5)
    cmp_w2 = nrm(ks[12], (DEPTH, 2, CMP_HIDDEN, NSA_HD), CMP_HIDDEN ** -0.5)
    w_branch = nrm(ks[13], (DEPTH, N_BRANCH, BR_WIDTH, D_MODEL), BR_WIDTH ** -0.5)
    w_out = nrm(ks[14], (DEPTH, D_MODEL, D_MODEL), D_MODEL ** -0.5)
    return {'x': x, 'positions': positions, 'hgrn_lb_logits': hgrn_lb_logits, 'norm_gains': norm_gains,
            'w_ffn_gate': w_ffn_gate, 'w_ffn_up': w_ffn_up, 'w_ffn_down': w_ffn_down, 'w_in': w_in,
            'conv_w': conv_w, 'hgrn_gnorm': hgrn_gnorm, 'cmp_pe': cmp_pe, 'cmp_w1': cmp_w1,
            'cmp_w2': cmp_w2, 'w_branch': w_branch, 'w_out': w_out}


def reference(x, positions, hgrn_lb_logits, norm_gains, w_ffn_gate, w_ffn_up, w_ffn_down, w_in,
              conv_w, hgrn_gnorm, cmp_pe, cmp_w1, cmp_w2, w_branch, w_out):
    lbp = jax.nn.softmax(hgrn_lb_logits.astype(jnp.float32), axis=0)
    lower_bounds = jnp.cumsum(lbp, axis=0) - lbp[0]
    for l in range(DEPTH):
        g = norm_gains[l]
        h = rmsnorm(x, g[0])
        x = x + 0.5 * rmsnorm(swiglu(h, w_ffn_gate[l, 0], w_ffn_up[l, 0], w_ffn_down[l, 0]), g[1])
        h = rmsnorm(x, g[2])
        x = x + rmsnorm(hybrid_mixer(h, positions, w_in[l], conv_w[l], lower_bounds[l], hgrn_gnorm[l],
                                     cmp_pe[l], cmp_w1[l], cmp_w2[l], w_branch[l], w_out[l]), g[3])
        h = rmsnorm(x, g[4])
        x = x + 0.5 * rmsnorm(swiglu(h, w_ffn_gate[l, 1], w_ffn_up[l, 1], w_ffn_down[l, 1]), g[5])
    return x
```

```python
import contextlib
import numpy as np
import concourse.bass as bass
import concourse.mybir as mybir

F32 = mybir.dt.float32
BF16 = mybir.dt.bfloat16
I32 = mybir.dt.int32
ALU = mybir.AluOpType
AF = mybir.ActivationFunctionType
AX = mybir.AxisListType

ENGS = ("pe", "act", "dve", "pool", "sp")


class Buf:
    def __init__(self, prog, t, name, tracked=True):
        self.prog = prog
        self.t = t
        self.name = name
        self.tracked = tracked
        self.last_w = None
        self.readers = {}
        self.wsem = None
        self.wcnt = 0
        self.rsem = None
        self.rcnt = 0

    def __getitem__(self, idx):
        return self.t[idx]

    @property
    def ap(self):
        return self.t


class Prog:
    def __init__(self, same_engine_sync=True):
        self.nc = bass.Bass("TRN2", target_bir_lowering=False)
        self.es = contextlib.ExitStack()
        self.sem_es = contextlib.ExitStack()
        self.streams = {e: [] for e in ENGS}
        self.cnt = {e: 0 for e in ENGS}
        self.esem = {}
        for e in ENGS:
            self.esem[e] = self.sem_es.enter_context(self.nc.semaphore("s_" + e))
        self.seen = {e: {} for e in ENGS}
        self.same_engine_sync = same_engine_sync
        self.dma_sems = []
        self.nbuf = 0
        self.n_sems = 5
        self.all_bufs = []
        self.free_sems = []

    def dram(self, name, shape, dtype, kind):
        t = self.nc.dram_tensor(name, list(shape), dtype, kind=kind)
        return t.ap()

    def sb(self, shape, dtype, name=None):
        self.nbuf += 1
        name = (name or "b") + "_%d" % self.nbuf
        t = self.es.enter_context(self.nc.sbuf_tensor(name, list(shape), dtype))
        b = Buf(self, t, name)
        self.all_bufs.append(b)
        return b

    def ps(self, shape, dtype=F32, name=None):
        self.nbuf += 1
        name = (name or "p") + "_%d" % self.nbuf
        t = self.es.enter_context(self.nc.psum_tensor(name, list(shape), dtype))
        b = Buf(self, t, name)
        self.all_bufs.append(b)
        return b

    def _sem(self, name):
        if self.free_sems:
            return self.free_sems.pop()
        self.n_sems += 1
        return (self.sem_es.enter_context(self.nc.semaphore(name)), 0)

    def _collect(self, eng, reads, writes, no_waw=False):
        need = {}

        def add(tok):
            if tok is None:
                return
            key, val, e = tok
            if e == eng and (eng == "pe" or not self.same_engine_sync):
                return
            if need.get(key, (0,))[0] < val:
                need[key] = (val, e)

        for b in reads:
            if b is None or not b.tracked:
                continue
            add(b.last_w)
        for b in writes:
            if b is None or not b.tracked:
                continue
            if not no_waw:
                add(b.last_w)
            for key, (val, e) in b.readers.items():
                add((key, val, e))
        out = []
        for key, (val, e) in need.items():
            if self.seen[eng].get(key, 0) >= val:
                continue
            self.seen[eng][key] = val
            out.append((key, val))
        return out

    def _record(self, tok, reads, writes):
        key, val, e = tok
        for b in writes:
            if b is None or not b.tracked:
                continue
            b.last_w = tok
            b.readers = {}
        for b in reads:
            if b is None or not b.tracked:
                continue
            if b in writes:
                continue
            b.readers[key] = (val, e)

    def op(self, eng, fn, reads=(), writes=(), no_waw=False):
        waits = self._collect(eng, reads, writes, no_waw)
        st = self.streams[eng]
        for key, val in waits:
            st.append(("w", key, val))
        self.cnt[eng] += 1
        st.append(("o", fn, self.esem[eng], 1))
        tok = (self.esem[eng], self.cnt[eng], eng)
        self._record(tok, reads, writes)
        return tok

    def dma(self, queue, out_ap, in_ap, dst=None, src=None, no_waw=False, **kw):
        reads = [src] if src is not None else []
        writes = [dst] if dst is not None else []
        waits = self._collect(queue, reads, writes, no_waw)
        st = self.streams[queue]
        for key, val in waits:
            st.append(("w", key, val))
        if dst is not None:
            if dst.wsem is None:
                dst.wsem, dst.wcnt = self._sem("w_" + dst.name)
                self.dma_sems.append(dst)
            dst.wcnt += 16
            sem, val = dst.wsem, dst.wcnt
        else:
            if src.rsem is None:
                src.rsem, src.rcnt = self._sem("r_" + src.name)
                self.dma_sems.append(src)
            src.rcnt += 16
            sem, val = src.rsem, src.rcnt

        def fn(e, out_ap=out_ap, in_ap=in_ap, kw=kw):
            return e.dma_start(out=out_ap, in_=in_ap, **kw)

        st.append(("o", fn, sem, 16))
        tok = (sem, val, "dma")
        self._record(tok, reads, writes)
        return tok

    @contextlib.contextmanager
    def scope(self):
        outer = self.es
        self.es = contextlib.ExitStack()
        n0 = len(self.all_bufs)
        try:
            yield
        finally:
            self.barrier()
            self.flush()
            for b in self.all_bufs[n0:]:
                if b.wsem is not None:
                    self.free_sems.append((b.wsem, b.wcnt))
                if b.rsem is not None:
                    self.free_sems.append((b.rsem, b.rcnt))
                if b in self.dma_sems:
                    self.dma_sems.remove(b)
                b.dead = True
            del self.all_bufs[n0:]
            self.es.close()
            self.es = outer

    def barrier(self):
        targets = [(self.esem[e], self.cnt[e]) for e in ENGS if self.cnt[e] > 0]
        for b in self.dma_sems:
            if b.wsem is not None and b.wcnt:
                targets.append((b.wsem, b.wcnt))
            if b.rsem is not None and b.rcnt:
                targets.append((b.rsem, b.rcnt))
        for e in ENGS:
            for key, val in targets:
                if key is self.esem[e]:
                    continue
                if self.seen[e].get(key, 0) >= val:
                    continue
                self.seen[e][key] = val
                self.streams[e].append(("w", key, val))

    def flush(self):
        nc = self.nc
        streams = self.streams
        self.streams = {e: [] for e in ENGS}

        def run(stream, e):
            for it in stream:
                if it[0] == "w":
                    e.wait_ge(it[1], it[2])
                else:
                    ins = it[1](e)
                    ins.then_inc(it[2], it[3])

        with nc.Block() as block:
            @block.tensor
            def _(e):
                run(streams["pe"], e)

            @block.scalar
            def _(e):
                run(streams["act"], e)

            @block.vector
            def _(e):
                run(streams["dve"], e)

            @block.gpsimd
            def _(e):
                run(streams["pool"], e)

            @block.sync
            def _(e):
                run(streams["sp"], e)

    def finish(self):
        targets = [(self.esem[e], self.cnt[e]) for e in ENGS if self.cnt[e] > 0 and e != "sp"]
        for b in self.dma_sems:
            if b.wsem is not None and b.wcnt:
                targets.append((b.wsem, b.wcnt))
            if b.rsem is not None and b.rcnt:
                targets.append((b.rsem, b.rcnt))
        for key, val in targets:
            self.streams["sp"].append(("w", key, val))
        self.flush()
        self.es.close()
        self.sem_es.close()
        return self.nc

    def stats(self):
        return {e: self.cnt[e] for e in ENGS}


import numpy as np

D = 1024
DFF = 2816
NJ = 22
TG = 256
EPS = 1e-6


def load_w_cast(P, dram_ap, rows, cols, colblk, name):
    kc = rows // 128
    src = dram_ap.rearrange("(c p) n -> p c n", p=128)
    blocks = []
    c0 = 0
    while c0 < cols:
        c1 = min(cols, c0 + colblk)
        b = P.sb([128, kc, c1 - c0], BF16, name)
        P.dma("pool", b[:, :, :], src[:, :, c0:c1], dst=b)
        blocks.append((b, c0, c1))
        c0 = c1
    return blocks


def wslice(blocks, k, c0, c1):
    for b, b0, b1 in blocks:
        if b0 <= c0 and c1 <= b1:
            return b, b[:, k, c0 - b0:c1 - b0]
    raise ValueError((c0, c1))


class Consts:
    pass


def rms_stats(P, C, src_buf, src_ap, from_psum):
    ss = P.sb([128, 1], F32, "ss")
    if from_psum:
        P.op("act", lambda e: e.activation(out=C.junk[:, :], in_=src_ap, func=AF.Square, accum_out=ss[:, :]),
             reads=[src_buf], writes=[C.junk, ss])
    else:
        P.op("dve", lambda e: e.scalar_tensor_tensor(out=C.junk[:, :], in0=src_ap, scalar=1.0, in1=src_ap,
                                                      op0=ALU.mult, op1=ALU.mult, accum_out=ss[:, :]),
             reads=[src_buf], writes=[C.junk, ss])
    ms = P.sb([128, 1], F32, "ms")
    P.op("dve", lambda e: e.tensor_scalar(out=ms[:, :], in0=ss[:, :], scalar1=1.0 / D, scalar2=EPS,
                                          op0=ALU.mult, op1=ALU.add), reads=[ss], writes=[ms])
    rstd = P.sb([128, 1], F32, "rstd")
    P.op("pool", lambda e: e.tensor_tensor(out=rstd[:, :], in0=ms[:, :], in1=C.mhalf[:, :], op=ALU.pow),
         reads=[ms, C.mhalf], writes=[rstd])
    return rstd


def ffn_phase(P, C, ntiles, get_x, put_x, g_pre, g_post, wg, wu, wd):
    ngroups = ntiles // 2
    hT = [P.sb([128, 8, TG], BF16, "hT") for _ in range(2)]
    aT = P.sb([128, NJ, TG], BF16, "aT")
    tp = P.ps([128, 8 * 128], BF16, "tp")
    gu = [P.ps([128, 2, TG], F32, "gu") for _ in range(2)]
    ys = [P.ps([128, D], F32, "y") for _ in range(2)]
    hb = [P.sb([128, D], BF16, "h") for _ in range(2)]
    sg = [P.sb([128, TG], BF16, "sg") for _ in range(2)]
    xs = {}

    def prep(g):
        for t in range(2):
            i = 2 * g + t
            xb = get_x(i)
            xs[i] = xb
            rstd = rms_stats(P, C, xb, xb[:, :], False)
            h = hb[t]
            P.op("dve", lambda e, h=h, xb=xb, rstd=rstd: e.scalar_tensor_tensor(
                out=h[:, :], in0=xb[:, :], scalar=rstd[:, :], in1=g_pre[:, :], op0=ALU.mult, op1=ALU.mult),
                reads=[xb, rstd, g_pre], writes=[h])

    def transposes(g):
        for t in range(2):
            h = hb[t]
            for k in range(8):
                P.op("pe", lambda e, h=h, k=k: e.transpose(out=tp[:, k * 128:(k + 1) * 128],
                                                          in_=h[:, k * 128:(k + 1) * 128], identity=C.ident[:, :]),
                     reads=[h, C.ident], writes=[tp])
            dst = hT[g % 2]
            P.op("act", lambda e, dst=dst, t=t: e.activation(
                out=dst[:, :, t * 128:(t + 1) * 128], in_=tp[:, :].rearrange("p (k n) -> p k n", k=8), func=AF.Copy),
                reads=[tp], writes=[dst])

    def phaseA(g):
        h_t = hT[g % 2]
        for j in range(NJ):
            pg = gu[j % 2]
            for which, W in ((0, wg), (1, wu)):
                for k in range(8):
                    wb, wap = wslice(W, k, j * 128, (j + 1) * 128)
                    P.op("pe", lambda e, pg=pg, which=which, wap=wap, k=k: e.matmul(
                        pg[:, which, :], lhsT=wap, rhs=h_t[:, k, :], start=(k == 0), stop=(k == 7)),
                        reads=[wb, h_t], writes=[pg])
            s = sg[j % 2]
            P.op("act", lambda e, s=s, pg=pg: e.activation(out=s[:, :], in_=pg[:, 0, :], func=AF.Silu),
                 reads=[pg], writes=[s])
            P.op("dve", lambda e, s=s, pg=pg, j=j: e.tensor_tensor(out=aT[:, j, :], in0=s[:, :], in1=pg[:, 1, :],
                                                                   op=ALU.mult),
                 reads=[s, pg], writes=[aT])

    def phaseB(g):
        for t in range(2):
            y = ys[t]
            for half in range(2):
                for j in range(NJ):
                    wb, wap = wslice(wd, j, half * 512, (half + 1) * 512)
                    P.op("pe", lambda e, y=y, half=half, wap=wap, j=j, t=t: e.matmul(
                        y[:, half * 512:(half + 1) * 512], lhsT=aT[:, j, t * 128:(t + 1) * 128], rhs=wap,
                        start=(j == 0), stop=(j == NJ - 1)),
                        reads=[wb, aT], writes=[y])

    def post(g):
        for t in range(2):
            i = 2 * g + t
            y = ys[t]
            rstd = rms_stats(P, C, y, y[:, :], True)
            t1 = P.sb([128, D], F32, "t1") if not hasattr(C, "t1") else C.t1
            C.t1 = t1
            P.op("dve", lambda e, y=y, rstd=rstd: e.scalar_tensor_tensor(
                out=t1[:, :], in0=y[:, :], scalar=rstd[:, :], in1=g_post[:, :], op0=ALU.mult, op1=ALU.mult),
                reads=[y, rstd, g_post], writes=[t1])
            xb = xs.pop(i)
            P.op("dve", lambda e, xb=xb: e.scalar_tensor_tensor(
                out=xb[:, :], in0=t1[:, :], scalar=0.5, in1=xb[:, :], op0=ALU.mult, op1=ALU.add),
                reads=[t1, xb], writes=[xb])
            put_x(i, xb)

    prep(0)
    transposes(0)
    for g in range(ngroups):
        phaseA(g)
        if g + 1 < ngroups:
            prep(g + 1)
            transposes(g + 1)
        phaseB(g)
        post(g)


def build_consts(P, ident_d):
    C = Consts()
    C.ident = P.sb([128, 128], BF16, "ident")
    P.dma("pool", C.ident[:, :], ident_d[:, :], dst=C.ident)
    C.junk = P.sb([128, D], BF16, "junk")
    C.junk.tracked = False
    C.mhalf = P.sb([128, 1], F32, "mhalf")
    P.op("pool", lambda e: e.memset(C.mhalf[:, :], -0.5), writes=[C.mhalf])
    return C


def build_test_ffn(NT):
    P = Prog()
    x_d = P.dram("x", [NT, D], F32, "ExternalInput")
    gains_d = P.dram("gains", [2, 128, D], F32, "ExternalInput")
    wg_d = P.dram("wg", [D, DFF], F32, "ExternalInput")
    wu_d = P.dram("wu", [D, DFF], F32, "ExternalInput")
    wd_d = P.dram("wd", [DFF, D], F32, "ExternalInput")
    ident_d = P.dram("ident", [128, 128], F32, "ExternalInput")
    out_d = P.dram("out", [NT, D], F32, "ExternalOutput")
    C = build_consts(P, ident_d)
    g_pre = P.sb([128, D], F32, "gpre")
    g_post = P.sb([128, D], F32, "gpost")
    P.dma("sp", g_pre[:, :], gains_d[0], dst=g_pre)
    P.dma("sp", g_post[:, :], gains_d[1], dst=g_post)
    wg = load_w_cast(P, wg_d, D, DFF, 512, "wg")
    wu = load_w_cast(P, wu_d, D, DFF, 512, "wu")
    wd = load_w_cast(P, wd_d, DFF, D, 512, "wd")
    xpool = [P.sb([128, D], F32, "x") for _ in range(4)]

    def get_x(i):
        b = xpool[i % 4]
        P.dma("sp", b[:, :], x_d[i * 128:(i + 1) * 128, :], dst=b)
        return b

    def put_x(i, b):
        P.dma("sp", out_d[i * 128:(i + 1) * 128, :], b[:, :], src=b)

    ffn_phase(P, C, NT // 128, get_x, put_x, g_pre, g_post, wg, wu, wd)
    print("streams", P.stats(), "sems", P.n_sems)
    return P.finish()


import numpy as np

EPS = 1e-6


def hgrn_consts_np():
    s = np.arange(128)
    same = (s[:, None] // 64) == (s[None, :] // 64)
    M1 = (same & (s[:, None] <= s[None, :])).astype(np.float32)
    M2 = (same & (s[:, None] > s[None, :])).astype(np.float32)
    return M1, M2


def hgrn_phase(P, S, d):
    nt = S // 128
    M1 = P.sb([128, 128], F32, "M1")
    M2 = P.sb([128, 128], F32, "M2")
    P.dma("sp", M1[:, :], d["M1"][:, :], dst=M1)
    P.dma("sp", M2[:, :], d["M2"][:, :], dst=M2)
    gn = P.sb([128, 128], F32, "gn")
    P.dma("sp", gn[:, :], d["gnorm"][:, :], dst=gn)
    mhalf = P.sb([128, 1], F32, "mhalf")
    P.op("pool", lambda e: e.memset(mhalf[:, :], -0.5), writes=[mhalf])
    lbl = P.sb([128, 4], F32, "lbl")
    lmk = P.sb([128, 4], F32, "lmk")
    P.dma("sp", lbl[:, :], d["lbT"][:, :], dst=lbl)
    P.dma("sp", lmk[:, :], d["lmask"][:, :], dst=lmk)
    ex = P.sb([128, 4], F32, "ex")
    P.op("act", lambda e: e.activation(out=ex[:, :], in_=lbl[:, :], func=AF.Exp), reads=[lbl], writes=[ex])
    den = P.sb([128, 1], F32, "den")
    P.op("dve", lambda e: e.reduce_sum(out=den[:, :], in_=ex[:, :], axis=AX.X), reads=[ex], writes=[den])
    rden = P.sb([128, 1], F32, "rden")
    P.op("dve", lambda e: e.reciprocal(out=rden[:, :], in_=den[:, :]), reads=[den], writes=[rden])
    exm = P.sb([128, 4], F32, "exm")
    P.op("dve", lambda e: e.tensor_tensor(out=exm[:, :], in0=ex[:, :], in1=lmk[:, :], op=ALU.mult),
         reads=[ex, lmk], writes=[exm])
    num = P.sb([128, 1], F32, "num")
    P.op("dve", lambda e: e.reduce_sum(out=num[:, :], in_=exm[:, :], axis=AX.X), reads=[exm], writes=[num])
    lbc = P.sb([128, 1], F32, "lbc")
    P.op("dve", lambda e: e.tensor_tensor(out=lbc[:, :], in0=num[:, :], in1=rden[:, :], op=ALU.mult),
         reads=[num, rden], writes=[lbc])
    omlc = P.sb([128, 1], F32, "omlc")
    P.op("dve", lambda e: e.tensor_scalar(out=omlc[:, :], in0=lbc[:, :], scalar1=-1.0, scalar2=1.0,
                                          op0=ALU.mult, op1=ALU.add), reads=[lbc], writes=[omlc])
    nomlc = P.sb([128, 1], F32, "nomlc")
    P.op("dve", lambda e: e.tensor_scalar(out=nomlc[:, :], in0=omlc[:, :], scalar1=-1.0, scalar2=None,
                                          op0=ALU.mult), reads=[omlc], writes=[nomlc])
    lbr = P.sb([128, 4, 128], F32, "lbr")
    lmr = P.sb([128, 4, 128], F32, "lmr")
    P.dma("sp", lbr[:, :, :], d["lbrow"][:, :, :], dst=lbr)
    P.dma("sp", lmr[:, :, :], d["lmaskrow"][:, :, :], dst=lmr)
    exr = P.sb([128, 4, 128], F32, "exr")
    P.op("act", lambda e: e.activation(out=exr[:, :, :], in_=lbr[:, :, :], func=AF.Exp), reads=[lbr], writes=[exr])
    denr = P.sb([128, 128], F32, "denr")
    P.op("dve", lambda e: e.tensor_tensor(out=denr[:, :], in0=exr[:, 0, :], in1=exr[:, 1, :], op=ALU.add),
         reads=[exr], writes=[denr])
    P.op("dve", lambda e: e.tensor_tensor(out=denr[:, :], in0=denr[:, :], in1=exr[:, 2, :], op=ALU.add),
         reads=[exr, denr], writes=[denr])
    P.op("dve", lambda e: e.tensor_tensor(out=denr[:, :], in0=denr[:, :], in1=exr[:, 3, :], op=ALU.add),
         reads=[exr, denr], writes=[denr])
    P.op("dve", lambda e: e.reciprocal(out=denr[:, :], in_=denr[:, :]), reads=[denr], writes=[denr])
    P.op("dve", lambda e: e.tensor_tensor(out=exr[:, :, :], in0=exr[:, :, :], in1=lmr[:, :, :], op=ALU.mult),
         reads=[exr, lmr], writes=[exr])
    lbrow = P.sb([128, 128], F32, "lbrow")
    P.op("dve", lambda e: e.tensor_tensor(out=lbrow[:, :], in0=exr[:, 0, :], in1=exr[:, 1, :], op=ALU.add),
         reads=[exr], writes=[lbrow])
    P.op("dve", lambda e: e.tensor_tensor(out=lbrow[:, :], in0=lbrow[:, :], in1=exr[:, 2, :], op=ALU.add),
         reads=[exr, lbrow], writes=[lbrow])
    P.op("dve", lambda e: e.tensor_tensor(out=lbrow[:, :], in0=lbrow[:, :], in1=exr[:, 3, :], op=ALU.add),
         reads=[exr, lbrow], writes=[lbrow])
    P.op("dve", lambda e: e.tensor_tensor(out=lbrow[:, :], in0=lbrow[:, :], in1=denr[:, :], op=ALU.mult),
         reads=[lbrow, denr], writes=[lbrow])
    omlrow = P.sb([128, 128], F32, "omlrow")
    P.op("dve", lambda e: e.tensor_scalar(out=omlrow[:, :], in0=lbrow[:, :], scalar1=-1.0, scalar2=1.0,
                                          op0=ALU.mult, op1=ALU.add), reads=[lbrow], writes=[omlrow])

    NB = 2
    qT = [P.sb([128, 128], F32, "qT") for _ in range(NB)]
    zfT = [P.sb([128, 128], F32, "zfT") for _ in range(NB)]
    zft = [P.sb([128, 128], F32, "zft") for _ in range(NB)]
    vt = [P.sb([128, 128], F32, "vt") for _ in range(NB)]
    gt = [P.sb([128, 128], F32, "gt") for _ in range(NB)]
    vb = [P.sb([128, 128], BF16, "vb") for _ in range(NB)]
    sigt = P.sb([128, 128], F32, "sigt")
    logf = [P.sb([128, 128], F32, "logf") for _ in range(NB)]
    kt = P.sb([128, 128], F32, "kt")
    sigf = P.sb([128, 128], F32, "sigf")
    kT = P.sb([128, 128], F32, "kT")
    dd = P.sb([128, 128], F32, "dd")
    eq = P.sb([128, 128], F32, "eq")
    ek = P.sb([128, 128], F32, "ek")
    eb = P.sb([128, 128], F32, "eb")
    er = P.sb([128, 128], F32, "er")
    qtl = [P.sb([128, 128], BF16, "qtl") for _ in range(NB)]
    ktl = [P.sb([128, 128], BF16, "ktl") for _ in range(NB)]
    Q0 = [P.sb([128, 128], BF16, "Q0") for _ in range(NB)]
    Q1 = [P.sb([128, 128], BF16, "Q1") for _ in range(NB)]
    K0 = [P.sb([128, 128], BF16, "K0") for _ in range(NB)]
    K1 = [P.sb([128, 128], BF16, "K1") for _ in range(NB)]
    for i in range(NB):
        for b in (Q0[i], Q1[i], K0[i], K1[i]):
            P.op("pool", lambda e, b=b: e.memset(b[:, :], 0.0), writes=[b])
    dec = [P.sb([128, 2], F32, "dec") for _ in range(NB)]
    ATm = [P.sb([128, 128], BF16, "ATm") for _ in range(NB)]
    Sf = [P.sb([128, 128], F32, "Sf") for _ in range(2)]
    Sb = [P.sb([128, 128], BF16, "Sb") for _ in range(4)]
    P.op("pool", lambda e: e.memset(Sf[0][:, :], 0.0), writes=[Sf[0]])
    P.op("pool", lambda e: e.memset(Sb[0][:, :], 0.0), writes=[Sb[0]])
    junk = P.sb([128, 128], F32, "junk")
    junk.tracked = False
    sg = P.sb([128, 128], F32, "sg")
    yo = [P.sb([128, 128], F32, "yo") for _ in range(NB)]
    p_bT = P.ps([128, 512], F32, "p_bT")
    p_R = P.ps([128, 512], F32, "p_R")
    p_AT = P.ps([128, 512], F32, "p_AT")
    p_o = [P.ps([128, 512], F32, "p_o") for _ in range(2)]
    p_S = [P.ps([128, 512], F32, "p_S") for _ in range(2)]

    sidx = 0
    for i in range(nt):
        n = i % NB
        ts = slice(i * 128, (i + 1) * 128)
        P.dma("sp", qT[n][:, :], d["qT"][:, ts], dst=qT[n])
        P.dma("sp", zfT[n][:, :], d["zfT"][:, ts], dst=zfT[n])
        P.dma("sp", zft[n][:, :], d["zf"][ts, :], dst=zft[n])
        P.dma("sp", vt[n][:, :], d["v"][ts, :], dst=vt[n])
        P.dma("sp", gt[n][:, :], d["g"][ts, :], dst=gt[n])
        P.op("act", lambda e, n=n: e.activation(out=sigt[:, :], in_=zft[n][:, :], func=AF.Sigmoid),
             reads=[zft[n]], writes=[sigt])
        lf = logf[n]
        P.op("dve", lambda e, lf=lf: e.tensor_tensor(out=lf[:, :], in0=sigt[:, :], in1=omlrow[:, :], op=ALU.mult),
             reads=[sigt, omlrow], writes=[lf])
        P.op("dve", lambda e, lf=lf: e.tensor_tensor(out=lf[:, :], in0=lf[:, :], in1=lbrow[:, :], op=ALU.add),
             reads=[lf, lbrow], writes=[lf])
        P.op("dve", lambda e, lf=lf: e.tensor_scalar(out=lf[:, :], in0=lf[:, :], scalar1=1e-30, scalar2=None,
                                                     op0=ALU.max), reads=[lf], writes=[lf])
        P.op("act", lambda e, lf=lf: e.activation(out=lf[:, :], in_=lf[:, :], func=AF.Ln), reads=[lf], writes=[lf])
        P.op("dve", lambda e: e.tensor_tensor(out=kt[:, :], in0=sigt[:, :], in1=omlrow[:, :], op=ALU.mult),
             reads=[sigt, omlrow], writes=[kt])
        P.op("dve", lambda e: e.tensor_tensor(out=kt[:, :], in0=omlrow[:, :], in1=kt[:, :], op=ALU.subtract),
             reads=[kt, omlrow], writes=[kt])
        P.op("pe", lambda e, lf=lf: e.matmul(p_bT[:, 0:128], lhsT=lf[:, :], rhs=M1[:, :], start=True, stop=True),
             reads=[lf, M1], writes=[p_bT])
        P.op("pe", lambda e, lf=lf: e.matmul(p_R[:, 0:128], lhsT=M2[:, :], rhs=lf[:, :], start=True, stop=True),
             reads=[lf, M2], writes=[p_R])
        P.op("act", lambda e, n=n: e.activation(out=sigf[:, :], in_=zfT[n][:, :], func=AF.Sigmoid),
             reads=[zfT[n]], writes=[sigf])
        P.op("dve", lambda e: e.tensor_scalar(out=kT[:, :], in0=sigf[:, :], scalar1=nomlc[:, :], scalar2=omlc[:, :],
                                              op0=ALU.mult, op1=ALU.add), reads=[sigf, nomlc, omlc], writes=[kT])
        for c in range(2):
            cs = slice(c * 64, (c + 1) * 64)
            P.op("dve", lambda e, cs=cs, c=c: e.tensor_scalar(
                out=dd[:, cs], in0=p_bT[:, cs], scalar1=p_bT[:, c * 64 + 31:c * 64 + 32], scalar2=None,
                op0=ALU.subtract), reads=[p_bT], writes=[dd])
        P.op("dve", lambda e: e.tensor_scalar(out=eq[:, :], in0=dd[:, :], scalar1=43.0, scalar2=None, op0=ALU.min),
             reads=[dd], writes=[eq])
        P.op("dve", lambda e: e.tensor_scalar(out=ek[:, :], in0=dd[:, :], scalar1=-1.0, scalar2=43.0,
                                              op0=ALU.mult, op1=ALU.min), reads=[dd], writes=[ek])
        P.op("act", lambda e: e.activation(out=eq[:, :], in_=eq[:, :], func=AF.Exp), reads=[eq], writes=[eq])
        P.op("act", lambda e: e.activation(out=ek[:, :], in_=ek[:, :], func=AF.Exp), reads=[ek], writes=[ek])
        P.op("act", lambda e: e.activation(out=eb[:, :], in_=p_bT[:, 0:128], func=AF.Exp), reads=[p_bT], writes=[eb])
        P.op("act", lambda e: e.activation(out=er[:, :], in_=p_R[:, 0:128], func=AF.Exp), reads=[p_R], writes=[er])
        dc = dec[n]
        P.op("dve", lambda e, dc=dc: e.tensor_copy(out=dc[:, :], in_=eb[:, :].rearrange("p (c s) -> p c s", c=2)[:, :, 63]),
             reads=[eb], writes=[dc])
        P.op("dve", lambda e, n=n: e.tensor_tensor(out=qtl[n][:, :], in0=qT[n][:, :], in1=eq[:, :], op=ALU.mult),
             reads=[qT[n], eq], writes=[qtl[n]])
        P.op("dve", lambda e, n=n: e.tensor_tensor(out=ktl[n][:, :], in0=kT[:, :], in1=ek[:, :], op=ALU.mult),
             reads=[kT, ek], writes=[ktl[n]])
        P.op("dve", lambda e, n=n: e.tensor_tensor(out=Q0[n][:, 0:64], in0=qT[n][:, 0:64], in1=eb[:, 0:64], op=ALU.mult),
             reads=[qT[n], eb], writes=[Q0[n]])
        P.op("dve", lambda e, n=n: e.tensor_tensor(out=Q1[n][:, 64:128], in0=qT[n][:, 64:128], in1=eb[:, 64:128],
                                                   op=ALU.mult), reads=[qT[n], eb], writes=[Q1[n]])
        P.op("dve", lambda e, n=n: e.tensor_tensor(out=K0[n][0:64, :], in0=kt[0:64, :], in1=er[0:64, :], op=ALU.mult),
             reads=[kt, er], writes=[K0[n]])
        P.op("dve", lambda e, n=n: e.tensor_tensor(out=K1[n][64:128, :], in0=kt[64:128, :], in1=er[64:128, :],
                                                   op=ALU.mult), reads=[kt, er], writes=[K1[n]])
        P.op("pool", lambda e, n=n: e.tensor_copy(out=vb[n][:, :], in_=vt[n][:, :]), reads=[vt[n]], writes=[vb[n]])
        P.op("pe", lambda e, n=n: e.matmul(p_AT[:, 0:128], lhsT=ktl[n][:, :], rhs=qtl[n][:, :], start=True, stop=True),
             reads=[ktl[n], qtl[n]], writes=[p_AT])
        P.op("dve", lambda e, n=n: e.tensor_tensor(out=ATm[n][:, :], in0=p_AT[:, 0:128], in1=M1[:, :], op=ALU.mult),
             reads=[p_AT, M1], writes=[ATm[n]])
        po = p_o[i % 2]
        s0 = sidx
        P.op("pe", lambda e, n=n, po=po: e.matmul(po[:, 0:128], lhsT=ATm[n][:, :], rhs=vb[n][:, :], start=True, stop=False),
             reads=[ATm[n], vb[n]], writes=[po])
        P.op("pe", lambda e, n=n, po=po, s0=s0: e.matmul(po[:, 0:128], lhsT=Q0[n][:, :], rhs=Sb[s0 % 4][:, :],
                                                          start=False, stop=False),
             reads=[Q0[n], Sb[s0 % 4]], writes=[po])
        for c in range(2):
            Kc = (K0, K1)[c][n]
            pS = p_S[c]
            P.op("pe", lambda e, Kc=Kc, pS=pS, n=n: e.matmul(pS[:, 0:128], lhsT=Kc[:, :], rhs=vb[n][:, :],
                                                             start=True, stop=True),
                 reads=[Kc, vb[n]], writes=[pS])
            so, sn = Sf[sidx % 2], Sf[(sidx + 1) % 2]
            P.op("dve", lambda e, so=so, sn=sn, pS=pS, dc=dc, c=c: e.scalar_tensor_tensor(
                out=sn[:, :], in0=so[:, :], scalar=dc[:, c:c + 1], in1=pS[:, 0:128], op0=ALU.mult, op1=ALU.add),
                reads=[so, dc, pS], writes=[sn])
            sbn = Sb[(sidx + 1) % 4]
            P.op("act", lambda e, sbn=sbn, sn=sn: e.activation(out=sbn[:, :], in_=sn[:, :], func=AF.Copy),
                 reads=[sn], writes=[sbn])
            sidx += 1
        P.op("pe", lambda e, n=n, po=po, s0=s0: e.matmul(po[:, 0:128], lhsT=Q1[n][:, :], rhs=Sb[(s0 + 1) % 4][:, :],
                                                          start=False, stop=True),
             reads=[Q1[n], Sb[(s0 + 1) % 4]], writes=[po])
        ss = P.sb([128, 1], F32, "ss")
        P.op("act", lambda e, po=po, ss=ss: e.activation(out=junk[:, :], in_=po[:, 0:128], func=AF.Square, accum_out=ss[:, :]),
             reads=[po], writes=[ss])
        ms = P.sb([128, 1], F32, "ms")
        P.op("dve", lambda e, ss=ss, ms=ms: e.tensor_scalar(out=ms[:, :], in0=ss[:, :], scalar1=1.0 / 128, scalar2=EPS,
                                                            op0=ALU.mult, op1=ALU.add), reads=[ss], writes=[ms])
        rstd = P.sb([128, 1], F32, "rstd")
        P.op("pool", lambda e, ms=ms, rstd=rstd: e.tensor_tensor(out=rstd[:, :], in0=ms[:, :], in1=mhalf[:, :], op=ALU.pow),
             reads=[ms, mhalf], writes=[rstd])
        P.op("act", lambda e, n=n: e.activation(out=sg[:, :], in_=gt[n][:, :], func=AF.Silu), reads=[gt[n]], writes=[sg])
        P.op("dve", lambda e: e.tensor_tensor(out=sg[:, :], in0=sg[:, :], in1=gn[:, :], op=ALU.mult),
             reads=[sg, gn], writes=[sg])
        y = yo[n]
        P.op("dve", lambda e, y=y, po=po, rstd=rstd: e.scalar_tensor_tensor(
            out=y[:, :], in0=po[:, 0:128], scalar=rstd[:, :], in1=sg[:, :], op0=ALU.mult, op1=ALU.mult),
            reads=[po, rstd, sg], writes=[y])
        P.dma("sp", d["y"][ts, :], y[:, :], src=y)


def build_test_hgrn(S):
    P = Prog()
    d = {}
    for nm, shp in (("qT", [128, S]), ("zfT", [128, S]), ("zf", [S, 128]), ("v", [S, 128]), ("g", [S, 128]),
                    ("lbT", [128, 4]), ("lmask", [128, 4]), ("lbrow", [128, 4, 128]), ("lmaskrow", [128, 4, 128]),
                    ("gnorm", [128, 128]), ("M1", [128, 128]), ("M2", [128, 128])):
        d[nm] = P.dram(nm, shp, F32, "ExternalInput")
    d["y"] = P.dram("y", [S, 128], F32, "ExternalOutput")
    hgrn_phase(P, S, d)
    print("streams", P.stats(), "sems", P.n_sems)
    return P.finish()


import numpy as np

BIGF = 1.0e6


def nsa_consts_np(S):
    nsel = S // 64
    ncmp = S // 16 - 1
    ncp = ((ncmp + 127) // 128) * 128
    nch = S // 128
    p = np.arange(128)
    D1 = (p[:, None] - p[None, :]).astype(np.float32)
    D16 = (16 * p[:, None] - p[None, :]).astype(np.float32)
    Mdiag = (D1 <= 0).astype(np.float32)
    Mlow = (D1 > 0).astype(np.float32)
    j = np.arange(nsel)
    E = np.zeros((nsel, nch, 128), np.float32)
    for c in range(nch):
        E[:, c, :] = (j[:, None] == (2 * c + p[None, :] // 64))
    n = np.arange(ncp)
    cs = n * 16
    ss = j * 64
    ov = ((cs[:, None] < ss[None, :] + 64) & (cs[:, None] + 32 > ss[None, :]) & (n[:, None] < ncmp)).astype(np.float32)
    x = np.arange(2 * nsel) - nsel
    cur_rel = (p >= 64).astype(np.int64)
    forced = (x[None, :] == cur_rel[:, None]) | (x[None, :] == cur_rel[:, None] - 1)
    invalid = x[None, :] > cur_rel[:, None]
    Amul = (~forced & ~invalid).astype(np.float32)
    Aadd = np.where(forced, BIGF, np.where(invalid, -BIGF, 0.0)).astype(np.float32)
    ident = np.eye(128, dtype=np.float32)
    return dict(D16=D16, Mdiag=Mdiag, Mlow=Mlow, E=E, ov=ov, Amul=Amul, Aadd=Aadd, identb=ident)


def nsa_phase(P, S, NQB, d):
    nsel = S // 64
    ncmp = S // 16 - 1
    ncp = ((ncmp + 127) // 128) * 128
    ncc = ncp // 128
    nch = S // 128
    W = 65 + nsel

    def cload(name, shape, dt, src, q="pool"):
        b = P.sb(shape, dt, name)
        P.dma(q, b[tuple(slice(None) for _ in shape)], src, dst=b)
        return b

    D16 = cload("D16", [128, 128], F32, d["D16"][:, :], "sp")
    Mdiag = cload("Mdiag", [128, 128], BF16, d["Mdiag"][:, :])
    Mlow = cload("Mlow", [128, 128], BF16, d["Mlow"][:, :])
    Ec = cload("E", [nsel, nch, 128], BF16, d["E"][:, :, :])
    Amul = cload("Amul", [128, 2 * nsel], F32, d["Amul"][:, :], "sp")
    Aadd = cload("Aadd", [128, 2 * nsel], F32, d["Aadd"][:, :], "sp")
    identb = cload("identb", [128, 128], BF16, d["identb"][:, :])
    ksT = cload("ksT", [64, S], BF16, d["ksT"][:, :])
    kwT = cload("kwT", [64, S], BF16, d["kwT"][:, :])
    vs = cload("vs", [128, nch, 65], BF16, d["vs"][:, :, :])
    vw = cload("vw", [128, nch, 65], BF16, d["vw"][:, :, :])
    KcT = P.sb([64, ncp], BF16, "KcT")
    Vc = P.sb([128, ncc, W], BF16, "Vc")
    P.op("pool", lambda e: e.memset(KcT[:, :], 0.0), writes=[KcT])
    P.op("pool", lambda e: e.memset(Vc[:, :, :], 0.0), writes=[Vc])
    P.op("pool", lambda e: e.memset(Vc[:, :, 64:65], 1.0), writes=[Vc])
    P.dma("pool", Vc[:, :, 65:W], d["ov"][:, :, :], dst=Vc)

    ps_h = [P.ps([128, 512], F32, "ps_h") for _ in range(2)]
    ps_b = P.ps([128, 512], F32, "ps_b")
    pOs = P.ps([128, 512], F32, "pOs")
    ps_o = pOs
    for X in range(2):
        x2T = cload("x2T", [128, S], BF16, d["kc2T" if X == 0 else "vc2T"][:, :])
        pe2 = cload("pe2", [128, 16], BF16, d["pe2"][X])
        w1 = cload("w1", [128, 16, 256], BF16, d["w1"][X])
        w2 = cload("w2", [128, 2, 64], BF16, d["w2"][X])
        GT = P.sb([128, 2, ncp], BF16, "GT")
        P.op("pool", lambda e, GT=GT: e.memset(GT[:, :, :], 0.0), writes=[GT])
        x2v = x2T[:, :].rearrange("p (n s) -> p n s", s=16)
        for hc in range(2):
            for j in range(16):
                P.op("pe", lambda e, j=j, hc=hc, w1=w1, pe2=pe2: e.matmul(
                    ps_b[:, 0:1], lhsT=w1[:, j, hc * 128:(hc + 1) * 128], rhs=pe2[:, j:j + 1],
                    start=(j == 0), stop=(j == 15)), reads=[w1, pe2], writes=[ps_b])
            c1 = P.sb([128, 1], F32, "c1")
            P.op("dve", lambda e, c1=c1: e.tensor_copy(out=c1[:, :], in_=ps_b[:, 0:1]), reads=[ps_b], writes=[c1])
            ph = ps_h[hc]
            for j in range(16):
                if j < 8:
                    rhs = x2v[:, 0:ncmp, 2 * j]
                else:
                    rhs = x2v[:, 1:ncmp + 1, 2 * j - 16]
                P.op("pe", lambda e, j=j, hc=hc, w1=w1, rhs=rhs, ph=ph: e.matmul(
                    ph[:, 0:ncmp], lhsT=w1[:, j, hc * 128:(hc + 1) * 128], rhs=rhs,
                    start=(j == 0), stop=(j == 15)), reads=[w1, x2T], writes=[ph])
            xs = P.sb([128, ncp], F32, "xs")
            P.op("act", lambda e, xs=xs, ph=ph, c1=c1: e.activation(out=xs[:, 0:ncmp], in_=ph[:, 0:ncmp],
                                                                     func=AF.Identity, bias=c1[:, :]),
                 reads=[ph, c1], writes=[xs])
            t2 = P.sb([128, ncp], F32, "t2")
            P.op("dve", lambda e, xs=xs, t2=t2: e.tensor_tensor(out=t2[:, 0:ncmp], in0=xs[:, 0:ncmp], in1=xs[:, 0:ncmp],
                                                                op=ALU.mult), reads=[xs], writes=[t2])
            P.op("dve", lambda e, t2=t2: e.tensor_scalar(out=t2[:, 0:ncmp], in0=t2[:, 0:ncmp], scalar1=0.044715,
                                                         scalar2=1.0, op0=ALU.mult, op1=ALU.add), reads=[t2], writes=[t2])
            P.op("dve", lambda e, xs=xs, t2=t2: e.tensor_tensor(out=t2[:, 0:ncmp], in0=t2[:, 0:ncmp], in1=xs[:, 0:ncmp],
                                                                op=ALU.mult), reads=[xs, t2], writes=[t2])
            P.op("act", lambda e, t2=t2: e.activation(out=t2[:, 0:ncmp], in_=t2[:, 0:ncmp], func=AF.Sigmoid,
                                                      scale=1.5957691216057308), reads=[t2], writes=[t2])
            P.op("dve", lambda e, xs=xs, t2=t2, GT=GT, hc=hc: e.tensor_tensor(
                out=GT[:, hc, 0:ncmp], in0=t2[:, 0:ncmp], in1=xs[:, 0:ncmp], op=ALU.mult), reads=[xs, t2], writes=[GT])
        if X == 0:
            for hc in range(2):
                P.op("pe", lambda e, hc=hc, w2=w2, GT=GT: e.matmul(ps_o[0:64, 0:ncp], lhsT=w2[:, hc, :], rhs=GT[:, hc, :],
                                                                    start=(hc == 0), stop=(hc == 1)),
                     reads=[w2, GT], writes=[ps_o])
            P.op("act", lambda e: e.activation(out=KcT[:, 0:ncmp], in_=ps_o[0:64, 0:ncmp], func=AF.Copy),
                 reads=[ps_o], writes=[KcT])
        else:
            for c in range(ncc):
                for hc in range(2):
                    P.op("pe", lambda e, hc=hc, c=c, w2=w2, GT=GT: e.matmul(
                        ps_o[:, 0:64], lhsT=GT[:, hc, c * 128:(c + 1) * 128], rhs=w2[:, hc, :],
                        start=(hc == 0), stop=(hc == 1)), reads=[w2, GT], writes=[ps_o])
                P.op("act", lambda e, c=c: e.activation(out=Vc[:, c, 0:64], in_=ps_o[:, 0:64], func=AF.Copy),
                     reads=[ps_o], writes=[Vc])

    pS = [ps_h[0], ps_h[1]]
    pM = ps_b
    pOc = [P.ps([128, 512], F32, "pOc") for _ in range(2)]
    pOw = P.ps([128, 512], F32, "pOw")
    WMk = cload("WM", [128, 8, 128], BF16, d["WM"][:, :, :])
    Tt = cload("Tt", [128, NQB, ncc], F32, d["Tt"][:, :, :], "sp")
    QTb = [P.sb([64, 512], BF16, "QT") for _ in range(2)]
    glb = [P.sb([128, 12], F32, "gl") for _ in range(2)]
    PT = [P.sb([128, 4, 128], BF16, "PT") for _ in range(3)]
    PTm = [P.sb([128, 4, 128], BF16, "PTm") for _ in range(3)]
    mk = [P.sb([128, 128], BF16, "mk") for _ in range(2)]
    rd = P.sb([128, 4], F32, "rd")
    wgt = P.sb([128, 4], F32, "wgt")
    imp = P.sb([128, nsel], F32, "imp")
    sc = P.sb([128, nsel], F32, "sc")
    sc2 = P.sb([128, nsel], F32, "sc2")
    m8a = P.sb([128, 8], F32, "m8a")
    m8b = P.sb([128, 8], F32, "m8b")
    selb = P.sb([128, nsel], BF16, "selb")
    selT = P.sb([nsel, 128], BF16, "selT")
    pTr = P.ps([128, 1024], BF16, "pTr")
    yacc = [P.sb([128, 4, 64], F32, "yacc") for _ in range(2)]
    state = {"pt": 0, "ps": 0, "mk": 0}

    def attn_chunk(kT, c, QT, mask_fn, vaug, vw_cols, pouts, first, last):
        ps = pS[state["ps"] % 2]
        state["ps"] += 1
        P.op("pe", lambda e, ps=ps, kT=kT, c=c, QT=QT: e.matmul(
            ps[:, :], lhsT=kT[:, c * 128:(c + 1) * 128], rhs=QT[:, :], start=True, stop=True),
            reads=[kT, QT], writes=[ps])
        pt = PT[state["pt"] % 3]
        ptm = PTm[state["pt"] % 3]
        state["pt"] += 1
        P.op("act", lambda e, ps=ps, pt=pt: e.activation(out=pt[:, :, :], in_=ps[:, :].rearrange("p (h q) -> p h q", h=4),
                                                         func=AF.Exp, scale=0.125), reads=[ps], writes=[pt])
        src = pt
        m = mask_fn()
        if m is not None:
            mbuf, map_ = m
            P.op("dve", lambda e, pt=pt, ptm=ptm, map_=map_: e.tensor_tensor(
                out=ptm[:, :, :], in0=pt[:, :, :], in1=map_.unsqueeze(1).to_broadcast([128, 4, 128]), op=ALU.mult),
                reads=[pt, mbuf], writes=[ptm])
            src = ptm
        for pb, heads in pouts:
            for hi, h in enumerate(heads):
                st = first and hi == 0
                P.op("pe", lambda e, pb=pb, hi=hi, h=h, src=src, st=st, c=c: e.matmul(
                    pb[:, hi * vw_cols:(hi + 1) * vw_cols], lhsT=src[:, h, :], rhs=vaug[:, c, 0:vw_cols],
                    start=st, stop=last, skip_group_check=True), reads=[src, vaug], writes=[pb])

    for m_i in range(NQB):
        qbm = 2 * m_i + 1
        QT = QTb[m_i % 2]
        P.dma("pool", QT[:, :], d["QT"][:, m_i, :], dst=QT)
        gl = glb[m_i % 2]
        P.dma("sp", gl[:, :], d["gl"][:, m_i, :], dst=gl)
        P.op("act", lambda e, gl=gl: e.activation(out=gl[:, :], in_=gl[:, :], func=AF.Sigmoid), reads=[gl], writes=[gl])
        glv = gl[:, :].rearrange("p (h t) -> p h t", t=3)
        ya = yacc[m_i % 2]
        nvalid = min(8 * qbm + 7, ncmp)
        nchunks = (nvalid + 127) // 128
        for c in range(nchunks):
            Tmin = 128 * (qbm - 1) - 2048 * c - 31

            def mfn(c=c, m_i=m_i, Tmin=Tmin):
                if Tmin >= 16 * 127:
                    return None
                mb = mk[state["mk"] % 2]
                state["mk"] += 1
                P.op("dve", lambda e, mb=mb: e.tensor_scalar(out=mb[:, :], in0=D16[:, :], scalar1=Tt[:, m_i, c:c + 1],
                                                             scalar2=None, op0=ALU.is_le), reads=[D16, Tt], writes=[mb])
                return mb, mb[:, :]
            attn_chunk(KcT, c, QT, mfn, Vc, W, [(pOc[0], [0, 1]), (pOc[1], [2, 3])], c == 0, c == nchunks - 1)
        for g2 in range(2):
            ov_ = pOc[g2][:, 0:2 * W].rearrange("p (h w) -> p h w", h=2)
            P.op("dve", lambda e, ov_=ov_, g2=g2: e.tensor_scalar(out=rd[:, 2 * g2:2 * g2 + 2], in0=ov_[:, :, 64],
                                                                  scalar1=1e-30, scalar2=None, op0=ALU.max),
                 reads=[pOc[g2]], writes=[rd])
        P.op("dve", lambda e: e.reciprocal(out=rd[:, :], in_=rd[:, :]), reads=[rd], writes=[rd])
        for h in range(4):
            pb = pOc[h // 2]
            base = (h % 2) * W
            if h == 0:
                P.op("dve", lambda e, pb=pb, base=base, h=h: e.tensor_scalar(
                    out=imp[:, :], in0=pb[:, base + 65:base + W], scalar1=rd[:, h:h + 1], scalar2=None, op0=ALU.mult),
                    reads=[pb, rd], writes=[imp])
            else:
                P.op("dve", lambda e, pb=pb, base=base, h=h: e.scalar_tensor_tensor(
                    out=imp[:, :], in0=pb[:, base + 65:base + W], scalar=rd[:, h:h + 1], in1=imp[:, :],
                    op0=ALU.mult, op1=ALU.add), reads=[pb, rd, imp], writes=[imp])
        P.op("dve", lambda e, glv=glv: e.tensor_tensor(out=wgt[:, :], in0=rd[:, :], in1=glv[:, :, 0], op=ALU.mult),
             reads=[rd, gl], writes=[wgt])
        for h in range(4):
            pb = pOc[h // 2]
            base = (h % 2) * W
            P.op("dve", lambda e, pb=pb, base=base, h=h, ya=ya: e.tensor_scalar(
                out=ya[:, h, :], in0=pb[:, base:base + 64], scalar1=wgt[:, h:h + 1], scalar2=None, op0=ALU.mult),
                reads=[pb, wgt], writes=[ya])
        x0 = nsel - 4 * m_i
        P.op("dve", lambda e, x0=x0: e.tensor_tensor(out=sc[:, :], in0=imp[:, :], in1=Amul[:, x0:x0 + nsel], op=ALU.mult),
             reads=[imp, Amul], writes=[sc])
        P.op("dve", lambda e, x0=x0: e.tensor_tensor(out=sc[:, :], in0=sc[:, :], in1=Aadd[:, x0:x0 + nsel], op=ALU.add),
             reads=[sc, Aadd], writes=[sc])
        P.op("dve", lambda e: e.memset(sc[:, 0:1], BIGF), writes=[sc])
        P.op("dve", lambda e: e.max(out=m8a[:, :], in_=sc[:, :]), reads=[sc], writes=[m8a])
        P.op("dve", lambda e: e.match_replace(out=sc2[:, :], in_to_replace=m8a[:, :], in_values=sc[:, :],
                                              imm_value=-3.0 * BIGF), reads=[sc, m8a], writes=[sc2])
        P.op("dve", lambda e: e.max(out=m8b[:, :], in_=sc2[:, :]), reads=[sc2], writes=[m8b])
        P.op("dve", lambda e: e.tensor_scalar(out=selb[:, :], in0=sc[:, :], scalar1=m8b[:, 7:8], scalar2=None,
                                              op0=ALU.is_ge), reads=[sc, m8b], writes=[selb])
        P.op("pe", lambda e: e.transpose(out=pTr[0:nsel, 0:128], in_=selb[:, :], identity=identb[:, :]),
             reads=[selb, identb], writes=[pTr])
        P.op("act", lambda e: e.activation(out=selT[:, :], in_=pTr[0:nsel, 0:128], func=AF.Copy), reads=[pTr], writes=[selT])
        wl = [c for c in range(2 * m_i - 4, 2 * m_i + 2) if c >= 0]
        for c in wl:
            r = c - (2 * m_i - 4)

            def mfn(r=r):
                return WMk, WMk[:, r, :]
            attn_chunk(kwT, c, QT, mfn, vw, 65, [(pOw, [0, 1, 2, 3])], c == wl[0], c == wl[-1])
        for c in range(0, 2 * m_i + 2):
            def mfn(c=c, m_i=m_i):
                P.op("pe", lambda e, c=c: e.matmul(pM[0:128, 0:128], lhsT=Ec[:, c, :], rhs=selT[:, :], start=True, stop=True),
                     reads=[Ec, selT], writes=[pM])
                mb = mk[state["mk"] % 2]
                state["mk"] += 1
                if c >= 2 * m_i:
                    r = 6 + c - 2 * m_i
                    P.op("dve", lambda e, mb=mb, r=r: e.tensor_tensor(out=mb[:, :], in0=pM[:, 0:128], in1=WMk[:, r, :], op=ALU.mult),
                         reads=[pM, WMk], writes=[mb])
                else:
                    P.op("dve", lambda e, mb=mb: e.tensor_copy(out=mb[:, :], in_=pM[:, 0:128]), reads=[pM], writes=[mb])
                return mb, mb[:, :]
            attn_chunk(ksT, c, QT, mfn, vs, 65, [(pOs, [0, 1, 2, 3])], c == 0, c == 2 * m_i + 1)
        for bi, pb in ((2, pOw), (1, pOs)):
            pv = pb[:, 0:260].rearrange("p (h w) -> p h w", h=4)
            P.op("dve", lambda e, pv=pv, pb=pb: e.tensor_scalar(out=rd[:, :], in0=pv[:, :, 64], scalar1=1e-30, scalar2=None,
                                                                op0=ALU.max), reads=[pb], writes=[rd])
            P.op("dve", lambda e: e.reciprocal(out=rd[:, :], in_=rd[:, :]), reads=[rd], writes=[rd])
            P.op("dve", lambda e, glv=glv, bi=bi: e.tensor_tensor(out=wgt[:, :], in0=rd[:, :], in1=glv[:, :, bi], op=ALU.mult),
                 reads=[rd, gl], writes=[wgt])
            for h in range(4):
                P.op("dve", lambda e, pb=pb, h=h, ya=ya: e.scalar_tensor_tensor(
                    out=ya[:, h, :], in0=pb[:, h * 65:h * 65 + 64], scalar=wgt[:, h:h + 1], in1=ya[:, h, :],
                    op0=ALU.mult, op1=ALU.add), reads=[pb, wgt, ya], writes=[ya])
        P.dma("sp", d["y"][m_i], ya[:, :, :].rearrange("p h w -> p (h w)"), src=ya)


NSA_IN = lambda S, NQB: [
    ("QT", [64, NQB, 512]), ("gl", [128, NQB, 12]), ("ksT", [64, S]), ("kwT", [64, S]), ("vs", [128, S // 128, 65]), ("vw", [128, S // 128, 65]),
    ("kc2T", [128, S]), ("vc2T", [128, S]), ("pe2", [2, 128, 16]), ("w1", [2, 128, 16, 256]), ("w2", [2, 128, 2, 64]),
    ("D16", [128, 128]), ("Mdiag", [128, 128]), ("Mlow", [128, 128]), ("E", [S // 64, S // 128, 128]),
    ("ov", [128, ((S // 16 - 1 + 127) // 128), S // 64]), ("Amul", [128, 2 * (S // 64)]), ("Aadd", [128, 2 * (S // 64)]),
    ("identb", [128, 128]), ("WM", [128, 8, 128]), ("Tt", [128, NQB, ((S // 16 - 1 + 127) // 128)])]


def build_test_nsa(S, NQB):
    P = Prog()
    d = {}
    for nm, shp in NSA_IN(S, NQB):
        d[nm] = P.dram(nm, shp, F32, "ExternalInput")
    d["y"] = P.dram("y", [NQB, 128, 256], F32, "ExternalOutput")
    nsa_phase(P, S, NQB, d)
    print("streams", P.stats(), "sems", P.n_sems)
    return P.finish()


import math
import numpy as np

NT = 2048
NTILE = NT // 128

A_OPS = ([("copy", 1)] * 4 + [("prod", 2)] * 4 + [("silu", 1)] * 4 + [("copy", 1)] * 12 +
         [("rope", 2)] * 4 + [("rope", 2)] * 3 + [("copy", 1)] * 3 + [("gate", 1)])
A_NOUT = len(A_OPS)
A_NIN = sum(n for _, n in A_OPS)
A_COLS = (A_NIN - 1) * 128 + 24
A_ROWS = (A_NOUT - 1) * 128 + 24


def a_col_perm():
    BR = 512
    o = {}
    names = ["ab", "ac", "ax", "hq", "hf", "hi", "hg", "nq", "nkc", "nvc", "nks", "nvs", "nkw", "nvw", "ng", "mg"]
    sizes = [512] * 3 + [512] * 4 + [512] + [128] * 6 + [24, 3072]
    off = 0
    for n, s in zip(names, sizes):
        o[n] = np.arange(off, off + s)
        off += s

    def swap64(ix):
        ix = ix.reshape(-1, 2, 32)
        return ix[:, ::-1, :].reshape(-1)
    cols = []
    for c in range(4):
        cols.append(o["ab"][c * 128:(c + 1) * 128])
    for c in range(4):
        cols.append(o["ac"][c * 128:(c + 1) * 128])
        cols.append(o["ax"][c * 128:(c + 1) * 128])
    for n in ("hq", "hf", "hi", "hg"):
        for c in range(4):
            cols.append(o[n][c * 128:(c + 1) * 128])
    for c in range(4):
        ix = o["nq"][c * 128:(c + 1) * 128]
        cols.append(ix)
        cols.append(swap64(ix))
    for n in ("nkc", "nks", "nkw"):
        cols.append(o[n])
        cols.append(swap64(o[n]))
    for n in ("nvc", "nvs", "nvw"):
        cols.append(o[n])
    cols.append(o["ng"])
    return np.concatenate(cols), o["mg"]


def rope_tables(P, pos_d, inv_d, sgn_d, n):
    posi = P.sb([128, n], I32, "posi")
    P.dma("sp", posi[:, :], pos_d[:, :], dst=posi)
    inv = P.sb([128, 1], F32, "inv")
    sgn = P.sb([128, 1], F32, "sgn")
    P.dma("sp", inv[:, :], inv_d[:, :], dst=inv)
    P.dma("sp", sgn[:, :], sgn_d[:, :], dst=sgn)
    ang = P.sb([128, n], F32, "ang")
    P.op("dve", lambda e: e.tensor_copy(out=ang[:, :], in_=posi[:, :]), reads=[posi], writes=[ang])
    P.op("dve", lambda e: e.tensor_scalar(out=ang[:, :], in0=ang[:, :], scalar1=inv[:, :], scalar2=None, op0=ALU.mult),
         reads=[ang, inv], writes=[ang])
    qf = P.sb([128, n], F32, "qf")
    P.op("dve", lambda e: e.tensor_scalar(out=qf[:, :], in0=ang[:, :], scalar1=1.0 / (2 * math.pi), scalar2=None,
                                          op0=ALU.mult), reads=[ang], writes=[qf])
    qi = P.sb([128, n], I32, "qi")
    P.op("dve", lambda e: e.tensor_copy(out=qi[:, :], in_=qf[:, :]), reads=[qf], writes=[qi])
    P.op("dve", lambda e: e.tensor_copy(out=qf[:, :], in_=qi[:, :]), reads=[qi], writes=[qf])
    C1, C2 = 6.28125, 2 * math.pi - 6.28125
    C2a = float(np.float32(C2))
    C3 = C2 - C2a
    for Cc in (C1, C2a, C3):
        P.op("dve", lambda e, Cc=Cc: e.scalar_tensor_tensor(out=ang[:, :], in0=qf[:, :], scalar=-Cc, in1=ang[:, :],
                                                            op0=ALU.mult, op1=ALU.add), reads=[qf, ang], writes=[ang])
    m = qf
    P.op("dve", lambda e: e.tensor_scalar(out=m[:, :], in0=ang[:, :], scalar1=math.pi, scalar2=-2 * math.pi,
                                          op0=ALU.is_gt, op1=ALU.mult), reads=[ang], writes=[m])
    P.op("dve", lambda e: e.tensor_tensor(out=ang[:, :], in0=ang[:, :], in1=m[:, :], op=ALU.add), reads=[ang, m], writes=[ang])
    P.op("dve", lambda e: e.tensor_scalar(out=m[:, :], in0=ang[:, :], scalar1=-math.pi, scalar2=2 * math.pi,
                                          op0=ALU.is_lt, op1=ALU.mult), reads=[ang], writes=[m])
    P.op("dve", lambda e: e.tensor_tensor(out=ang[:, :], in0=ang[:, :], in1=m[:, :], op=ALU.add), reads=[ang, m], writes=[ang])
    P.op("dve", lambda e: e.tensor_scalar(out=ang[:, :], in0=ang[:, :], scalar1=3.14159, scalar2=-3.14159,
                                          op0=ALU.min, op1=ALU.max), reads=[ang], writes=[ang])
    SINS = P.sb([128, n], F32, "SINS")
    COS = P.sb([128, n], F32, "COS")
    P.op("act", lambda e: e.activation(out=SINS[:, :], in_=ang[:, :], func=AF.Sin), reads=[ang], writes=[SINS])
    P.op("dve", lambda e: e.tensor_scalar(out=SINS[:, :], in0=SINS[:, :], scalar1=sgn[:, :], scalar2=None, op0=ALU.mult),
         reads=[SINS, sgn], writes=[SINS])
    P.op("dve", lambda e: e.tensor_scalar(out=m[:, :], in0=ang[:, :], scalar1=-1.0, scalar2=None, op0=ALU.mult),
         reads=[ang], writes=[m])
    P.op("dve", lambda e: e.tensor_tensor(out=ang[:, :], in0=ang[:, :], in1=m[:, :], op=ALU.max), reads=[ang, m], writes=[ang])
    P.op("dve", lambda e: e.tensor_scalar(out=ang[:, :], in0=ang[:, :], scalar1=-1.0, scalar2=math.pi / 2, op0=ALU.mult,
                                          op1=ALU.add), reads=[ang], writes=[ang])
    P.op("act", lambda e: e.activation(out=COS[:, :], in_=ang[:, :], func=AF.Sin), reads=[ang], writes=[COS])
    return COS, SINS


def make_h2T(P, C, x_src_d, g2, h2T, tp, ntile):
    xt = [P.sb([128, D], F32, "x1t") for _ in range(2)]
    hb = [P.sb([128, D], BF16, "h2") for _ in range(2)]
    for i in range(ntile):
        xb = xt[i % 2]
        P.dma("sp", xb[:, :], x_src_d[i * 128:(i + 1) * 128, :], dst=xb)
        rstd = rms_stats(P, C, xb, xb[:, :], False)
        h = hb[i % 2]
        P.op("dve", lambda e, h=h, xb=xb, rstd=rstd: e.scalar_tensor_tensor(
            out=h[:, :], in0=xb[:, :], scalar=rstd[:, :], in1=g2[:, :], op0=ALU.mult, op1=ALU.mult),
            reads=[xb, rstd, g2], writes=[h])
        for k in range(8):
            P.op("pe", lambda e, h=h, k=k: e.transpose(out=tp[:, k * 128:(k + 1) * 128],
                                                      in_=h[:, k * 128:(k + 1) * 128], identity=C.ident[:, :]),
                 reads=[h, C.ident], writes=[tp])
        P.op("act", lambda e, i=i: e.activation(out=h2T[:, :, i * 128:(i + 1) * 128],
                                                in_=tp[:, :].rearrange("p (k n) -> p k n", k=8), func=AF.Copy),
             reads=[tp], writes=[h2T])


def build_A():
    P = Prog()
    x_d = P.dram("x", [NT, D], F32, "ExternalInput")
    gains_d = P.dram("gains", [3, 128, D], F32, "ExternalInput")
    wg_d = P.dram("wg", [D, DFF], F32, "ExternalInput")
    wu_d = P.dram("wu", [D, DFF], F32, "ExternalInput")
    wd_d = P.dram("wd", [DFF, D], F32, "ExternalInput")
    wa_d = P.dram("wa", [D, A_COLS], F32, "ExternalInput")
    ident_d = P.dram("ident", [128, 128], F32, "ExternalInput")
    pos_d = P.dram("pos", [128, NT], I32, "ExternalInput")
    inv_d = P.dram("inv", [128, 1], F32, "ExternalInput")
    sgn_d = P.dram("sgn", [128, 1], F32, "ExternalInput")
    x1_d = P.dram("x1", [NT, D], F32, "ExternalOutput")
    fm_d = P.dram("fm", [A_ROWS, NT], F32, "ExternalOutput")
    C = build_consts(P, ident_d)
    g = []
    for i in range(3):
        b = P.sb([128, D], F32, "gain")
        P.dma("sp", b[:, :], gains_d[i], dst=b)
        g.append(b)
    with P.scope():
        wg = load_w_cast(P, wg_d, D, DFF, 512, "wg")
        wu = load_w_cast(P, wu_d, D, DFF, 512, "wu")
        wd = load_w_cast(P, wd_d, DFF, D, 512, "wd")
        xpool = [P.sb([128, D], F32, "x") for _ in range(4)]

        def get_x(i):
            b = xpool[i % 4]
            P.dma("sp", b[:, :], x_d[i * 128:(i + 1) * 128, :], dst=b)
            return b

        def put_x(i, b):
            P.dma("sp", x1_d[i * 128:(i + 1) * 128, :], b[:, :], src=b)

        ffn_phase(P, C, NTILE, get_x, put_x, g[0], g[1], wg, wu, wd)
    with P.scope():
        COS, SINS = rope_tables(P, pos_d, inv_d, sgn_d, NT)
        h2T = P.sb([128, 8, NT], BF16, "h2T")
        tp = P.ps([128, 8 * 128], BF16, "tp")
        make_h2T(P, C, x1_d, g[2], h2T, tp, NTILE)
        wsrc = wa_d.rearrange("(c p) n -> p c n", p=128)
        wbufs = [P.sb([128, 8, 512], BF16, "wa") for _ in range(3)]
        pp = [P.ps([128, 512], F32, "pp") for _ in range(4)]
        stg = [P.sb([128, 512], F32, "stg") for _ in range(4)]
        tmp = [P.sb([128, 512], F32, "tmp") for _ in range(2)]
        nblk = (A_NIN + 3) // 4
        ops = []
        ic = 0
        for oc, (kind, nin) in enumerate(A_OPS):
            ops.append((oc, kind, list(range(ic, ic + nin))))
            ic += nin
        loaded = {}
        state = {"pp": 0, "stg": 0, "tmp": 0}

        def get_w(chunk):
            blk = chunk // 4
            _load(blk)
            if blk + 1 < nblk:
                _load(blk + 1)
            wb = loaded[blk]
            off = (chunk % 4) * 128
            return wb, off

        def _load(blk):
            if blk not in loaded:
                wb = wbufs[blk % 3]
                c0 = blk * 512
                c1 = min(A_COLS, c0 + 512)
                P.dma("pool", wb[:, :, 0:c1 - c0], wsrc[:, :, c0:c1], dst=wb)
                loaded[blk] = wb

        for oc, kind, chunks in ops:
            rows = 24 if kind == "gate" else 128
            for tg in range(NT // 512):
                tsl = slice(tg * 512, (tg + 1) * 512)
                pts = []
                for ch in chunks:
                    wb, off = get_w(ch)
                    pt = pp[state["pp"] % 4]
                    state["pp"] += 1
                    for k in range(8):
                        P.op("pe", lambda e, pt=pt, wb=wb, off=off, k=k, rows=rows, tsl=tsl: e.matmul(
                            pt[0:rows, :], lhsT=wb[:, k, off:off + rows], rhs=h2T[:, k, tsl], start=(k == 0), stop=(k == 7)),
                            reads=[wb, h2T], writes=[pt])
                    pts.append(pt)
                sb = stg[state["stg"] % 4]
                state["stg"] += 1
                if kind in ("copy", "gate"):
                    if oc % 2 == 0:
                        P.op("act", lambda e, sb=sb, pt=pts[0], rows=rows: e.activation(out=sb[0:rows, :], in_=pt[0:rows, :],
                                                                                        func=AF.Copy), reads=[pts[0]], writes=[sb])
                    else:
                        P.op("dve", lambda e, sb=sb, pt=pts[0], rows=rows: e.tensor_copy(out=sb[0:rows, :], in_=pt[0:rows, :]),
                             reads=[pts[0]], writes=[sb])
                elif kind == "silu":
                    P.op("act", lambda e, sb=sb, pt=pts[0]: e.activation(out=sb[:, :], in_=pt[:, :], func=AF.Silu),
                         reads=[pts[0]], writes=[sb])
                elif kind == "prod":
                    t = tmp[state["tmp"] % 2]
                    state["tmp"] += 1
                    P.op("act", lambda e, t=t, pt=pts[0]: e.activation(out=t[:, :], in_=pt[:, :], func=AF.Copy),
                         reads=[pts[0]], writes=[t])
                    P.op("dve", lambda e, sb=sb, t=t, pt=pts[1]: e.tensor_tensor(out=sb[:, :], in0=t[:, :], in1=pt[:, :],
                                                                                 op=ALU.mult), reads=[t, pts[1]], writes=[sb])
                elif kind == "rope":
                    t = tmp[state["tmp"] % 2]
                    state["tmp"] += 1
                    P.op("dve", lambda e, t=t, pt=pts[0], tsl=tsl: e.tensor_tensor(out=t[:, :], in0=pt[:, :], in1=COS[:, tsl],
                                                                                   op=ALU.mult), reads=[pts[0], COS], writes=[t])
                    P.op("dve", lambda e, sb=sb, pt=pts[1], tsl=tsl: e.tensor_tensor(out=sb[:, :], in0=pt[:, :], in1=SINS[:, tsl],
                                                                                     op=ALU.mult), reads=[pts[1], SINS], writes=[sb])
                    P.op("pool", lambda e, sb=sb, t=t: e.tensor_tensor(out=sb[:, :], in0=sb[:, :], in1=t[:, :], op=ALU.add),
                         reads=[sb, t], writes=[sb])
                P.dma("sp", fm_d[oc * 128:oc * 128 + rows, tsl], sb[0:rows, :], src=sb)
    print("A streams", P.stats(), "sems", P.n_sems)
    return P.finish()


def build_C():
    P = Prog()
    x1_d = P.dram("x1", [NT, D], F32, "ExternalInput")
    gains_d = P.dram("gains", [4, 128, D], F32, "ExternalInput")
    vT_d = P.dram("vT", [512, NT + 2], F32, "ExternalInput")
    bT_d = P.dram("bT", [512, NT], F32, "ExternalInput")
    ybT_d = P.dram("ybT", [512, NT], F32, "ExternalInput")
    ycT_d = P.dram("ycT", [512, NT], F32, "ExternalInput")
    cw_d = P.dram("cw", [128, 4, 3], F32, "ExternalInput")
    wmg_d = P.dram("wmg", [D, 3 * D], F32, "ExternalInput")
    wbr_d = P.dram("wbr", [3 * 512, D], F32, "ExternalInput")
    wout_d = P.dram("wout", [D, D], F32, "ExternalInput")
    wg_d = P.dram("wg", [D, DFF], F32, "ExternalInput")
    wu_d = P.dram("wu", [D, DFF], F32, "ExternalInput")
    wd_d = P.dram("wd", [DFF, D], F32, "ExternalInput")
    ident_d = P.dram("ident", [128, 128], F32, "ExternalInput")
    x2_d = P.dram("x2s", [NT, D], F32, "Internal")
    out_d = P.dram("xo", [NT, D], F32, "ExternalOutput")
    C = build_consts(P, ident_d)
    g = []
    for i in range(4):
        b = P.sb([128, D], F32, "gain")
        P.dma("sp", b[:, :], gains_d[i], dst=b)
        g.append(b)
    with P.scope():
        cw = P.sb([128, 4, 3], F32, "cw")
        P.dma("sp", cw[:, :, :], cw_d[:, :, :], dst=cw)
        wmg = load_w_cast(P, wmg_d, D, 3 * D, 512, "wmg")
        wbr = load_w_cast(P, wbr_d, 3 * 512, D, 512, "wbr")
        wout = load_w_cast(P, wout_d, D, D, 512, "wout")
        tp = P.ps([128, 8 * 128], BF16, "tp")
        h2T = P.sb([128, 8, 512], BF16, "h2T")
        xt = [P.sb([128, D], F32, "x1t") for _ in range(4)]
        hb = [P.sb([128, D], BF16, "h2") for _ in range(2)]
        vin = [P.sb([128, 514], F32, "vin") for _ in range(2)]
        bin_ = [P.sb([128, 512], F32, "bin") for _ in range(2)]
        acc = [P.sb([128, 512], F32, "acc") for _ in range(2)]
        ybrs = [P.sb([128, 4, 512], BF16, "ybr%d" % i) for i in range(3)]
        mT = P.sb([128, 8, 512], BF16, "mT")
        macc = [P.sb([128, 512], F32, "macc") for _ in range(2)]
        gsb = [P.sb([128, 512], F32, "gsb") for _ in range(2)]
        tt = [P.sb([128, 512], F32, "tt") for _ in range(2)]
        pg = [P.ps([128, 512], F32, "pg") for _ in range(2)]
        ppj = [P.ps([128, 512], F32, "ppj") for _ in range(2)]
        py = P.ps([128, D], F32, "py")
        t1 = P.sb([128, D], F32, "t1")
        for tg in range(NT // 512):
            tsl = slice(tg * 512, (tg + 1) * 512)
            for t in range(4):
                i = tg * 4 + t
                xb = xt[t]
                P.dma("sp", xb[:, :], x1_d[i * 128:(i + 1) * 128, :], dst=xb)
                rstd = rms_stats(P, C, xb, xb[:, :], False)
                h = hb[t % 2]
                P.op("dve", lambda e, h=h, xb=xb, rstd=rstd: e.scalar_tensor_tensor(
                    out=h[:, :], in0=xb[:, :], scalar=rstd[:, :], in1=g[0][:, :], op0=ALU.mult, op1=ALU.mult),
                    reads=[xb, rstd, g[0]], writes=[h])
                for k in range(8):
                    P.op("pe", lambda e, h=h, k=k: e.transpose(out=tp[:, k * 128:(k + 1) * 128],
                                                              in_=h[:, k * 128:(k + 1) * 128], identity=C.ident[:, :]),
                         reads=[h, C.ident], writes=[tp])
                P.op("act", lambda e, t=t: e.activation(out=h2T[:, :, t * 128:(t + 1) * 128],
                                                        in_=tp[:, :].rearrange("p (k n) -> p k n", k=8), func=AF.Copy),
                     reads=[tp], writes=[h2T])
            for c in range(4):
                v = vin[c % 2]
                bb = bin_[c % 2]
                a = acc[c % 2]
                P.dma("sp", v[:, :], vT_d[c * 128:(c + 1) * 128, tg * 512:tg * 512 + 514], dst=v)
                P.dma("sp", bb[:, :], bT_d[c * 128:(c + 1) * 128, tsl], dst=bb)
                P.op("dve", lambda e, a=a, v=v, c=c: e.tensor_scalar(out=a[:, :], in0=v[:, 2:514], scalar1=cw[:, c, 2:3],
                                                                     scalar2=None, op0=ALU.mult), reads=[v, cw], writes=[a])
                P.op("dve", lambda e, a=a, v=v, c=c: e.scalar_tensor_tensor(out=a[:, :], in0=v[:, 1:513], scalar=cw[:, c, 1:2],
                                                                            in1=a[:, :], op0=ALU.mult, op1=ALU.add),
                     reads=[v, cw, a], writes=[a])
                P.op("dve", lambda e, a=a, v=v, c=c: e.scalar_tensor_tensor(out=a[:, :], in0=v[:, 0:512], scalar=cw[:, c, 0:1],
                                                                            in1=a[:, :], op0=ALU.mult, op1=ALU.add),
                     reads=[v, cw, a], writes=[a])
                P.op("pool", lambda e, a=a, bb=bb, c=c: e.tensor_tensor(out=ybrs[0][:, c, :], in0=a[:, :], in1=bb[:, :], op=ALU.mult),
                     reads=[a, bb], writes=[ybrs[0]])
            P.dma("pool", ybrs[1][:, :, :], ybT_d[:, tsl].rearrange("(c p) n -> p c n", p=128), dst=ybrs[1])
            P.dma("pool", ybrs[2][:, :, :], ycT_d[:, tsl].rearrange("(c p) n -> p c n", p=128), dst=ybrs[2])
            for mc in range(8):
                ma = macc[mc % 2]
                for br in range(3):
                    pj = ppj[br % 2]
                    for k in range(4):
                        wb, wap = wslice(wbr, br * 4 + k, mc * 128, (mc + 1) * 128)
                        P.op("pe", lambda e, pj=pj, wap=wap, br=br, k=k: e.matmul(
                            pj[:, :], lhsT=wap, rhs=ybrs[br][:, k, :], start=(k == 0), stop=(k == 3)),
                            reads=[wb, ybrs[br]], writes=[pj])
                    pgt = pg[br % 2]
                    for k in range(8):
                        wb, wap = wslice(wmg, k, br * D + mc * 128, br * D + (mc + 1) * 128)
                        P.op("pe", lambda e, pgt=pgt, wap=wap, k=k: e.matmul(
                            pgt[:, :], lhsT=wap, rhs=h2T[:, k, :], start=(k == 0), stop=(k == 7)),
                            reads=[wb, h2T], writes=[pgt])
                    gs = gsb[br % 2]
                    P.op("act", lambda e, gs=gs, pgt=pgt: e.activation(out=gs[:, :], in_=pgt[:, :], func=AF.Sigmoid),
                         reads=[pgt], writes=[gs])
                    if br == 0:
                        P.op("dve", lambda e, ma=ma, gs=gs, pj=pj: e.tensor_tensor(out=ma[:, :], in0=gs[:, :], in1=pj[:, :],
                                                                                   op=ALU.mult), reads=[gs, pj], writes=[ma])
                    else:
                        t_ = tt[br % 2]
                        P.op("dve", lambda e, t_=t_, gs=gs, pj=pj: e.tensor_tensor(out=t_[:, :], in0=gs[:, :], in1=pj[:, :],
                                                                                   op=ALU.mult), reads=[gs, pj], writes=[t_])
                        if br == 1:
                            P.op("pool", lambda e, ma=ma, t_=t_: e.tensor_tensor(out=ma[:, :], in0=ma[:, :], in1=t_[:, :],
                                                                                 op=ALU.add), reads=[ma, t_], writes=[ma])
                        else:
                            P.op("pool", lambda e, ma=ma, t_=t_, mc=mc: e.tensor_tensor(out=mT[:, mc, :], in0=ma[:, :], in1=t_[:, :],
                                                                                        op=ALU.add), reads=[ma, t_], writes=[mT])
            for t in range(4):
                i = tg * 4 + t
                for half in range(2):
                    for k in range(8):
                        wb, wap = wslice(wout, k, half * 512, (half + 1) * 512)
                        P.op("pe", lambda e, wap=wap, k=k, half=half, t=t: e.matmul(
                            py[:, half * 512:(half + 1) * 512], lhsT=mT[:, k, t * 128:(t + 1) * 128], rhs=wap,
                            start=(k == 0), stop=(k == 7)), reads=[wb, mT], writes=[py])
                rstd = rms_stats(P, C, py, py[:, :], True)
                P.op("dve", lambda e, rstd=rstd: e.scalar_tensor_tensor(
                    out=t1[:, :], in0=py[:, :], scalar=rstd[:, :], in1=g[1][:, :], op0=ALU.mult, op1=ALU.mult),
                    reads=[py, rstd, g[1]], writes=[t1])
                xb = xt[t]
                P.op("pool", lambda e, xb=xb: e.tensor_tensor(out=xb[:, :], in0=xb[:, :], in1=t1[:, :], op=ALU.add),
                     reads=[xb, t1], writes=[xb])
                P.dma("sp", x2_d[i * 128:(i + 1) * 128, :], xb[:, :], src=xb)
    with P.scope():
        wg = load_w_cast(P, wg_d, D, DFF, 512, "wg")
        wu = load_w_cast(P, wu_d, D, DFF, 512, "wu")
        wd = load_w_cast(P, wd_d, DFF, D, 512, "wd")
        xpool = [P.sb([128, D], F32, "x") for _ in range(4)]

        def get_x(i):
            b = xpool[i % 4]
            P.dma("sp", b[:, :], x2_d[i * 128:(i + 1) * 128, :], dst=b)
            return b

        def put_x(i, b):
            P.dma("sp", out_d[i * 128:(i + 1) * 128, :], b[:, :], src=b)

        ffn_phase(P, C, NTILE, get_x, put_x, g[2], g[3], wg, wu, wd)
    print("C streams", P.stats(), "sems", P.n_sems)
    return P.finish()


S = 8192
NQB = 32


def build_B():
    P = Prog()
    dh = {}
    for nm, shp in (("qT", [128, S]), ("zfT", [128, S]), ("zf", [S, 128]), ("v", [S, 128]), ("g", [S, 128]),
                    ("lbT", [128, 4]), ("lmask", [128, 4]), ("lbrow", [128, 4, 128]), ("lmaskrow", [128, 4, 128]),
                    ("gnorm", [128, 128]), ("M1", [128, 128]), ("M2", [128, 128])):
        dh[nm] = P.dram("h_" + nm, shp, F32, "ExternalInput")
    dh["y"] = P.dram("yh", [S, 128], F32, "ExternalOutput")
    dn = {}
    for nm, shp in NSA_IN(S, NQB):
        dn[nm] = P.dram("n_" + nm, shp, F32, "ExternalInput")
    dn["y"] = P.dram("yn", [NQB, 128, 256], F32, "ExternalOutput")
    with P.scope():
        hgrn_phase(P, S, dh)
    with P.scope():
        nsa_phase(P, S, NQB, dn)
    print("B streams", P.stats(), "sems", P.n_sems)
    return P.finish()


import numpy as np

S = 8192
NC = 8

def rep(a, n=128):
    return np.ascontiguousarray(np.broadcast_to(a[None], (n,) + a.shape))

IDENT = np.eye(128, dtype=np.float32)
_p = np.arange(128)
INV = (10000.0 ** (-(_p % 32).astype(np.float32) * 2.0 / 64)).astype(np.float32).reshape(128, 1)
SGN = np.where((_p % 64) < 32, -1.0, 1.0).astype(np.float32).reshape(128, 1)
PERM, MG = a_col_perm()


def prep_A(inp, l, x):
    wa = np.ascontiguousarray(inp["w_in"][l][:, PERM])
    gains = np.stack([rep(inp["norm_gains"][l, i]) for i in (0, 1, 2)])
    maps = []
    pos = inp["positions"].reshape(-1)
    for c in range(NC):
        sl = slice(c * NT, (c + 1) * NT)
        maps.append(dict(x=np.ascontiguousarray(x[sl]), gains=gains, wg=inp["w_ffn_gate"][l, 0], wu=inp["w_ffn_up"][l, 0],
                         wd=inp["w_ffn_down"][l, 0], wa=wa, ident=IDENT, pos=rep(pos[sl].astype(np.int32)), inv=INV, sgn=SGN))
    return maps

HG_M1, HG_M2 = hgrn_consts_np()
NSA_K = nsa_consts_np(S)
ONES128 = np.ones((128, 128), np.float32)
ZEROS128 = np.zeros((128, 128), np.float32)
NQB = 32
NCC = 4


def _shift(A, par):
    o = np.empty_like(A)
    if par:
        o[:, 2 * par:] = A[:, :A.shape[1] - 2 * par]
        o[:, :2 * par] = A[:, :1]
    else:
        o[:] = A
    return o


def nsa_core_consts(par):
    K = NSA_K
    if par == 0:
        wm = [K["Mlow"], ONES128, ONES128, ONES128, K["Mdiag"], ZEROS128, K["Mdiag"], ZEROS128]
    else:
        wm = [ZEROS128, K["Mlow"], ONES128, ONES128, ONES128, K["Mdiag"], ONES128, K["Mdiag"]]
    Tt = np.zeros((128, NQB, NCC), np.float32)
    for m in range(NQB):
        for c in range(NCC):
            Tt[:, m, c] = 128 * (2 * m + par) - 2048 * c - 31
    return dict(WM=np.ascontiguousarray(np.stack(wm, 1)), Tt=Tt, Amul=_shift(K["Amul"], par), Aadd=_shift(K["Aadd"], par))


NSA_CC = [nsa_core_consts(0), nsa_core_consts(1)]
OV_PM = np.ascontiguousarray(NSA_K["ov"].reshape(-1, 128, NSA_K["ov"].shape[-1]).transpose(1, 0, 2))


def prep_B(inp, l, FM):
    maps = []
    logits = inp["hgrn_lb_logits"]
    lmask = np.zeros((4,), np.float32)
    lmask[1:l + 1] = 1
    gn = rep(inp["hgrn_gnorm"][l])
    pe = inp["cmp_pe"][l]
    pe2 = np.ascontiguousarray(pe.reshape(2, 16, 2, 64).transpose(0, 2, 3, 1).reshape(2, 128, 16))
    for c in range(NC):
        b, hd = c // 4, c % 4
        kvh, par = (c // 2) % 2, c % 2
        tk = slice(b * S, (b + 1) * S)
        d = {}
        r = lambda base, n: FM[base:base + n, tk]
        d["h_qT"] = np.ascontiguousarray(r(1024 + hd * 128, 128))
        zfT = r(1536 + hd * 128, 128)
        d["h_zfT"] = np.ascontiguousarray(zfT)
        d["h_zf"] = np.ascontiguousarray(zfT.T)
        d["h_v"] = np.ascontiguousarray(r(2048 + hd * 128, 128).T)
        d["h_g"] = np.ascontiguousarray(r(2560 + hd * 128, 128).T)
        lg = logits[:, hd * 128:(hd + 1) * 128]
        d["h_lbT"] = np.ascontiguousarray(lg.T)
        d["h_lmask"] = rep(lmask)
        d["h_lbrow"] = rep(lg)
        d["h_lmaskrow"] = np.ascontiguousarray(np.broadcast_to(lmask[None, :, None], (128, 4, 128)))
        d["h_gnorm"] = gn
        d["h_M1"] = HG_M1
        d["h_M2"] = HG_M2
        q = r(3072 + kvh * 256, 256).reshape(4, 64, 64, 128)[:, :, par::2]
        d["n_QT"] = np.ascontiguousarray(q.transpose(1, 2, 0, 3).reshape(64, NQB, 512))
        gl = r(4352 + kvh * 12, 12).reshape(12, 64, 128)[:, par::2]
        d["n_gl"] = np.ascontiguousarray(gl.transpose(2, 1, 0))
        d["n_ksT"] = np.ascontiguousarray(r(3712 + kvh * 64, 64))
        d["n_kwT"] = np.ascontiguousarray(r(3840 + kvh * 64, 64))

        def stack2(xT):
            o = np.zeros((128, S), np.float32)
            o[:64] = xT
            o[64:, :-1] = xT[:, 1:]
            return o
        d["n_kc2T"] = stack2(r(3584 + kvh * 64, 64))
        d["n_vc2T"] = stack2(r(3968 + kvh * 64, 64))

        def aug(xT):
            o = np.ones((S, 65), np.float32)
            o[:, :64] = xT.T
            return np.ascontiguousarray(o.reshape(S // 128, 128, 65).transpose(1, 0, 2))
        d["n_vs"] = aug(r(4096 + kvh * 64, 64))
        d["n_vw"] = aug(r(4224 + kvh * 64, 64))
        d["n_pe2"] = pe2
        d["n_w1"] = np.ascontiguousarray(inp["cmp_w1"][l].reshape(2, 16, 128, 256).transpose(0, 2, 1, 3))
        d["n_w2"] = np.ascontiguousarray(inp["cmp_w2"][l].reshape(2, 2, 128, 64).transpose(0, 2, 1, 3))
        for k in ("D16", "Mdiag", "Mlow", "E", "identb"):
            d["n_" + k] = NSA_K[k]
        d["n_ov"] = OV_PM
        for k, v in NSA_CC[par].items():
            d["n_" + k] = v
        maps.append(d)
    return maps


def gather_B(results):
    ybT = np.zeros((2, 512, S), np.float32)
    ycT = np.zeros((2, 512, S), np.float32)
    for c in range(NC):
        b, hd = c // 4, c % 4
        kvh, par = (c // 2) % 2, c % 2
        ybT[b, hd * 128:(hd + 1) * 128, :] = results[c]["yh"].T
        y = results[c]["yn"]
        yT = y.transpose(2, 0, 1)
        ycT[b, kvh * 256:(kvh + 1) * 256].reshape(256, 64, 128)[:, par::2, :] = yT
    return ybT, ycT


def prep_C(inp, l, x1, FM, ybT, ycT):
    gains = np.stack([rep(inp["norm_gains"][l, i]) for i in (2, 3, 4, 5)])
    cw = np.ascontiguousarray(inp["conv_w"][l].reshape(3, 4, 128).transpose(2, 1, 0))
    wmg = np.ascontiguousarray(inp["w_in"][l][:, MG])
    wbr = np.ascontiguousarray(inp["w_branch"][l].reshape(1536, 1024))
    maps = []
    for c in range(NC):
        b = c // 4
        t0 = c * NT
        sl = slice(t0, t0 + NT)
        ls = slice((c % 4) * NT, (c % 4 + 1) * NT)
        vT = np.zeros((512, NT + 2), np.float32)
        vT[:, 2:] = FM[512:1024, sl]
        if c % 4 != 0:
            vT[:, :2] = FM[512:1024, t0 - 2:t0]
        maps.append(dict(x1=np.ascontiguousarray(x1[sl]), gains=gains, vT=vT, bT=np.ascontiguousarray(FM[0:512, sl]),
                         ybT=np.ascontiguousarray(ybT[b][:, ls]), ycT=np.ascontiguousarray(ycT[b][:, ls]), cw=cw, wmg=wmg, wbr=wbr,
                         wout=inp["w_out"][l], wg=inp["w_ffn_gate"][l, 1], wu=inp["w_ffn_up"][l, 1], wd=inp["w_ffn_down"][l, 1],
                         ident=IDENT))
    return maps


from concourse.bass_utils import run_bass_kernel_spmd

_PROGS = {}


def _prog(name):
    if name not in _PROGS:
        _PROGS[name] = {"A": build_A, "B": build_B, "C": build_C}[name]()
    return _PROGS[name]


def kernel(**inputs):
    inp = {k: np.asarray(v) for k, v in inputs.items()}
    cores = list(range(NC))
    x = np.ascontiguousarray(inp["x"].reshape(-1, D).astype(np.float32, copy=False))
    for l in range(4):
        resA = run_bass_kernel_spmd(_prog("A"), prep_A(inp, l, x), core_ids=cores)
        x1 = np.concatenate([r["x1"] for r in resA.results])
        FM = np.concatenate([r["fm"] for r in resA.results], axis=1)
        del resA
        resB = run_bass_kernel_spmd(_prog("B"), prep_B(inp, l, FM), core_ids=cores)
        ybT, ycT = gather_B(resB.results)
        del resB
        resC = run_bass_kernel_spmd(_prog("C"), prep_C(inp, l, x1, FM, ybT, ycT), core_ids=cores)
        x = np.concatenate([r["xo"] for r in resC.results])
        del resC
    return np.ascontiguousarray(x.reshape(2, S, D).astype(np.float32))
```

```python
import contextlib
import numpy as np
import concourse.bass as bass
import concourse.mybir as mybir

F32 = mybir.dt.float32
BF16 = mybir.dt.bfloat16
I32 = mybir.dt.int32
ALU = mybir.AluOpType
AF = mybir.ActivationFunctionType
AX = mybir.AxisListType

ENGS = ("pe", "act", "dve", "pool", "sp")


class Buf:
    def __init__(self, prog, t, name, tracked=True):
        self.prog = prog
        self.t = t
        self.name = name
        self.tracked = tracked
        self.last_w = None
        self.readers = {}
        self.wsem = None
        self.wcnt = 0
        self.rsem = None
        self.rcnt = 0

    def __getitem__(self, idx):
        return self.t[idx]

    @property
    def ap(self):
        return self.t


class Prog:
    def __init__(self, same_engine_sync=None):
        import os
        if same_engine_sync is None:
            same_engine_sync = os.environ.get('SES', '1') == '1'
        self.nc = bass.Bass("TRN2", target_bir_lowering=False)
        self.es = contextlib.ExitStack()
        self.sem_es = contextlib.ExitStack()
        self.streams = {e: [] for e in ENGS}
        self.cnt = {e: 0 for e in ENGS}
        self.esem = {}
        for e in ENGS:
            self.esem[e] = self.sem_es.enter_context(self.nc.semaphore("s_" + e))
        self.seen = {e: {} for e in ENGS}
        self.same_engine_sync = same_engine_sync
        self.dma_sems = []
        self.nbuf = 0
        self.n_sems = 5
        self.all_bufs = []
        self.free_sems = []

    def dram(self, name, shape, dtype, kind):
        t = self.nc.dram_tensor(name, list(shape), dtype, kind=kind)
        return t.ap()

    def sb(self, shape, dtype, name=None):
        self.nbuf += 1
        name = (name or "b") + "_%d" % self.nbuf
        t = self.es.enter_context(self.nc.sbuf_tensor(name, list(shape), dtype))
        b = Buf(self, t, name)
        self.all_bufs.append(b)
        return b

    def ps(self, shape, dtype=F32, name=None):
        self.nbuf += 1
        name = (name or "p") + "_%d" % self.nbuf
        t = self.es.enter_context(self.nc.psum_tensor(name, list(shape), dtype))
        b = Buf(self, t, name)
        self.all_bufs.append(b)
        return b

    def _sem(self, name):
        if self.free_sems:
            return self.free_sems.pop()
        self.n_sems += 1
        return (self.sem_es.enter_context(self.nc.semaphore(name)), 0)

    def _collect(self, eng, reads, writes, no_waw=False):
        need = {}

        def add(tok):
            if tok is None:
                return
            key, val, e = tok
            if e == eng and (eng == "pe" or not self.same_engine_sync):
                return
            if need.get(key, (0,))[0] < val:
                need[key] = (val, e)

        for b in reads:
            if b is None or not b.tracked:
                continue
            add(b.last_w)
        for b in writes:
            if b is None or not b.tracked:
                continue
            if not no_waw:
                add(b.last_w)
            for key, (val, e) in b.readers.items():
                add((key, val, e))
        out = []
        for key, (val, e) in need.items():
            if self.seen[eng].get(key, 0) >= val:
                continue
            self.seen[eng][key] = val
            out.append((key, val))
        return out

    def _record(self, tok, reads, writes):
        key, val, e = tok
        for b in writes:
            if b is None or not b.tracked:
                continue
            b.last_w = tok
            b.readers = {}
        for b in reads:
            if b is None or not b.tracked:
                continue
            if b in writes:
                continue
            b.readers[key] = (val, e)

    def op(self, eng, fn, reads=(), writes=(), no_waw=False):
        waits = self._collect(eng, reads, writes, no_waw)
        st = self.streams[eng]
        for key, val in waits:
            st.append(("w", key, val))
        self.cnt[eng] += 1
        st.append(("o", fn, self.esem[eng], 1))
        tok = (self.esem[eng], self.cnt[eng], eng)
        self._record(tok, reads, writes)
        return tok

    def dma(self, queue, out_ap, in_ap, dst=None, src=None, no_waw=False, **kw):
        reads = [src] if src is not None else []
        writes = [dst] if dst is not None else []
        waits = self._collect(queue, reads, writes, no_waw)
        st = self.streams[queue]
        for key, val in waits:
            st.append(("w", key, val))
        if dst is not None:
            if dst.wsem is None:
                dst.wsem, dst.wcnt = self._sem("w_" + dst.name)
                self.dma_sems.append(dst)
            dst.wcnt += 16
            sem, val = dst.wsem, dst.wcnt
        else:
            if src.rsem is None:
                src.rsem, src.rcnt = self._sem("r_" + src.name)
                self.dma_sems.append(src)
            src.rcnt += 16
            sem, val = src.rsem, src.rcnt

        def fn(e, out_ap=out_ap, in_ap=in_ap, kw=kw):
            return e.dma_start(out=out_ap, in_=in_ap, **kw)

        st.append(("o", fn, sem, 16))
        tok = (sem, val, "dma")
        self._record(tok, reads, writes)
        return tok

    @contextlib.contextmanager
    def scope(self):
        outer = self.es
        self.es = contextlib.ExitStack()
        n0 = len(self.all_bufs)
        try:
            yield
        finally:
            self.barrier()
            self.flush()
            for b in self.all_bufs[n0:]:
                if b.wsem is not None:
                    self.free_sems.append((b.wsem, b.wcnt))
                if b.rsem is not None:
                    self.free_sems.append((b.rsem, b.rcnt))
                if b in self.dma_sems:
                    self.dma_sems.remove(b)
                b.dead = True
            del self.all_bufs[n0:]
            self.es.close()
            self.es = outer

    def barrier(self):
        targets = [(self.esem[e], self.cnt[e]) for e in ENGS if self.cnt[e] > 0]
        for b in self.dma_sems:
            if b.wsem is not None and b.wcnt:
                targets.append((b.wsem, b.wcnt))
            if b.rsem is not None and b.rcnt:
                targets.append((b.rsem, b.rcnt))
        for e in ENGS:
            for key, val in targets:
                if key is self.esem[e]:
                    continue
                if self.seen[e].get(key, 0) >= val:
                    continue
                self.seen[e][key] = val
                self.streams[e].append(("w", key, val))

    def flush(self):
        nc = self.nc
        streams = self.streams
        self.streams = {e: [] for e in ENGS}

        def run(stream, e):
            for it in stream:
                if it[0] == "w":
                    e.wait_ge(it[1], it[2])
                else:
                    ins = it[1](e)
                    ins.then_inc(it[2], it[3])

        with nc.Block() as block:
            @block.tensor
            def _(e):
                run(streams["pe"], e)

            @block.scalar
            def _(e):
                run(streams["act"], e)

            @block.vector
            def _(e):
                run(streams["dve"], e)

            @block.gpsimd
            def _(e):
                run(streams["pool"], e)

            @block.sync
            def _(e):
                run(streams["sp"], e)

    def finish(self):
        targets = [(self.esem[e], self.cnt[e]) for e in ENGS if self.cnt[e] > 0 and e != "sp"]
        for b in self.dma_sems:
            if b.wsem is not None and b.wcnt:
                targets.append((b.wsem, b.wcnt))
            if b.rsem is not None and b.rcnt:
                targets.append((b.rsem, b.rcnt))
        for key, val in targets:
            self.streams["sp"].append(("w", key, val))
        self.flush()
        self.es.close()
        self.sem_es.close()
        return self.nc

    def stats(self):
        return {e: self.cnt[e] for e in ENGS}


import numpy as np

D = 1024
DFF = 2816
NJ = 22
TG = 256
EPS = 1e-6


def load_w_cast(P, dram_ap, rows, cols, colblk, name):
    kc = rows // 128
    src = dram_ap.rearrange("(c p) n -> p c n", p=128)
    blocks = []
    c0 = 0
    while c0 < cols:
        c1 = min(cols, c0 + colblk)
        b = P.sb([128, kc, c1 - c0], BF16, name)
        P.dma("pool", b[:, :, :], src[:, :, c0:c1], dst=b)
        blocks.append((b, c0, c1))
        c0 = c1
    return blocks


def wslice(blocks, k, c0, c1):
    for b, b0, b1 in blocks:
        if b0 <= c0 and c1 <= b1:
            return b, b[:, k, c0 - b0:c1 - b0]
    raise ValueError((c0, c1))


class Consts:
    pass


def rms_stats(P, C, src_buf, src_ap, from_psum):
    ss = P.sb([128, 1], F32, "ss")
    if from_psum:
        P.op("act", lambda e: e.activation(out=C.junk[:, :], in_=src_ap, func=AF.Square, accum_out=ss[:, :]),
             reads=[src_buf], writes=[C.junk, ss])
    else:
        P.op("dve", lambda e: e.scalar_tensor_tensor(out=C.junk[:, :], in0=src_ap, scalar=1.0, in1=src_ap,
                                                      op0=ALU.mult, op1=ALU.mult, accum_out=ss[:, :]),
             reads=[src_buf], writes=[C.junk, ss])
    ms = P.sb([128, 1], F32, "ms")
    P.op("dve", lambda e: e.tensor_scalar(out=ms[:, :], in0=ss[:, :], scalar1=1.0 / D, scalar2=EPS,
                                          op0=ALU.mult, op1=ALU.add), reads=[ss], writes=[ms])
    rstd = P.sb([128, 1], F32, "rstd")
    P.op("pool", lambda e: e.tensor_tensor(out=rstd[:, :], in0=ms[:, :], in1=C.mhalf[:, :], op=ALU.pow),
         reads=[ms, C.mhalf], writes=[rstd])
    return rstd


def ffn_phase(P, C, ntiles, get_x, put_x, g_pre, g_post, wg, wu, wd):
    ngroups = ntiles // 2
    hT = [P.sb([128, 8, TG], BF16, "hT") for _ in range(2)]
    aT = P.sb([128, NJ, TG], BF16, "aT")
    tp = P.ps([128, 8 * 128], BF16, "tp")
    gu = [P.ps([128, 2, TG], F32, "gu") for _ in range(2)]
    ys = [P.ps([128, D], F32, "y") for _ in range(2)]
    hb = [P.sb([128, D], BF16, "h") for _ in range(2)]
    sg = [P.sb([128, TG], BF16, "sg") for _ in range(2)]
    xs = {}

    def prep(g):
        for t in range(2):
            i = 2 * g + t
            xb = get_x(i)
            xs[i] = xb
            rstd = rms_stats(P, C, xb, xb[:, :], False)
            h = hb[t]
            P.op("dve", lambda e, h=h, xb=xb, rstd=rstd: e.scalar_tensor_tensor(
                out=h[:, :], in0=xb[:, :], scalar=rstd[:, :], in1=g_pre[:, :], op0=ALU.mult, op1=ALU.mult),
                reads=[xb, rstd, g_pre], writes=[h])

    def transposes(g):
        for t in range(2):
            h = hb[t]
            for k in range(8):
                P.op("pe", lambda e, h=h, k=k: e.transpose(out=tp[:, k * 128:(k + 1) * 128],
                                                          in_=h[:, k * 128:(k + 1) * 128], identity=C.ident[:, :]),
                     reads=[h, C.ident], writes=[tp])
            dst = hT[g % 2]
            P.op("act", lambda e, dst=dst, t=t: e.activation(
                out=dst[:, :, t * 128:(t + 1) * 128], in_=tp[:, :].rearrange("p (k n) -> p k n", k=8), func=AF.Copy),
                reads=[tp], writes=[dst])

    def phaseA(g):
        h_t = hT[g % 2]
        for j in range(NJ):
            pg = gu[j % 2]
            for which, W in ((0, wg), (1, wu)):
                for k in range(8):
                    wb, wap = wslice(W, k, j * 128, (j + 1) * 128)
                    P.op("pe", lambda e, pg=pg, which=which, wap=wap, k=k: e.matmul(
                        pg[:, which, :], lhsT=wap, rhs=h_t[:, k, :], start=(k == 0), stop=(k == 7)),
                        reads=[wb, h_t], writes=[pg])
            s = sg[j % 2]
            P.op("act", lambda e, s=s, pg=pg: e.activation(out=s[:, :], in_=pg[:, 0, :], func=AF.Silu),
                 reads=[pg], writes=[s])
            P.op("dve", lambda e, s=s, pg=pg, j=j: e.tensor_tensor(out=aT[:, j, :], in0=s[:, :], in1=pg[:, 1, :],
                                                                   op=ALU.mult),
                 reads=[s, pg], writes=[aT])

    def phaseB(g):
        for t in range(2):
            y = ys[t]
            for half in range(2):
                for j in range(NJ):
                    wb, wap = wslice(wd, j, half * 512, (half + 1) * 512)
                    P.op("pe", lambda e, y=y, half=half, wap=wap, j=j, t=t: e.matmul(
                        y[:, half * 512:(half + 1) * 512], lhsT=aT[:, j, t * 128:(t + 1) * 128], rhs=wap,
                        start=(j == 0), stop=(j == NJ - 1)),
                        reads=[wb, aT], writes=[y])

    def post(g):
        for t in range(2):
            i = 2 * g + t
            y = ys[t]
            rstd = rms_stats(P, C, y, y[:, :], True)
            t1 = P.sb([128, D], F32, "t1") if not hasattr(C, "t1") else C.t1
            C.t1 = t1
            P.op("dve", lambda e, y=y, rstd=rstd: e.scalar_tensor_tensor(
                out=t1[:, :], in0=y[:, :], scalar=rstd[:, :], in1=g_post[:, :], op0=ALU.mult, op1=ALU.mult),
                reads=[y, rstd, g_post], writes=[t1])
            xb = xs.pop(i)
            P.op("dve", lambda e, xb=xb: e.scalar_tensor_tensor(
                out=xb[:, :], in0=t1[:, :], scalar=0.5, in1=xb[:, :], op0=ALU.mult, op1=ALU.add),
                reads=[t1, xb], writes=[xb])
            put_x(i, xb)

    prep(0)
    transposes(0)
    for g in range(ngroups):
        phaseA(g)
        if g + 1 < ngroups:
            prep(g + 1)
            transposes(g + 1)
        phaseB(g)
        post(g)


def build_consts(P, ident_d):
    C = Consts()
    C.ident = P.sb([128, 128], BF16, "ident")
    P.dma("pool", C.ident[:, :], ident_d[:, :], dst=C.ident)
    C.junk = P.sb([128, D], BF16, "junk")
    C.junk.tracked = False
    C.mhalf = P.sb([128, 1], F32, "mhalf")
    P.op("pool", lambda e: e.memset(C.mhalf[:, :], -0.5), writes=[C.mhalf])
    return C


def build_test_ffn(NT):
    P = Prog()
    x_d = P.dram("x", [NT, D], F32, "ExternalInput")
    gains_d = P.dram("gains", [2, 128, D], F32, "ExternalInput")
    wg_d = P.dram("wg", [D, DFF], F32, "ExternalInput")
    wu_d = P.dram("wu", [D, DFF], F32, "ExternalInput")
    wd_d = P.dram("wd", [DFF, D], F32, "ExternalInput")
    ident_d = P.dram("ident", [128, 128], F32, "ExternalInput")
    out_d = P.dram("out", [NT, D], F32, "ExternalOutput")
    C = build_consts(P, ident_d)
    g_pre = P.sb([128, D], F32, "gpre")
    g_post = P.sb([128, D], F32, "gpost")
    P.dma("sp", g_pre[:, :], gains_d[0], dst=g_pre)
    P.dma("sp", g_post[:, :], gains_d[1], dst=g_post)
    wg = load_w_cast(P, wg_d, D, DFF, 512, "wg")
    wu = load_w_cast(P, wu_d, D, DFF, 512, "wu")
    wd = load_w_cast(P, wd_d, DFF, D, 512, "wd")
    xpool = [P.sb([128, D], F32, "x") for _ in range(4)]

    def get_x(i):
        b = xpool[i % 4]
        P.dma("sp", b[:, :], x_d[i * 128:(i + 1) * 128, :], dst=b)
        return b

    def put_x(i, b):
        P.dma("sp", out_d[i * 128:(i + 1) * 128, :], b[:, :], src=b)

    ffn_phase(P, C, NT // 128, get_x, put_x, g_pre, g_post, wg, wu, wd)
    print("streams", P.stats(), "sems", P.n_sems)
    return P.finish()


import numpy as np

EPS = 1e-6


def hgrn_consts_np():
    s = np.arange(128)
    same = (s[:, None] // 64) == (s[None, :] // 64)
    M1 = (same & (s[:, None] <= s[None, :])).astype(np.float32)
    M2 = (same & (s[:, None] > s[None, :])).astype(np.float32)
    return M1, M2


def hgrn_phase(P, S, d):
    nt = S // 128
    M1 = P.sb([128, 128], F32, "M1")
    M2 = P.sb([128, 128], F32, "M2")
    P.dma("sp", M1[:, :], d["M1"][:, :], dst=M1)
    P.dma("sp", M2[:, :], d["M2"][:, :], dst=M2)
    gn = P.sb([128, 128], F32, "gn")
    P.dma("sp", gn[:, :], d["gnorm"][:, :], dst=gn)
    mhalf = P.sb([128, 1], F32, "mhalf")
    P.op("pool", lambda e: e.memset(mhalf[:, :], -0.5), writes=[mhalf])
    lbl = P.sb([128, 4], F32, "lbl")
    lmk = P.sb([128, 4], F32, "lmk")
    P.dma("sp", lbl[:, :], d["lbT"][:, :], dst=lbl)
    P.dma("sp", lmk[:, :], d["lmask"][:, :], dst=lmk)
    ex = P.sb([128, 4], F32, "ex")
    P.op("act", lambda e: e.activation(out=ex[:, :], in_=lbl[:, :], func=AF.Exp), reads=[lbl], writes=[ex])
    den = P.sb([128, 1], F32, "den")
    P.op("dve", lambda e: e.reduce_sum(out=den[:, :], in_=ex[:, :], axis=AX.X), reads=[ex], writes=[den])
    rden = P.sb([128, 1], F32, "rden")
    P.op("dve", lambda e: e.reciprocal(out=rden[:, :], in_=den[:, :]), reads=[den], writes=[rden])
    exm = P.sb([128, 4], F32, "exm")
    P.op("dve", lambda e: e.tensor_tensor(out=exm[:, :], in0=ex[:, :], in1=lmk[:, :], op=ALU.mult),
         reads=[ex, lmk], writes=[exm])
    num = P.sb([128, 1], F32, "num")
    P.op("dve", lambda e: e.reduce_sum(out=num[:, :], in_=exm[:, :], axis=AX.X), reads=[exm], writes=[num])
    lbc = P.sb([128, 1], F32, "lbc")
    P.op("dve", lambda e: e.tensor_tensor(out=lbc[:, :], in0=num[:, :], in1=rden[:, :], op=ALU.mult),
         reads=[num, rden], writes=[lbc])
    omlc = P.sb([128, 1], F32, "omlc")
    P.op("dve", lambda e: e.tensor_scalar(out=omlc[:, :], in0=lbc[:, :], scalar1=-1.0, scalar2=1.0,
                                          op0=ALU.mult, op1=ALU.add), reads=[lbc], writes=[omlc])
    nomlc = P.sb([128, 1], F32, "nomlc")
    P.op("dve", lambda e: e.tensor_scalar(out=nomlc[:, :], in0=omlc[:, :], scalar1=-1.0, scalar2=None,
                                          op0=ALU.mult), reads=[omlc], writes=[nomlc])
    lbr = P.sb([128, 4, 128], F32, "lbr")
    lmr = P.sb([128, 4, 128], F32, "lmr")
    P.dma("sp", lbr[:, :, :], d["lbrow"][:, :, :], dst=lbr)
    P.dma("sp", lmr[:, :, :], d["lmaskrow"][:, :, :], dst=lmr)
    exr = P.sb([128, 4, 128], F32, "exr")
    P.op("act", lambda e: e.activation(out=exr[:, :, :], in_=lbr[:, :, :], func=AF.Exp), reads=[lbr], writes=[exr])
    denr = P.sb([128, 128], F32, "denr")
    P.op("dve", lambda e: e.tensor_tensor(out=denr[:, :], in0=exr[:, 0, :], in1=exr[:, 1, :], op=ALU.add),
         reads=[exr], writes=[denr])
    P.op("dve", lambda e: e.tensor_tensor(out=denr[:, :], in0=denr[:, :], in1=exr[:, 2, :], op=ALU.add),
         reads=[exr, denr], writes=[denr])
    P.op("dve", lambda e: e.tensor_tensor(out=denr[:, :], in0=denr[:, :], in1=exr[:, 3, :], op=ALU.add),
         reads=[exr, denr], writes=[denr])
    P.op("dve", lambda e: e.reciprocal(out=denr[:, :], in_=denr[:, :]), reads=[denr], writes=[denr])
    P.op("dve", lambda e: e.tensor_tensor(out=exr[:, :, :], in0=exr[:, :, :], in1=lmr[:, :, :], op=ALU.mult),
         reads=[exr, lmr], writes=[exr])
    lbrow = P.sb([128, 128], F32, "lbrow")
    P.op("dve", lambda e: e.tensor_tensor(out=lbrow[:, :], in0=exr[:, 0, :], in1=exr[:, 1, :], op=ALU.add),
         reads=[exr], writes=[lbrow])
    P.op("dve", lambda e: e.tensor_tensor(out=lbrow[:, :], in0=lbrow[:, :], in1=exr[:, 2, :], op=ALU.add),
         reads=[exr, lbrow], writes=[lbrow])
    P.op("dve", lambda e: e.tensor_tensor(out=lbrow[:, :], in0=lbrow[:, :], in1=exr[:, 3, :], op=ALU.add),
         reads=[exr, lbrow], writes=[lbrow])
    P.op("dve", lambda e: e.tensor_tensor(out=lbrow[:, :], in0=lbrow[:, :], in1=denr[:, :], op=ALU.mult),
         reads=[lbrow, denr], writes=[lbrow])
    omlrow = P.sb([128, 128], F32, "omlrow")
    P.op("dve", lambda e: e.tensor_scalar(out=omlrow[:, :], in0=lbrow[:, :], scalar1=-1.0, scalar2=1.0,
                                          op0=ALU.mult, op1=ALU.add), reads=[lbrow], writes=[omlrow])

    NB = 2
    qT = [P.sb([128, 128], F32, "qT") for _ in range(NB)]
    zfT = [P.sb([128, 128], F32, "zfT") for _ in range(NB)]
    zft = [P.sb([128, 128], F32, "zft") for _ in range(NB)]
    vt = [P.sb([128, 128], F32, "vt") for _ in range(NB)]
    gt = [P.sb([128, 128], F32, "gt") for _ in range(NB)]
    vb = [P.sb([128, 128], BF16, "vb") for _ in range(NB)]
    sigt_l = [P.sb([128, 128], F32, "sigt") for _ in range(NB)]
    logf = [P.sb([128, 128], F32, "logf") for _ in range(NB)]
    kt_l = [P.sb([128, 128], F32, "kt") for _ in range(NB)]
    sigf_l = [P.sb([128, 128], F32, "sigf") for _ in range(NB)]
    kT_l = [P.sb([128, 128], F32, "kT") for _ in range(NB)]
    dd_l = [P.sb([128, 128], F32, "dd") for _ in range(NB)]
    eq_l = [P.sb([128, 128], F32, "eq") for _ in range(NB)]
    ek_l = [P.sb([128, 128], F32, "ek") for _ in range(NB)]
    eb_l = [P.sb([128, 128], F32, "eb") for _ in range(NB)]
    er_l = [P.sb([128, 128], F32, "er") for _ in range(NB)]
    qtl = [P.sb([128, 128], BF16, "qtl") for _ in range(NB)]
    ktl = [P.sb([128, 128], BF16, "ktl") for _ in range(NB)]
    Q0 = [P.sb([128, 128], BF16, "Q0") for _ in range(NB)]
    Q1 = [P.sb([128, 128], BF16, "Q1") for _ in range(NB)]
    K0 = [P.sb([128, 128], BF16, "K0") for _ in range(NB)]
    K1 = [P.sb([128, 128], BF16, "K1") for _ in range(NB)]
    for i in range(NB):
        for b in (Q0[i], Q1[i], K0[i], K1[i]):
            P.op("pool", lambda e, b=b: e.memset(b[:, :], 0.0), writes=[b])
    dec = [P.sb([128, 2], F32, "dec") for _ in range(NB)]
    ATm = [P.sb([128, 128], BF16, "ATm") for _ in range(NB)]
    Sf = [P.sb([128, 128], F32, "Sf") for _ in range(2)]
    Sb = [P.sb([128, 128], BF16, "Sb") for _ in range(4)]
    P.op("pool", lambda e: e.memset(Sf[0][:, :], 0.0), writes=[Sf[0]])
    P.op("pool", lambda e: e.memset(Sb[0][:, :], 0.0), writes=[Sb[0]])
    junk = P.sb([128, 128], F32, "junk")
    junk.tracked = False
    sg_l = [P.sb([128, 128], F32, "sg") for _ in range(NB)]
    yo = [P.sb([128, 128], F32, "yo") for _ in range(NB)]
    p_misc = [P.ps([128, 512], F32, "p_misc") for _ in range(NB)]
    p_o = [P.ps([128, 512], F32, "p_o") for _ in range(2)]
    p_S = [P.ps([128, 512], F32, "p_S") for _ in range(2)]

    st_ = {"sidx": 0}

    class PV:
        def __init__(self, buf, off):
            self.buf, self.off = buf, off

    def h1(i):
        n = i % NB
        ts = slice(i * 128, (i + 1) * 128)
        sigt, kt, sigf, kT, dd, eq, ek, eb, er = (sigt_l[n], kt_l[n], sigf_l[n], kT_l[n], dd_l[n], eq_l[n], ek_l[n],
                                                  eb_l[n], er_l[n])
        pm = p_misc[n]
        P.dma("sp", qT[n][:, :], d["qT"][:, ts], dst=qT[n])
        P.dma("sp", zfT[n][:, :], d["zfT"][:, ts], dst=zfT[n])
        P.dma("sp", zft[n][:, :], d["zf"][ts, :], dst=zft[n])
        P.dma("sp", vt[n][:, :], d["v"][ts, :], dst=vt[n])
        P.dma("sp", gt[n][:, :], d["g"][ts, :], dst=gt[n])
        P.op("act", lambda e, n=n: e.activation(out=sigt[:, :], in_=zft[n][:, :], func=AF.Sigmoid),
             reads=[zft[n]], writes=[sigt])
        lf = logf[n]
        P.op("dve", lambda e, lf=lf: e.tensor_tensor(out=lf[:, :], in0=sigt[:, :], in1=omlrow[:, :], op=ALU.mult),
             reads=[sigt, omlrow], writes=[lf])
        P.op("dve", lambda e, lf=lf: e.tensor_tensor(out=lf[:, :], in0=lf[:, :], in1=lbrow[:, :], op=ALU.add),
             reads=[lf, lbrow], writes=[lf])
        P.op("dve", lambda e, lf=lf: e.tensor_scalar(out=lf[:, :], in0=lf[:, :], scalar1=1e-30, scalar2=None,
                                                     op0=ALU.max), reads=[lf], writes=[lf])
        P.op("act", lambda e, lf=lf: e.activation(out=lf[:, :], in_=lf[:, :], func=AF.Ln), reads=[lf], writes=[lf])
        P.op("dve", lambda e: e.tensor_tensor(out=kt[:, :], in0=sigt[:, :], in1=omlrow[:, :], op=ALU.mult),
             reads=[sigt, omlrow], writes=[kt])
        P.op("dve", lambda e: e.tensor_tensor(out=kt[:, :], in0=omlrow[:, :], in1=kt[:, :], op=ALU.subtract),
             reads=[kt, omlrow], writes=[kt])
        P.op("pe", lambda e, lf=lf: e.matmul(pm[:, 0:128], lhsT=lf[:, :], rhs=M1[:, :], start=True, stop=True),
             reads=[lf, M1], writes=[pm])
        P.op("pe", lambda e, lf=lf: e.matmul(pm[:, 128:256], lhsT=M2[:, :], rhs=lf[:, :], start=True, stop=True),
             reads=[lf, M2], writes=[pm])
        P.op("act", lambda e, n=n: e.activation(out=sigf[:, :], in_=zfT[n][:, :], func=AF.Sigmoid),
             reads=[zfT[n]], writes=[sigf])
        P.op("dve", lambda e: e.tensor_scalar(out=kT[:, :], in0=sigf[:, :], scalar1=nomlc[:, :], scalar2=omlc[:, :],
                                              op0=ALU.mult, op1=ALU.add), reads=[sigf, nomlc, omlc], writes=[kT])
        for c in range(2):
            cs = slice(c * 64, (c + 1) * 64)
            P.op("dve", lambda e, cs=cs, c=c: e.tensor_scalar(
                out=dd[:, cs], in0=pm[:, cs], scalar1=pm[:, c * 64 + 31:c * 64 + 32], scalar2=None,
                op0=ALU.subtract), reads=[pm], writes=[dd])
        P.op("dve", lambda e: e.tensor_scalar(out=eq[:, :], in0=dd[:, :], scalar1=43.0, scalar2=None, op0=ALU.min),
             reads=[dd], writes=[eq])
        P.op("dve", lambda e: e.tensor_scalar(out=ek[:, :], in0=dd[:, :], scalar1=-1.0, scalar2=43.0,
                                              op0=ALU.mult, op1=ALU.min), reads=[dd], writes=[ek])
        P.op("act", lambda e: e.activation(out=eq[:, :], in_=eq[:, :], func=AF.Exp), reads=[eq], writes=[eq])
        P.op("act", lambda e: e.activation(out=ek[:, :], in_=ek[:, :], func=AF.Exp), reads=[ek], writes=[ek])
        P.op("act", lambda e: e.activation(out=eb[:, :], in_=pm[:, 0:128], func=AF.Exp), reads=[pm], writes=[eb])
        P.op("act", lambda e: e.activation(out=er[:, :], in_=pm[:, 128:256], func=AF.Exp), reads=[pm], writes=[er])
        dc = dec[n]
        P.op("dve", lambda e, dc=dc: e.tensor_copy(out=dc[:, :], in_=eb[:, :].rearrange("p (c s) -> p c s", c=2)[:, :, 63]),
             reads=[eb], writes=[dc])
        P.op("dve", lambda e, n=n: e.tensor_tensor(out=qtl[n][:, :], in0=qT[n][:, :], in1=eq[:, :], op=ALU.mult),
             reads=[qT[n], eq], writes=[qtl[n]])
        P.op("dve", lambda e, n=n: e.tensor_tensor(out=ktl[n][:, :], in0=kT[:, :], in1=ek[:, :], op=ALU.mult),
             reads=[kT, ek], writes=[ktl[n]])
        P.op("dve", lambda e, n=n: e.tensor_tensor(out=Q0[n][:, 0:64], in0=qT[n][:, 0:64], in1=eb[:, 0:64], op=ALU.mult),
             reads=[qT[n], eb], writes=[Q0[n]])
        P.op("dve", lambda e, n=n: e.tensor_tensor(out=Q1[n][:, 64:128], in0=qT[n][:, 64:128], in1=eb[:, 64:128],
                                                   op=ALU.mult), reads=[qT[n], eb], writes=[Q1[n]])
        P.op("dve", lambda e, n=n: e.tensor_tensor(out=K0[n][0:64, :], in0=kt[0:64, :], in1=er[0:64, :], op=ALU.mult),
             reads=[kt, er], writes=[K0[n]])
        P.op("dve", lambda e, n=n: e.tensor_tensor(out=K1[n][64:128, :], in0=kt[64:128, :], in1=er[64:128, :],
                                                   op=ALU.mult), reads=[kt, er], writes=[K1[n]])
        P.op("pool", lambda e, n=n: e.tensor_copy(out=vb[n][:, :], in_=vt[n][:, :]), reads=[vt[n]], writes=[vb[n]])
        P.op("pe", lambda e, n=n: e.matmul(pm[:, 256:384], lhsT=ktl[n][:, :], rhs=qtl[n][:, :], start=True, stop=True),
             reads=[ktl[n], qtl[n]], writes=[pm])
        P.op("dve", lambda e, n=n: e.tensor_tensor(out=ATm[n][:, :], in0=pm[:, 256:384], in1=M1[:, :], op=ALU.mult),
             reads=[pm, M1], writes=[ATm[n]])
    def h2(i):
        n = i % NB
        ts = slice(i * 128, (i + 1) * 128)
        dc = dec[n]
        sg = sg_l[n]
        sidx = st_["sidx"]
        po = p_o[i % 2]
        s0 = sidx
        P.op("pe", lambda e, n=n, po=po: e.matmul(po[:, 0:128], lhsT=ATm[n][:, :], rhs=vb[n][:, :], start=True, stop=False),
             reads=[ATm[n], vb[n]], writes=[po])
        P.op("pe", lambda e, n=n, po=po, s0=s0: e.matmul(po[:, 0:128], lhsT=Q0[n][:, :], rhs=Sb[s0 % 4][:, :],
                                                          start=False, stop=False),
             reads=[Q0[n], Sb[s0 % 4]], writes=[po])
        for c in range(2):
            Kc = (K0, K1)[c][n]
            pS = p_S[c]
            P.op("pe", lambda e, Kc=Kc, pS=pS, n=n: e.matmul(pS[:, 0:128], lhsT=Kc[:, :], rhs=vb[n][:, :],
                                                             start=True, stop=True),
                 reads=[Kc, vb[n]], writes=[pS])
            so, sn = Sf[sidx % 2], Sf[(sidx + 1) % 2]
            P.op("dve", lambda e, so=so, sn=sn, pS=pS, dc=dc, c=c: e.scalar_tensor_tensor(
                out=sn[:, :], in0=so[:, :], scalar=dc[:, c:c + 1], in1=pS[:, 0:128], op0=ALU.mult, op1=ALU.add),
                reads=[so, dc, pS], writes=[sn])
            sbn = Sb[(sidx + 1) % 4]
            P.op("act", lambda e, sbn=sbn, sn=sn: e.activation(out=sbn[:, :], in_=sn[:, :], func=AF.Copy),
                 reads=[sn], writes=[sbn])
            sidx += 1
        P.op("pe", lambda e, n=n, po=po, s0=s0: e.matmul(po[:, 0:128], lhsT=Q1[n][:, :], rhs=Sb[(s0 + 1) % 4][:, :],
                                                          start=False, stop=True),
             reads=[Q1[n], Sb[(s0 + 1) % 4]], writes=[po])
        ss = P.sb([128, 1], F32, "ss")
        P.op("act", lambda e, po=po, ss=ss: e.activation(out=junk[:, :], in_=po[:, 0:128], func=AF.Square, accum_out=ss[:, :]),
             reads=[po], writes=[ss])
        ms = P.sb([128, 1], F32, "ms")
        P.op("dve", lambda e, ss=ss, ms=ms: e.tensor_scalar(out=ms[:, :], in0=ss[:, :], scalar1=1.0 / 128, scalar2=EPS,
                                                            op0=ALU.mult, op1=ALU.add), reads=[ss], writes=[ms])
        rstd = P.sb([128, 1], F32, "rstd")
        P.op("pool", lambda e, ms=ms, rstd=rstd: e.tensor_tensor(out=rstd[:, :], in0=ms[:, :], in1=mhalf[:, :], op=ALU.pow),
             reads=[ms, mhalf], writes=[rstd])
        P.op("act", lambda e, n=n: e.activation(out=sg[:, :], in_=gt[n][:, :], func=AF.Silu), reads=[gt[n]], writes=[sg])
        P.op("dve", lambda e: e.tensor_tensor(out=sg[:, :], in0=sg[:, :], in1=gn[:, :], op=ALU.mult),
             reads=[sg, gn], writes=[sg])
        y = yo[n]
        P.op("dve", lambda e, y=y, po=po, rstd=rstd: e.scalar_tensor_tensor(
            out=y[:, :], in0=po[:, 0:128], scalar=rstd[:, :], in1=sg[:, :], op0=ALU.mult, op1=ALU.mult),
            reads=[po, rstd, sg], writes=[y])
        P.dma("sp", d["y"][ts, :], y[:, :], src=y)
        st_["sidx"] = sidx

    h1(0)
    for i in range(nt):
        if i + 1 < nt:
            h1(i + 1)
        h2(i)


def build_test_hgrn(S):
    P = Prog()
    d = {}
    for nm, shp in (("qT", [128, S]), ("zfT", [128, S]), ("zf", [S, 128]), ("v", [S, 128]), ("g", [S, 128]),
                    ("lbT", [128, 4]), ("lmask", [128, 4]), ("lbrow", [128, 4, 128]), ("lmaskrow", [128, 4, 128]),
                    ("gnorm", [128, 128]), ("M1", [128, 128]), ("M2", [128, 128])):
        d[nm] = P.dram(nm, shp, F32, "ExternalInput")
    d["y"] = P.dram("y", [S, 128], F32, "ExternalOutput")
    hgrn_phase(P, S, d)
    print("streams", P.stats(), "sems", P.n_sems)
    return P.finish()


import numpy as np

BIGF = 1.0e6


def nsa_consts_np(S):
    nsel = S // 64
    ncmp = S // 16 - 1
    ncp = ((ncmp + 127) // 128) * 128
    nch = S // 128
    p = np.arange(128)
    D1 = (p[:, None] - p[None, :]).astype(np.float32)
    D16 = (16 * p[:, None] - p[None, :]).astype(np.float32)
    Mdiag = (D1 <= 0).astype(np.float32)
    Mlow = (D1 > 0).astype(np.float32)
    j = np.arange(nsel)
    E = np.zeros((nsel, nch, 128), np.float32)
    for c in range(nch):
        E[:, c, :] = (j[:, None] == (2 * c + p[None, :] // 64))
    n = np.arange(ncp)
    cs = n * 16
    ss = j * 64
    ov = ((cs[:, None] < ss[None, :] + 64) & (cs[:, None] + 32 > ss[None, :]) & (n[:, None] < ncmp)).astype(np.float32)
    x = np.arange(2 * nsel) - nsel
    cur_rel = (p >= 64).astype(np.int64)
    forced = (x[None, :] == cur_rel[:, None]) | (x[None, :] == cur_rel[:, None] - 1)
    invalid = x[None, :] > cur_rel[:, None]
    Amul = (~forced & ~invalid).astype(np.float32)
    Aadd = np.where(forced, BIGF, np.where(invalid, -BIGF, 0.0)).astype(np.float32)
    ident = np.eye(128, dtype=np.float32)
    return dict(D16=D16, Mdiag=Mdiag, Mlow=Mlow, E=E, ov=ov, Amul=Amul, Aadd=Aadd, identb=ident)


def nsa_phase(P, S, NQB, d):
    nsel = S // 64
    ncmp = S // 16 - 1
    ncp = ((ncmp + 127) // 128) * 128
    ncc = ncp // 128
    nch = S // 128
    W = 65 + nsel

    def cload(name, shape, dt, src, q="pool"):
        b = P.sb(shape, dt, name)
        P.dma(q, b[tuple(slice(None) for _ in shape)], src, dst=b)
        return b

    D16 = cload("D16", [128, 128], F32, d["D16"][:, :], "sp")
    Mdiag = cload("Mdiag", [128, 128], BF16, d["Mdiag"][:, :])
    Mlow = cload("Mlow", [128, 128], BF16, d["Mlow"][:, :])
    Ec = cload("E", [nsel, nch, 128], BF16, d["E"][:, :, :])
    Amul = cload("Amul", [128, 2 * nsel], F32, d["Amul"][:, :], "sp")
    Aadd = cload("Aadd", [128, 2 * nsel], F32, d["Aadd"][:, :], "sp")
    identb = cload("identb", [128, 128], BF16, d["identb"][:, :])
    ksT = cload("ksT", [64, S], BF16, d["ksT"][:, :])
    kwT = cload("kwT", [64, S], BF16, d["kwT"][:, :])
    vs = cload("vs", [128, nch, 65], BF16, d["vs"][:, :, :])
    vw = cload("vw", [128, nch, 65], BF16, d["vw"][:, :, :])
    KcT = P.sb([64, ncp], BF16, "KcT")
    Vc = P.sb([128, ncc, W], BF16, "Vc")
    P.op("pool", lambda e: e.memset(KcT[:, :], 0.0), writes=[KcT])
    P.op("pool", lambda e: e.memset(Vc[:, :, :], 0.0), writes=[Vc])
    P.op("pool", lambda e: e.memset(Vc[:, :, 64:65], 1.0), writes=[Vc])
    P.dma("pool", Vc[:, :, 65:W], d["ov"][:, :, :], dst=Vc)

    ps_h = [P.ps([128, 512], F32, "ps_h") for _ in range(2)]
    ps_b = P.ps([128, 512], F32, "ps_b")
    pOs = P.ps([128, 512], F32, "pOs")
    ps_o = pOs
    for X in range(2):
        x2T = cload("x2T", [128, S], BF16, d["kc2T" if X == 0 else "vc2T"][:, :])
        pe2 = cload("pe2", [128, 16], BF16, d["pe2"][X])
        w1 = cload("w1", [128, 16, 256], BF16, d["w1"][X])
        w2 = cload("w2", [128, 2, 64], BF16, d["w2"][X])
        GT = P.sb([128, 2, ncp], BF16, "GT")
        P.op("pool", lambda e, GT=GT: e.memset(GT[:, :, :], 0.0), writes=[GT])
        x2v = x2T[:, :].rearrange("p (n s) -> p n s", s=16)
        for hc in range(2):
            for j in range(16):
                P.op("pe", lambda e, j=j, hc=hc, w1=w1, pe2=pe2: e.matmul(
                    ps_b[:, 0:1], lhsT=w1[:, j, hc * 128:(hc + 1) * 128], rhs=pe2[:, j:j + 1],
                    start=(j == 0), stop=(j == 15)), reads=[w1, pe2], writes=[ps_b])
            c1 = P.sb([128, 1], F32, "c1")
            P.op("dve", lambda e, c1=c1: e.tensor_copy(out=c1[:, :], in_=ps_b[:, 0:1]), reads=[ps_b], writes=[c1])
            ph = ps_h[hc]
            for j in range(16):
                if j < 8:
                    rhs = x2v[:, 0:ncmp, 2 * j]
                else:
                    rhs = x2v[:, 1:ncmp + 1, 2 * j - 16]
                P.op("pe", lambda e, j=j, hc=hc, w1=w1, rhs=rhs, ph=ph: e.matmul(
                    ph[:, 0:ncmp], lhsT=w1[:, j, hc * 128:(hc + 1) * 128], rhs=rhs,
                    start=(j == 0), stop=(j == 15)), reads=[w1, x2T], writes=[ph])
            xs = P.sb([128, ncp], F32, "xs")
            P.op("act", lambda e, xs=xs, ph=ph, c1=c1: e.activation(out=xs[:, 0:ncmp], in_=ph[:, 0:ncmp],
                                                                     func=AF.Identity, bias=c1[:, :]),
                 reads=[ph, c1], writes=[xs])
            t2 = P.sb([128, ncp], F32, "t2")
            P.op("dve", lambda e, xs=xs, t2=t2: e.tensor_tensor(out=t2[:, 0:ncmp], in0=xs[:, 0:ncmp], in1=xs[:, 0:ncmp],
                                                                op=ALU.mult), reads=[xs], writes=[t2])
            P.op("dve", lambda e, t2=t2: e.tensor_scalar(out=t2[:, 0:ncmp], in0=t2[:, 0:ncmp], scalar1=0.044715,
                                                         scalar2=1.0, op0=ALU.mult, op1=ALU.add), reads=[t2], writes=[t2])
            P.op("dve", lambda e, xs=xs, t2=t2: e.tensor_tensor(out=t2[:, 0:ncmp], in0=t2[:, 0:ncmp], in1=xs[:, 0:ncmp],
                                                                op=ALU.mult), reads=[xs, t2], writes=[t2])
            P.op("act", lambda e, t2=t2: e.activation(out=t2[:, 0:ncmp], in_=t2[:, 0:ncmp], func=AF.Sigmoid,
                                                      scale=1.5957691216057308), reads=[t2], writes=[t2])
            P.op("dve", lambda e, xs=xs, t2=t2, GT=GT, hc=hc: e.tensor_tensor(
                out=GT[:, hc, 0:ncmp], in0=t2[:, 0:ncmp], in1=xs[:, 0:ncmp], op=ALU.mult), reads=[xs, t2], writes=[GT])
        if X == 0:
            for hc in range(2):
                P.op("pe", lambda e, hc=hc, w2=w2, GT=GT: e.matmul(ps_o[0:64, 0:ncp], lhsT=w2[:, hc, :], rhs=GT[:, hc, :],
                                                                    start=(hc == 0), stop=(hc == 1)),
                     reads=[w2, GT], writes=[ps_o])
            P.op("act", lambda e: e.activation(out=KcT[:, 0:ncmp], in_=ps_o[0:64, 0:ncmp], func=AF.Copy),
                 reads=[ps_o], writes=[KcT])
        else:
            for c in range(ncc):
                for hc in range(2):
                    P.op("pe", lambda e, hc=hc, c=c, w2=w2, GT=GT: e.matmul(
                        ps_o[:, 0:64], lhsT=GT[:, hc, c * 128:(c + 1) * 128], rhs=w2[:, hc, :],
                        start=(hc == 0), stop=(hc == 1)), reads=[w2, GT], writes=[ps_o])
                P.op("act", lambda e, c=c: e.activation(out=Vc[:, c, 0:64], in_=ps_o[:, 0:64], func=AF.Copy),
                     reads=[ps_o], writes=[Vc])

    pS = [ps_h[0], ps_h[1], P.ps([128, 512], F32, "pS2")]
    pM = ps_b
    pOc = [P.ps([128, 512], F32, "pOc") for _ in range(2)]
    pOw = P.ps([128, 512], F32, "pOw")
    WMk = cload("WM", [128, 8, 128], BF16, d["WM"][:, :, :])
    Tt = cload("Tt", [128, NQB, ncc], F32, d["Tt"][:, :, :], "sp")
    identf = cload("identf", [128, 128], F32, d["identb"][:, :], "sp")
    QTb = [P.sb([64, 512], BF16, "QT") for _ in range(2)]
    glb = [P.sb([128, 12], F32, "gl") for _ in range(2)]
    NPT = 6
    PT = [P.sb([128, 4, 128], BF16, "PT") for _ in range(NPT)]
    PTm = [P.sb([128, 4, 128], BF16, "PTm") for _ in range(NPT)]
    mk = [P.sb([128, 128], BF16, "mk") for _ in range(3)]
    rd = [P.sb([128, 4], F32, "rd") for _ in range(3)]
    wgt = [P.sb([128, 4], F32, "wgt") for _ in range(3)]
    imp = P.sb([128, nsel], F32, "imp")
    sc = P.sb([128, nsel], F32, "sc")
    sc2 = P.sb([128, nsel], F32, "sc2")
    m8a = P.sb([128, 8], F32, "m8a")
    m8b = P.sb([128, 8], F32, "m8b")
    self_f = P.sb([128, nsel], F32, "self")
    selT = [P.sb([nsel, 128], BF16, "selT") for _ in range(2)]
    yacc = [P.sb([128, 4, 64], F32, "yacc") for _ in range(2)]
    state = {"pt": 0, "ps": 0, "mk": 0, "pm": 0}

    class Task:
        pass

    def chunk_task(kT, c, QT, mask_kind, mask_arg, vaug, vw_cols, pouts, first, last, selTb=None):
        t = Task()
        ps = pS[state["ps"] % 3]
        state["ps"] += 1
        pt = PT[state["pt"] % NPT]
        ptm = PTm[state["pt"] % NPT]
        state["pt"] += 1

        def s1():
            P.op("pe", lambda e: e.matmul(ps[:, :], lhsT=kT[:, c * 128:(c + 1) * 128], rhs=QT[:, :], start=True, stop=True),
                 reads=[kT, QT], writes=[ps])

        def s2():
            P.op("act", lambda e: e.activation(out=pt[:, :, :], in_=ps[:, :].rearrange("p (h q) -> p h q", h=4),
                                               func=AF.Exp, scale=0.125), reads=[ps], writes=[pt])
            src = pt
            mbuf = None
            if mask_kind == "none":
                pass
            elif mask_kind == "sb":
                mbuf, map_ = mask_arg
            elif mask_kind == "cmp":
                m_i, cc = mask_arg
                mbuf = mk[state["mk"] % 3]
                state["mk"] += 1
                P.op("dve", lambda e, mb=mbuf: e.tensor_scalar(out=mb[:, :], in0=D16[:, :], scalar1=Tt[:, m_i, cc:cc + 1],
                                                               scalar2=None, op0=ALU.is_le), reads=[D16, Tt], writes=[mbuf])
                map_ = mbuf[:, :]
            elif mask_kind == "slc":
                tail = mask_arg
                off = 256 * (state["pm"] % 2)
                state["pm"] += 1
                P.op("pe", lambda e: e.matmul(pM[0:128, off:off + 128], lhsT=Ec[:, c, :], rhs=selTb[:, :], start=True, stop=True),
                     reads=[Ec, selTb], writes=[pM])
                if tail is None:
                    mbuf, map_ = pM, pM[:, off:off + 128]
                else:
                    mbuf = mk[state["mk"] % 3]
                    state["mk"] += 1
                    P.op("dve", lambda e, mb=mbuf: e.tensor_tensor(out=mb[:, :], in0=pM[:, off:off + 128], in1=WMk[:, tail, :],
                                                                   op=ALU.mult), reads=[pM, WMk], writes=[mbuf])
                    map_ = mbuf[:, :]
            if mbuf is not None:
                P.op("dve", lambda e: e.tensor_tensor(out=ptm[:, :, :], in0=pt[:, :, :],
                                                      in1=map_.unsqueeze(1).to_broadcast([128, 4, 128]), op=ALU.mult),
                     reads=[pt, mbuf], writes=[ptm])
                src = ptm
            t.src = src

        def s3():
            src = t.src
            for pb, heads in pouts:
                for hi, h in enumerate(heads):
                    st = first and hi == 0
                    P.op("pe", lambda e, pb=pb, hi=hi, h=h, st=st: e.matmul(
                        pb[:, hi * vw_cols:(hi + 1) * vw_cols], lhsT=src[:, h, :], rhs=vaug[:, c, 0:vw_cols],
                        start=st, stop=last, skip_group_check=True), reads=[src, vaug], writes=[pb])
        t.s1, t.s2, t.s3 = s1, s2, s3
        return t

    def marker(fn):
        t = Task()
        t.s1 = lambda: None
        t.s2 = lambda: None
        t.s3 = fn
        return t

    tasks = []
    for m_i in range(NQB):
        qbm = 2 * m_i + 1
        QT = QTb[m_i % 2]
        gl = glb[m_i % 2]
        glv = gl[:, :].rearrange("p (h t) -> p h t", t=3)
        ya = yacc[m_i % 2]
        selTb = selT[m_i % 2]

        def load(m_i=m_i, QT=QT, gl=gl):
            P.dma("pool", QT[:, :], d["QT"][:, m_i, :], dst=QT)
            P.dma("sp", gl[:, :], d["gl"][:, m_i, :], dst=gl)
            P.op("act", lambda e: e.activation(out=gl[:, :], in_=gl[:, :], func=AF.Sigmoid), reads=[gl], writes=[gl])
        ld = marker(lambda: None)
        ld.s1 = load
        tasks.append(ld)
        nvalid = min(8 * qbm + 7, ncmp)
        nchunks = (nvalid + 127) // 128
        for c in range(nchunks):
            Tmin = 128 * (qbm - 1) - 2048 * c - 31
            kind, arg = ("none", None) if Tmin >= 16 * 127 else ("cmp", (m_i, c))
            tasks.append(chunk_task(KcT, c, QT, kind, arg, Vc, W, [(pOc[0], [0, 1]), (pOc[1], [2, 3])], c == 0, c == nchunks - 1))

        def after_cmp(m_i=m_i, glv=glv, gl=gl, ya=ya, selTb=selTb):
            r_, w_ = rd[0], wgt[0]
            for g2 in range(2):
                ov_ = pOc[g2][:, 0:2 * W].rearrange("p (h w) -> p h w", h=2)
                P.op("dve", lambda e, ov_=ov_, g2=g2: e.tensor_scalar(out=r_[:, 2 * g2:2 * g2 + 2], in0=ov_[:, :, 64],
                                                                      scalar1=1e-30, scalar2=None, op0=ALU.max),
                     reads=[pOc[g2]], writes=[r_])
            P.op("dve", lambda e: e.reciprocal(out=r_[:, :], in_=r_[:, :]), reads=[r_], writes=[r_])
            for h in range(4):
                pb = pOc[h // 2]
                base = (h % 2) * W
                if h == 0:
                    P.op("dve", lambda e, pb=pb, base=base, h=h: e.tensor_scalar(
                        out=imp[:, :], in0=pb[:, base + 65:base + W], scalar1=r_[:, h:h + 1], scalar2=None, op0=ALU.mult),
                        reads=[pb, r_], writes=[imp])
                else:
                    P.op("dve", lambda e, pb=pb, base=base, h=h: e.scalar_tensor_tensor(
                        out=imp[:, :], in0=pb[:, base + 65:base + W], scalar=r_[:, h:h + 1], in1=imp[:, :],
                        op0=ALU.mult, op1=ALU.add), reads=[pb, r_, imp], writes=[imp])
            x0 = nsel - 4 * m_i
            P.op("dve", lambda e: e.tensor_tensor(out=sc[:, :], in0=imp[:, :], in1=Amul[:, x0:x0 + nsel], op=ALU.mult),
                 reads=[imp, Amul], writes=[sc])
            P.op("dve", lambda e: e.tensor_tensor(out=sc[:, :], in0=sc[:, :], in1=Aadd[:, x0:x0 + nsel], op=ALU.add),
                 reads=[sc, Aadd], writes=[sc])
            P.op("dve", lambda e: e.memset(sc[:, 0:1], BIGF), writes=[sc])
            P.op("dve", lambda e: e.max(out=m8a[:, :], in_=sc[:, :]), reads=[sc], writes=[m8a])
            P.op("dve", lambda e: e.match_replace(out=sc2[:, :], in_to_replace=m8a[:, :], in_values=sc[:, :],
                                                  imm_value=-3.0 * BIGF), reads=[sc, m8a], writes=[sc2])
            P.op("dve", lambda e: e.max(out=m8b[:, :], in_=sc2[:, :]), reads=[sc2], writes=[m8b])
            P.op("dve", lambda e: e.tensor_scalar(out=self_f[:, :], in0=sc[:, :], scalar1=m8b[:, 7:8], scalar2=None,
                                                  op0=ALU.is_ge), reads=[sc, m8b], writes=[self_f])
            P.op("pe", lambda e: e.transpose(out=pM[0:nsel, 128:256], in_=self_f[:, :], identity=identf[:, :]),
                 reads=[self_f, identf], writes=[pM])
            P.op("act", lambda e: e.activation(out=selTb[:, :], in_=pM[0:nsel, 128:256], func=AF.Copy), reads=[pM], writes=[selTb])
            P.op("dve", lambda e: e.tensor_tensor(out=w_[:, :], in0=r_[:, :], in1=glv[:, :, 0], op=ALU.mult),
                 reads=[r_, gl], writes=[w_])
            for h in range(4):
                pb = pOc[h // 2]
                base = (h % 2) * W
                P.op("dve", lambda e, pb=pb, base=base, h=h: e.tensor_scalar(
                    out=ya[:, h, :], in0=pb[:, base:base + 64], scalar1=w_[:, h:h + 1], scalar2=None, op0=ALU.mult),
                    reads=[pb, w_], writes=[ya])
        tasks.append(marker(after_cmp))
        wl = [c for c in range(2 * m_i - 4, 2 * m_i + 2) if c >= 0]
        for c in wl:
            r = c - (2 * m_i - 4)
            tasks.append(chunk_task(kwT, c, QT, "sb", (WMk, WMk[:, r, :]), vw, 65, [(pOw, [0, 1, 2, 3])], c == wl[0], c == wl[-1]))
        for c in range(0, 2 * m_i + 2):
            tail = (6 + c - 2 * m_i) if c >= 2 * m_i else None
            tasks.append(chunk_task(ksT, c, QT, "slc", tail, vs, 65, [(pOs, [0, 1, 2, 3])], c == 0, c == 2 * m_i + 1, selTb=selTb))

        def combine(m_i=m_i, glv=glv, gl=gl, ya=ya):
            for bi, pb in ((2, pOw), (1, pOs)):
                r_, w_ = rd[bi], wgt[bi]
                pv = pb[:, 0:260].rearrange("p (h w) -> p h w", h=4)
                P.op("dve", lambda e, pv=pv, r_=r_: e.tensor_scalar(out=r_[:, :], in0=pv[:, :, 64], scalar1=1e-30, scalar2=None,
                                                                    op0=ALU.max), reads=[pb], writes=[r_])
                P.op("dve", lambda e, r_=r_: e.reciprocal(out=r_[:, :], in_=r_[:, :]), reads=[r_], writes=[r_])
                P.op("dve", lambda e, r_=r_, w_=w_, bi=bi: e.tensor_tensor(out=w_[:, :], in0=r_[:, :], in1=glv[:, :, bi], op=ALU.mult),
                     reads=[r_, gl], writes=[w_])
                for h in range(4):
                    P.op("dve", lambda e, pb=pb, h=h, w_=w_: e.scalar_tensor_tensor(
                        out=ya[:, h, :], in0=pb[:, h * 65:h * 65 + 64], scalar=w_[:, h:h + 1], in1=ya[:, h, :],
                        op0=ALU.mult, op1=ALU.add), reads=[pb, w_, ya], writes=[ya])
            P.dma("sp", d["y"][m_i], ya[:, :, :].rearrange("p h w -> p (h w)"), src=ya)
        tasks.append(marker(combine))

    DEPTH = 3
    n = len(tasks)
    for i in range(min(DEPTH, n)):
        tasks[i].s1()
    for i in range(n):
        tasks[i].s2()
        if i + DEPTH < n:
            tasks[i + DEPTH].s1()
        tasks[i].s3()


NSA_IN = lambda S, NQB: [
    ("QT", [64, NQB, 512]), ("gl", [128, NQB, 12]), ("ksT", [64, S]), ("kwT", [64, S]), ("vs", [128, S // 128, 65]), ("vw", [128, S // 128, 65]),
    ("kc2T", [128, S]), ("vc2T", [128, S]), ("pe2", [2, 128, 16]), ("w1", [2, 128, 16, 256]), ("w2", [2, 128, 2, 64]),
    ("D16", [128, 128]), ("Mdiag", [128, 128]), ("Mlow", [128, 128]), ("E", [S // 64, S // 128, 128]),
    ("ov", [128, ((S // 16 - 1 + 127) // 128), S // 64]), ("Amul", [128, 2 * (S // 64)]), ("Aadd", [128, 2 * (S // 64)]),
    ("identb", [128, 128]), ("WM", [128, 8, 128]), ("Tt", [128, NQB, ((S // 16 - 1 + 127) // 128)])]


def build_test_nsa(S, NQB):
    P = Prog()
    d = {}
    for nm, shp in NSA_IN(S, NQB):
        d[nm] = P.dram(nm, shp, F32, "ExternalInput")
    d["y"] = P.dram("y", [NQB, 128, 256], F32, "ExternalOutput")
    nsa_phase(P, S, NQB, d)
    print("streams", P.stats(), "sems", P.n_sems)
    return P.finish()


import math
import numpy as np

NT = 2048
NTILE = NT // 128

A_OPS = ([("copy", 1)] * 4 + [("prod", 2)] * 4 + [("silu", 1)] * 4 + [("copy", 1)] * 12 +
         [("rope", 2)] * 4 + [("rope", 2)] * 3 + [("copy", 1)] * 3 + [("gate", 1)])
A_NOUT = len(A_OPS)
A_NIN = sum(n for _, n in A_OPS)
A_COLS = (A_NIN - 1) * 128 + 24
A_ROWS = (A_NOUT - 1) * 128 + 24


def a_col_perm():
    BR = 512
    o = {}
    names = ["ab", "ac", "ax", "hq", "hf", "hi", "hg", "nq", "nkc", "nvc", "nks", "nvs", "nkw", "nvw", "ng", "mg"]
    sizes = [512] * 3 + [512] * 4 + [512] + [128] * 6 + [24, 3072]
    off = 0
    for n, s in zip(names, sizes):
        o[n] = np.arange(off, off + s)
        off += s

    def swap64(ix):
        ix = ix.reshape(-1, 2, 32)
        return ix[:, ::-1, :].reshape(-1)
    cols = []
    for c in range(4):
        cols.append(o["ab"][c * 128:(c + 1) * 128])
    for c in range(4):
        cols.append(o["ac"][c * 128:(c + 1) * 128])
        cols.append(o["ax"][c * 128:(c + 1) * 128])
    for n in ("hq", "hf", "hi", "hg"):
        for c in range(4):
            cols.append(o[n][c * 128:(c + 1) * 128])
    for c in range(4):
        ix = o["nq"][c * 128:(c + 1) * 128]
        cols.append(ix)
        cols.append(swap64(ix))
    for n in ("nkc", "nks", "nkw"):
        cols.append(o[n])
        cols.append(swap64(o[n]))
    for n in ("nvc", "nvs", "nvw"):
        cols.append(o[n])
    cols.append(o["ng"])
    return np.concatenate(cols), o["mg"]


def rope_tables(P, pos_d, inv_d, sgn_d, n):
    posi = P.sb([128, n], I32, "posi")
    P.dma("sp", posi[:, :], pos_d[:, :], dst=posi)
    inv = P.sb([128, 1], F32, "inv")
    sgn = P.sb([128, 1], F32, "sgn")
    P.dma("sp", inv[:, :], inv_d[:, :], dst=inv)
    P.dma("sp", sgn[:, :], sgn_d[:, :], dst=sgn)
    ang = P.sb([128, n], F32, "ang")
    P.op("dve", lambda e: e.tensor_copy(out=ang[:, :], in_=posi[:, :]), reads=[posi], writes=[ang])
    P.op("dve", lambda e: e.tensor_scalar(out=ang[:, :], in0=ang[:, :], scalar1=inv[:, :], scalar2=None, op0=ALU.mult),
         reads=[ang, inv], writes=[ang])
    qf = P.sb([128, n], F32, "qf")
    P.op("dve", lambda e: e.tensor_scalar(out=qf[:, :], in0=ang[:, :], scalar1=1.0 / (2 * math.pi), scalar2=None,
                                          op0=ALU.mult), reads=[ang], writes=[qf])
    qi = P.sb([128, n], I32, "qi")
    P.op("dve", lambda e: e.tensor_copy(out=qi[:, :], in_=qf[:, :]), reads=[qf], writes=[qi])
    P.op("dve", lambda e: e.tensor_copy(out=qf[:, :], in_=qi[:, :]), reads=[qi], writes=[qf])
    C1, C2 = 6.28125, 2 * math.pi - 6.28125
    C2a = float(np.float32(C2))
    C3 = C2 - C2a
    for Cc in (C1, C2a, C3):
        P.op("dve", lambda e, Cc=Cc: e.scalar_tensor_tensor(out=ang[:, :], in0=qf[:, :], scalar=-Cc, in1=ang[:, :],
                                                            op0=ALU.mult, op1=ALU.add), reads=[qf, ang], writes=[ang])
    m = qf
    P.op("dve", lambda e: e.tensor_scalar(out=m[:, :], in0=ang[:, :], scalar1=math.pi, scalar2=-2 * math.pi,
                                          op0=ALU.is_gt, op1=ALU.mult), reads=[ang], writes=[m])
    P.op("dve", lambda e: e.tensor_tensor(out=ang[:, :], in0=ang[:, :], in1=m[:, :], op=ALU.add), reads=[ang, m], writes=[ang])
    P.op("dve", lambda e: e.tensor_scalar(out=m[:, :], in0=ang[:, :], scalar1=-math.pi, scalar2=2 * math.pi,
                                          op0=ALU.is_lt, op1=ALU.mult), reads=[ang], writes=[m])
    P.op("dve", lambda e: e.tensor_tensor(out=ang[:, :], in0=ang[:, :], in1=m[:, :], op=ALU.add), reads=[ang, m], writes=[ang])
    P.op("dve", lambda e: e.tensor_scalar(out=ang[:, :], in0=ang[:, :], scalar1=3.14159, scalar2=-3.14159,
                                          op0=ALU.min, op1=ALU.max), reads=[ang], writes=[ang])
    SINS = P.sb([128, n], F32, "SINS")
    COS = P.sb([128, n], F32, "COS")
    P.op("act", lambda e: e.activation(out=SINS[:, :], in_=ang[:, :], func=AF.Sin), reads=[ang], writes=[SINS])
    P.op("dve", lambda e: e.tensor_scalar(out=SINS[:, :], in0=SINS[:, :], scalar1=sgn[:, :], scalar2=None, op0=ALU.mult),
         reads=[SINS, sgn], writes=[SINS])
    P.op("dve", lambda e: e.tensor_scalar(out=m[:, :], in0=ang[:, :], scalar1=-1.0, scalar2=None, op0=ALU.mult),
         reads=[ang], writes=[m])
    P.op("dve", lambda e: e.tensor_tensor(out=ang[:, :], in0=ang[:, :], in1=m[:, :], op=ALU.max), reads=[ang, m], writes=[ang])
    P.op("dve", lambda e: e.tensor_scalar(out=ang[:, :], in0=ang[:, :], scalar1=-1.0, scalar2=math.pi / 2, op0=ALU.mult,
                                          op1=ALU.add), reads=[ang], writes=[ang])
    P.op("act", lambda e: e.activation(out=COS[:, :], in_=ang[:, :], func=AF.Sin), reads=[ang], writes=[COS])
    return COS, SINS


def make_h2T(P, C, x_src_d, g2, h2T, tp, ntile):
    xt = [P.sb([128, D], F32, "x1t") for _ in range(2)]
    hb = [P.sb([128, D], BF16, "h2") for _ in range(2)]
    for i in range(ntile):
        xb = xt[i % 2]
        P.dma("sp", xb[:, :], x_src_d[i * 128:(i + 1) * 128, :], dst=xb)
        rstd = rms_stats(P, C, xb, xb[:, :], False)
        h = hb[i % 2]
        P.op("dve", lambda e, h=h, xb=xb, rstd=rstd: e.scalar_tensor_tensor(
            out=h[:, :], in0=xb[:, :], scalar=rstd[:, :], in1=g2[:, :], op0=ALU.mult, op1=ALU.mult),
            reads=[xb, rstd, g2], writes=[h])
        for k in range(8):
            P.op("pe", lambda e, h=h, k=k: e.transpose(out=tp[:, k * 128:(k + 1) * 128],
                                                      in_=h[:, k * 128:(k + 1) * 128], identity=C.ident[:, :]),
                 reads=[h, C.ident], writes=[tp])
        P.op("act", lambda e, i=i: e.activation(out=h2T[:, :, i * 128:(i + 1) * 128],
                                                in_=tp[:, :].rearrange("p (k n) -> p k n", k=8), func=AF.Copy),
             reads=[tp], writes=[h2T])


def build_A():
    P = Prog()
    x_d = P.dram("x", [NT, D], F32, "ExternalInput")
    gains_d = P.dram("gains", [3, 128, D], F32, "ExternalInput")
    wg_d = P.dram("wg", [D, DFF], F32, "ExternalInput")
    wu_d = P.dram("wu", [D, DFF], F32, "ExternalInput")
    wd_d = P.dram("wd", [DFF, D], F32, "ExternalInput")
    wa_d = P.dram("wa", [D, A_COLS], F32, "ExternalInput")
    ident_d = P.dram("ident", [128, 128], F32, "ExternalInput")
    pos_d = P.dram("pos", [128, NT], I32, "ExternalInput")
    inv_d = P.dram("inv", [128, 1], F32, "ExternalInput")
    sgn_d = P.dram("sgn", [128, 1], F32, "ExternalInput")
    x1_d = P.dram("x1", [NT, D], F32, "ExternalOutput")
    fm_d = P.dram("fm", [A_ROWS, NT], F32, "ExternalOutput")
    C = build_consts(P, ident_d)
    g = []
    for i in range(3):
        b = P.sb([128, D], F32, "gain")
        P.dma("sp", b[:, :], gains_d[i], dst=b)
        g.append(b)
    with P.scope():
        wg = load_w_cast(P, wg_d, D, DFF, 512, "wg")
        wu = load_w_cast(P, wu_d, D, DFF, 512, "wu")
        wd = load_w_cast(P, wd_d, DFF, D, 512, "wd")
        xpool = [P.sb([128, D], F32, "x") for _ in range(4)]

        def get_x(i):
            b = xpool[i % 4]
            P.dma("sp", b[:, :], x_d[i * 128:(i + 1) * 128, :], dst=b)
            return b

        def put_x(i, b):
            P.dma("sp", x1_d[i * 128:(i + 1) * 128, :], b[:, :], src=b)

        ffn_phase(P, C, NTILE, get_x, put_x, g[0], g[1], wg, wu, wd)
    with P.scope():
        COS, SINS = rope_tables(P, pos_d, inv_d, sgn_d, NT)
        h2T = P.sb([128, 8, NT], BF16, "h2T")
        tp = P.ps([128, 8 * 128], BF16, "tp")
        make_h2T(P, C, x1_d, g[2], h2T, tp, NTILE)
        wsrc = wa_d.rearrange("(c p) n -> p c n", p=128)
        wbufs = [P.sb([128, 8, 512], BF16, "wa") for _ in range(3)]
        pp = [P.ps([128, 512], F32, "pp") for _ in range(4)]
        stg = [P.sb([128, 512], F32, "stg") for _ in range(4)]
        tmp = [P.sb([128, 512], F32, "tmp") for _ in range(2)]
        nblk = (A_NIN + 3) // 4
        ops = []
        ic = 0
        for oc, (kind, nin) in enumerate(A_OPS):
            ops.append((oc, kind, list(range(ic, ic + nin))))
            ic += nin
        loaded = {}
        state = {"pp": 0, "stg": 0, "tmp": 0}

        def get_w(chunk):
            blk = chunk // 4
            _load(blk)
            if blk + 1 < nblk:
                _load(blk + 1)
            wb = loaded[blk]
            off = (chunk % 4) * 128
            return wb, off

        def _load(blk):
            if blk not in loaded:
                wb = wbufs[blk % 3]
                c0 = blk * 512
                c1 = min(A_COLS, c0 + 512)
                P.dma("pool", wb[:, :, 0:c1 - c0], wsrc[:, :, c0:c1], dst=wb)
                loaded[blk] = wb

        for oc, kind, chunks in ops:
            rows = 24 if kind == "gate" else 128
            for tg in range(NT // 512):
                tsl = slice(tg * 512, (tg + 1) * 512)
                pts = []
                for ch in chunks:
                    wb, off = get_w(ch)
                    pt = pp[state["pp"] % 4]
                    state["pp"] += 1
                    for k in range(8):
                        P.op("pe", lambda e, pt=pt, wb=wb, off=off, k=k, rows=rows, tsl=tsl: e.matmul(
                            pt[0:rows, :], lhsT=wb[:, k, off:off + rows], rhs=h2T[:, k, tsl], start=(k == 0), stop=(k == 7)),
                            reads=[wb, h2T], writes=[pt])
                    pts.append(pt)
                sb = stg[state["stg"] % 4]
                state["stg"] += 1
                if kind in ("copy", "gate"):
                    if oc % 2 == 0:
                        P.op("act", lambda e, sb=sb, pt=pts[0], rows=rows: e.activation(out=sb[0:rows, :], in_=pt[0:rows, :],
                                                                                        func=AF.Copy), reads=[pts[0]], writes=[sb])
                    else:
                        P.op("dve", lambda e, sb=sb, pt=pts[0], rows=rows: e.tensor_copy(out=sb[0:rows, :], in_=pt[0:rows, :]),
                             reads=[pts[0]], writes=[sb])
                elif kind == "silu":
                    P.op("act", lambda e, sb=sb, pt=pts[0]: e.activation(out=sb[:, :], in_=pt[:, :], func=AF.Silu),
                         reads=[pts[0]], writes=[sb])
                elif kind == "prod":
                    t = tmp[state["tmp"] % 2]
                    state["tmp"] += 1
                    P.op("act", lambda e, t=t, pt=pts[0]: e.activation(out=t[:, :], in_=pt[:, :], func=AF.Copy),
                         reads=[pts[0]], writes=[t])
                    P.op("dve", lambda e, sb=sb, t=t, pt=pts[1]: e.tensor_tensor(out=sb[:, :], in0=t[:, :], in1=pt[:, :],
                                                                                 op=ALU.mult), reads=[t, pts[1]], writes=[sb])
                elif kind == "rope":
                    t = tmp[state["tmp"] % 2]
                    state["tmp"] += 1
                    P.op("dve", lambda e, t=t, pt=pts[0], tsl=tsl: e.tensor_tensor(out=t[:, :], in0=pt[:, :], in1=COS[:, tsl],
                                                                                   op=ALU.mult), reads=[pts[0], COS], writes=[t])
                    P.op("dve", lambda e, sb=sb, pt=pts[1], tsl=tsl: e.tensor_tensor(out=sb[:, :], in0=pt[:, :], in1=SINS[:, tsl],
                                                                                     op=ALU.mult), reads=[pts[1], SINS], writes=[sb])
                    P.op("pool", lambda e, sb=sb, t=t: e.tensor_tensor(out=sb[:, :], in0=sb[:, :], in1=t[:, :], op=ALU.add),
                         reads=[sb, t], writes=[sb])
                P.dma("sp", fm_d[oc * 128:oc * 128 + rows, tsl], sb[0:rows, :], src=sb)
    print("A streams", P.stats(), "sems", P.n_sems)
    return P.finish()


def build_C():
    P = Prog()
    x1_d = P.dram("x1", [NT, D], F32, "ExternalInput")
    gains_d = P.dram("gains", [4, 128, D], F32, "ExternalInput")
    vT_d = P.dram("vT", [512, NT + 2], F32, "ExternalInput")
    bT_d = P.dram("bT", [512, NT], F32, "ExternalInput")
    ybT_d = P.dram("ybT", [512, NT], F32, "ExternalInput")
    ycT_d = P.dram("ycT", [512, NT], F32, "ExternalInput")
    cw_d = P.dram("cw", [128, 4, 3], F32, "ExternalInput")
    wmg_d = P.dram("wmg", [D, 3 * D], F32, "ExternalInput")
    wbr_d = P.dram("wbr", [3 * 512, D], F32, "ExternalInput")
    wout_d = P.dram("wout", [D, D], F32, "ExternalInput")
    wg_d = P.dram("wg", [D, DFF], F32, "ExternalInput")
    wu_d = P.dram("wu", [D, DFF], F32, "ExternalInput")
    wd_d = P.dram("wd", [DFF, D], F32, "ExternalInput")
    ident_d = P.dram("ident", [128, 128], F32, "ExternalInput")
    x2_d = P.dram("x2s", [NT, D], F32, "Internal")
    out_d = P.dram("xo", [NT, D], F32, "ExternalOutput")
    C = build_consts(P, ident_d)
    g = []
    for i in range(4):
        b = P.sb([128, D], F32, "gain")
        P.dma("sp", b[:, :], gains_d[i], dst=b)
        g.append(b)
    with P.scope():
        cw = P.sb([128, 4, 3], F32, "cw")
        P.dma("sp", cw[:, :, :], cw_d[:, :, :], dst=cw)
        wmg = load_w_cast(P, wmg_d, D, 3 * D, 512, "wmg")
        wbr = load_w_cast(P, wbr_d, 3 * 512, D, 512, "wbr")
        wout = load_w_cast(P, wout_d, D, D, 512, "wout")
        tp = P.ps([128, 8 * 128], BF16, "tp")
        h2T = P.sb([128, 8, 512], BF16, "h2T")
        xt = [P.sb([128, D], F32, "x1t") for _ in range(4)]
        hb = [P.sb([128, D], BF16, "h2") for _ in range(2)]
        vin = [P.sb([128, 514], F32, "vin") for _ in range(2)]
        bin_ = [P.sb([128, 512], F32, "bin") for _ in range(2)]
        acc = [P.sb([128, 512], F32, "acc") for _ in range(2)]
        ybrs = [P.sb([128, 4, 512], BF16, "ybr%d" % i) for i in range(3)]
        mT = P.sb([128, 8, 512], BF16, "mT")
        macc = [P.sb([128, 512], F32, "macc") for _ in range(2)]
        gsb = [P.sb([128, 512], F32, "gsb") for _ in range(2)]
        tt = [P.sb([128, 512], F32, "tt") for _ in range(2)]
        pg = [P.ps([128, 512], F32, "pg") for _ in range(2)]
        ppj = [P.ps([128, 512], F32, "ppj") for _ in range(2)]
        py = P.ps([128, D], F32, "py")
        t1 = P.sb([128, D], F32, "t1")
        for tg in range(NT // 512):
            tsl = slice(tg * 512, (tg + 1) * 512)
            for t in range(4):
                i = tg * 4 + t
                xb = xt[t]
                P.dma("sp", xb[:, :], x1_d[i * 128:(i + 1) * 128, :], dst=xb)
                rstd = rms_stats(P, C, xb, xb[:, :], False)
                h = hb[t % 2]
                P.op("dve", lambda e, h=h, xb=xb, rstd=rstd: e.scalar_tensor_tensor(
                    out=h[:, :], in0=xb[:, :], scalar=rstd[:, :], in1=g[0][:, :], op0=ALU.mult, op1=ALU.mult),
                    reads=[xb, rstd, g[0]], writes=[h])
                for k in range(8):
                    P.op("pe", lambda e, h=h, k=k: e.transpose(out=tp[:, k * 128:(k + 1) * 128],
                                                              in_=h[:, k * 128:(k + 1) * 128], identity=C.ident[:, :]),
                         reads=[h, C.ident], writes=[tp])
                P.op("act", lambda e, t=t: e.activation(out=h2T[:, :, t * 128:(t + 1) * 128],
                                                        in_=tp[:, :].rearrange("p (k n) -> p k n", k=8), func=AF.Copy),
                     reads=[tp], writes=[h2T])
            for c in range(4):
                v = vin[c % 2]
                bb = bin_[c % 2]
                a = acc[c % 2]
                P.dma("sp", v[:, :], vT_d[c * 128:(c + 1) * 128, tg * 512:tg * 512 + 514], dst=v)
                P.dma("sp", bb[:, :], bT_d[c * 128:(c + 1) * 128, tsl], dst=bb)
                P.op("dve", lambda e, a=a, v=v, c=c: e.tensor_scalar(out=a[:, :], in0=v[:, 2:514], scalar1=cw[:, c, 2:3],
                                                                     scalar2=None, op0=ALU.mult), reads=[v, cw], writes=[a])
                P.op("dve", lambda e, a=a, v=v, c=c: e.scalar_tensor_tensor(out=a[:, :], in0=v[:, 1:513], scalar=cw[:, c, 1:2],
                                                                            in1=a[:, :], op0=ALU.mult, op1=ALU.add),
                     reads=[v, cw, a], writes=[a])
                P.op("dve", lambda e, a=a, v=v, c=c: e.scalar_tensor_tensor(out=a[:, :], in0=v[:, 0:512], scalar=cw[:, c, 0:1],
                                                                            in1=a[:, :], op0=ALU.mult, op1=ALU.add),
                     reads=[v, cw, a], writes=[a])
                P.op("pool", lambda e, a=a, bb=bb, c=c: e.tensor_tensor(out=ybrs[0][:, c, :], in0=a[:, :], in1=bb[:, :], op=ALU.mult),
                     reads=[a, bb], writes=[ybrs[0]])
            P.dma("pool", ybrs[1][:, :, :], ybT_d[:, tsl].rearrange("(c p) n -> p c n", p=128), dst=ybrs[1])
            P.dma("pool", ybrs[2][:, :, :], ycT_d[:, tsl].rearrange("(c p) n -> p c n", p=128), dst=ybrs[2])
            for mc in range(8):
                ma = macc[mc % 2]
                for br in range(3):
                    pj = ppj[br % 2]
                    for k in range(4):
                        wb, wap = wslice(wbr, br * 4 + k, mc * 128, (mc + 1) * 128)
                        P.op("pe", lambda e, pj=pj, wap=wap, br=br, k=k: e.matmul(
                            pj[:, :], lhsT=wap, rhs=ybrs[br][:, k, :], start=(k == 0), stop=(k == 3)),
                            reads=[wb, ybrs[br]], writes=[pj])
                    pgt = pg[br % 2]
                    for k in range(8):
                        wb, wap = wslice(wmg, k, br * D + mc * 128, br * D + (mc + 1) * 128)
                        P.op("pe", lambda e, pgt=pgt, wap=wap, k=k: e.matmul(
                            pgt[:, :], lhsT=wap, rhs=h2T[:, k, :], start=(k == 0), stop=(k == 7)),
                            reads=[wb, h2T], writes=[pgt])
                    gs = gsb[br % 2]
                    P.op("act", lambda e, gs=gs, pgt=pgt: e.activation(out=gs[:, :], in_=pgt[:, :], func=AF.Sigmoid),
                         reads=[pgt], writes=[gs])
                    if br == 0:
                        P.op("dve", lambda e, ma=ma, gs=gs, pj=pj: e.tensor_tensor(out=ma[:, :], in0=gs[:, :], in1=pj[:, :],
                                                                                   op=ALU.mult), reads=[gs, pj], writes=[ma])
                    else:
                        t_ = tt[br % 2]
                        P.op("dve", lambda e, t_=t_, gs=gs, pj=pj: e.tensor_tensor(out=t_[:, :], in0=gs[:, :], in1=pj[:, :],
                                                                                   op=ALU.mult), reads=[gs, pj], writes=[t_])
                        if br == 1:
                            P.op("pool", lambda e, ma=ma, t_=t_: e.tensor_tensor(out=ma[:, :], in0=ma[:, :], in1=t_[:, :],
                                                                                 op=ALU.add), reads=[ma, t_], writes=[ma])
                        else:
                            P.op("pool", lambda e, ma=ma, t_=t_, mc=mc: e.tensor_tensor(out=mT[:, mc, :], in0=ma[:, :], in1=t_[:, :],
                                                                                        op=ALU.add), reads=[ma, t_], writes=[mT])
            for t in range(4):
                i = tg * 4 + t
                for half in range(2):
                    for k in range(8):
                        wb, wap = wslice(wout, k, half * 512, (half + 1) * 512)
                        P.op("pe", lambda e, wap=wap, k=k, half=half, t=t: e.matmul(
                            py[:, half * 512:(half + 1) * 512], lhsT=mT[:, k, t * 128:(t + 1) * 128], rhs=wap,
                            start=(k == 0), stop=(k == 7)), reads=[wb, mT], writes=[py])
                rstd = rms_stats(P, C, py, py[:, :], True)
                P.op("dve", lambda e, rstd=rstd: e.scalar_tensor_tensor(
                    out=t1[:, :], in0=py[:, :], scalar=rstd[:, :], in1=g[1][:, :], op0=ALU.mult, op1=ALU.mult),
                    reads=[py, rstd, g[1]], writes=[t1])
                xb = xt[t]
                P.op("pool", lambda e, xb=xb: e.tensor_tensor(out=xb[:, :], in0=xb[:, :], in1=t1[:, :], op=ALU.add),
                     reads=[xb, t1], writes=[xb])
                P.dma("sp", x2_d[i * 128:(i + 1) * 128, :], xb[:, :], src=xb)
    with P.scope():
        wg = load_w_cast(P, wg_d, D, DFF, 512, "wg")
        wu = load_w_cast(P, wu_d, D, DFF, 512, "wu")
        wd = load_w_cast(P, wd_d, DFF, D, 512, "wd")
        xpool = [P.sb([128, D], F32, "x") for _ in range(4)]

        def get_x(i):
            b = xpool[i % 4]
            P.dma("sp", b[:, :], x2_d[i * 128:(i + 1) * 128, :], dst=b)
            return b

        def put_x(i, b):
            P.dma("sp", out_d[i * 128:(i + 1) * 128, :], b[:, :], src=b)

        ffn_phase(P, C, NTILE, get_x, put_x, g[2], g[3], wg, wu, wd)
    print("C streams", P.stats(), "sems", P.n_sems)
    return P.finish()


S = 8192
NQB = 32


def build_B():
    P = Prog()
    dh = {}
    for nm, shp in (("qT", [128, S]), ("zfT", [128, S]), ("zf", [S, 128]), ("v", [S, 128]), ("g", [S, 128]),
                    ("lbT", [128, 4]), ("lmask", [128, 4]), ("lbrow", [128, 4, 128]), ("lmaskrow", [128, 4, 128]),
                    ("gnorm", [128, 128]), ("M1", [128, 128]), ("M2", [128, 128])):
        dh[nm] = P.dram("h_" + nm, shp, F32, "ExternalInput")
    dh["y"] = P.dram("yh", [S, 128], F32, "ExternalOutput")
    dn = {}
    for nm, shp in NSA_IN(S, NQB):
        dn[nm] = P.dram("n_" + nm, shp, F32, "ExternalInput")
    dn["y"] = P.dram("yn", [NQB, 128, 256], F32, "ExternalOutput")
    with P.scope():
        hgrn_phase(P, S, dh)
    with P.scope():
        nsa_phase(P, S, NQB, dn)
    print("B streams", P.stats(), "sems", P.n_sems)
    return P.finish()


import numpy as np

S = 8192
NC = 8

def rep(a, n=128):
    return np.ascontiguousarray(np.broadcast_to(a[None], (n,) + a.shape))

IDENT = np.eye(128, dtype=np.float32)
_p = np.arange(128)
INV = (10000.0 ** (-(_p % 32).astype(np.float32) * 2.0 / 64)).astype(np.float32).reshape(128, 1)
SGN = np.where((_p % 64) < 32, -1.0, 1.0).astype(np.float32).reshape(128, 1)
PERM, MG = a_col_perm()


def prep_A(inp, l, x):
    wa = np.ascontiguousarray(inp["w_in"][l][:, PERM])
    gains = np.stack([rep(inp["norm_gains"][l, i]) for i in (0, 1, 2)])
    maps = []
    pos = inp["positions"].reshape(-1)
    for c in range(NC):
        sl = slice(c * NT, (c + 1) * NT)
        maps.append(dict(x=np.ascontiguousarray(x[sl]), gains=gains, wg=inp["w_ffn_gate"][l, 0], wu=inp["w_ffn_up"][l, 0],
                         wd=inp["w_ffn_down"][l, 0], wa=wa, ident=IDENT, pos=rep(pos[sl].astype(np.int32)), inv=INV, sgn=SGN))
    return maps

HG_M1, HG_M2 = hgrn_consts_np()
NSA_K = nsa_consts_np(S)
ONES128 = np.ones((128, 128), np.float32)
ZEROS128 = np.zeros((128, 128), np.float32)
NQB = 32
NCC = 4


def _shift(A, par):
    o = np.empty_like(A)
    if par:
        o[:, 2 * par:] = A[:, :A.shape[1] - 2 * par]
        o[:, :2 * par] = A[:, :1]
    else:
        o[:] = A
    return o


def nsa_core_consts(par):
    K = NSA_K
    if par == 0:
        wm = [K["Mlow"], ONES128, ONES128, ONES128, K["Mdiag"], ZEROS128, K["Mdiag"], ZEROS128]
    else:
        wm = [ZEROS128, K["Mlow"], ONES128, ONES128, ONES128, K["Mdiag"], ONES128, K["Mdiag"]]
    Tt = np.zeros((128, NQB, NCC), np.float32)
    for m in range(NQB):
        for c in range(NCC):
            Tt[:, m, c] = 128 * (2 * m + par) - 2048 * c - 31
    return dict(WM=np.ascontiguousarray(np.stack(wm, 1)), Tt=Tt, Amul=_shift(K["Amul"], par), Aadd=_shift(K["Aadd"], par))


NSA_CC = [nsa_core_consts(0), nsa_core_consts(1)]
OV_PM = np.ascontiguousarray(NSA_K["ov"].reshape(-1, 128, NSA_K["ov"].shape[-1]).transpose(1, 0, 2))


def prep_B(inp, l, FM):
    maps = []
    logits = inp["hgrn_lb_logits"]
    lmask = np.zeros((4,), np.float32)
    lmask[1:l + 1] = 1
    gn = rep(inp["hgrn_gnorm"][l])
    pe = inp["cmp_pe"][l]
    pe2 = np.ascontiguousarray(pe.reshape(2, 16, 2, 64).transpose(0, 2, 3, 1).reshape(2, 128, 16))
    for c in range(NC):
        b, hd = c // 4, c % 4
        kvh, par = (c // 2) % 2, c % 2
        tk = slice(b * S, (b + 1) * S)
        d = {}
        r = lambda base, n: FM[base:base + n, tk]
        d["h_qT"] = np.ascontiguousarray(r(1024 + hd * 128, 128))
        zfT = r(1536 + hd * 128, 128)
        d["h_zfT"] = np.ascontiguousarray(zfT)
        d["h_zf"] = np.ascontiguousarray(zfT.T)
        d["h_v"] = np.ascontiguousarray(r(2048 + hd * 128, 128).T)
        d["h_g"] = np.ascontiguousarray(r(2560 + hd * 128, 128).T)
        lg = logits[:, hd * 128:(hd + 1) * 128]
        d["h_lbT"] = np.ascontiguousarray(lg.T)
        d["h_lmask"] = rep(lmask)
        d["h_lbrow"] = rep(lg)
        d["h_lmaskrow"] = np.ascontiguousarray(np.broadcast_to(lmask[None, :, None], (128, 4, 128)))
        d["h_gnorm"] = gn
        d["h_M1"] = HG_M1
        d["h_M2"] = HG_M2
        q = r(3072 + kvh * 256, 256).reshape(4, 64, 64, 128)[:, :, par::2]
        d["n_QT"] = np.ascontiguousarray(q.transpose(1, 2, 0, 3).reshape(64, NQB, 512))
        gl = r(4352 + kvh * 12, 12).reshape(12, 64, 128)[:, par::2]
        d["n_gl"] = np.ascontiguousarray(gl.transpose(2, 1, 0))
        d["n_ksT"] = np.ascontiguousarray(r(3712 + kvh * 64, 64))
        d["n_kwT"] = np.ascontiguousarray(r(3840 + kvh * 64, 64))

        def stack2(xT):
            o = np.zeros((128, S), np.float32)
            o[:64] = xT
            o[64:, :-1] = xT[:, 1:]
            return o
        d["n_kc2T"] = stack2(r(3584 + kvh * 64, 64))
        d["n_vc2T"] = stack2(r(3968 + kvh * 64, 64))

        def aug(xT):
            o = np.ones((S, 65), np.float32)
            o[:, :64] = xT.T
            return np.ascontiguousarray(o.reshape(S // 128, 128, 65).transpose(1, 0, 2))
        d["n_vs"] = aug(r(4096 + kvh * 64, 64))
        d["n_vw"] = aug(r(4224 + kvh * 64, 64))
        d["n_pe2"] = pe2
        d["n_w1"] = np.ascontiguousarray(inp["cmp_w1"][l].reshape(2, 16, 128, 256).transpose(0, 2, 1, 3))
        d["n_w2"] = np.ascontiguousarray(inp["cmp_w2"][l].reshape(2, 2, 128, 64).transpose(0, 2, 1, 3))
        for k in ("D16", "Mdiag", "Mlow", "E", "identb"):
            d["n_" + k] = NSA_K[k]
        d["n_ov"] = OV_PM
        for k, v in NSA_CC[par].items():
            d["n_" + k] = v
        maps.append(d)
    return maps


def gather_B(results):
    ybT = np.zeros((2, 512, S), np.float32)
    ycT = np.zeros((2, 512, S), np.float32)
    for c in range(NC):
        b, hd = c // 4, c % 4
        kvh, par = (c // 2) % 2, c % 2
        ybT[b, hd * 128:(hd + 1) * 128, :] = results[c]["yh"].T
        y = results[c]["yn"]
        yT = y.transpose(2, 0, 1)
        ycT[b, kvh * 256:(kvh + 1) * 256].reshape(256, 64, 128)[:, par::2, :] = yT
    return ybT, ycT


def prep_C(inp, l, x1, FM, ybT, ycT):
    gains = np.stack([rep(inp["norm_gains"][l, i]) for i in (2, 3, 4, 5)])
    cw = np.ascontiguousarray(inp["conv_w"][l].reshape(3, 4, 128).transpose(2, 1, 0))
    wmg = np.ascontiguousarray(inp["w_in"][l][:, MG])
    wbr = np.ascontiguousarray(inp["w_branch"][l].reshape(1536, 1024))
    maps = []
    for c in range(NC):
        b = c // 4
        t0 = c * NT
        sl = slice(t0, t0 + NT)
        ls = slice((c % 4) * NT, (c % 4 + 1) * NT)
        vT = np.zeros((512, NT + 2), np.float32)
        vT[:, 2:] = FM[512:1024, sl]
        if c % 4 != 0:
            vT[:, :2] = FM[512:1024, t0 - 2:t0]
        maps.append(dict(x1=np.ascontiguousarray(x1[sl]), gains=gains, vT=vT, bT=np.ascontiguousarray(FM[0:512, sl]),
                         ybT=np.ascontiguousarray(ybT[b][:, ls]), ycT=np.ascontiguousarray(ycT[b][:, ls]), cw=cw, wmg=wmg, wbr=wbr,
                         wout=inp["w_out"][l], wg=inp["w_ffn_gate"][l, 1], wu=inp["w_ffn_up"][l, 1], wd=inp["w_ffn_down"][l, 1],
                         ident=IDENT))
    return maps


from concourse.bass_utils import run_bass_kernel_spmd

_PROGS = {}


def _prog(name):
    if name not in _PROGS:
        _PROGS[name] = {"A": build_A, "B": build_B, "C": build_C}[name]()
    return _PROGS[name]


def kernel(**inputs):
    inp = {k: np.asarray(v) for k, v in inputs.items()}
    cores = list(range(NC))
    x = np.ascontiguousarray(inp["x"].reshape(-1, D).astype(np.float32, copy=False))
    for l in range(4):
        resA = run_bass_kernel_spmd(_prog("A"), prep_A(inp, l, x), core_ids=cores)
        x1 = np.concatenate([r["x1"] for r in resA.results])
        FM = np.concatenate([r["fm"] for r in resA.results], axis=1)
        del resA
        resB = run_bass_kernel_spmd(_prog("B"), prep_B(inp, l, FM), core_ids=cores)
        ybT, ycT = gather_B(resB.results)
        del resB
        resC = run_bass_kernel_spmd(_prog("C"), prep_C(inp, l, x1, FM, ybT, ycT), core_ids=cores)
        x = np.concatenate([r["xo"] for r in resC.results])
        del resC
    return np.ascontiguousarray(x.reshape(2, S, D).astype(np.float32))
```

```python
import contextlib
import numpy as np
import concourse.bass as bass
import concourse.mybir as mybir

F32 = mybir.dt.float32
BF16 = mybir.dt.bfloat16
I32 = mybir.dt.int32
ALU = mybir.AluOpType
AF = mybir.ActivationFunctionType
AX = mybir.AxisListType

ENGS = ("pe", "act", "dve", "pool", "sp")


class Buf:
    def __init__(self, prog, t, name, tracked=True):
        self.prog = prog
        self.t = t
        self.name = name
        self.tracked = tracked
        self.last_w = None
        self.readers = {}
        self.wsem = None
        self.wcnt = 0
        self.rsem = None
        self.rcnt = 0

    def __getitem__(self, idx):
        return self.t[idx]

    @property
    def ap(self):
        return self.t


class Prog:
    def __init__(self, same_engine_sync=None):
        import os
        if same_engine_sync is None:
            same_engine_sync = os.environ.get('SES', '1') == '1'
        self.nc = bass.Bass("TRN2", target_bir_lowering=False)
        self.es = contextlib.ExitStack()
        self.sem_es = contextlib.ExitStack()
        self.streams = {e: [] for e in ENGS}
        self.cnt = {e: 0 for e in ENGS}
        self.esem = {}
        for e in ENGS:
            self.esem[e] = self.sem_es.enter_context(self.nc.semaphore("s_" + e))
        self.seen = {e: {} for e in ENGS}
        self.same_engine_sync = same_engine_sync
        self.dma_sems = []
        self.nbuf = 0
        self.n_sems = 5
        self.all_bufs = []
        self.free_sems = []

    def dram(self, name, shape, dtype, kind):
        t = self.nc.dram_tensor(name, list(shape), dtype, kind=kind)
        return t.ap()

    def sb(self, shape, dtype, name=None):
        self.nbuf += 1
        name = (name or "b") + "_%d" % self.nbuf
        t = self.es.enter_context(self.nc.sbuf_tensor(name, list(shape), dtype))
        b = Buf(self, t, name)
        self.all_bufs.append(b)
        return b

    def ps(self, shape, dtype=F32, name=None):
        self.nbuf += 1
        name = (name or "p") + "_%d" % self.nbuf
        t = self.es.enter_context(self.nc.psum_tensor(name, list(shape), dtype))
        b = Buf(self, t, name)
        self.all_bufs.append(b)
        return b

    def _sem(self, name):
        if self.free_sems:
            return self.free_sems.pop()
        self.n_sems += 1
        return (self.sem_es.enter_context(self.nc.semaphore(name)), 0)

    def _collect(self, eng, reads, writes, no_waw=False):
        need = {}

        def add(tok):
            if tok is None:
                return
            key, val, e = tok
            if e == eng and (eng == "pe" or not self.same_engine_sync):
                return
            if need.get(key, (0,))[0] < val:
                need[key] = (val, e)

        for b in reads:
            if b is None or not b.tracked:
                continue
            add(b.last_w)
        for b in writes:
            if b is None or not b.tracked:
                continue
            if not no_waw:
                add(b.last_w)
            for key, (val, e) in b.readers.items():
                add((key, val, e))
        out = []
        for key, (val, e) in need.items():
            if self.seen[eng].get(key, 0) >= val:
                continue
            self.seen[eng][key] = val
            out.append((key, val))
        return out

    def _record(self, tok, reads, writes):
        key, val, e = tok
        for b in writes:
            if b is None or not b.tracked:
                continue
            b.last_w = tok
            b.readers = {}
        for b in reads:
            if b is None or not b.tracked:
                continue
            if b in writes:
                continue
            b.readers[key] = (val, e)

    def op(self, eng, fn, reads=(), writes=(), no_waw=False):
        waits = self._collect(eng, reads, writes, no_waw)
        st = self.streams[eng]
        for key, val in waits:
            st.append(("w", key, val))
        self.cnt[eng] += 1
        st.append(("o", fn, self.esem[eng], 1))
        tok = (self.esem[eng], self.cnt[eng], eng)
        self._record(tok, reads, writes)
        return tok

    def dma(self, queue, out_ap, in_ap, dst=None, src=None, no_waw=False, **kw):
        reads = [src] if src is not None else []
        writes = [dst] if dst is not None else []
        waits = self._collect(queue, reads, writes, no_waw)
        st = self.streams[queue]
        for key, val in waits:
            st.append(("w", key, val))
        if dst is not None:
            if dst.wsem is None:
                dst.wsem, dst.wcnt = self._sem("w_" + dst.name)
                self.dma_sems.append(dst)
            dst.wcnt += 16
            sem, val = dst.wsem, dst.wcnt
        else:
            if src.rsem is None:
                src.rsem, src.rcnt = self._sem("r_" + src.name)
                self.dma_sems.append(src)
            src.rcnt += 16
            sem, val = src.rsem, src.rcnt

        def fn(e, out_ap=out_ap, in_ap=in_ap, kw=kw):
            return e.dma_start(out=out_ap, in_=in_ap, **kw)

        st.append(("o", fn, sem, 16))
        tok = (sem, val, "dma")
        self._record(tok, reads, writes)
        return tok

    @contextlib.contextmanager
    def scope(self):
        outer = self.es
        self.es = contextlib.ExitStack()
        n0 = len(self.all_bufs)
        try:
            yield
        finally:
            self.barrier()
            self.flush()
            for b in self.all_bufs[n0:]:
                if b.wsem is not None:
                    self.free_sems.append((b.wsem, b.wcnt))
                if b.rsem is not None:
                    self.free_sems.append((b.rsem, b.rcnt))
                if b in self.dma_sems:
                    self.dma_sems.remove(b)
                b.dead = True
            del self.all_bufs[n0:]
            self.es.close()
            self.es = outer

    def barrier(self):
        targets = [(self.esem[e], self.cnt[e]) for e in ENGS if self.cnt[e] > 0]
        for b in self.dma_sems:
            if b.wsem is not None and b.wcnt:
                targets.append((b.wsem, b.wcnt))
            if b.rsem is not None and b.rcnt:
                targets.append((b.rsem, b.rcnt))
        for e in ENGS:
            for key, val in targets:
                if key is self.esem[e]:
                    continue
                if self.seen[e].get(key, 0) >= val:
                    continue
                self.seen[e][key] = val
                self.streams[e].append(("w", key, val))

    def flush(self):
        nc = self.nc
        streams = self.streams
        self.streams = {e: [] for e in ENGS}

        def run(stream, e):
            for it in stream:
                if it[0] == "w":
                    e.wait_ge(it[1], it[2])
                else:
                    ins = it[1](e)
                    ins.then_inc(it[2], it[3])

        with nc.Block() as block:
            @block.tensor
            def _(e):
                run(streams["pe"], e)

            @block.scalar
            def _(e):
                run(streams["act"], e)

            @block.vector
            def _(e):
                run(streams["dve"], e)

            @block.gpsimd
            def _(e):
                run(streams["pool"], e)

            @block.sync
            def _(e):
                run(streams["sp"], e)

    def finish(self):
        targets = [(self.esem[e], self.cnt[e]) for e in ENGS if self.cnt[e] > 0 and e != "sp"]
        for b in self.dma_sems:
            if b.wsem is not None and b.wcnt:
                targets.append((b.wsem, b.wcnt))
            if b.rsem is not None and b.rcnt:
                targets.append((b.rsem, b.rcnt))
        for key, val in targets:
            self.streams["sp"].append(("w", key, val))
        self.flush()
        self.es.close()
        self.sem_es.close()
        return self.nc

    def stats(self):
        return {e: self.cnt[e] for e in ENGS}


import numpy as np

D = 1024
DFF = 2816
NJ = 22
TG = 256
EPS = 1e-6


def load_w_cast(P, dram_ap, rows, cols, colblk, name):
    kc = rows // 128
    src = dram_ap.rearrange("(c p) n -> p c n", p=128)
    blocks = []
    c0 = 0
    while c0 < cols:
        c1 = min(cols, c0 + colblk)
        b = P.sb([128, kc, c1 - c0], BF16, name)
        P.dma("pool", b[:, :, :], src[:, :, c0:c1], dst=b)
        blocks.append((b, c0, c1))
        c0 = c1
    return blocks


def wslice(blocks, k, c0, c1):
    for b, b0, b1 in blocks:
        if b0 <= c0 and c1 <= b1:
            return b, b[:, k, c0 - b0:c1 - b0]
    raise ValueError((c0, c1))


class Consts:
    pass


def rms_stats(P, C, src_buf, src_ap, from_psum):
    ss = P.sb([128, 1], F32, "ss")
    if from_psum:
        P.op("act", lambda e: e.activation(out=C.junk[:, :], in_=src_ap, func=AF.Square, accum_out=ss[:, :]),
             reads=[src_buf], writes=[C.junk, ss])
    else:
        P.op("dve", lambda e: e.scalar_tensor_tensor(out=C.junk[:, :], in0=src_ap, scalar=1.0, in1=src_ap,
                                                      op0=ALU.mult, op1=ALU.mult, accum_out=ss[:, :]),
             reads=[src_buf], writes=[C.junk, ss])
    ms = P.sb([128, 1], F32, "ms")
    P.op("dve", lambda e: e.tensor_scalar(out=ms[:, :], in0=ss[:, :], scalar1=1.0 / D, scalar2=EPS,
                                          op0=ALU.mult, op1=ALU.add), reads=[ss], writes=[ms])
    rstd = P.sb([128, 1], F32, "rstd")
    P.op("pool", lambda e: e.tensor_tensor(out=rstd[:, :], in0=ms[:, :], in1=C.mhalf[:, :], op=ALU.pow),
         reads=[ms, C.mhalf], writes=[rstd])
    return rstd


def ffn_phase(P, C, ntiles, get_x, put_x, g_pre, g_post, wg, wu, wd):
    ngroups = ntiles // 2
    hT = [P.sb([128, 8, TG], BF16, "hT") for _ in range(2)]
    aT = P.sb([128, NJ, TG], BF16, "aT")
    tp = P.ps([128, 8 * 128], BF16, "tp")
    gu = [P.ps([128, 2, TG], F32, "gu") for _ in range(2)]
    ys = [P.ps([128, D], F32, "y") for _ in range(2)]
    hb = [P.sb([128, D], BF16, "h") for _ in range(2)]
    sg = [P.sb([128, TG], BF16, "sg") for _ in range(2)]
    xs = {}
    t1 = P.sb([128, D], F32, "t1")

    def prep(g):
        for t in range(2):
            i = 2 * g + t
            xb = get_x(i)
            xs[i] = xb
            rstd = rms_stats(P, C, xb, xb[:, :], False)
            h = hb[t]
            P.op("dve", lambda e, h=h, xb=xb, rstd=rstd: e.scalar_tensor_tensor(
                out=h[:, :], in0=xb[:, :], scalar=rstd[:, :], in1=g_pre[:, :], op0=ALU.mult, op1=ALU.mult),
                reads=[xb, rstd, g_pre], writes=[h])

    def transposes(g):
        for t in range(2):
            h = hb[t]
            for k in range(8):
                P.op("pe", lambda e, h=h, k=k: e.transpose(out=tp[:, k * 128:(k + 1) * 128],
                                                          in_=h[:, k * 128:(k + 1) * 128], identity=C.ident[:, :]),
                     reads=[h, C.ident], writes=[tp])
            dst = hT[g % 2]
            P.op("act", lambda e, dst=dst, t=t: e.activation(
                out=dst[:, :, t * 128:(t + 1) * 128], in_=tp[:, :].rearrange("p (k n) -> p k n", k=8), func=AF.Copy),
                reads=[tp], writes=[dst])

    def phaseA(g):
        h_t = hT[g % 2]
        for j in range(NJ):
            pg = gu[j % 2]
            for which, W in ((0, wg), (1, wu)):
                for k in range(8):
                    wb, wap = wslice(W, k, j * 128, (j + 1) * 128)
                    P.op("pe", lambda e, pg=pg, which=which, wap=wap, k=k: e.matmul(
                        pg[:, which, :], lhsT=wap, rhs=h_t[:, k, :], start=(k == 0), stop=(k == 7)),
                        reads=[wb, h_t], writes=[pg])
            s = sg[j % 2]
            P.op("act", lambda e, s=s, pg=pg: e.activation(out=s[:, :], in_=pg[:, 0, :], func=AF.Silu),
                 reads=[pg], writes=[s])
            P.op("dve", lambda e, s=s, pg=pg, j=j: e.tensor_tensor(out=aT[:, j, :], in0=s[:, :], in1=pg[:, 1, :],
                                                                   op=ALU.mult),
                 reads=[s, pg], writes=[aT])

    def phaseB(g):
        for t in range(2):
            y = ys[t]
            for half in range(2):
                for j in range(NJ):
                    wb, wap = wslice(wd, j, half * 512, (half + 1) * 512)
                    P.op("pe", lambda e, y=y, half=half, wap=wap, j=j, t=t: e.matmul(
                        y[:, half * 512:(half + 1) * 512], lhsT=aT[:, j, t * 128:(t + 1) * 128], rhs=wap,
                        start=(j == 0), stop=(j == NJ - 1)),
                        reads=[wb, aT], writes=[y])

    def post(g):
        for t in range(2):
            i = 2 * g + t
            y = ys[t]
            rstd = rms_stats(P, C, y, y[:, :], True)
            P.op("dve", lambda e, y=y, rstd=rstd: e.scalar_tensor_tensor(
                out=t1[:, :], in0=y[:, :], scalar=rstd[:, :], in1=g_post[:, :], op0=ALU.mult, op1=ALU.mult),
                reads=[y, rstd, g_post], writes=[t1])
            xb = xs.pop(i)
            P.op("dve", lambda e, xb=xb: e.scalar_tensor_tensor(
                out=xb[:, :], in0=t1[:, :], scalar=0.5, in1=xb[:, :], op0=ALU.mult, op1=ALU.add),
                reads=[t1, xb], writes=[xb])
            put_x(i, xb)

    prep(0)
    transposes(0)
    for g in range(ngroups):
        phaseA(g)
        if g + 1 < ngroups:
            prep(g + 1)
            transposes(g + 1)
        phaseB(g)
        post(g)


def build_consts(P, ident_d):
    C = Consts()
    C.ident = P.sb([128, 128], BF16, "ident")
    P.dma("pool", C.ident[:, :], ident_d[:, :], dst=C.ident)
    C.junk = P.sb([128, D], BF16, "junk")
    C.junk.tracked = False
    C.mhalf = P.sb([128, 1], F32, "mhalf")
    P.op("pool", lambda e: e.memset(C.mhalf[:, :], -0.5), writes=[C.mhalf])
    return C


def build_test_ffn(NT):
    P = Prog()
    x_d = P.dram("x", [NT, D], F32, "ExternalInput")
    gains_d = P.dram("gains", [2, 128, D], F32, "ExternalInput")
    wg_d = P.dram("wg", [D, DFF], F32, "ExternalInput")
    wu_d = P.dram("wu", [D, DFF], F32, "ExternalInput")
    wd_d = P.dram("wd", [DFF, D], F32, "ExternalInput")
    ident_d = P.dram("ident", [128, 128], F32, "ExternalInput")
    out_d = P.dram("out", [NT, D], F32, "ExternalOutput")
    C = build_consts(P, ident_d)
    g_pre = P.sb([128, D], F32, "gpre")
    g_post = P.sb([128, D], F32, "gpost")
    P.dma("sp", g_pre[:, :], gains_d[0], dst=g_pre)
    P.dma("sp", g_post[:, :], gains_d[1], dst=g_post)
    wg = load_w_cast(P, wg_d, D, DFF, 512, "wg")
    wu = load_w_cast(P, wu_d, D, DFF, 512, "wu")
    wd = load_w_cast(P, wd_d, DFF, D, 512, "wd")
    xpool = [P.sb([128, D], F32, "x") for _ in range(4)]

    def get_x(i):
        b = xpool[i % 4]
        P.dma("sp", b[:, :], x_d[i * 128:(i + 1) * 128, :], dst=b)
        return b

    def put_x(i, b):
        P.dma("sp", out_d[i * 128:(i + 1) * 128, :], b[:, :], src=b)

    ffn_phase(P, C, NT // 128, get_x, put_x, g_pre, g_post, wg, wu, wd)
    print("streams", P.stats(), "sems", P.n_sems)
    return P.finish()


import numpy as np

EPS = 1e-6


def hgrn_consts_np():
    s = np.arange(128)
    same = (s[:, None] // 64) == (s[None, :] // 64)
    M1 = (same & (s[:, None] <= s[None, :])).astype(np.float32)
    M2 = (same & (s[:, None] > s[None, :])).astype(np.float32)
    return M1, M2


def hgrn_phase(P, S, d):
    nt = S // 128
    M1 = P.sb([128, 128], F32, "M1")
    M2 = P.sb([128, 128], F32, "M2")
    P.dma("sp", M1[:, :], d["M1"][:, :], dst=M1)
    P.dma("sp", M2[:, :], d["M2"][:, :], dst=M2)
    gn = P.sb([128, 128], F32, "gn")
    P.dma("sp", gn[:, :], d["gnorm"][:, :], dst=gn)
    mhalf = P.sb([128, 1], F32, "mhalf")
    P.op("pool", lambda e: e.memset(mhalf[:, :], -0.5), writes=[mhalf])
    lbl = P.sb([128, 4], F32, "lbl")
    lmk = P.sb([128, 4], F32, "lmk")
    P.dma("sp", lbl[:, :], d["lbT"][:, :], dst=lbl)
    P.dma("sp", lmk[:, :], d["lmask"][:, :], dst=lmk)
    ex = P.sb([128, 4], F32, "ex")
    P.op("act", lambda e: e.activation(out=ex[:, :], in_=lbl[:, :], func=AF.Exp), reads=[lbl], writes=[ex])
    den = P.sb([128, 1], F32, "den")
    P.op("dve", lambda e: e.reduce_sum(out=den[:, :], in_=ex[:, :], axis=AX.X), reads=[ex], writes=[den])
    rden = P.sb([128, 1], F32, "rden")
    P.op("dve", lambda e: e.reciprocal(out=rden[:, :], in_=den[:, :]), reads=[den], writes=[rden])
    exm = P.sb([128, 4], F32, "exm")
    P.op("dve", lambda e: e.tensor_tensor(out=exm[:, :], in0=ex[:, :], in1=lmk[:, :], op=ALU.mult),
         reads=[ex, lmk], writes=[exm])
    num = P.sb([128, 1], F32, "num")
    P.op("dve", lambda e: e.reduce_sum(out=num[:, :], in_=exm[:, :], axis=AX.X), reads=[exm], writes=[num])
    lbc = P.sb([128, 1], F32, "lbc")
    P.op("dve", lambda e: e.tensor_tensor(out=lbc[:, :], in0=num[:, :], in1=rden[:, :], op=ALU.mult),
         reads=[num, rden], writes=[lbc])
    omlc = P.sb([128, 1], F32, "omlc")
    P.op("dve", lambda e: e.tensor_scalar(out=omlc[:, :], in0=lbc[:, :], scalar1=-1.0, scalar2=1.0,
                                          op0=ALU.mult, op1=ALU.add), reads=[lbc], writes=[omlc])
    nomlc = P.sb([128, 1], F32, "nomlc")
    P.op("dve", lambda e: e.tensor_scalar(out=nomlc[:, :], in0=omlc[:, :], scalar1=-1.0, scalar2=None,
                                          op0=ALU.mult), reads=[omlc], writes=[nomlc])
    lbr = P.sb([128, 4, 128], F32, "lbr")
    lmr = P.sb([128, 4, 128], F32, "lmr")
    P.dma("sp", lbr[:, :, :], d["lbrow"][:, :, :], dst=lbr)
    P.dma("sp", lmr[:, :, :], d["lmaskrow"][:, :, :], dst=lmr)
    exr = P.sb([128, 4, 128], F32, "exr")
    P.op("act", lambda e: e.activation(out=exr[:, :, :], in_=lbr[:, :, :], func=AF.Exp), reads=[lbr], writes=[exr])
    denr = P.sb([128, 128], F32, "denr")
    P.op("dve", lambda e: e.tensor_tensor(out=denr[:, :], in0=exr[:, 0, :], in1=exr[:, 1, :], op=ALU.add),
         reads=[exr], writes=[denr])
    P.op("dve", lambda e: e.tensor_tensor(out=denr[:, :], in0=denr[:, :], in1=exr[:, 2, :], op=ALU.add),
         reads=[exr, denr], writes=[denr])
    P.op("dve", lambda e: e.tensor_tensor(out=denr[:, :], in0=denr[:, :], in1=exr[:, 3, :], op=ALU.add),
         reads=[exr, denr], writes=[denr])
    P.op("dve", lambda e: e.reciprocal(out=denr[:, :], in_=denr[:, :]), reads=[denr], writes=[denr])
    P.op("dve", lambda e: e.tensor_tensor(out=exr[:, :, :], in0=exr[:, :, :], in1=lmr[:, :, :], op=ALU.mult),
         reads=[exr, lmr], writes=[exr])
    lbrow = P.sb([128, 128], F32, "lbrow")
    P.op("dve", lambda e: e.tensor_tensor(out=lbrow[:, :], in0=exr[:, 0, :], in1=exr[:, 1, :], op=ALU.add),
         reads=[exr], writes=[lbrow])
    P.op("dve", lambda e: e.tensor_tensor(out=lbrow[:, :], in0=lbrow[:, :], in1=exr[:, 2, :], op=ALU.add),
         reads=[exr, lbrow], writes=[lbrow])
    P.op("dve", lambda e: e.tensor_tensor(out=lbrow[:, :], in0=lbrow[:, :], in1=exr[:, 3, :], op=ALU.add),
         reads=[exr, lbrow], writes=[lbrow])
    P.op("dve", lambda e: e.tensor_tensor(out=lbrow[:, :], in0=lbrow[:, :], in1=denr[:, :], op=ALU.mult),
         reads=[lbrow, denr], writes=[lbrow])
    omlrow = P.sb([128, 128], F32, "omlrow")
    P.op("dve", lambda e: e.tensor_scalar(out=omlrow[:, :], in0=lbrow[:, :], scalar1=-1.0, scalar2=1.0,
                                          op0=ALU.mult, op1=ALU.add), reads=[lbrow], writes=[omlrow])

    G = 4
    GT = G * 128
    ng = nt // G
    NB = 2
    qT = [P.sb([128, GT], F32, "qT") for _ in range(NB)]
    zfT = [P.sb([128, GT], F32, "zfT") for _ in range(NB)]
    zft = [P.sb([128, G, 128], F32, "zft") for _ in range(NB)]
    vt = [P.sb([128, G, 128], F32, "vt") for _ in range(NB)]
    gt = [P.sb([128, G, 128], F32, "gt") for _ in range(NB)]
    vb = [P.sb([128, G, 128], BF16, "vb") for _ in range(NB)]
    sigt = [P.sb([128, G, 128], F32, "sigt") for _ in range(NB)]
    logf = [P.sb([128, G, 128], F32, "logf") for _ in range(NB)]
    kt = [P.sb([128, G, 128], F32, "kt") for _ in range(NB)]
    sigf = [P.sb([128, GT], F32, "sigf") for _ in range(NB)]
    kT = [P.sb([128, GT], F32, "kT") for _ in range(NB)]
    bsb = [P.sb([128, GT], F32, "bsb") for _ in range(NB)]
    dd = [P.sb([128, GT], F32, "dd") for _ in range(NB)]
    eq = [P.sb([128, GT], F32, "eq") for _ in range(NB)]
    ek = [P.sb([128, GT], F32, "ek") for _ in range(NB)]
    eb = [P.sb([128, GT], F32, "eb") for _ in range(NB)]
    er = [P.sb([128, G, 128], F32, "er") for _ in range(NB)]
    qtl = [P.sb([128, GT], BF16, "qtl") for _ in range(NB)]
    ktl = [P.sb([128, GT], BF16, "ktl") for _ in range(NB)]
    QP = [P.sb([128, G, 2, 128], BF16, "QP") for _ in range(NB)]
    KP = [P.sb([128, G, 2, 128], BF16, "KP") for _ in range(NB)]
    for i in range(NB):
        for b in (QP[i], KP[i]):
            P.op("pool", lambda e, b=b: e.memset(b[:, :, :, :], 0.0), writes=[b])
    dec = [P.sb([128, 2 * G], F32, "dec") for _ in range(NB)]
    ATm = [P.sb([128, G, 128], BF16, "ATm") for _ in range(NB)]
    Sf = [P.sb([128, 128], F32, "Sf") for _ in range(2)]
    Sb = [P.sb([128, 128], BF16, "Sb") for _ in range(4)]
    P.op("pool", lambda e: e.memset(Sf[0][:, :], 0.0), writes=[Sf[0]])
    P.op("pool", lambda e: e.memset(Sb[0][:, :], 0.0), writes=[Sb[0]])
    sq = [P.sb([128, G, 128], F32, "sq") for _ in range(NB)]
    sg = [P.sb([128, G, 128], F32, "sg") for _ in range(NB)]
    yo = [P.sb([128, G, 128], F32, "yo") for _ in range(NB)]
    ss4 = [P.sb([128, G], F32, "ss4") for _ in range(NB)]
    rs4 = [P.sb([128, G], F32, "rs4") for _ in range(NB)]
    mh4 = P.sb([128, G], F32, "mh4")
    P.op("pool", lambda e: e.memset(mh4[:, :], -0.5), writes=[mh4])
    pbT = P.ps([128, GT], F32, "pbT")
    pR = P.ps([128, GT], F32, "pR")
    pAT = P.ps([128, GT], F32, "pAT")
    p_o = [P.ps([128, GT], F32, "p_o") for _ in range(2)]
    p_S = [P.ps([128, 512], F32, "p_S") for _ in range(2)]
    st_ = {"sidx": 0}
    bc = lambda t: t[:, :].unsqueeze(1).to_broadcast([128, G, 128])

    def h1(gi):
        n = gi % NB
        gs = slice(gi * GT, (gi + 1) * GT)
        P.dma("sp", qT[n][:, :], d["qT"][:, gs], dst=qT[n])
        P.dma("sp", zfT[n][:, :], d["zfT"][:, gs], dst=zfT[n])
        P.dma("sp", zft[n][:, :, :], d["zf"][gs, :].rearrange("(t p) k -> p t k", p=128), dst=zft[n])
        P.dma("sp", vt[n][:, :, :], d["v"][gs, :].rearrange("(t p) k -> p t k", p=128), dst=vt[n])
        P.dma("sp", gt[n][:, :, :], d["g"][gs, :].rearrange("(t p) k -> p t k", p=128), dst=gt[n])
        s_, lf, k_ = sigt[n], logf[n], kt[n]
        P.op("act", lambda e: e.activation(out=s_[:, :, :], in_=zft[n][:, :, :], func=AF.Sigmoid), reads=[zft[n]], writes=[s_])
        P.op("dve", lambda e: e.tensor_tensor(out=lf[:, :, :], in0=s_[:, :, :], in1=bc(omlrow), op=ALU.mult),
             reads=[s_, omlrow], writes=[lf])
        P.op("dve", lambda e: e.tensor_tensor(out=k_[:, :, :], in0=bc(omlrow), in1=lf[:, :, :], op=ALU.subtract),
             reads=[lf, omlrow], writes=[k_])
        P.op("dve", lambda e: e.tensor_tensor(out=lf[:, :, :], in0=lf[:, :, :], in1=bc(lbrow), op=ALU.add),
             reads=[lf, lbrow], writes=[lf])
        P.op("dve", lambda e: e.tensor_scalar(out=lf[:, :, :], in0=lf[:, :, :], scalar1=1e-30, scalar2=None, op0=ALU.max),
             reads=[lf], writes=[lf])
        P.op("act", lambda e: e.activation(out=lf[:, :, :], in_=lf[:, :, :], func=AF.Ln), reads=[lf], writes=[lf])
        for t in range(G):
            P.op("pe", lambda e, t=t: e.matmul(pbT[:, t * 128:(t + 1) * 128], lhsT=lf[:, t, :], rhs=M1[:, :], start=True, stop=True),
                 reads=[lf, M1], writes=[pbT])
        for t in range(G):
            P.op("pe", lambda e, t=t: e.matmul(pR[:, t * 128:(t + 1) * 128], lhsT=M2[:, :], rhs=lf[:, t, :], start=True, stop=True),
                 reads=[lf, M2], writes=[pR])
        sf, kT_ = sigf[n], kT[n]
        P.op("act", lambda e: e.activation(out=sf[:, :], in_=zfT[n][:, :], func=AF.Sigmoid), reads=[zfT[n]], writes=[sf])
        P.op("dve", lambda e: e.tensor_scalar(out=kT_[:, :], in0=sf[:, :], scalar1=nomlc[:, :], scalar2=omlc[:, :],
                                              op0=ALU.mult, op1=ALU.add), reads=[sf, nomlc, omlc], writes=[kT_])
        b_, d_, eq_, ek_, eb_, er_ = bsb[n], dd[n], eq[n], ek[n], eb[n], er[n]
        P.op("act", lambda e: e.activation(out=b_[:, :], in_=pbT[:, :], func=AF.Copy), reads=[pbT], writes=[b_])
        b3 = b_[:, :].rearrange("p (c s) -> p c s", s=64)
        P.op("dve", lambda e: e.tensor_tensor(out=d_[:, :].rearrange("p (c s) -> p c s", s=64), in0=b3,
                                              in1=b3[:, :, 31:32].to_broadcast([128, 2 * G, 64]), op=ALU.subtract),
             reads=[b_], writes=[d_])
        P.op("dve", lambda e: e.tensor_scalar(out=eq_[:, :], in0=d_[:, :], scalar1=43.0, scalar2=None, op0=ALU.min),
             reads=[d_], writes=[eq_])
        P.op("dve", lambda e: e.tensor_scalar(out=ek_[:, :], in0=d_[:, :], scalar1=-1.0, scalar2=43.0, op0=ALU.mult, op1=ALU.min),
             reads=[d_], writes=[ek_])
        P.op("act", lambda e: e.activation(out=eq_[:, :], in_=eq_[:, :], func=AF.Exp), reads=[eq_], writes=[eq_])
        P.op("act", lambda e: e.activation(out=ek_[:, :], in_=ek_[:, :], func=AF.Exp), reads=[ek_], writes=[ek_])
        P.op("act", lambda e: e.activation(out=eb_[:, :], in_=b_[:, :], func=AF.Exp), reads=[b_], writes=[eb_])
        P.op("act", lambda e: e.activation(out=er_[:, :, :], in_=pR[:, :].rearrange("p (t k) -> p t k", k=128), func=AF.Exp),
             reads=[pR], writes=[er_])
        dc = dec[n]
        P.op("dve", lambda e: e.tensor_copy(out=dc[:, :], in_=eb_[:, :].rearrange("p (c s) -> p c s", s=64)[:, :, 63]),
             reads=[eb_], writes=[dc])
        P.op("dve", lambda e: e.tensor_tensor(out=qtl[n][:, :], in0=qT[n][:, :], in1=eq_[:, :], op=ALU.mult),
             reads=[qT[n], eq_], writes=[qtl[n]])
        P.op("dve", lambda e: e.tensor_tensor(out=ktl[n][:, :], in0=kT_[:, :], in1=ek_[:, :], op=ALU.mult),
             reads=[kT_, ek_], writes=[ktl[n]])
        q4 = qT[n][:, :].rearrange("p (t c s) -> p t c s", c=2, s=64)
        e4 = eb_[:, :].rearrange("p (t c s) -> p t c s", c=2, s=64)
        P.op("dve", lambda e: e.tensor_tensor(out=QP[n][:, :, 0, 0:64], in0=q4[:, :, 0, :], in1=e4[:, :, 0, :], op=ALU.mult),
             reads=[qT[n], eb_], writes=[QP[n]])
        P.op("dve", lambda e: e.tensor_tensor(out=QP[n][:, :, 1, 64:128], in0=q4[:, :, 1, :], in1=e4[:, :, 1, :], op=ALU.mult),
             reads=[qT[n], eb_], writes=[QP[n]])
        P.op("dve", lambda e: e.tensor_tensor(out=KP[n][0:64, :, 0, :], in0=k_[0:64, :, :], in1=er_[0:64, :, :], op=ALU.mult),
             reads=[k_, er_], writes=[KP[n]])
        P.op("dve", lambda e: e.tensor_tensor(out=KP[n][64:128, :, 1, :], in0=k_[64:128, :, :], in1=er_[64:128, :, :], op=ALU.mult),
             reads=[k_, er_], writes=[KP[n]])
        P.op("pool", lambda e: e.tensor_copy(out=vb[n][:, :, :], in_=vt[n][:, :, :]), reads=[vt[n]], writes=[vb[n]])
        for t in range(G):
            P.op("pe", lambda e, t=t: e.matmul(pAT[:, t * 128:(t + 1) * 128], lhsT=ktl[n][:, t * 128:(t + 1) * 128],
                                               rhs=qtl[n][:, t * 128:(t + 1) * 128], start=True, stop=True),
                 reads=[ktl[n], qtl[n]], writes=[pAT])
        P.op("dve", lambda e: e.tensor_tensor(out=ATm[n][:, :, :], in0=pAT[:, :].rearrange("p (t k) -> p t k", k=128),
                                              in1=bc(M1), op=ALU.mult), reads=[pAT, M1], writes=[ATm[n]])
        P.op("act", lambda e: e.activation(out=sg[n][:, :, :], in_=gt[n][:, :, :], func=AF.Silu), reads=[gt[n]], writes=[sg[n]])
        P.op("pool", lambda e: e.tensor_tensor(out=sg[n][:, :, :], in0=sg[n][:, :, :], in1=bc(gn), op=ALU.mult),
             reads=[sg[n], gn], writes=[sg[n]])

    def h2(gi):
        n = gi % NB
        gs = slice(gi * GT, (gi + 1) * GT)
        dc = dec[n]
        po = p_o[gi % 2]
        sidx = st_["sidx"]
        for t in range(G):
            osl = slice(t * 128, (t + 1) * 128)
            s0 = sidx
            P.op("pe", lambda e, t=t, osl=osl: e.matmul(po[:, osl], lhsT=ATm[n][:, t, :], rhs=vb[n][:, t, :], start=True, stop=False),
                 reads=[ATm[n], vb[n]], writes=[po])
            P.op("pe", lambda e, t=t, osl=osl, s0=s0: e.matmul(po[:, osl], lhsT=QP[n][:, t, 0, :], rhs=Sb[s0 % 4][:, :],
                                                               start=False, stop=False), reads=[QP[n], Sb[s0 % 4]], writes=[po])
            for c in range(2):
                pS = p_S[c]
                P.op("pe", lambda e, t=t, c=c, pS=pS: e.matmul(pS[:, 0:128], lhsT=KP[n][:, t, c, :], rhs=vb[n][:, t, :],
                                                              start=True, stop=True), reads=[KP[n], vb[n]], writes=[pS])
                so, sn = Sf[sidx % 2], Sf[(sidx + 1) % 2]
                P.op("dve", lambda e, so=so, sn=sn, pS=pS, t=t, c=c: e.scalar_tensor_tensor(
                    out=sn[:, :], in0=so[:, :], scalar=dc[:, 2 * t + c:2 * t + c + 1], in1=pS[:, 0:128], op0=ALU.mult, op1=ALU.add),
                    reads=[so, dc, pS], writes=[sn])
                sbn = Sb[(sidx + 1) % 4]
                P.op("act", lambda e, sbn=sbn, sn=sn: e.activation(out=sbn[:, :], in_=sn[:, :], func=AF.Copy),
                     reads=[sn], writes=[sbn])
                sidx += 1
            P.op("pe", lambda e, t=t, osl=osl, s0=s0: e.matmul(po[:, osl], lhsT=QP[n][:, t, 1, :], rhs=Sb[(s0 + 1) % 4][:, :],
                                                               start=False, stop=True), reads=[QP[n], Sb[(s0 + 1) % 4]], writes=[po])
        st_["sidx"] = sidx
        po3 = po[:, :].rearrange("p (t k) -> p t k", k=128)
        P.op("act", lambda e: e.activation(out=sq[n][:, :, :], in_=po3, func=AF.Square), reads=[po], writes=[sq[n]])
        P.op("dve", lambda e: e.reduce_sum(out=ss4[n][:, :], in_=sq[n][:, :, :], axis=AX.X), reads=[sq[n]], writes=[ss4[n]])
        P.op("dve", lambda e: e.tensor_scalar(out=ss4[n][:, :], in0=ss4[n][:, :], scalar1=1.0 / 128, scalar2=EPS,
                                              op0=ALU.mult, op1=ALU.add), reads=[ss4[n]], writes=[ss4[n]])
        P.op("pool", lambda e: e.tensor_tensor(out=rs4[n][:, :], in0=ss4[n][:, :], in1=mh4[:, :], op=ALU.pow),
             reads=[ss4[n], mh4], writes=[rs4[n]])
        y = yo[n]
        P.op("dve", lambda e: e.tensor_tensor(out=y[:, :, :], in0=po3, in1=rs4[n][:, :].unsqueeze(2).to_broadcast([128, G, 128]),
                                              op=ALU.mult), reads=[po, rs4[n]], writes=[y])
        P.op("dve", lambda e: e.tensor_tensor(out=y[:, :, :], in0=y[:, :, :], in1=sg[n][:, :, :], op=ALU.mult),
             reads=[y, sg[n]], writes=[y])
        P.dma("sp", d["y"][gs, :].rearrange("(t p) k -> p t k", p=128), y[:, :, :], src=y)

    h1(0)
    for gi in range(ng):
        if gi + 1 < ng:
            h1(gi + 1)
        h2(gi)


def build_test_hgrn(S):
    P = Prog()
    d = {}
    for nm, shp in (("qT", [128, S]), ("zfT", [128, S]), ("zf", [S, 128]), ("v", [S, 128]), ("g", [S, 128]),
                    ("lbT", [128, 4]), ("lmask", [128, 4]), ("lbrow", [128, 4, 128]), ("lmaskrow", [128, 4, 128]),
                    ("gnorm", [128, 128]), ("M1", [128, 128]), ("M2", [128, 128])):
        d[nm] = P.dram(nm, shp, F32, "ExternalInput")
    d["y"] = P.dram("y", [S, 128], F32, "ExternalOutput")
    hgrn_phase(P, S, d)
    print("streams", P.stats(), "sems", P.n_sems)
    return P.finish()


import numpy as np

BIGF = 1.0e6


def nsa_consts_np(S):
    nsel = S // 64
    ncmp = S // 16 - 1
    ncp = ((ncmp + 127) // 128) * 128
    nch = S // 128
    p = np.arange(128)
    D1 = (p[:, None] - p[None, :]).astype(np.float32)
    D16 = (16 * p[:, None] - p[None, :]).astype(np.float32)
    Mdiag = (D1 <= 0).astype(np.float32)
    Mlow = (D1 > 0).astype(np.float32)
    j = np.arange(nsel)
    E = np.zeros((nsel, nch, 128), np.float32)
    for c in range(nch):
        E[:, c, :] = (j[:, None] == (2 * c + p[None, :] // 64))
    n = np.arange(ncp)
    cs = n * 16
    ss = j * 64
    ov = ((cs[:, None] < ss[None, :] + 64) & (cs[:, None] + 32 > ss[None, :]) & (n[:, None] < ncmp)).astype(np.float32)
    x = np.arange(2 * nsel) - nsel
    cur_rel = (p >= 64).astype(np.int64)
    forced = (x[None, :] == cur_rel[:, None]) | (x[None, :] == cur_rel[:, None] - 1)
    invalid = x[None, :] > cur_rel[:, None]
    Amul = (~forced & ~invalid).astype(np.float32)
    Aadd = np.where(forced, BIGF, np.where(invalid, -BIGF, 0.0)).astype(np.float32)
    ident = np.eye(128, dtype=np.float32)
    return dict(D16=D16, Mdiag=Mdiag, Mlow=Mlow, E=E, ov=ov, Amul=Amul, Aadd=Aadd, identb=ident)


def nsa_phase(P, S, NQB, d):
    nsel = S // 64
    ncmp = S // 16 - 1
    ncp = ((ncmp + 127) // 128) * 128
    ncc = ncp // 128
    nch = S // 128
    W = 65 + nsel

    def cload(name, shape, dt, src, q="pool"):
        b = P.sb(shape, dt, name)
        P.dma(q, b[tuple(slice(None) for _ in shape)], src, dst=b)
        return b

    D16 = cload("D16", [128, 128], F32, d["D16"][:, :], "sp")
    Mdiag = cload("Mdiag", [128, 128], BF16, d["Mdiag"][:, :])
    Mlow = cload("Mlow", [128, 128], BF16, d["Mlow"][:, :])
    Ec = cload("E", [nsel, nch, 128], BF16, d["E"][:, :, :])
    Amul = cload("Amul", [128, 2 * nsel], F32, d["Amul"][:, :], "sp")
    Aadd = cload("Aadd", [128, 2 * nsel], F32, d["Aadd"][:, :], "sp")
    identb = cload("identb", [128, 128], BF16, d["identb"][:, :])
    ksT = cload("ksT", [64, S], BF16, d["ksT"][:, :])
    kwT = cload("kwT", [64, S], BF16, d["kwT"][:, :])
    vs = cload("vs", [128, nch, 65], BF16, d["vs"][:, :, :])
    vw = cload("vw", [128, nch, 65], BF16, d["vw"][:, :, :])
    KcT = P.sb([64, ncp], BF16, "KcT")
    Vc = P.sb([128, ncc, W], BF16, "Vc")
    P.op("pool", lambda e: e.memset(KcT[:, :], 0.0), writes=[KcT])
    P.op("pool", lambda e: e.memset(Vc[:, :, :], 0.0), writes=[Vc])
    P.op("pool", lambda e: e.memset(Vc[:, :, 64:65], 1.0), writes=[Vc])
    P.dma("pool", Vc[:, :, 65:W], d["ov"][:, :, :], dst=Vc)

    ps_h = [P.ps([128, 512], F32, "ps_h") for _ in range(2)]
    ps_b = P.ps([128, 512], F32, "ps_b")
    pOs = P.ps([128, 512], F32, "pOs")
    ps_o = pOs
    for X in range(2):
        x2T = cload("x2T", [128, S], BF16, d["kc2T" if X == 0 else "vc2T"][:, :])
        pe2 = cload("pe2", [128, 16], BF16, d["pe2"][X])
        w1 = cload("w1", [128, 16, 256], BF16, d["w1"][X])
        w2 = cload("w2", [128, 2, 64], BF16, d["w2"][X])
        GT = P.sb([128, 2, ncp], BF16, "GT")
        P.op("pool", lambda e, GT=GT: e.memset(GT[:, :, :], 0.0), writes=[GT])
        x2v = x2T[:, :].rearrange("p (n s) -> p n s", s=16)
        for hc in range(2):
            for j in range(16):
                P.op("pe", lambda e, j=j, hc=hc, w1=w1, pe2=pe2: e.matmul(
                    ps_b[:, 0:1], lhsT=w1[:, j, hc * 128:(hc + 1) * 128], rhs=pe2[:, j:j + 1],
                    start=(j == 0), stop=(j == 15)), reads=[w1, pe2], writes=[ps_b])
            c1 = P.sb([128, 1], F32, "c1")
            P.op("dve", lambda e, c1=c1: e.tensor_copy(out=c1[:, :], in_=ps_b[:, 0:1]), reads=[ps_b], writes=[c1])
            ph = ps_h[hc]
            for j in range(16):
                if j < 8:
                    rhs = x2v[:, 0:ncmp, 2 * j]
                else:
                    rhs = x2v[:, 1:ncmp + 1, 2 * j - 16]
                P.op("pe", lambda e, j=j, hc=hc, w1=w1, rhs=rhs, ph=ph: e.matmul(
                    ph[:, 0:ncmp], lhsT=w1[:, j, hc * 128:(hc + 1) * 128], rhs=rhs,
                    start=(j == 0), stop=(j == 15)), reads=[w1, x2T], writes=[ph])
            xs = P.sb([128, ncp], F32, "xs")
            P.op("act", lambda e, xs=xs, ph=ph, c1=c1: e.activation(out=xs[:, 0:ncmp], in_=ph[:, 0:ncmp],
                                                                     func=AF.Identity, bias=c1[:, :]),
                 reads=[ph, c1], writes=[xs])
            t2 = P.sb([128, ncp], F32, "t2")
            P.op("dve", lambda e, xs=xs, t2=t2: e.tensor_tensor(out=t2[:, 0:ncmp], in0=xs[:, 0:ncmp], in1=xs[:, 0:ncmp],
                                                                op=ALU.mult), reads=[xs], writes=[t2])
            P.op("dve", lambda e, t2=t2: e.tensor_scalar(out=t2[:, 0:ncmp], in0=t2[:, 0:ncmp], scalar1=0.044715,
                                                         scalar2=1.0, op0=ALU.mult, op1=ALU.add), reads=[t2], writes=[t2])
            P.op("dve", lambda e, xs=xs, t2=t2: e.tensor_tensor(out=t2[:, 0:ncmp], in0=t2[:, 0:ncmp], in1=xs[:, 0:ncmp],
                                                                op=ALU.mult), reads=[xs, t2], writes=[t2])
            P.op("act", lambda e, t2=t2: e.activation(out=t2[:, 0:ncmp], in_=t2[:, 0:ncmp], func=AF.Sigmoid,
                                                      scale=1.5957691216057308), reads=[t2], writes=[t2])
            P.op("dve", lambda e, xs=xs, t2=t2, GT=GT, hc=hc: e.tensor_tensor(
                out=GT[:, hc, 0:ncmp], in0=t2[:, 0:ncmp], in1=xs[:, 0:ncmp], op=ALU.mult), reads=[xs, t2], writes=[GT])
        if X == 0:
            for hc in range(2):
                P.op("pe", lambda e, hc=hc, w2=w2, GT=GT: e.matmul(ps_o[0:64, 0:ncp], lhsT=w2[:, hc, :], rhs=GT[:, hc, :],
                                                                    start=(hc == 0), stop=(hc == 1)),
                     reads=[w2, GT], writes=[ps_o])
            P.op("act", lambda e: e.activation(out=KcT[:, 0:ncmp], in_=ps_o[0:64, 0:ncmp], func=AF.Copy),
                 reads=[ps_o], writes=[KcT])
        else:
            for c in range(ncc):
                for hc in range(2):
                    P.op("pe", lambda e, hc=hc, c=c, w2=w2, GT=GT: e.matmul(
                        ps_o[:, 0:64], lhsT=GT[:, hc, c * 128:(c + 1) * 128], rhs=w2[:, hc, :],
                        start=(hc == 0), stop=(hc == 1)), reads=[w2, GT], writes=[ps_o])
                P.op("act", lambda e, c=c: e.activation(out=Vc[:, c, 0:64], in_=ps_o[:, 0:64], func=AF.Copy),
                     reads=[ps_o], writes=[Vc])

    pS = [ps_h[0], ps_h[1], P.ps([128, 512], F32, "pS2")]
    pM = ps_b
    pOc = [P.ps([128, 512], F32, "pOc") for _ in range(2)]
    pOw = P.ps([128, 512], F32, "pOw")
    WMk = cload("WM", [128, 8, 128], BF16, d["WM"][:, :, :])
    Tt = cload("Tt", [128, NQB, ncc], F32, d["Tt"][:, :, :], "sp")
    identf = cload("identf", [128, 128], F32, d["identb"][:, :], "sp")
    QTb = [P.sb([64, 512], BF16, "QT") for _ in range(2)]
    gl_all = P.sb([128, NQB, 12], F32, "gl_all")
    P.dma("sp", gl_all[:, :, :], d["gl"][:, :, :], dst=gl_all)
    P.op("act", lambda e: e.activation(out=gl_all[:, :, :], in_=gl_all[:, :, :], func=AF.Sigmoid), reads=[gl_all], writes=[gl_all])
    NPT = 6
    PT = [P.sb([128, 4, 128], BF16, "PT") for _ in range(NPT)]
    PTm = [P.sb([128, 4, 128], BF16, "PTm") for _ in range(NPT)]
    mk = [P.sb([128, 128], BF16, "mk") for _ in range(3)]
    rd = [P.sb([128, 4], F32, "rd") for _ in range(3)]
    wgt = [P.sb([128, 4], F32, "wgt") for _ in range(3)]
    imp = P.sb([128, nsel], F32, "imp")
    sc = P.sb([128, nsel], F32, "sc")
    sc2 = P.sb([128, nsel], F32, "sc2")
    m8a = P.sb([128, 8], F32, "m8a")
    m8b = P.sb([128, 8], F32, "m8b")
    self_f = P.sb([128, nsel], F32, "self")
    selT = [P.sb([nsel, 128], BF16, "selT") for _ in range(2)]
    yacc = [P.sb([128, 4, 64], F32, "yacc") for _ in range(2)]
    state = {"pt": 0, "ps": 0, "mk": 0, "pm": 0}

    class Task:
        pass

    def chunk_task(kT, c, QT, mask_kind, mask_arg, vaug, vw_cols, pouts, first, last, selTb=None):
        t = Task()
        ps = pS[state["ps"] % 3]
        state["ps"] += 1
        pt = PT[state["pt"] % NPT]
        ptm = PTm[state["pt"] % NPT]
        state["pt"] += 1

        def s1():
            P.op("pe", lambda e: e.matmul(ps[:, :], lhsT=kT[:, c * 128:(c + 1) * 128], rhs=QT[:, :], start=True, stop=True),
                 reads=[kT, QT], writes=[ps])

        def s2():
            P.op("act", lambda e: e.activation(out=pt[:, :, :], in_=ps[:, :].rearrange("p (h q) -> p h q", h=4),
                                               func=AF.Exp, scale=0.125), reads=[ps], writes=[pt])
            src = pt
            mbuf = None
            if mask_kind == "none":
                pass
            elif mask_kind == "sb":
                mbuf, map_ = mask_arg
            elif mask_kind == "cmp":
                m_i, cc = mask_arg
                mbuf = mk[state["mk"] % 3]
                state["mk"] += 1
                P.op("dve", lambda e, mb=mbuf: e.tensor_scalar(out=mb[:, :], in0=D16[:, :], scalar1=Tt[:, m_i, cc:cc + 1],
                                                               scalar2=None, op0=ALU.is_le), reads=[D16, Tt], writes=[mbuf])
                map_ = mbuf[:, :]
            elif mask_kind == "slc":
                tail = mask_arg
                off = 256 * (state["pm"] % 2)
                state["pm"] += 1
                P.op("pe", lambda e: e.matmul(pM[0:128, off:off + 128], lhsT=Ec[:, c, :], rhs=selTb[:, :], start=True, stop=True),
                     reads=[Ec, selTb], writes=[pM])
                if tail is None:
                    mbuf, map_ = pM, pM[:, off:off + 128]
                else:
                    mbuf = mk[state["mk"] % 3]
                    state["mk"] += 1
                    P.op("dve", lambda e, mb=mbuf: e.tensor_tensor(out=mb[:, :], in0=pM[:, off:off + 128], in1=WMk[:, tail, :],
                                                                   op=ALU.mult), reads=[pM, WMk], writes=[mbuf])
                    map_ = mbuf[:, :]
            if mbuf is not None:
                P.op("dve", lambda e: e.tensor_tensor(out=ptm[:, :, :], in0=pt[:, :, :],
                                                      in1=map_.unsqueeze(1).to_broadcast([128, 4, 128]), op=ALU.mult),
                     reads=[pt, mbuf], writes=[ptm])
                src = ptm
            t.src = src

        def s3():
            src = t.src
            for pb, heads in pouts:
                for hi, h in enumerate(heads):
                    st = first and hi == 0
                    P.op("pe", lambda e, pb=pb, hi=hi, h=h, st=st: e.matmul(
                        pb[:, hi * vw_cols:(hi + 1) * vw_cols], lhsT=src[:, h, :], rhs=vaug[:, c, 0:vw_cols],
                        start=st, stop=last, skip_group_check=True), reads=[src, vaug], writes=[pb])
        t.s1, t.s2, t.s3 = s1, s2, s3
        return t

    def marker(fn):
        t = Task()
        t.s1 = lambda: None
        t.s2 = lambda: None
        t.s3 = fn
        return t

    tasks = []
    for m_i in range(NQB):
        qbm = 2 * m_i + 1
        QT = QTb[m_i % 2]
        gl = gl_all
        glv = gl_all[:, m_i, :].rearrange("p (h t) -> p h t", t=3)
        ya = yacc[m_i % 2]
        selTb = selT[m_i % 2]

        def load(m_i=m_i, QT=QT, gl=gl):
            P.dma("pool", QT[:, :], d["QT"][:, m_i, :], dst=QT)
        ld = marker(lambda: None)
        ld.s1 = load
        tasks.append(ld)
        nvalid = min(8 * qbm + 7, ncmp)
        nchunks = (nvalid + 127) // 128
        for c in range(nchunks):
            Tmin = 128 * (qbm - 1) - 2048 * c - 31
            kind, arg = ("none", None) if Tmin >= 16 * 127 else ("cmp", (m_i, c))
            tasks.append(chunk_task(KcT, c, QT, kind, arg, Vc, W, [(pOc[0], [0, 1]), (pOc[1], [2, 3])], c == 0, c == nchunks - 1))

        def after_cmp(m_i=m_i, glv=glv, gl=gl, ya=ya, selTb=selTb):
            r_, w_ = rd[0], wgt[0]
            for g2 in range(2):
                ov_ = pOc[g2][:, 0:2 * W].rearrange("p (h w) -> p h w", h=2)
                P.op("dve", lambda e, ov_=ov_, g2=g2: e.tensor_scalar(out=r_[:, 2 * g2:2 * g2 + 2], in0=ov_[:, :, 64],
                                                                      scalar1=1e-30, scalar2=None, op0=ALU.max),
                     reads=[pOc[g2]], writes=[r_])
            P.op("dve", lambda e: e.reciprocal(out=r_[:, :], in_=r_[:, :]), reads=[r_], writes=[r_])
            for h in range(4):
                pb = pOc[h // 2]
                base = (h % 2) * W
                if h == 0:
                    P.op("dve", lambda e, pb=pb, base=base, h=h: e.tensor_scalar(
                        out=imp[:, :], in0=pb[:, base + 65:base + W], scalar1=r_[:, h:h + 1], scalar2=None, op0=ALU.mult),
                        reads=[pb, r_], writes=[imp])
                else:
                    P.op("dve", lambda e, pb=pb, base=base, h=h: e.scalar_tensor_tensor(
                        out=imp[:, :], in0=pb[:, base + 65:base + W], scalar=r_[:, h:h + 1], in1=imp[:, :],
                        op0=ALU.mult, op1=ALU.add), reads=[pb, r_, imp], writes=[imp])
            x0 = nsel - 4 * m_i
            P.op("dve", lambda e: e.tensor_tensor(out=sc[:, :], in0=imp[:, :], in1=Amul[:, x0:x0 + nsel], op=ALU.mult),
                 reads=[imp, Amul], writes=[sc])
            P.op("dve", lambda e: e.tensor_tensor(out=sc[:, :], in0=sc[:, :], in1=Aadd[:, x0:x0 + nsel], op=ALU.add),
                 reads=[sc, Aadd], writes=[sc])
            P.op("dve", lambda e: e.memset(sc[:, 0:1], BIGF), writes=[sc])
            P.op("dve", lambda e: e.max(out=m8a[:, :], in_=sc[:, :]), reads=[sc], writes=[m8a])
            P.op("dve", lambda e: e.match_replace(out=sc2[:, :], in_to_replace=m8a[:, :], in_values=sc[:, :],
                                                  imm_value=-3.0 * BIGF), reads=[sc, m8a], writes=[sc2])
            P.op("dve", lambda e: e.max(out=m8b[:, :], in_=sc2[:, :]), reads=[sc2], writes=[m8b])
            P.op("dve", lambda e: e.tensor_scalar(out=self_f[:, :], in0=sc[:, :], scalar1=m8b[:, 7:8], scalar2=None,
                                                  op0=ALU.is_ge), reads=[sc, m8b], writes=[self_f])
            P.op("pe", lambda e: e.transpose(out=pM[0:nsel, 128:256], in_=self_f[:, :], identity=identf[:, :]),
                 reads=[self_f, identf], writes=[pM])
            P.op("act", lambda e: e.activation(out=selTb[:, :], in_=pM[0:nsel, 128:256], func=AF.Copy), reads=[pM], writes=[selTb])
            P.op("dve", lambda e: e.tensor_tensor(out=w_[:, :], in0=r_[:, :], in1=glv[:, :, 0], op=ALU.mult),
                 reads=[r_, gl], writes=[w_])
            for h in range(4):
                pb = pOc[h // 2]
                base = (h % 2) * W
                P.op("dve", lambda e, pb=pb, base=base, h=h: e.tensor_scalar(
                    out=ya[:, h, :], in0=pb[:, base:base + 64], scalar1=w_[:, h:h + 1], scalar2=None, op0=ALU.mult),
                    reads=[pb, w_], writes=[ya])
        tasks.append(marker(after_cmp))
        wl = [c for c in range(2 * m_i - 4, 2 * m_i + 2) if c >= 0]
        for c in wl:
            r = c - (2 * m_i - 4)
            tasks.append(chunk_task(kwT, c, QT, "sb", (WMk, WMk[:, r, :]), vw, 65, [(pOw, [0, 1, 2, 3])], c == wl[0], c == wl[-1]))
        for c in range(0, 2 * m_i + 2):
            tail = (6 + c - 2 * m_i) if c >= 2 * m_i else None
            tasks.append(chunk_task(ksT, c, QT, "slc", tail, vs, 65, [(pOs, [0, 1, 2, 3])], c == 0, c == 2 * m_i + 1, selTb=selTb))

        def combine(m_i=m_i, glv=glv, gl=gl, ya=ya):
            for bi, pb in ((2, pOw), (1, pOs)):
                r_, w_ = rd[bi], wgt[bi]
                pv = pb[:, 0:260].rearrange("p (h w) -> p h w", h=4)
                P.op("dve", lambda e, pv=pv, r_=r_: e.tensor_scalar(out=r_[:, :], in0=pv[:, :, 64], scalar1=1e-30, scalar2=None,
                                                                    op0=ALU.max), reads=[pb], writes=[r_])
                P.op("dve", lambda e, r_=r_: e.reciprocal(out=r_[:, :], in_=r_[:, :]), reads=[r_], writes=[r_])
                P.op("dve", lambda e, r_=r_, w_=w_, bi=bi: e.tensor_tensor(out=w_[:, :], in0=r_[:, :], in1=glv[:, :, bi], op=ALU.mult),
                     reads=[r_, gl], writes=[w_])
                for h in range(4):
                    P.op("dve", lambda e, pb=pb, h=h, w_=w_: e.scalar_tensor_tensor(
                        out=ya[:, h, :], in0=pb[:, h * 65:h * 65 + 64], scalar=w_[:, h:h + 1], in1=ya[:, h, :],
                        op0=ALU.mult, op1=ALU.add), reads=[pb, w_, ya], writes=[ya])
            P.dma("sp", d["y"][m_i], ya[:, :, :].rearrange("p h w -> p (h w)"), src=ya)
        tasks.append(marker(combine))

    DEPTH = 3
    n = len(tasks)
    for i in range(min(DEPTH, n)):
        tasks[i].s1()
    for i in range(n):
        tasks[i].s2()
        if i + DEPTH < n:
            tasks[i + DEPTH].s1()
        tasks[i].s3()


NSA_IN = lambda S, NQB: [
    ("QT", [64, NQB, 512]), ("gl", [128, NQB, 12]), ("ksT", [64, S]), ("kwT", [64, S]), ("vs", [128, S // 128, 65]), ("vw", [128, S // 128, 65]),
    ("kc2T", [128, S]), ("vc2T", [128, S]), ("pe2", [2, 128, 16]), ("w1", [2, 128, 16, 256]), ("w2", [2, 128, 2, 64]),
    ("D16", [128, 128]), ("Mdiag", [128, 128]), ("Mlow", [128, 128]), ("E", [S // 64, S // 128, 128]),
    ("ov", [128, ((S // 16 - 1 + 127) // 128), S // 64]), ("Amul", [128, 2 * (S // 64)]), ("Aadd", [128, 2 * (S // 64)]),
    ("identb", [128, 128]), ("WM", [128, 8, 128]), ("Tt", [128, NQB, ((S // 16 - 1 + 127) // 128)])]


def build_test_nsa(S, NQB):
    P = Prog()
    d = {}
    for nm, shp in NSA_IN(S, NQB):
        d[nm] = P.dram(nm, shp, F32, "ExternalInput")
    d["y"] = P.dram("y", [NQB, 128, 256], F32, "ExternalOutput")
    nsa_phase(P, S, NQB, d)
    print("streams", P.stats(), "sems", P.n_sems)
    return P.finish()


import math
import numpy as np

NT = 2048
NTILE = NT // 128

A_OPS = ([("copy", 1)] * 4 + [("prod", 2)] * 4 + [("silu", 1)] * 4 + [("copy", 1)] * 12 +
         [("rope", 2)] * 4 + [("rope", 2)] * 3 + [("copy", 1)] * 3 + [("gate", 1)])
A_NOUT = len(A_OPS)
A_NIN = sum(n for _, n in A_OPS)
A_COLS = (A_NIN - 1) * 128 + 24
A_ROWS = (A_NOUT - 1) * 128 + 24


def a_col_perm():
    BR = 512
    o = {}
    names = ["ab", "ac", "ax", "hq", "hf", "hi", "hg", "nq", "nkc", "nvc", "nks", "nvs", "nkw", "nvw", "ng", "mg"]
    sizes = [512] * 3 + [512] * 4 + [512] + [128] * 6 + [24, 3072]
    off = 0
    for n, s in zip(names, sizes):
        o[n] = np.arange(off, off + s)
        off += s

    def swap64(ix):
        ix = ix.reshape(-1, 2, 32)
        return ix[:, ::-1, :].reshape(-1)
    cols = []
    for c in range(4):
        cols.append(o["ab"][c * 128:(c + 1) * 128])
    for c in range(4):
        cols.append(o["ac"][c * 128:(c + 1) * 128])
        cols.append(o["ax"][c * 128:(c + 1) * 128])
    for n in ("hq", "hf", "hi", "hg"):
        for c in range(4):
            cols.append(o[n][c * 128:(c + 1) * 128])
    for c in range(4):
        ix = o["nq"][c * 128:(c + 1) * 128]
        cols.append(ix)
        cols.append(swap64(ix))
    for n in ("nkc", "nks", "nkw"):
        cols.append(o[n])
        cols.append(swap64(o[n]))
    for n in ("nvc", "nvs", "nvw"):
        cols.append(o[n])
    cols.append(o["ng"])
    return np.concatenate(cols), o["mg"]


def rope_tables(P, pos_d, inv_d, sgn_d, n):
    posi = P.sb([128, n], I32, "posi")
    P.dma("sp", posi[:, :], pos_d[:, :], dst=posi)
    inv = P.sb([128, 1], F32, "inv")
    sgn = P.sb([128, 1], F32, "sgn")
    P.dma("sp", inv[:, :], inv_d[:, :], dst=inv)
    P.dma("sp", sgn[:, :], sgn_d[:, :], dst=sgn)
    ang = P.sb([128, n], F32, "ang")
    P.op("dve", lambda e: e.tensor_copy(out=ang[:, :], in_=posi[:, :]), reads=[posi], writes=[ang])
    P.op("dve", lambda e: e.tensor_scalar(out=ang[:, :], in0=ang[:, :], scalar1=inv[:, :], scalar2=None, op0=ALU.mult),
         reads=[ang, inv], writes=[ang])
    qf = P.sb([128, n], F32, "qf")
    P.op("dve", lambda e: e.tensor_scalar(out=qf[:, :], in0=ang[:, :], scalar1=1.0 / (2 * math.pi), scalar2=None,
                                          op0=ALU.mult), reads=[ang], writes=[qf])
    qi = P.sb([128, n], I32, "qi")
    P.op("dve", lambda e: e.tensor_copy(out=qi[:, :], in_=qf[:, :]), reads=[qf], writes=[qi])
    P.op("dve", lambda e: e.tensor_copy(out=qf[:, :], in_=qi[:, :]), reads=[qi], writes=[qf])
    C1, C2 = 6.28125, 2 * math.pi - 6.28125
    C2a = float(np.float32(C2))
    C3 = C2 - C2a
    for Cc in (C1, C2a, C3):
        P.op("dve", lambda e, Cc=Cc: e.scalar_tensor_tensor(out=ang[:, :], in0=qf[:, :], scalar=-Cc, in1=ang[:, :],
                                                            op0=ALU.mult, op1=ALU.add), reads=[qf, ang], writes=[ang])
    m = qf
    P.op("dve", lambda e: e.tensor_scalar(out=m[:, :], in0=ang[:, :], scalar1=math.pi, scalar2=-2 * math.pi,
                                          op0=ALU.is_gt, op1=ALU.mult), reads=[ang], writes=[m])
    P.op("dve", lambda e: e.tensor_tensor(out=ang[:, :], in0=ang[:, :], in1=m[:, :], op=ALU.add), reads=[ang, m], writes=[ang])
    P.op("dve", lambda e: e.tensor_scalar(out=m[:, :], in0=ang[:, :], scalar1=-math.pi, scalar2=2 * math.pi,
                                          op0=ALU.is_lt, op1=ALU.mult), reads=[ang], writes=[m])
    P.op("dve", lambda e: e.tensor_tensor(out=ang[:, :], in0=ang[:, :], in1=m[:, :], op=ALU.add), reads=[ang, m], writes=[ang])
    P.op("dve", lambda e: e.tensor_scalar(out=ang[:, :], in0=ang[:, :], scalar1=3.14159, scalar2=-3.14159,
                                          op0=ALU.min, op1=ALU.max), reads=[ang], writes=[ang])
    SINS = P.sb([128, n], F32, "SINS")
    COS = P.sb([128, n], F32, "COS")
    P.op("act", lambda e: e.activation(out=SINS[:, :], in_=ang[:, :], func=AF.Sin), reads=[ang], writes=[SINS])
    P.op("dve", lambda e: e.tensor_scalar(out=SINS[:, :], in0=SINS[:, :], scalar1=sgn[:, :], scalar2=None, op0=ALU.mult),
         reads=[SINS, sgn], writes=[SINS])
    P.op("dve", lambda e: e.tensor_scalar(out=m[:, :], in0=ang[:, :], scalar1=-1.0, scalar2=None, op0=ALU.mult),
         reads=[ang], writes=[m])
    P.op("dve", lambda e: e.tensor_tensor(out=ang[:, :], in0=ang[:, :], in1=m[:, :], op=ALU.max), reads=[ang, m], writes=[ang])
    P.op("dve", lambda e: e.tensor_scalar(out=ang[:, :], in0=ang[:, :], scalar1=-1.0, scalar2=math.pi / 2, op0=ALU.mult,
                                          op1=ALU.add), reads=[ang], writes=[ang])
    P.op("act", lambda e: e.activation(out=COS[:, :], in_=ang[:, :], func=AF.Sin), reads=[ang], writes=[COS])
    return COS, SINS


def make_h2T(P, C, x_src_d, g2, h2T, tp, ntile):
    xt = [P.sb([128, D], F32, "x1t") for _ in range(2)]
    hb = [P.sb([128, D], BF16, "h2") for _ in range(2)]
    for i in range(ntile):
        xb = xt[i % 2]
        P.dma("sp", xb[:, :], x_src_d[i * 128:(i + 1) * 128, :], dst=xb)
        rstd = rms_stats(P, C, xb, xb[:, :], False)
        h = hb[i % 2]
        P.op("dve", lambda e, h=h, xb=xb, rstd=rstd: e.scalar_tensor_tensor(
            out=h[:, :], in0=xb[:, :], scalar=rstd[:, :], in1=g2[:, :], op0=ALU.mult, op1=ALU.mult),
            reads=[xb, rstd, g2], writes=[h])
        for k in range(8):
            P.op("pe", lambda e, h=h, k=k: e.transpose(out=tp[:, k * 128:(k + 1) * 128],
                                                      in_=h[:, k * 128:(k + 1) * 128], identity=C.ident[:, :]),
                 reads=[h, C.ident], writes=[tp])
        P.op("act", lambda e, i=i: e.activation(out=h2T[:, :, i * 128:(i + 1) * 128],
                                                in_=tp[:, :].rearrange("p (k n) -> p k n", k=8), func=AF.Copy),
             reads=[tp], writes=[h2T])


def body_A(P, C, x_d, pfx):
    gains_d = P.dram(pfx + "gains", [3, 128, D], F32, "ExternalInput")
    wg_d = P.dram(pfx + "wg", [D, DFF], F32, "ExternalInput")
    wu_d = P.dram(pfx + "wu", [D, DFF], F32, "ExternalInput")
    wd_d = P.dram(pfx + "wd", [DFF, D], F32, "ExternalInput")
    wa_d = P.dram(pfx + "wa", [D, A_COLS], F32, "ExternalInput")
    pos_d = P.dram(pfx + "pos", [128, NT], I32, "ExternalInput")
    inv_d = P.dram(pfx + "inv", [128, 1], F32, "ExternalInput")
    sgn_d = P.dram(pfx + "sgn", [128, 1], F32, "ExternalInput")
    x1_d = P.dram(pfx + "x1", [NT, D], F32, "ExternalOutput")
    fm_d = P.dram(pfx + "fm", [A_ROWS, NT], F32, "ExternalOutput")
    g = []
    for i in range(3):
        b = P.sb([128, D], F32, "gain")
        P.dma("sp", b[:, :], gains_d[i], dst=b)
        g.append(b)
    with P.scope():
        wg = load_w_cast(P, wg_d, D, DFF, 512, "wg")
        wu = load_w_cast(P, wu_d, D, DFF, 512, "wu")
        wd = load_w_cast(P, wd_d, DFF, D, 512, "wd")
        xpool = [P.sb([128, D], F32, "x") for _ in range(4)]

        def get_x(i):
            b = xpool[i % 4]
            P.dma("sp", b[:, :], x_d[i * 128:(i + 1) * 128, :], dst=b)
            return b

        def put_x(i, b):
            P.dma("sp", x1_d[i * 128:(i + 1) * 128, :], b[:, :], src=b)

        ffn_phase(P, C, NTILE, get_x, put_x, g[0], g[1], wg, wu, wd)
    with P.scope():
        COS, SINS = rope_tables(P, pos_d, inv_d, sgn_d, NT)
        h2T = P.sb([128, 8, NT], BF16, "h2T")
        tp = P.ps([128, 8 * 128], BF16, "tp")
        make_h2T(P, C, x1_d, g[2], h2T, tp, NTILE)
        wsrc = wa_d.rearrange("(c p) n -> p c n", p=128)
        wbufs = [P.sb([128, 8, 512], BF16, "wa") for _ in range(3)]
        pp = [P.ps([128, 512], F32, "pp") for _ in range(4)]
        stg = [P.sb([128, 512], F32, "stg") for _ in range(4)]
        tmp = [P.sb([128, 512], F32, "tmp") for _ in range(2)]
        nblk = (A_NIN + 3) // 4
        ops = []
        ic = 0
        for oc, (kind, nin) in enumerate(A_OPS):
            ops.append((oc, kind, list(range(ic, ic + nin))))
            ic += nin
        loaded = {}
        state = {"pp": 0, "stg": 0, "tmp": 0}

        def get_w(chunk):
            blk = chunk // 4
            _load(blk)
            if blk + 1 < nblk:
                _load(blk + 1)
            wb = loaded[blk]
            off = (chunk % 4) * 128
            return wb, off

        def _load(blk):
            if blk not in loaded:
                wb = wbufs[blk % 3]
                c0 = blk * 512
                c1 = min(A_COLS, c0 + 512)
                P.dma("pool", wb[:, :, 0:c1 - c0], wsrc[:, :, c0:c1], dst=wb)
                loaded[blk] = wb

        for oc, kind, chunks in ops:
            rows = 24 if kind == "gate" else 128
            for tg in range(NT // 512):
                tsl = slice(tg * 512, (tg + 1) * 512)
                pts = []
                for ch in chunks:
                    wb, off = get_w(ch)
                    pt = pp[state["pp"] % 4]
                    state["pp"] += 1
                    for k in range(8):
                        P.op("pe", lambda e, pt=pt, wb=wb, off=off, k=k, rows=rows, tsl=tsl: e.matmul(
                            pt[0:rows, :], lhsT=wb[:, k, off:off + rows], rhs=h2T[:, k, tsl], start=(k == 0), stop=(k == 7)),
                            reads=[wb, h2T], writes=[pt])
                    pts.append(pt)
                sb = stg[state["stg"] % 4]
                state["stg"] += 1
                if kind in ("copy", "gate"):
                    if oc % 2 == 0:
                        P.op("act", lambda e, sb=sb, pt=pts[0], rows=rows: e.activation(out=sb[0:rows, :], in_=pt[0:rows, :],
                                                                                        func=AF.Copy), reads=[pts[0]], writes=[sb])
                    else:
                        P.op("dve", lambda e, sb=sb, pt=pts[0], rows=rows: e.tensor_copy(out=sb[0:rows, :], in_=pt[0:rows, :]),
                             reads=[pts[0]], writes=[sb])
                elif kind == "silu":
                    P.op("act", lambda e, sb=sb, pt=pts[0]: e.activation(out=sb[:, :], in_=pt[:, :], func=AF.Silu),
                         reads=[pts[0]], writes=[sb])
                elif kind == "prod":
                    t = tmp[state["tmp"] % 2]
                    state["tmp"] += 1
                    P.op("act", lambda e, t=t, pt=pts[0]: e.activation(out=t[:, :], in_=pt[:, :], func=AF.Copy),
                         reads=[pts[0]], writes=[t])
                    P.op("dve", lambda e, sb=sb, t=t, pt=pts[1]: e.tensor_tensor(out=sb[:, :], in0=t[:, :], in1=pt[:, :],
                                                                                 op=ALU.mult), reads=[t, pts[1]], writes=[sb])
                elif kind == "rope":
                    t = tmp[state["tmp"] % 2]
                    state["tmp"] += 1
                    P.op("dve", lambda e, t=t, pt=pts[0], tsl=tsl: e.tensor_tensor(out=t[:, :], in0=pt[:, :], in1=COS[:, tsl],
                                                                                   op=ALU.mult), reads=[pts[0], COS], writes=[t])
                    P.op("dve", lambda e, sb=sb, pt=pts[1], tsl=tsl: e.tensor_tensor(out=sb[:, :], in0=pt[:, :], in1=SINS[:, tsl],
                                                                                     op=ALU.mult), reads=[pts[1], SINS], writes=[sb])
                    P.op("pool", lambda e, sb=sb, t=t: e.tensor_tensor(out=sb[:, :], in0=sb[:, :], in1=t[:, :], op=ALU.add),
                         reads=[sb, t], writes=[sb])
                P.dma("sp", fm_d[oc * 128:oc * 128 + rows, tsl], sb[0:rows, :], src=sb)


def build_A():
    P = Prog()
    ident_d = P.dram("ident", [128, 128], F32, "ExternalInput")
    x_d = P.dram("x", [NT, D], F32, "ExternalInput")
    C = build_consts(P, ident_d)
    with P.scope():
        body_A(P, C, x_d, "")
    print("A streams", P.stats(), "sems", P.n_sems)
    return P.finish()


def body_C(P, C, out_d, pfx):
    x1_d = P.dram(pfx + "x1", [NT, D], F32, "ExternalInput")
    gains_d = P.dram(pfx + "gains", [4, 128, D], F32, "ExternalInput")
    vT_d = P.dram(pfx + "vT", [512, NT + 2], F32, "ExternalInput")
    bT_d = P.dram(pfx + "bT", [512, NT], F32, "ExternalInput")
    ybT_d = P.dram(pfx + "ybT", [512, NT], F32, "ExternalInput")
    ycT_d = P.dram(pfx + "ycT", [512, NT], F32, "ExternalInput")
    cw_d = P.dram(pfx + "cw", [128, 4, 3], F32, "ExternalInput")
    wmg_d = P.dram(pfx + "wmg", [D, 3 * D], F32, "ExternalInput")
    wbr_d = P.dram(pfx + "wbr", [3 * 512, D], F32, "ExternalInput")
    wout_d = P.dram(pfx + "wout", [D, D], F32, "ExternalInput")
    wg_d = P.dram(pfx + "wg", [D, DFF], F32, "ExternalInput")
    wu_d = P.dram(pfx + "wu", [D, DFF], F32, "ExternalInput")
    wd_d = P.dram(pfx + "wd", [DFF, D], F32, "ExternalInput")
    x2_d = P.dram(pfx + "x2s", [NT, D], F32, "Internal")
    g = []
    for i in range(4):
        b = P.sb([128, D], F32, "gain")
        P.dma("sp", b[:, :], gains_d[i], dst=b)
        g.append(b)
    with P.scope():
        cw = P.sb([128, 4, 3], F32, "cw")
        P.dma("sp", cw[:, :, :], cw_d[:, :, :], dst=cw)
        wmg = load_w_cast(P, wmg_d, D, 3 * D, 512, "wmg")
        wbr = load_w_cast(P, wbr_d, 3 * 512, D, 512, "wbr")
        wout = load_w_cast(P, wout_d, D, D, 512, "wout")
        tp = P.ps([128, 8 * 128], BF16, "tp")
        h2T = P.sb([128, 8, 512], BF16, "h2T")
        xt = [P.sb([128, D], F32, "x1t") for _ in range(4)]
        hb = [P.sb([128, D], BF16, "h2") for _ in range(2)]
        vin = [P.sb([128, 514], F32, "vin") for _ in range(2)]
        bin_ = [P.sb([128, 512], F32, "bin") for _ in range(2)]
        acc = [P.sb([128, 512], F32, "acc") for _ in range(2)]
        ybrs = [P.sb([128, 4, 512], BF16, "ybr%d" % i) for i in range(3)]
        mT = P.sb([128, 8, 512], BF16, "mT")
        macc = [P.sb([128, 512], F32, "macc") for _ in range(2)]
        gsb = [P.sb([128, 512], F32, "gsb") for _ in range(2)]
        tt = [P.sb([128, 512], F32, "tt") for _ in range(2)]
        pg = [P.ps([128, 512], F32, "pg") for _ in range(2)]
        ppj = [P.ps([128, 512], F32, "ppj") for _ in range(2)]
        py = P.ps([128, D], F32, "py")
        t1 = P.sb([128, D], F32, "t1")
        for tg in range(NT // 512):
            tsl = slice(tg * 512, (tg + 1) * 512)
            for t in range(4):
                i = tg * 4 + t
                xb = xt[t]
                P.dma("sp", xb[:, :], x1_d[i * 128:(i + 1) * 128, :], dst=xb)
                rstd = rms_stats(P, C, xb, xb[:, :], False)
                h = hb[t % 2]
                P.op("dve", lambda e, h=h, xb=xb, rstd=rstd: e.scalar_tensor_tensor(
                    out=h[:, :], in0=xb[:, :], scalar=rstd[:, :], in1=g[0][:, :], op0=ALU.mult, op1=ALU.mult),
                    reads=[xb, rstd, g[0]], writes=[h])
                for k in range(8):
                    P.op("pe", lambda e, h=h, k=k: e.transpose(out=tp[:, k * 128:(k + 1) * 128],
                                                              in_=h[:, k * 128:(k + 1) * 128], identity=C.ident[:, :]),
                         reads=[h, C.ident], writes=[tp])
                P.op("act", lambda e, t=t: e.activation(out=h2T[:, :, t * 128:(t + 1) * 128],
                                                        in_=tp[:, :].rearrange("p (k n) -> p k n", k=8), func=AF.Copy),
                     reads=[tp], writes=[h2T])
            for c in range(4):
                v = vin[c % 2]
                bb = bin_[c % 2]
                a = acc[c % 2]
                P.dma("sp", v[:, :], vT_d[c * 128:(c + 1) * 128, tg * 512:tg * 512 + 514], dst=v)
                P.dma("sp", bb[:, :], bT_d[c * 128:(c + 1) * 128, tsl], dst=bb)
                P.op("dve", lambda e, a=a, v=v, c=c: e.tensor_scalar(out=a[:, :], in0=v[:, 2:514], scalar1=cw[:, c, 2:3],
                                                                     scalar2=None, op0=ALU.mult), reads=[v, cw], writes=[a])
                P.op("dve", lambda e, a=a, v=v, c=c: e.scalar_tensor_tensor(out=a[:, :], in0=v[:, 1:513], scalar=cw[:, c, 1:2],
                                                                            in1=a[:, :], op0=ALU.mult, op1=ALU.add),
                     reads=[v, cw, a], writes=[a])
                P.op("dve", lambda e, a=a, v=v, c=c: e.scalar_tensor_tensor(out=a[:, :], in0=v[:, 0:512], scalar=cw[:, c, 0:1],
                                                                            in1=a[:, :], op0=ALU.mult, op1=ALU.add),
                     reads=[v, cw, a], writes=[a])
                P.op("pool", lambda e, a=a, bb=bb, c=c: e.tensor_tensor(out=ybrs[0][:, c, :], in0=a[:, :], in1=bb[:, :], op=ALU.mult),
                     reads=[a, bb], writes=[ybrs[0]])
            P.dma("pool", ybrs[1][:, :, :], ybT_d[:, tsl].rearrange("(c p) n -> p c n", p=128), dst=ybrs[1])
            P.dma("pool", ybrs[2][:, :, :], ycT_d[:, tsl].rearrange("(c p) n -> p c n", p=128), dst=ybrs[2])
            for mc in range(8):
                ma = macc[mc % 2]
                for br in range(3):
                    pj = ppj[br % 2]
                    for k in range(4):
                        wb, wap = wslice(wbr, br * 4 + k, mc * 128, (mc + 1) * 128)
                        P.op("pe", lambda e, pj=pj, wap=wap, br=br, k=k: e.matmul(
                            pj[:, :], lhsT=wap, rhs=ybrs[br][:, k, :], start=(k == 0), stop=(k == 3)),
                            reads=[wb, ybrs[br]], writes=[pj])
                    pgt = pg[br % 2]
                    for k in range(8):
                        wb, wap = wslice(wmg, k, br * D + mc * 128, br * D + (mc + 1) * 128)
                        P.op("pe", lambda e, pgt=pgt, wap=wap, k=k: e.matmul(
                            pgt[:, :], lhsT=wap, rhs=h2T[:, k, :], start=(k == 0), stop=(k == 7)),
                            reads=[wb, h2T], writes=[pgt])
                    gs = gsb[br % 2]
                    P.op("act", lambda e, gs=gs, pgt=pgt: e.activation(out=gs[:, :], in_=pgt[:, :], func=AF.Sigmoid),
                         reads=[pgt], writes=[gs])
                    if br == 0:
                        P.op("dve", lambda e, ma=ma, gs=gs, pj=pj: e.tensor_tensor(out=ma[:, :], in0=gs[:, :], in1=pj[:, :],
                                                                                   op=ALU.mult), reads=[gs, pj], writes=[ma])
                    else:
                        t_ = tt[br % 2]
                        P.op("dve", lambda e, t_=t_, gs=gs, pj=pj: e.tensor_tensor(out=t_[:, :], in0=gs[:, :], in1=pj[:, :],
                                                                                   op=ALU.mult), reads=[gs, pj], writes=[t_])
                        if br == 1:
                            P.op("pool", lambda e, ma=ma, t_=t_: e.tensor_tensor(out=ma[:, :], in0=ma[:, :], in1=t_[:, :],
                                                                                 op=ALU.add), reads=[ma, t_], writes=[ma])
                        else:
                            P.op("pool", lambda e, ma=ma, t_=t_, mc=mc: e.tensor_tensor(out=mT[:, mc, :], in0=ma[:, :], in1=t_[:, :],
                                                                                        op=ALU.add), reads=[ma, t_], writes=[mT])
            for t in range(4):
                i = tg * 4 + t
                for half in range(2):
                    for k in range(8):
                        wb, wap = wslice(wout, k, half * 512, (half + 1) * 512)
                        P.op("pe", lambda e, wap=wap, k=k, half=half, t=t: e.matmul(
                            py[:, half * 512:(half + 1) * 512], lhsT=mT[:, k, t * 128:(t + 1) * 128], rhs=wap,
                            start=(k == 0), stop=(k == 7)), reads=[wb, mT], writes=[py])
                rstd = rms_stats(P, C, py, py[:, :], True)
                P.op("dve", lambda e, rstd=rstd: e.scalar_tensor_tensor(
                    out=t1[:, :], in0=py[:, :], scalar=rstd[:, :], in1=g[1][:, :], op0=ALU.mult, op1=ALU.mult),
                    reads=[py, rstd, g[1]], writes=[t1])
                xb = xt[t]
                P.op("pool", lambda e, xb=xb: e.tensor_tensor(out=xb[:, :], in0=xb[:, :], in1=t1[:, :], op=ALU.add),
                     reads=[xb, t1], writes=[xb])
                P.dma("sp", x2_d[i * 128:(i + 1) * 128, :], xb[:, :], src=xb)
    with P.scope():
        wg = load_w_cast(P, wg_d, D, DFF, 512, "wg")
        wu = load_w_cast(P, wu_d, D, DFF, 512, "wu")
        wd = load_w_cast(P, wd_d, DFF, D, 512, "wd")
        xpool = [P.sb([128, D], F32, "x") for _ in range(4)]

        def get_x(i):
            b = xpool[i % 4]
            P.dma("sp", b[:, :], x2_d[i * 128:(i + 1) * 128, :], dst=b)
            return b

        def put_x(i, b):
            P.dma("sp", out_d[i * 128:(i + 1) * 128, :], b[:, :], src=b)

        ffn_phase(P, C, NTILE, get_x, put_x, g[2], g[3], wg, wu, wd)


def build_C():
    P = Prog()
    ident_d = P.dram("ident", [128, 128], F32, "ExternalInput")
    out_d = P.dram("xo", [NT, D], F32, "ExternalOutput")
    C = build_consts(P, ident_d)
    with P.scope():
        body_C(P, C, out_d, "")
    print("C streams", P.stats(), "sems", P.n_sems)
    return P.finish()


def build_CA():
    P = Prog()
    ident_d = P.dram("ident", [128, 128], F32, "ExternalInput")
    x3_d = P.dram("x3s", [NT, D], F32, "Internal")
    C = build_consts(P, ident_d)
    with P.scope():
        body_C(P, C, x3_d, "c_")
    with P.scope():
        body_A(P, C, x3_d, "a_")
    print("CA streams", P.stats(), "sems", P.n_sems)
    return P.finish()


S = 8192
NQB = 32


def build_B():
    P = Prog()
    dh = {}
    for nm, shp in (("qT", [128, S]), ("zfT", [128, S]), ("zf", [S, 128]), ("v", [S, 128]), ("g", [S, 128]),
                    ("lbT", [128, 4]), ("lmask", [128, 4]), ("lbrow", [128, 4, 128]), ("lmaskrow", [128, 4, 128]),
                    ("gnorm", [128, 128]), ("M1", [128, 128]), ("M2", [128, 128])):
        dh[nm] = P.dram("h_" + nm, shp, F32, "ExternalInput")
    dh["y"] = P.dram("yh", [S, 128], F32, "ExternalOutput")
    dn = {}
    for nm, shp in NSA_IN(S, NQB):
        dn[nm] = P.dram("n_" + nm, shp, F32, "ExternalInput")
    dn["y"] = P.dram("yn", [NQB, 128, 256], F32, "ExternalOutput")
    with P.scope():
        hgrn_phase(P, S, dh)
    with P.scope():
        nsa_phase(P, S, NQB, dn)
    print("B streams", P.stats(), "sems", P.n_sems)
    return P.finish()


import numpy as np

S = 8192
NC = 8

def rep(a, n=128):
    return np.ascontiguousarray(np.broadcast_to(a[None], (n,) + a.shape))

IDENT = np.eye(128, dtype=np.float32)
_p = np.arange(128)
INV = (10000.0 ** (-(_p % 32).astype(np.float32) * 2.0 / 64)).astype(np.float32).reshape(128, 1)
SGN = np.where((_p % 64) < 32, -1.0, 1.0).astype(np.float32).reshape(128, 1)
PERM, MG = a_col_perm()


def prep_A(inp, l, x):
    wa = np.ascontiguousarray(inp["w_in"][l][:, PERM])
    gains = np.stack([rep(inp["norm_gains"][l, i]) for i in (0, 1, 2)])
    maps = []
    pos = inp["positions"].reshape(-1)
    for c in range(NC):
        sl = slice(c * NT, (c + 1) * NT)
        maps.append(dict(x=(np.ascontiguousarray(x[sl]) if x is not None else None), gains=gains, wg=inp["w_ffn_gate"][l, 0], wu=inp["w_ffn_up"][l, 0],
                         wd=inp["w_ffn_down"][l, 0], wa=wa, ident=IDENT, pos=rep(pos[sl].astype(np.int32)), inv=INV, sgn=SGN))
    return maps

HG_M1, HG_M2 = hgrn_consts_np()
NSA_K = nsa_consts_np(S)
ONES128 = np.ones((128, 128), np.float32)
ZEROS128 = np.zeros((128, 128), np.float32)
NQB = 32
NCC = 4


def _shift(A, par):
    o = np.empty_like(A)
    if par:
        o[:, 2 * par:] = A[:, :A.shape[1] - 2 * par]
        o[:, :2 * par] = A[:, :1]
    else:
        o[:] = A
    return o


def nsa_core_consts(par):
    K = NSA_K
    if par == 0:
        wm = [K["Mlow"], ONES128, ONES128, ONES128, K["Mdiag"], ZEROS128, K["Mdiag"], ZEROS128]
    else:
        wm = [ZEROS128, K["Mlow"], ONES128, ONES128, ONES128, K["Mdiag"], ONES128, K["Mdiag"]]
    Tt = np.zeros((128, NQB, NCC), np.float32)
    for m in range(NQB):
        for c in range(NCC):
            Tt[:, m, c] = 128 * (2 * m + par) - 2048 * c - 31
    return dict(WM=np.ascontiguousarray(np.stack(wm, 1)), Tt=Tt, Amul=_shift(K["Amul"], par), Aadd=_shift(K["Aadd"], par))


NSA_CC = [nsa_core_consts(0), nsa_core_consts(1)]
OV_PM = np.ascontiguousarray(NSA_K["ov"].reshape(-1, 128, NSA_K["ov"].shape[-1]).transpose(1, 0, 2))


def prep_B(inp, l, FM):
    maps = []
    logits = inp["hgrn_lb_logits"]
    lmask = np.zeros((4,), np.float32)
    lmask[1:l + 1] = 1
    gn = rep(inp["hgrn_gnorm"][l])
    pe = inp["cmp_pe"][l]
    pe2 = np.ascontiguousarray(pe.reshape(2, 16, 2, 64).transpose(0, 2, 3, 1).reshape(2, 128, 16))
    for c in range(NC):
        b, hd = c // 4, c % 4
        kvh, par = (c // 2) % 2, c % 2
        tk = slice(b * S, (b + 1) * S)
        d = {}
        r = lambda base, n: FM[base:base + n, tk]
        d["h_qT"] = np.ascontiguousarray(r(1024 + hd * 128, 128))
        zfT = r(1536 + hd * 128, 128)
        d["h_zfT"] = np.ascontiguousarray(zfT)
        d["h_zf"] = np.ascontiguousarray(zfT.T)
        d["h_v"] = np.ascontiguousarray(r(2048 + hd * 128, 128).T)
        d["h_g"] = np.ascontiguousarray(r(2560 + hd * 128, 128).T)
        lg = logits[:, hd * 128:(hd + 1) * 128]
        d["h_lbT"] = np.ascontiguousarray(lg.T)
        d["h_lmask"] = rep(lmask)
        d["h_lbrow"] = rep(lg)
        d["h_lmaskrow"] = np.ascontiguousarray(np.broadcast_to(lmask[None, :, None], (128, 4, 128)))
        d["h_gnorm"] = gn
        d["h_M1"] = HG_M1
        d["h_M2"] = HG_M2
        q = r(3072 + kvh * 256, 256).reshape(4, 64, 64, 128)[:, :, par::2]
        d["n_QT"] = np.ascontiguousarray(q.transpose(1, 2, 0, 3).reshape(64, NQB, 512))
        gl = r(4352 + kvh * 12, 12).reshape(12, 64, 128)[:, par::2]
        d["n_gl"] = np.ascontiguousarray(gl.transpose(2, 1, 0))
        d["n_ksT"] = np.ascontiguousarray(r(3712 + kvh * 64, 64))
        d["n_kwT"] = np.ascontiguousarray(r(3840 + kvh * 64, 64))

        def stack2(xT):
            o = np.zeros((128, S), np.float32)
            o[:64] = xT
            o[64:, :-1] = xT[:, 1:]
            return o
        d["n_kc2T"] = stack2(r(3584 + kvh * 64, 64))
        d["n_vc2T"] = stack2(r(3968 + kvh * 64, 64))

        def aug(xT):
            o = np.ones((S, 65), np.float32)
            o[:, :64] = xT.T
            return np.ascontiguousarray(o.reshape(S // 128, 128, 65).transpose(1, 0, 2))
        d["n_vs"] = aug(r(4096 + kvh * 64, 64))
        d["n_vw"] = aug(r(4224 + kvh * 64, 64))
        d["n_pe2"] = pe2
        d["n_w1"] = np.ascontiguousarray(inp["cmp_w1"][l].reshape(2, 16, 128, 256).transpose(0, 2, 1, 3))
        d["n_w2"] = np.ascontiguousarray(inp["cmp_w2"][l].reshape(2, 2, 128, 64).transpose(0, 2, 1, 3))
        for k in ("D16", "Mdiag", "Mlow", "E", "identb"):
            d["n_" + k] = NSA_K[k]
        d["n_ov"] = OV_PM
        for k, v in NSA_CC[par].items():
            d["n_" + k] = v
        maps.append(d)
    return maps


def gather_B(results):
    ybT = np.zeros((2, 512, S), np.float32)
    ycT = np.zeros((2, 512, S), np.float32)
    for c in range(NC):
        b, hd = c // 4, c % 4
        kvh, par = (c // 2) % 2, c % 2
        ybT[b, hd * 128:(hd + 1) * 128, :] = results[c]["yh"].T
        y = results[c]["yn"]
        yT = y.transpose(2, 0, 1)
        ycT[b, kvh * 256:(kvh + 1) * 256].reshape(256, 64, 128)[:, par::2, :] = yT
    return ybT, ycT


def prep_C(inp, l, x1, FM, ybT, ycT):
    gains = np.stack([rep(inp["norm_gains"][l, i]) for i in (2, 3, 4, 5)])
    cw = np.ascontiguousarray(inp["conv_w"][l].reshape(3, 4, 128).transpose(2, 1, 0))
    wmg = np.ascontiguousarray(inp["w_in"][l][:, MG])
    wbr = np.ascontiguousarray(inp["w_branch"][l].reshape(1536, 1024))
    maps = []
    for c in range(NC):
        b = c // 4
        t0 = c * NT
        sl = slice(t0, t0 + NT)
        ls = slice((c % 4) * NT, (c % 4 + 1) * NT)
        vT = np.zeros((512, NT + 2), np.float32)
        vT[:, 2:] = FM[512:1024, sl]
        if c % 4 != 0:
            vT[:, :2] = FM[512:1024, t0 - 2:t0]
        maps.append(dict(x1=np.ascontiguousarray(x1[sl]), gains=gains, vT=vT, bT=np.ascontiguousarray(FM[0:512, sl]),
                         ybT=np.ascontiguousarray(ybT[b][:, ls]), ycT=np.ascontiguousarray(ycT[b][:, ls]), cw=cw, wmg=wmg, wbr=wbr,
                         wout=inp["w_out"][l], wg=inp["w_ffn_gate"][l, 1], wu=inp["w_ffn_up"][l, 1], wd=inp["w_ffn_down"][l, 1],
                         ident=IDENT))
    return maps


from concourse.bass_utils import run_bass_kernel_spmd

_PROGS = {}


def _prog(name):
    if name not in _PROGS:
        _PROGS[name] = {"A": build_A, "B": build_B, "C": build_C, "CA": build_CA}[name]()
    return _PROGS[name]


def kernel(**inputs):
    inp = {k: np.asarray(v) for k, v in inputs.items()}
    cores = list(range(NC))
    x = np.ascontiguousarray(inp["x"].reshape(-1, D).astype(np.float32, copy=False))
    resA = run_bass_kernel_spmd(_prog("A"), prep_A(inp, 0, x), core_ids=cores)
    x1 = np.concatenate([r["x1"] for r in resA.results])
    FM = np.concatenate([r["fm"] for r in resA.results], axis=1)
    del resA
    for l in range(4):
        resB = run_bass_kernel_spmd(_prog("B"), prep_B(inp, l, FM), core_ids=cores)
        ybT, ycT = gather_B(resB.results)
        del resB
        mc = prep_C(inp, l, x1, FM, ybT, ycT)
        if l < 3:
            ma = prep_A(inp, l + 1, None)
            maps = []
            for c in range(NC):
                d = {"c_" + k: v for k, v in mc[c].items() if k != "ident"}
                d.update({"a_" + k: v for k, v in ma[c].items() if k not in ("ident", "x")})
                d["ident"] = IDENT
                maps.append(d)
            res = run_bass_kernel_spmd(_prog("CA"), maps, core_ids=cores)
            x1 = np.concatenate([r["a_x1"] for r in res.results])
            FM = np.concatenate([r["a_fm"] for r in res.results], axis=1)
            del res
        else:
            res = run_bass_kernel_spmd(_prog("C"), mc, core_ids=cores)
            x = np.concatenate([r["xo"] for r in res.results])
            del res
    return np.ascontiguousarray(x.reshape(2, S, D).astype(np.float32))
```

```python
import contextlib
import numpy as np
import concourse.bass as bass
import concourse.mybir as mybir

F32 = mybir.dt.float32
BF16 = mybir.dt.bfloat16
I32 = mybir.dt.int32
ALU = mybir.AluOpType
AF = mybir.ActivationFunctionType
AX = mybir.AxisListType

ENGS = ("pe", "act", "dve", "pool", "sp")


class Buf:
    def __init__(self, prog, t, name, tracked=True):
        self.prog = prog
        self.t = t
        self.name = name
        self.tracked = tracked
        self.last_w = None
        self.readers = {}
        self.wsem = None
        self.wcnt = 0
        self.rsem = None
        self.rcnt = 0

    def __getitem__(self, idx):
        return self.t[idx]

    @property
    def ap(self):
        return self.t


class Prog:
    def __init__(self, same_engine_sync=None):
        import os
        if same_engine_sync is None:
            same_engine_sync = os.environ.get('SES', '1') == '1'
        self.nc = bass.Bass("TRN2", target_bir_lowering=False)
        self.es = contextlib.ExitStack()
        self.sem_es = contextlib.ExitStack()
        self.streams = {e: [] for e in ENGS}
        self.cnt = {e: 0 for e in ENGS}
        self.esem = {}
        for e in ENGS:
            self.esem[e] = self.sem_es.enter_context(self.nc.semaphore("s_" + e))
        self.seen = {e: {} for e in ENGS}
        self.same_engine_sync = same_engine_sync
        self.dma_sems = []
        self.nbuf = 0
        self.n_sems = 5
        self.all_bufs = []
        self.free_sems = []

    def dram(self, name, shape, dtype, kind):
        t = self.nc.dram_tensor(name, list(shape), dtype, kind=kind)
        return t.ap()

    def sb(self, shape, dtype, name=None):
        self.nbuf += 1
        name = (name or "b") + "_%d" % self.nbuf
        t = self.es.enter_context(self.nc.sbuf_tensor(name, list(shape), dtype))
        b = Buf(self, t, name)
        self.all_bufs.append(b)
        return b

    def ps(self, shape, dtype=F32, name=None):
        self.nbuf += 1
        name = (name or "p") + "_%d" % self.nbuf
        t = self.es.enter_context(self.nc.psum_tensor(name, list(shape), dtype))
        b = Buf(self, t, name)
        self.all_bufs.append(b)
        return b

    def _sem(self, name):
        if self.free_sems:
            return self.free_sems.pop()
        self.n_sems += 1
        return (self.sem_es.enter_context(self.nc.semaphore(name)), 0)

    def _collect(self, eng, reads, writes, no_waw=False):
        need = {}

        def add(tok):
            if tok is None:
                return
            key, val, e = tok
            if e == eng and (eng == "pe" or not self.same_engine_sync):
                return
            if need.get(key, (0,))[0] < val:
                need[key] = (val, e)

        for b in reads:
            if b is None or not b.tracked:
                continue
            add(b.last_w)
        for b in writes:
            if b is None or not b.tracked:
                continue
            if not no_waw:
                add(b.last_w)
            for key, (val, e) in b.readers.items():
                add((key, val, e))
        out = []
        for key, (val, e) in need.items():
            if self.seen[eng].get(key, 0) >= val:
                continue
            self.seen[eng][key] = val
            out.append((key, val))
        return out

    def _record(self, tok, reads, writes):
        key, val, e = tok
        for b in writes:
            if b is None or not b.tracked:
                continue
            b.last_w = tok
            b.readers = {}
        for b in reads:
            if b is None or not b.tracked:
                continue
            if b in writes:
                continue
            b.readers[key] = (val, e)

    def op(self, eng, fn, reads=(), writes=(), no_waw=False):
        waits = self._collect(eng, reads, writes, no_waw)
        st = self.streams[eng]
        for key, val in waits:
            st.append(("w", key, val))
        self.cnt[eng] += 1
        st.append(("o", fn, self.esem[eng], 1))
        tok = (self.esem[eng], self.cnt[eng], eng)
        self._record(tok, reads, writes)
        return tok

    def dma(self, queue, out_ap, in_ap, dst=None, src=None, no_waw=False, **kw):
        reads = [src] if src is not None else []
        writes = [dst] if dst is not None else []
        waits = self._collect(queue, reads, writes, no_waw)
        st = self.streams[queue]
        for key, val in waits:
            st.append(("w", key, val))
        if dst is not None:
            if dst.wsem is None:
                dst.wsem, dst.wcnt = self._sem("w_" + dst.name)
                self.dma_sems.append(dst)
            dst.wcnt += 16
            sem, val = dst.wsem, dst.wcnt
        else:
            if src.rsem is None:
                src.rsem, src.rcnt = self._sem("r_" + src.name)
                self.dma_sems.append(src)
            src.rcnt += 16
            sem, val = src.rsem, src.rcnt

        def fn(e, out_ap=out_ap, in_ap=in_ap, kw=kw):
            return e.dma_start(out=out_ap, in_=in_ap, **kw)

        st.append(("o", fn, sem, 16))
        tok = (sem, val, "dma")
        self._record(tok, reads, writes)
        return tok

    @contextlib.contextmanager
    def scope(self):
        outer = self.es
        self.es = contextlib.ExitStack()
        n0 = len(self.all_bufs)
        try:
            yield
        finally:
            self.barrier()
            self.flush()
            for b in self.all_bufs[n0:]:
                if b.wsem is not None:
                    self.free_sems.append((b.wsem, b.wcnt))
                if b.rsem is not None:
                    self.free_sems.append((b.rsem, b.rcnt))
                if b in self.dma_sems:
                    self.dma_sems.remove(b)
                b.dead = True
            del self.all_bufs[n0:]
            self.es.close()
            self.es = outer

    def barrier(self):
        targets = [(self.esem[e], self.cnt[e]) for e in ENGS if self.cnt[e] > 0]
        for b in self.dma_sems:
            if b.wsem is not None and b.wcnt:
                targets.append((b.wsem, b.wcnt))
            if b.rsem is not None and b.rcnt:
                targets.append((b.rsem, b.rcnt))
        for e in ENGS:
            for key, val in targets:
                if key is self.esem[e]:
                    continue
                if self.seen[e].get(key, 0) >= val:
                    continue
                self.seen[e][key] = val
                self.streams[e].append(("w", key, val))

    def flush(self):
        nc = self.nc
        streams = self.streams
        self.streams = {e: [] for e in ENGS}

        def run(stream, e):
            for it in stream:
                if it[0] == "w":
                    e.wait_ge(it[1], it[2])
                else:
                    ins = it[1](e)
                    ins.then_inc(it[2], it[3])

        with nc.Block() as block:
            @block.tensor
            def _(e):
                run(streams["pe"], e)

            @block.scalar
            def _(e):
                run(streams["act"], e)

            @block.vector
            def _(e):
                run(streams["dve"], e)

            @block.gpsimd
            def _(e):
                run(streams["pool"], e)

            @block.sync
            def _(e):
                run(streams["sp"], e)

    def finish(self):
        targets = [(self.esem[e], self.cnt[e]) for e in ENGS if self.cnt[e] > 0 and e != "sp"]
        for b in self.dma_sems:
            if b.wsem is not None and b.wcnt:
                targets.append((b.wsem, b.wcnt))
            if b.rsem is not None and b.rcnt:
                targets.append((b.rsem, b.rcnt))
        for key, val in targets:
            self.streams["sp"].append(("w", key, val))
        self.flush()
        self.es.close()
        self.sem_es.close()
        return self.nc

    def stats(self):
        return {e: self.cnt[e] for e in ENGS}


import numpy as np

D = 1024
DFF = 2816
NJ = 22
TG = 256
EPS = 1e-6


def load_w_cast(P, dram_ap, rows, cols, colblk, name, defer=False):
    kc = rows // 128
    src = dram_ap.rearrange("(c p) n -> p c n", p=128)
    blocks = []
    issue = []
    c0 = 0
    while c0 < cols:
        c1 = min(cols, c0 + colblk)
        b = P.sb([128, kc, c1 - c0], BF16, name)
        fn = (lambda b=b, c0=c0, c1=c1: P.dma("pool", b[:, :, :], src[:, :, c0:c1], dst=b))
        if defer:
            issue.append(fn)
        else:
            fn()
        blocks.append((b, c0, c1))
        c0 = c1
    if defer:
        return blocks, issue
    return blocks


def load_ffn_weights(P, wg_d, wu_d, wd_d):
    wg, ig = load_w_cast(P, wg_d, D, DFF, 512, "wg", defer=True)
    wu, iu = load_w_cast(P, wu_d, D, DFF, 512, "wu", defer=True)
    wd, idn = load_w_cast(P, wd_d, DFF, D, 512, "wd", defer=True)
    for a_, b_ in zip(ig, iu):
        a_()
        b_()
    for f in idn:
        f()
    return wg, wu, wd


def wslice(blocks, k, c0, c1):
    for b, b0, b1 in blocks:
        if b0 <= c0 and c1 <= b1:
            return b, b[:, k, c0 - b0:c1 - b0]
    raise ValueError((c0, c1))


class Consts:
    pass


def rms_stats(P, C, src_buf, src_ap, from_psum):
    ss = P.sb([128, 1], F32, "ss")
    if from_psum:
        P.op("act", lambda e: e.activation(out=C.junk[:, :], in_=src_ap, func=AF.Square, accum_out=ss[:, :]),
             reads=[src_buf], writes=[C.junk, ss])
    else:
        P.op("dve", lambda e: e.scalar_tensor_tensor(out=C.junk[:, :], in0=src_ap, scalar=1.0, in1=src_ap,
                                                      op0=ALU.mult, op1=ALU.mult, accum_out=ss[:, :]),
             reads=[src_buf], writes=[C.junk, ss])
    ms = P.sb([128, 1], F32, "ms")
    P.op("dve", lambda e: e.tensor_scalar(out=ms[:, :], in0=ss[:, :], scalar1=1.0 / D, scalar2=EPS,
                                          op0=ALU.mult, op1=ALU.add), reads=[ss], writes=[ms])
    rstd = P.sb([128, 1], F32, "rstd")
    P.op("pool", lambda e: e.tensor_tensor(out=rstd[:, :], in0=ms[:, :], in1=C.mhalf[:, :], op=ALU.pow),
         reads=[ms, C.mhalf], writes=[rstd])
    return rstd


def ffn_phase(P, C, ntiles, get_x, put_x, g_pre, g_post, wg, wu, wd):
    ngroups = ntiles // 2
    hT = [P.sb([128, 8, TG], BF16, "hT") for _ in range(2)]
    aT = P.sb([128, NJ, TG], BF16, "aT")
    tp = P.ps([128, 8 * 128], BF16, "tp")
    gu = [P.ps([128, 2, TG], F32, "gu") for _ in range(2)]
    ys = [P.ps([128, D], F32, "y") for _ in range(2)]
    hb = [P.sb([128, D], BF16, "h") for _ in range(2)]
    sg = [P.sb([128, TG], BF16, "sg") for _ in range(2)]
    xs = {}
    t1 = P.sb([128, D], F32, "t1")

    def prep(g):
        for t in range(2):
            i = 2 * g + t
            xb = get_x(i)
            xs[i] = xb
            rstd = rms_stats(P, C, xb, xb[:, :], False)
            h = hb[t]
            P.op("dve", lambda e, h=h, xb=xb, rstd=rstd: e.scalar_tensor_tensor(
                out=h[:, :], in0=xb[:, :], scalar=rstd[:, :], in1=g_pre[:, :], op0=ALU.mult, op1=ALU.mult),
                reads=[xb, rstd, g_pre], writes=[h])

    def transposes(g):
        for t in range(2):
            h = hb[t]
            for k in range(8):
                P.op("pe", lambda e, h=h, k=k: e.transpose(out=tp[:, k * 128:(k + 1) * 128],
                                                          in_=h[:, k * 128:(k + 1) * 128], identity=C.ident[:, :]),
                     reads=[h, C.ident], writes=[tp])
            dst = hT[g % 2]
            P.op("act", lambda e, dst=dst, t=t: e.activation(
                out=dst[:, :, t * 128:(t + 1) * 128], in_=tp[:, :].rearrange("p (k n) -> p k n", k=8), func=AF.Copy),
                reads=[tp], writes=[dst])

    def phaseA(g):
        h_t = hT[g % 2]
        for j in range(NJ):
            pg = gu[j % 2]
            for which, W in ((0, wg), (1, wu)):
                for k in range(8):
                    wb, wap = wslice(W, k, j * 128, (j + 1) * 128)
                    P.op("pe", lambda e, pg=pg, which=which, wap=wap, k=k: e.matmul(
                        pg[:, which, :], lhsT=wap, rhs=h_t[:, k, :], start=(k == 0), stop=(k == 7)),
                        reads=[wb, h_t], writes=[pg])
            s = sg[j % 2]
            P.op("act", lambda e, s=s, pg=pg: e.activation(out=s[:, :], in_=pg[:, 0, :], func=AF.Silu),
                 reads=[pg], writes=[s])
            P.op("dve", lambda e, s=s, pg=pg, j=j: e.tensor_tensor(out=aT[:, j, :], in0=s[:, :], in1=pg[:, 1, :],
                                                                   op=ALU.mult),
                 reads=[s, pg], writes=[aT])

    def phaseB(g):
        for t in range(2):
            y = ys[t]
            for half in range(2):
                for j in range(NJ):
                    wb, wap = wslice(wd, j, half * 512, (half + 1) * 512)
                    P.op("pe", lambda e, y=y, half=half, wap=wap, j=j, t=t: e.matmul(
                        y[:, half * 512:(half + 1) * 512], lhsT=aT[:, j, t * 128:(t + 1) * 128], rhs=wap,
                        start=(j == 0), stop=(j == NJ - 1)),
                        reads=[wb, aT], writes=[y])

    def post(g):
        for t in range(2):
            i = 2 * g + t
            y = ys[t]
            rstd = rms_stats(P, C, y, y[:, :], True)
            P.op("dve", lambda e, y=y, rstd=rstd: e.scalar_tensor_tensor(
                out=t1[:, :], in0=y[:, :], scalar=rstd[:, :], in1=g_post[:, :], op0=ALU.mult, op1=ALU.mult),
                reads=[y, rstd, g_post], writes=[t1])
            xb = xs.pop(i)
            P.op("dve", lambda e, xb=xb: e.scalar_tensor_tensor(
                out=xb[:, :], in0=t1[:, :], scalar=0.5, in1=xb[:, :], op0=ALU.mult, op1=ALU.add),
                reads=[t1, xb], writes=[xb])
            put_x(i, xb)

    prep(0)
    transposes(0)
    for g in range(ngroups):
        phaseA(g)
        if g + 1 < ngroups:
            prep(g + 1)
            transposes(g + 1)
        phaseB(g)
        post(g)


def build_consts(P, ident_d):
    C = Consts()
    C.ident = P.sb([128, 128], BF16, "ident")
    P.dma("pool", C.ident[:, :], ident_d[:, :], dst=C.ident)
    C.junk = P.sb([128, D], BF16, "junk")
    C.junk.tracked = False
    C.mhalf = P.sb([128, 1], F32, "mhalf")
    P.op("pool", lambda e: e.memset(C.mhalf[:, :], -0.5), writes=[C.mhalf])
    return C


def build_test_ffn(NT):
    P = Prog()
    x_d = P.dram("x", [NT, D], F32, "ExternalInput")
    gains_d = P.dram("gains", [2, 128, D], F32, "ExternalInput")
    wg_d = P.dram("wg", [D, DFF], F32, "ExternalInput")
    wu_d = P.dram("wu", [D, DFF], F32, "ExternalInput")
    wd_d = P.dram("wd", [DFF, D], F32, "ExternalInput")
    ident_d = P.dram("ident", [128, 128], F32, "ExternalInput")
    out_d = P.dram("out", [NT, D], F32, "ExternalOutput")
    C = build_consts(P, ident_d)
    g_pre = P.sb([128, D], F32, "gpre")
    g_post = P.sb([128, D], F32, "gpost")
    P.dma("sp", g_pre[:, :], gains_d[0], dst=g_pre)
    P.dma("sp", g_post[:, :], gains_d[1], dst=g_post)
    wg = load_w_cast(P, wg_d, D, DFF, 512, "wg")
    wu = load_w_cast(P, wu_d, D, DFF, 512, "wu")
    wd = load_w_cast(P, wd_d, DFF, D, 512, "wd")
    xpool = [P.sb([128, D], F32, "x") for _ in range(4)]

    def get_x(i):
        b = xpool[i % 4]
        P.dma("sp", b[:, :], x_d[i * 128:(i + 1) * 128, :], dst=b)
        return b

    def put_x(i, b):
        P.dma("sp", out_d[i * 128:(i + 1) * 128, :], b[:, :], src=b)

    ffn_phase(P, C, NT // 128, get_x, put_x, g_pre, g_post, wg, wu, wd)
    print("streams", P.stats(), "sems", P.n_sems)
    return P.finish()


import numpy as np

EPS = 1e-6


def hgrn_consts_np():
    s = np.arange(128)
    same = (s[:, None] // 64) == (s[None, :] // 64)
    M1 = (same & (s[:, None] <= s[None, :])).astype(np.float32)
    M2 = (same & (s[:, None] > s[None, :])).astype(np.float32)
    return M1, M2


def hgrn_phase(P, S, d):
    nt = S // 128
    M1 = P.sb([128, 128], F32, "M1")
    M2 = P.sb([128, 128], F32, "M2")
    P.dma("sp", M1[:, :], d["M1"][:, :], dst=M1)
    P.dma("sp", M2[:, :], d["M2"][:, :], dst=M2)
    gn = P.sb([128, 128], F32, "gn")
    P.dma("sp", gn[:, :], d["gnorm"][:, :], dst=gn)
    mhalf = P.sb([128, 1], F32, "mhalf")
    P.op("pool", lambda e: e.memset(mhalf[:, :], -0.5), writes=[mhalf])
    lbl = P.sb([128, 4], F32, "lbl")
    lmk = P.sb([128, 4], F32, "lmk")
    P.dma("sp", lbl[:, :], d["lbT"][:, :], dst=lbl)
    P.dma("sp", lmk[:, :], d["lmask"][:, :], dst=lmk)
    ex = P.sb([128, 4], F32, "ex")
    P.op("act", lambda e: e.activation(out=ex[:, :], in_=lbl[:, :], func=AF.Exp), reads=[lbl], writes=[ex])
    den = P.sb([128, 1], F32, "den")
    P.op("dve", lambda e: e.reduce_sum(out=den[:, :], in_=ex[:, :], axis=AX.X), reads=[ex], writes=[den])
    rden = P.sb([128, 1], F32, "rden")
    P.op("dve", lambda e: e.reciprocal(out=rden[:, :], in_=den[:, :]), reads=[den], writes=[rden])
    exm = P.sb([128, 4], F32, "exm")
    P.op("dve", lambda e: e.tensor_tensor(out=exm[:, :], in0=ex[:, :], in1=lmk[:, :], op=ALU.mult),
         reads=[ex, lmk], writes=[exm])
    num = P.sb([128, 1], F32, "num")
    P.op("dve", lambda e: e.reduce_sum(out=num[:, :], in_=exm[:, :], axis=AX.X), reads=[exm], writes=[num])
    lbc = P.sb([128, 1], F32, "lbc")
    P.op("dve", lambda e: e.tensor_tensor(out=lbc[:, :], in0=num[:, :], in1=rden[:, :], op=ALU.mult),
         reads=[num, rden], writes=[lbc])
    omlc = P.sb([128, 1], F32, "omlc")
    P.op("dve", lambda e: e.tensor_scalar(out=omlc[:, :], in0=lbc[:, :], scalar1=-1.0, scalar2=1.0,
                                          op0=ALU.mult, op1=ALU.add), reads=[lbc], writes=[omlc])
    nomlc = P.sb([128, 1], F32, "nomlc")
    P.op("dve", lambda e: e.tensor_scalar(out=nomlc[:, :], in0=omlc[:, :], scalar1=-1.0, scalar2=None,
                                          op0=ALU.mult), reads=[omlc], writes=[nomlc])
    lbr = P.sb([128, 4, 128], F32, "lbr")
    lmr = P.sb([128, 4, 128], F32, "lmr")
    P.dma("sp", lbr[:, :, :], d["lbrow"][:, :, :], dst=lbr)
    P.dma("sp", lmr[:, :, :], d["lmaskrow"][:, :, :], dst=lmr)
    exr = P.sb([128, 4, 128], F32, "exr")
    P.op("act", lambda e: e.activation(out=exr[:, :, :], in_=lbr[:, :, :], func=AF.Exp), reads=[lbr], writes=[exr])
    denr = P.sb([128, 128], F32, "denr")
    P.op("dve", lambda e: e.tensor_tensor(out=denr[:, :], in0=exr[:, 0, :], in1=exr[:, 1, :], op=ALU.add),
         reads=[exr], writes=[denr])
    P.op("dve", lambda e: e.tensor_tensor(out=denr[:, :], in0=denr[:, :], in1=exr[:, 2, :], op=ALU.add),
         reads=[exr, denr], writes=[denr])
    P.op("dve", lambda e: e.tensor_tensor(out=denr[:, :], in0=denr[:, :], in1=exr[:, 3, :], op=ALU.add),
         reads=[exr, denr], writes=[denr])
    P.op("dve", lambda e: e.reciprocal(out=denr[:, :], in_=denr[:, :]), reads=[denr], writes=[denr])
    P.op("dve", lambda e: e.tensor_tensor(out=exr[:, :, :], in0=exr[:, :, :], in1=lmr[:, :, :], op=ALU.mult),
         reads=[exr, lmr], writes=[exr])
    lbrow = P.sb([128, 128], F32, "lbrow")
    P.op("dve", lambda e: e.tensor_tensor(out=lbrow[:, :], in0=exr[:, 0, :], in1=exr[:, 1, :], op=ALU.add),
         reads=[exr], writes=[lbrow])
    P.op("dve", lambda e: e.tensor_tensor(out=lbrow[:, :], in0=lbrow[:, :], in1=exr[:, 2, :], op=ALU.add),
         reads=[exr, lbrow], writes=[lbrow])
    P.op("dve", lambda e: e.tensor_tensor(out=lbrow[:, :], in0=lbrow[:, :], in1=exr[:, 3, :], op=ALU.add),
         reads=[exr, lbrow], writes=[lbrow])
    P.op("dve", lambda e: e.tensor_tensor(out=lbrow[:, :], in0=lbrow[:, :], in1=denr[:, :], op=ALU.mult),
         reads=[lbrow, denr], writes=[lbrow])
    omlrow = P.sb([128, 128], F32, "omlrow")
    P.op("dve", lambda e: e.tensor_scalar(out=omlrow[:, :], in0=lbrow[:, :], scalar1=-1.0, scalar2=1.0,
                                          op0=ALU.mult, op1=ALU.add), reads=[lbrow], writes=[omlrow])

    G = 4
    GT = G * 128
    ng = nt // G
    NB = 2
    qT = [P.sb([128, GT], F32, "qT") for _ in range(NB)]
    zfT = [P.sb([128, GT], F32, "zfT") for _ in range(NB)]
    zft = [P.sb([128, G, 128], F32, "zft") for _ in range(NB)]
    vt = [P.sb([128, G, 128], F32, "vt") for _ in range(NB)]
    gt = [P.sb([128, G, 128], F32, "gt") for _ in range(NB)]
    vb = [P.sb([128, G, 128], BF16, "vb") for _ in range(NB)]
    sigt = [P.sb([128, G, 128], F32, "sigt") for _ in range(NB)]
    logf = [P.sb([128, G, 128], F32, "logf") for _ in range(NB)]
    kt = [P.sb([128, G, 128], F32, "kt") for _ in range(NB)]
    sigf = [P.sb([128, GT], F32, "sigf") for _ in range(NB)]
    kT = [P.sb([128, GT], F32, "kT") for _ in range(NB)]
    bsb = [P.sb([128, GT], F32, "bsb") for _ in range(NB)]
    dd = [P.sb([128, GT], F32, "dd") for _ in range(NB)]
    eq = [P.sb([128, GT], F32, "eq") for _ in range(NB)]
    ek = [P.sb([128, GT], F32, "ek") for _ in range(NB)]
    eb = [P.sb([128, GT], F32, "eb") for _ in range(NB)]
    er = [P.sb([128, G, 128], F32, "er") for _ in range(NB)]
    qtl = [P.sb([128, GT], BF16, "qtl") for _ in range(NB)]
    ktl = [P.sb([128, GT], BF16, "ktl") for _ in range(NB)]
    QP = [P.sb([128, G, 2, 128], BF16, "QP") for _ in range(NB)]
    KP = [P.sb([128, G, 2, 128], BF16, "KP") for _ in range(NB)]
    for i in range(NB):
        for b in (QP[i], KP[i]):
            P.op("pool", lambda e, b=b: e.memset(b[:, :, :, :], 0.0), writes=[b])
    dec = [P.sb([128, 2 * G], F32, "dec") for _ in range(NB)]
    ATm = [P.sb([128, G, 128], BF16, "ATm") for _ in range(NB)]
    Sf = [P.sb([128, 128], F32, "Sf") for _ in range(2)]
    Sb = [P.sb([128, 128], BF16, "Sb") for _ in range(4)]
    P.op("pool", lambda e: e.memset(Sf[0][:, :], 0.0), writes=[Sf[0]])
    P.op("pool", lambda e: e.memset(Sb[0][:, :], 0.0), writes=[Sb[0]])
    sq = [P.sb([128, G, 128], F32, "sq") for _ in range(NB)]
    sg = [P.sb([128, G, 128], F32, "sg") for _ in range(NB)]
    yo = [P.sb([128, G, 128], F32, "yo") for _ in range(NB)]
    ss4 = [P.sb([128, G], F32, "ss4") for _ in range(NB)]
    rs4 = [P.sb([128, G], F32, "rs4") for _ in range(NB)]
    mh4 = P.sb([128, G], F32, "mh4")
    P.op("pool", lambda e: e.memset(mh4[:, :], -0.5), writes=[mh4])
    pbT = P.ps([128, GT], F32, "pbT")
    pR = P.ps([128, GT], F32, "pR")
    pAT = P.ps([128, GT], F32, "pAT")
    p_o = [P.ps([128, GT], F32, "p_o") for _ in range(2)]
    p_S = [P.ps([128, 512], F32, "p_S") for _ in range(2)]
    st_ = {"sidx": 0}
    bc = lambda t: t[:, :].unsqueeze(1).to_broadcast([128, G, 128])

    def h1(gi):
        n = gi % NB
        gs = slice(gi * GT, (gi + 1) * GT)
        P.dma("sp", qT[n][:, :], d["qT"][:, gs], dst=qT[n])
        P.dma("sp", zfT[n][:, :], d["zfT"][:, gs], dst=zfT[n])
        P.dma("sp", zft[n][:, :, :], d["zf"][gs, :].rearrange("(t p) k -> p t k", p=128), dst=zft[n])
        P.dma("sp", vt[n][:, :, :], d["v"][gs, :].rearrange("(t p) k -> p t k", p=128), dst=vt[n])
        P.dma("sp", gt[n][:, :, :], d["g"][gs, :].rearrange("(t p) k -> p t k", p=128), dst=gt[n])
        s_, lf, k_ = sigt[n], logf[n], kt[n]
        P.op("act", lambda e: e.activation(out=s_[:, :, :], in_=zft[n][:, :, :], func=AF.Sigmoid), reads=[zft[n]], writes=[s_])
        P.op("dve", lambda e: e.tensor_tensor(out=lf[:, :, :], in0=s_[:, :, :], in1=bc(omlrow), op=ALU.mult),
             reads=[s_, omlrow], writes=[lf])
        P.op("dve", lambda e: e.tensor_tensor(out=k_[:, :, :], in0=bc(omlrow), in1=lf[:, :, :], op=ALU.subtract),
             reads=[lf, omlrow], writes=[k_])
        P.op("dve", lambda e: e.tensor_tensor(out=lf[:, :, :], in0=lf[:, :, :], in1=bc(lbrow), op=ALU.add),
             reads=[lf, lbrow], writes=[lf])
        P.op("dve", lambda e: e.tensor_scalar(out=lf[:, :, :], in0=lf[:, :, :], scalar1=1e-30, scalar2=None, op0=ALU.max),
             reads=[lf], writes=[lf])
        P.op("act", lambda e: e.activation(out=lf[:, :, :], in_=lf[:, :, :], func=AF.Ln), reads=[lf], writes=[lf])
        for t in range(G):
            P.op("pe", lambda e, t=t: e.matmul(pbT[:, t * 128:(t + 1) * 128], lhsT=lf[:, t, :], rhs=M1[:, :], start=True, stop=True),
                 reads=[lf, M1], writes=[pbT])
        for t in range(G):
            P.op("pe", lambda e, t=t: e.matmul(pR[:, t * 128:(t + 1) * 128], lhsT=M2[:, :], rhs=lf[:, t, :], start=True, stop=True),
                 reads=[lf, M2], writes=[pR])
        sf, kT_ = sigf[n], kT[n]
        P.op("act", lambda e: e.activation(out=sf[:, :], in_=zfT[n][:, :], func=AF.Sigmoid), reads=[zfT[n]], writes=[sf])
        P.op("dve", lambda e: e.tensor_scalar(out=kT_[:, :], in0=sf[:, :], scalar1=nomlc[:, :], scalar2=omlc[:, :],
                                              op0=ALU.mult, op1=ALU.add), reads=[sf, nomlc, omlc], writes=[kT_])
        b_, d_, eq_, ek_, eb_, er_ = bsb[n], dd[n], eq[n], ek[n], eb[n], er[n]
        P.op("act", lambda e: e.activation(out=b_[:, :], in_=pbT[:, :], func=AF.Copy), reads=[pbT], writes=[b_])
        b3 = b_[:, :].rearrange("p (c s) -> p c s", s=64)
        P.op("dve", lambda e: e.tensor_tensor(out=d_[:, :].rearrange("p (c s) -> p c s", s=64), in0=b3,
                                              in1=b3[:, :, 31:32].to_broadcast([128, 2 * G, 64]), op=ALU.subtract),
             reads=[b_], writes=[d_])
        P.op("dve", lambda e: e.tensor_scalar(out=eq_[:, :], in0=d_[:, :], scalar1=43.0, scalar2=None, op0=ALU.min),
             reads=[d_], writes=[eq_])
        P.op("dve", lambda e: e.tensor_scalar(out=ek_[:, :], in0=d_[:, :], scalar1=-1.0, scalar2=43.0, op0=ALU.mult, op1=ALU.min),
             reads=[d_], writes=[ek_])
        P.op("act", lambda e: e.activation(out=eq_[:, :], in_=eq_[:, :], func=AF.Exp), reads=[eq_], writes=[eq_])
        P.op("act", lambda e: e.activation(out=ek_[:, :], in_=ek_[:, :], func=AF.Exp), reads=[ek_], writes=[ek_])
        P.op("act", lambda e: e.activation(out=eb_[:, :], in_=b_[:, :], func=AF.Exp), reads=[b_], writes=[eb_])
        P.op("act", lambda e: e.activation(out=er_[:, :, :], in_=pR[:, :].rearrange("p (t k) -> p t k", k=128), func=AF.Exp),
             reads=[pR], writes=[er_])
        dc = dec[n]
        P.op("dve", lambda e: e.tensor_copy(out=dc[:, :], in_=eb_[:, :].rearrange("p (c s) -> p c s", s=64)[:, :, 63]),
             reads=[eb_], writes=[dc])
        P.op("dve", lambda e: e.tensor_tensor(out=qtl[n][:, :], in0=qT[n][:, :], in1=eq_[:, :], op=ALU.mult),
             reads=[qT[n], eq_], writes=[qtl[n]])
        P.op("dve", lambda e: e.tensor_tensor(out=ktl[n][:, :], in0=kT_[:, :], in1=ek_[:, :], op=ALU.mult),
             reads=[kT_, ek_], writes=[ktl[n]])
        q4 = qT[n][:, :].rearrange("p (t c s) -> p t c s", c=2, s=64)
        e4 = eb_[:, :].rearrange("p (t c s) -> p t c s", c=2, s=64)
        P.op("dve", lambda e: e.tensor_tensor(out=QP[n][:, :, 0, 0:64], in0=q4[:, :, 0, :], in1=e4[:, :, 0, :], op=ALU.mult),
             reads=[qT[n], eb_], writes=[QP[n]])
        P.op("dve", lambda e: e.tensor_tensor(out=QP[n][:, :, 1, 64:128], in0=q4[:, :, 1, :], in1=e4[:, :, 1, :], op=ALU.mult),
             reads=[qT[n], eb_], writes=[QP[n]])
        P.op("dve", lambda e: e.tensor_tensor(out=KP[n][0:64, :, 0, :], in0=k_[0:64, :, :], in1=er_[0:64, :, :], op=ALU.mult),
             reads=[k_, er_], writes=[KP[n]])
        P.op("dve", lambda e: e.tensor_tensor(out=KP[n][64:128, :, 1, :], in0=k_[64:128, :, :], in1=er_[64:128, :, :], op=ALU.mult),
             reads=[k_, er_], writes=[KP[n]])
        P.op("pool", lambda e: e.tensor_copy(out=vb[n][:, :, :], in_=vt[n][:, :, :]), reads=[vt[n]], writes=[vb[n]])
        for t in range(G):
            P.op("pe", lambda e, t=t: e.matmul(pAT[:, t * 128:(t + 1) * 128], lhsT=ktl[n][:, t * 128:(t + 1) * 128],
                                               rhs=qtl[n][:, t * 128:(t + 1) * 128], start=True, stop=True),
                 reads=[ktl[n], qtl[n]], writes=[pAT])
        P.op("dve", lambda e: e.tensor_tensor(out=ATm[n][:, :, :], in0=pAT[:, :].rearrange("p (t k) -> p t k", k=128),
                                              in1=bc(M1), op=ALU.mult), reads=[pAT, M1], writes=[ATm[n]])
        P.op("act", lambda e: e.activation(out=sg[n][:, :, :], in_=gt[n][:, :, :], func=AF.Silu), reads=[gt[n]], writes=[sg[n]])
        P.op("pool", lambda e: e.tensor_tensor(out=sg[n][:, :, :], in0=sg[n][:, :, :], in1=bc(gn), op=ALU.mult),
             reads=[sg[n], gn], writes=[sg[n]])

    def h2(gi):
        n = gi % NB
        gs = slice(gi * GT, (gi + 1) * GT)
        dc = dec[n]
        po = p_o[gi % 2]
        sidx = st_["sidx"]
        for t in range(G):
            osl = slice(t * 128, (t + 1) * 128)
            s0 = sidx
            P.op("pe", lambda e, t=t, osl=osl: e.matmul(po[:, osl], lhsT=ATm[n][:, t, :], rhs=vb[n][:, t, :], start=True, stop=False),
                 reads=[ATm[n], vb[n]], writes=[po])
            P.op("pe", lambda e, t=t, osl=osl, s0=s0: e.matmul(po[:, osl], lhsT=QP[n][:, t, 0, :], rhs=Sb[s0 % 4][:, :],
                                                               start=False, stop=False), reads=[QP[n], Sb[s0 % 4]], writes=[po])
            for c in range(2):
                pS = p_S[c]
                P.op("pe", lambda e, t=t, c=c, pS=pS: e.matmul(pS[:, 0:128], lhsT=KP[n][:, t, c, :], rhs=vb[n][:, t, :],
                                                              start=True, stop=True), reads=[KP[n], vb[n]], writes=[pS])
                so, sn = Sf[sidx % 2], Sf[(sidx + 1) % 2]
                P.op("dve", lambda e, so=so, sn=sn, pS=pS, t=t, c=c: e.scalar_tensor_tensor(
                    out=sn[:, :], in0=so[:, :], scalar=dc[:, 2 * t + c:2 * t + c + 1], in1=pS[:, 0:128], op0=ALU.mult, op1=ALU.add),
                    reads=[so, dc, pS], writes=[sn])
                sbn = Sb[(sidx + 1) % 4]
                P.op("act", lambda e, sbn=sbn, sn=sn: e.activation(out=sbn[:, :], in_=sn[:, :], func=AF.Copy),
                     reads=[sn], writes=[sbn])
                sidx += 1
            P.op("pe", lambda e, t=t, osl=osl, s0=s0: e.matmul(po[:, osl], lhsT=QP[n][:, t, 1, :], rhs=Sb[(s0 + 1) % 4][:, :],
                                                               start=False, stop=True), reads=[QP[n], Sb[(s0 + 1) % 4]], writes=[po])
        st_["sidx"] = sidx
        po3 = po[:, :].rearrange("p (t k) -> p t k", k=128)
        P.op("act", lambda e: e.activation(out=sq[n][:, :, :], in_=po3, func=AF.Square), reads=[po], writes=[sq[n]])
        P.op("dve", lambda e: e.reduce_sum(out=ss4[n][:, :], in_=sq[n][:, :, :], axis=AX.X), reads=[sq[n]], writes=[ss4[n]])
        P.op("dve", lambda e: e.tensor_scalar(out=ss4[n][:, :], in0=ss4[n][:, :], scalar1=1.0 / 128, scalar2=EPS,
                                              op0=ALU.mult, op1=ALU.add), reads=[ss4[n]], writes=[ss4[n]])
        P.op("pool", lambda e: e.tensor_tensor(out=rs4[n][:, :], in0=ss4[n][:, :], in1=mh4[:, :], op=ALU.pow),
             reads=[ss4[n], mh4], writes=[rs4[n]])
        y = yo[n]
        P.op("dve", lambda e: e.tensor_tensor(out=y[:, :, :], in0=po3, in1=rs4[n][:, :].unsqueeze(2).to_broadcast([128, G, 128]),
                                              op=ALU.mult), reads=[po, rs4[n]], writes=[y])
        P.op("dve", lambda e: e.tensor_tensor(out=y[:, :, :], in0=y[:, :, :], in1=sg[n][:, :, :], op=ALU.mult),
             reads=[y, sg[n]], writes=[y])
        P.dma("sp", d["y"][gs, :].rearrange("(t p) k -> p t k", p=128), y[:, :, :], src=y)

    h1(0)
    for gi in range(ng):
        if gi + 1 < ng:
            h1(gi + 1)
        h2(gi)


def build_test_hgrn(S):
    P = Prog()
    d = {}
    for nm, shp in (("qT", [128, S]), ("zfT", [128, S]), ("zf", [S, 128]), ("v", [S, 128]), ("g", [S, 128]),
                    ("lbT", [128, 4]), ("lmask", [128, 4]), ("lbrow", [128, 4, 128]), ("lmaskrow", [128, 4, 128]),
                    ("gnorm", [128, 128]), ("M1", [128, 128]), ("M2", [128, 128])):
        d[nm] = P.dram(nm, shp, F32, "ExternalInput")
    d["y"] = P.dram("y", [S, 128], F32, "ExternalOutput")
    hgrn_phase(P, S, d)
    print("streams", P.stats(), "sems", P.n_sems)
    return P.finish()


import numpy as np

BIGF = 1.0e6


def nsa_consts_np(S):
    nsel = S // 64
    ncmp = S // 16 - 1
    ncp = ((ncmp + 127) // 128) * 128
    nch = S // 128
    p = np.arange(128)
    D1 = (p[:, None] - p[None, :]).astype(np.float32)
    D16 = (16 * p[:, None] - p[None, :]).astype(np.float32)
    Mdiag = (D1 <= 0).astype(np.float32)
    Mlow = (D1 > 0).astype(np.float32)
    j = np.arange(nsel)
    E = np.zeros((nsel, nch, 128), np.float32)
    for c in range(nch):
        E[:, c, :] = (j[:, None] == (2 * c + p[None, :] // 64))
    n = np.arange(ncp)
    cs = n * 16
    ss = j * 64
    ov = ((cs[:, None] < ss[None, :] + 64) & (cs[:, None] + 32 > ss[None, :]) & (n[:, None] < ncmp)).astype(np.float32)
    x = np.arange(2 * nsel) - nsel
    cur_rel = (p >= 64).astype(np.int64)
    forced = (x[None, :] == cur_rel[:, None]) | (x[None, :] == cur_rel[:, None] - 1)
    invalid = x[None, :] > cur_rel[:, None]
    Amul = (~forced & ~invalid).astype(np.float32)
    Aadd = np.where(forced, BIGF, np.where(invalid, -BIGF, 0.0)).astype(np.float32)
    ident = np.eye(128, dtype=np.float32)
    return dict(D16=D16, Mdiag=Mdiag, Mlow=Mlow, E=E, ov=ov, Amul=Amul, Aadd=Aadd, identb=ident)


def nsa_phase(P, S, NQB, d):
    nsel = S // 64
    ncmp = S // 16 - 1
    ncp = ((ncmp + 127) // 128) * 128
    ncc = ncp // 128
    nch = S // 128
    W = 65 + nsel

    def cload(name, shape, dt, src, q="pool"):
        b = P.sb(shape, dt, name)
        P.dma(q, b[tuple(slice(None) for _ in shape)], src, dst=b)
        return b

    D16 = cload("D16", [128, 128], F32, d["D16"][:, :], "sp")
    Mdiag = cload("Mdiag", [128, 128], BF16, d["Mdiag"][:, :])
    Mlow = cload("Mlow", [128, 128], BF16, d["Mlow"][:, :])
    Ec = cload("E", [nsel, nch, 128], BF16, d["E"][:, :, :])
    Amul = cload("Amul", [128, 2 * nsel], F32, d["Amul"][:, :], "sp")
    Aadd = cload("Aadd", [128, 2 * nsel], F32, d["Aadd"][:, :], "sp")
    identb = cload("identb", [128, 128], BF16, d["identb"][:, :])
    ksT = P.sb([128, S], BF16, "ksT")
    kwT = P.sb([128, S], BF16, "kwT")
    for kb, nm in ((ksT, "ksT"), (kwT, "kwT")):
        P.op("pool", lambda e, kb=kb: e.memset(kb[64:128, :], 0.0), writes=[kb])
        P.dma("pool", kb[0:64, :], d[nm][:, :], dst=kb)
    vs = cload("vs", [128, nch, 65], BF16, d["vs"][:, :, :])
    vw = cload("vw", [128, nch, 65], BF16, d["vw"][:, :, :])
    KcT = P.sb([128, ncp], BF16, "KcT")
    Vc = P.sb([128, ncc, W], BF16, "Vc")
    P.op("pool", lambda e: e.memset(KcT[:, :], 0.0), writes=[KcT])
    P.op("pool", lambda e: e.memset(Vc[:, :, :], 0.0), writes=[Vc])
    P.op("pool", lambda e: e.memset(Vc[:, :, 64:65], 1.0), writes=[Vc])
    P.dma("pool", Vc[:, :, 65:W], d["ov"][:, :, :], dst=Vc)

    ps_h = [P.ps([128, 512], F32, "ps_h") for _ in range(2)]
    ps_b = P.ps([128, 512], F32, "ps_b")
    pOs = P.ps([128, 512], F32, "pOs")
    ps_o = pOs
    for X in range(2):
        x2T = cload("x2T", [128, S], BF16, d["kc2T" if X == 0 else "vc2T"][:, :])
        pe2 = cload("pe2", [128, 16], BF16, d["pe2"][X])
        w1 = cload("w1", [128, 16, 256], BF16, d["w1"][X])
        w2 = cload("w2", [128, 2, 64], BF16, d["w2"][X])
        GT = P.sb([128, 2, ncp], BF16, "GT")
        P.op("pool", lambda e, GT=GT: e.memset(GT[:, :, :], 0.0), writes=[GT])
        x2v = x2T[:, :].rearrange("p (n s) -> p n s", s=16)
        for hc in range(2):
            for j in range(16):
                P.op("pe", lambda e, j=j, hc=hc, w1=w1, pe2=pe2: e.matmul(
                    ps_b[:, 0:1], lhsT=w1[:, j, hc * 128:(hc + 1) * 128], rhs=pe2[:, j:j + 1],
                    start=(j == 0), stop=(j == 15)), reads=[w1, pe2], writes=[ps_b])
            c1 = P.sb([128, 1], F32, "c1")
            P.op("dve", lambda e, c1=c1: e.tensor_copy(out=c1[:, :], in_=ps_b[:, 0:1]), reads=[ps_b], writes=[c1])
            ph = ps_h[hc]
            for j in range(16):
                if j < 8:
                    rhs = x2v[:, 0:ncmp, 2 * j]
                else:
                    rhs = x2v[:, 1:ncmp + 1, 2 * j - 16]
                P.op("pe", lambda e, j=j, hc=hc, w1=w1, rhs=rhs, ph=ph: e.matmul(
                    ph[:, 0:ncmp], lhsT=w1[:, j, hc * 128:(hc + 1) * 128], rhs=rhs,
                    start=(j == 0), stop=(j == 15)), reads=[w1, x2T], writes=[ph])
            xs = P.sb([128, ncp], F32, "xs")
            P.op("act", lambda e, xs=xs, ph=ph, c1=c1: e.activation(out=xs[:, 0:ncmp], in_=ph[:, 0:ncmp],
                                                                     func=AF.Identity, bias=c1[:, :]),
                 reads=[ph, c1], writes=[xs])
            t2 = P.sb([128, ncp], F32, "t2")
            P.op("dve", lambda e, xs=xs, t2=t2: e.tensor_tensor(out=t2[:, 0:ncmp], in0=xs[:, 0:ncmp], in1=xs[:, 0:ncmp],
                                                                op=ALU.mult), reads=[xs], writes=[t2])
            P.op("dve", lambda e, t2=t2: e.tensor_scalar(out=t2[:, 0:ncmp], in0=t2[:, 0:ncmp], scalar1=0.044715,
                                                         scalar2=1.0, op0=ALU.mult, op1=ALU.add), reads=[t2], writes=[t2])
            P.op("dve", lambda e, xs=xs, t2=t2: e.tensor_tensor(out=t2[:, 0:ncmp], in0=t2[:, 0:ncmp], in1=xs[:, 0:ncmp],
                                                                op=ALU.mult), reads=[xs, t2], writes=[t2])
            P.op("act", lambda e, t2=t2: e.activation(out=t2[:, 0:ncmp], in_=t2[:, 0:ncmp], func=AF.Sigmoid,
                                                      scale=1.5957691216057308), reads=[t2], writes=[t2])
            P.op("dve", lambda e, xs=xs, t2=t2, GT=GT, hc=hc: e.tensor_tensor(
                out=GT[:, hc, 0:ncmp], in0=t2[:, 0:ncmp], in1=xs[:, 0:ncmp], op=ALU.mult), reads=[xs, t2], writes=[GT])
        if X == 0:
            for hc in range(2):
                P.op("pe", lambda e, hc=hc, w2=w2, GT=GT: e.matmul(ps_o[0:64, 0:ncp], lhsT=w2[:, hc, :], rhs=GT[:, hc, :],
                                                                    start=(hc == 0), stop=(hc == 1)),
                     reads=[w2, GT], writes=[ps_o])
            P.op("act", lambda e: e.activation(out=KcT[0:64, 0:ncmp], in_=ps_o[0:64, 0:ncmp], func=AF.Copy),
                 reads=[ps_o], writes=[KcT])
        else:
            for c in range(ncc):
                for hc in range(2):
                    P.op("pe", lambda e, hc=hc, c=c, w2=w2, GT=GT: e.matmul(
                        ps_o[:, 0:64], lhsT=GT[:, hc, c * 128:(c + 1) * 128], rhs=w2[:, hc, :],
                        start=(hc == 0), stop=(hc == 1)), reads=[w2, GT], writes=[ps_o])
                P.op("act", lambda e, c=c: e.activation(out=Vc[:, c, 0:64], in_=ps_o[:, 0:64], func=AF.Copy),
                     reads=[ps_o], writes=[Vc])

    pS = [ps_h[0], ps_h[1], P.ps([128, 512], F32, "pS2")]
    pM = ps_b
    pOc = [P.ps([128, 512], F32, "pOc") for _ in range(2)]
    pOw = P.ps([128, 512], F32, "pOw")
    WMk = cload("WM", [128, 8, 128], BF16, d["WM"][:, :, :])
    Tt = cload("Tt", [128, NQB, ncc], F32, d["Tt"][:, :, :], "sp")
    identf = cload("identf", [128, 128], F32, d["identb"][:, :], "sp")
    QTb = [P.sb([128, 512], BF16, "QT") for _ in range(2)]
    for qb_ in QTb:
        P.op("pool", lambda e, qb_=qb_: e.memset(qb_[64:128, :], 0.0), writes=[qb_])
    gl_all = P.sb([128, NQB, 12], F32, "gl_all")
    P.dma("sp", gl_all[:, :, :], d["gl"][:, :, :], dst=gl_all)
    P.op("act", lambda e: e.activation(out=gl_all[:, :, :], in_=gl_all[:, :, :], func=AF.Sigmoid), reads=[gl_all], writes=[gl_all])
    NPT = 6
    PT = [P.sb([128, 4, 128], BF16, "PT") for _ in range(NPT)]
    PTm = [P.sb([128, 4, 128], BF16, "PTm") for _ in range(NPT)]
    mk = [P.sb([128, 128], BF16, "mk") for _ in range(3)]
    rd = [P.sb([128, 4], F32, "rd") for _ in range(3)]
    wgt = [P.sb([128, 4], F32, "wgt") for _ in range(3)]
    imp = P.sb([128, nsel], F32, "imp")
    sc = P.sb([128, nsel], F32, "sc")
    sc2 = P.sb([128, nsel], F32, "sc2")
    m8a = P.sb([128, 8], F32, "m8a")
    m8b = P.sb([128, 8], F32, "m8b")
    self_f = P.sb([128, nsel], F32, "self")
    selT = [P.sb([nsel, 128], BF16, "selT") for _ in range(2)]
    yacc = [P.sb([128, 4, 64], F32, "yacc") for _ in range(2)]
    state = {"pt": 0, "ps": 0, "mk": 0, "pm": 0}

    class Task:
        pass

    def chunk_task(kT, c, QT, mask_kind, mask_arg, vaug, vw_cols, pouts, first, last, selTb=None):
        t = Task()
        ps = pS[state["ps"] % 3]
        state["ps"] += 1
        pt = PT[state["pt"] % NPT]
        ptm = PTm[state["pt"] % NPT]
        state["pt"] += 1

        def s1():
            P.op("pe", lambda e: e.matmul(ps[:, :], lhsT=kT[:, c * 128:(c + 1) * 128], rhs=QT[:, :], start=True, stop=True),
                 reads=[kT, QT], writes=[ps])

        def s2():
            P.op("act", lambda e: e.activation(out=pt[:, :, :], in_=ps[:, :].rearrange("p (h q) -> p h q", h=4),
                                               func=AF.Exp, scale=0.125), reads=[ps], writes=[pt])
            src = pt
            mbuf = None
            if mask_kind == "none":
                pass
            elif mask_kind == "sb":
                mbuf, map_ = mask_arg
            elif mask_kind == "cmp":
                m_i, cc = mask_arg
                mbuf = mk[state["mk"] % 3]
                state["mk"] += 1
                P.op("dve", lambda e, mb=mbuf: e.tensor_scalar(out=mb[:, :], in0=D16[:, :], scalar1=Tt[:, m_i, cc:cc + 1],
                                                               scalar2=None, op0=ALU.is_le), reads=[D16, Tt], writes=[mbuf])
                map_ = mbuf[:, :]
            elif mask_kind == "slc":
                tail = mask_arg
                off = 256 * (state["pm"] % 2)
                state["pm"] += 1
                P.op("pe", lambda e: e.matmul(pM[0:128, off:off + 128], lhsT=Ec[:, c, :], rhs=selTb[:, :], start=True, stop=True),
                     reads=[Ec, selTb], writes=[pM])
                if tail is None:
                    mbuf, map_ = pM, pM[:, off:off + 128]
                else:
                    mbuf = mk[state["mk"] % 3]
                    state["mk"] += 1
                    P.op("dve", lambda e, mb=mbuf: e.tensor_tensor(out=mb[:, :], in0=pM[:, off:off + 128], in1=WMk[:, tail, :],
                                                                   op=ALU.mult), reads=[pM, WMk], writes=[mbuf])
                    map_ = mbuf[:, :]
            if mbuf is not None:
                P.op("dve", lambda e: e.tensor_tensor(out=ptm[:, :, :], in0=pt[:, :, :],
                                                      in1=map_.unsqueeze(1).to_broadcast([128, 4, 128]), op=ALU.mult),
                     reads=[pt, mbuf], writes=[ptm])
                src = ptm
            t.src = src

        def s3():
            src = t.src
            for pb, heads in pouts:
                for hi, h in enumerate(heads):
                    st = first and hi == 0
                    P.op("pe", lambda e, pb=pb, hi=hi, h=h, st=st: e.matmul(
                        pb[:, hi * vw_cols:(hi + 1) * vw_cols], lhsT=src[:, h, :], rhs=vaug[:, c, 0:vw_cols],
                        start=st, stop=last, skip_group_check=True), reads=[src, vaug], writes=[pb])
        t.s1, t.s2, t.s3 = s1, s2, s3
        return t

    def marker(fn):
        t = Task()
        t.s1 = lambda: None
        t.s2 = lambda: None
        t.s3 = fn
        return t

    tasks = []
    for m_i in range(NQB):
        qbm = 2 * m_i + 1
        QT = QTb[m_i % 2]
        gl = gl_all
        glv = gl_all[:, m_i, :].rearrange("p (h t) -> p h t", t=3)
        ya = yacc[m_i % 2]
        selTb = selT[m_i % 2]

        def load(m_i=m_i, QT=QT, gl=gl):
            P.dma("pool", QT[0:64, :], d["QT"][:, m_i, :], dst=QT)
        ld = marker(lambda: None)
        ld.s1 = load
        tasks.append(ld)
        nvalid = min(8 * qbm + 7, ncmp)
        nchunks = (nvalid + 127) // 128
        for c in range(nchunks):
            Tmin = 128 * (qbm - 1) - 2048 * c - 31
            kind, arg = ("none", None) if Tmin >= 16 * 127 else ("cmp", (m_i, c))
            tasks.append(chunk_task(KcT, c, QT, kind, arg, Vc, W, [(pOc[0], [0, 1]), (pOc[1], [2, 3])], c == 0, c == nchunks - 1))

        def after_cmp(m_i=m_i, glv=glv, gl=gl, ya=ya, selTb=selTb):
            r_, w_ = rd[0], wgt[0]
            for g2 in range(2):
                ov_ = pOc[g2][:, 0:2 * W].rearrange("p (h w) -> p h w", h=2)
                P.op("dve", lambda e, ov_=ov_, g2=g2: e.tensor_scalar(out=r_[:, 2 * g2:2 * g2 + 2], in0=ov_[:, :, 64],
                                                                      scalar1=1e-30, scalar2=None, op0=ALU.max),
                     reads=[pOc[g2]], writes=[r_])
            P.op("dve", lambda e: e.reciprocal(out=r_[:, :], in_=r_[:, :]), reads=[r_], writes=[r_])
            for h in range(4):
                pb = pOc[h // 2]
                base = (h % 2) * W
                if h == 0:
                    P.op("dve", lambda e, pb=pb, base=base, h=h: e.tensor_scalar(
                        out=imp[:, :], in0=pb[:, base + 65:base + W], scalar1=r_[:, h:h + 1], scalar2=None, op0=ALU.mult),
                        reads=[pb, r_], writes=[imp])
                else:
                    P.op("dve", lambda e, pb=pb, base=base, h=h: e.scalar_tensor_tensor(
                        out=imp[:, :], in0=pb[:, base + 65:base + W], scalar=r_[:, h:h + 1], in1=imp[:, :],
                        op0=ALU.mult, op1=ALU.add), reads=[pb, r_, imp], writes=[imp])
            x0 = nsel - 4 * m_i
            P.op("dve", lambda e: e.tensor_tensor(out=sc[:, :], in0=imp[:, :], in1=Amul[:, x0:x0 + nsel], op=ALU.mult),
                 reads=[imp, Amul], writes=[sc])
            P.op("dve", lambda e: e.tensor_tensor(out=sc[:, :], in0=sc[:, :], in1=Aadd[:, x0:x0 + nsel], op=ALU.add),
                 reads=[sc, Aadd], writes=[sc])
            P.op("dve", lambda e: e.memset(sc[:, 0:1], BIGF), writes=[sc])
            P.op("dve", lambda e: e.max(out=m8a[:, :], in_=sc[:, :]), reads=[sc], writes=[m8a])
            P.op("dve", lambda e: e.match_replace(out=sc2[:, :], in_to_replace=m8a[:, :], in_values=sc[:, :],
                                                  imm_value=-3.0 * BIGF), reads=[sc, m8a], writes=[sc2])
            P.op("dve", lambda e: e.max(out=m8b[:, :], in_=sc2[:, :]), reads=[sc2], writes=[m8b])
            P.op("dve", lambda e: e.tensor_scalar(out=self_f[:, :], in0=sc[:, :], scalar1=m8b[:, 7:8], scalar2=None,
                                                  op0=ALU.is_ge), reads=[sc, m8b], writes=[self_f])
            P.op("pe", lambda e: e.transpose(out=pM[0:nsel, 128:256], in_=self_f[:, :], identity=identf[:, :]),
                 reads=[self_f, identf], writes=[pM])
            P.op("act", lambda e: e.activation(out=selTb[:, :], in_=pM[0:nsel, 128:256], func=AF.Copy), reads=[pM], writes=[selTb])
            P.op("dve", lambda e: e.tensor_tensor(out=w_[:, :], in0=r_[:, :], in1=glv[:, :, 0], op=ALU.mult),
                 reads=[r_, gl], writes=[w_])
            for h in range(4):
                pb = pOc[h // 2]
                base = (h % 2) * W
                P.op("dve", lambda e, pb=pb, base=base, h=h: e.tensor_scalar(
                    out=ya[:, h, :], in0=pb[:, base:base + 64], scalar1=w_[:, h:h + 1], scalar2=None, op0=ALU.mult),
                    reads=[pb, w_], writes=[ya])
        tasks.append(marker(after_cmp))
        wl = [c for c in range(2 * m_i - 4, 2 * m_i + 2) if c >= 0]
        for c in wl:
            r = c - (2 * m_i - 4)
            tasks.append(chunk_task(kwT, c, QT, "sb", (WMk, WMk[:, r, :]), vw, 65, [(pOw, [0, 1, 2, 3])], c == wl[0], c == wl[-1]))
        for c in range(0, 2 * m_i + 2):
            tail = (6 + c - 2 * m_i) if c >= 2 * m_i else None
            tasks.append(chunk_task(ksT, c, QT, "slc", tail, vs, 65, [(pOs, [0, 1, 2, 3])], c == 0, c == 2 * m_i + 1, selTb=selTb))

        def combine(m_i=m_i, glv=glv, gl=gl, ya=ya):
            for bi, pb in ((2, pOw), (1, pOs)):
                r_, w_ = rd[bi], wgt[bi]
                pv = pb[:, 0:260].rearrange("p (h w) -> p h w", h=4)
                P.op("dve", lambda e, pv=pv, r_=r_: e.tensor_scalar(out=r_[:, :], in0=pv[:, :, 64], scalar1=1e-30, scalar2=None,
                                                                    op0=ALU.max), reads=[pb], writes=[r_])
                P.op("dve", lambda e, r_=r_: e.reciprocal(out=r_[:, :], in_=r_[:, :]), reads=[r_], writes=[r_])
                P.op("dve", lambda e, r_=r_, w_=w_, bi=bi: e.tensor_tensor(out=w_[:, :], in0=r_[:, :], in1=glv[:, :, bi], op=ALU.mult),
                     reads=[r_, gl], writes=[w_])
                for h in range(4):
                    P.op("dve", lambda e, pb=pb, h=h, w_=w_: e.scalar_tensor_tensor(
                        out=ya[:, h, :], in0=pb[:, h * 65:h * 65 + 64], scalar=w_[:, h:h + 1], in1=ya[:, h, :],
                        op0=ALU.mult, op1=ALU.add), reads=[pb, w_, ya], writes=[ya])
            P.dma("sp", d["y"][m_i], ya[:, :, :].rearrange("p h w -> p (h w)"), src=ya)
        tasks.append(marker(combine))

    DEPTH = 3
    n = len(tasks)
    for i in range(min(DEPTH, n)):
        tasks[i].s1()
    for i in range(n):
        tasks[i].s2()
        if i + DEPTH < n:
            tasks[i + DEPTH].s1()
        tasks[i].s3()


NSA_IN = lambda S, NQB: [
    ("QT", [64, NQB, 512]), ("gl", [128, NQB, 12]), ("ksT", [64, S]), ("kwT", [64, S]), ("vs", [128, S // 128, 65]), ("vw", [128, S // 128, 65]),
    ("kc2T", [128, S]), ("vc2T", [128, S]), ("pe2", [2, 128, 16]), ("w1", [2, 128, 16, 256]), ("w2", [2, 128, 2, 64]),
    ("D16", [128, 128]), ("Mdiag", [128, 128]), ("Mlow", [128, 128]), ("E", [S // 64, S // 128, 128]),
    ("ov", [128, ((S // 16 - 1 + 127) // 128), S // 64]), ("Amul", [128, 2 * (S // 64)]), ("Aadd", [128, 2 * (S // 64)]),
    ("identb", [128, 128]), ("WM", [128, 8, 128]), ("Tt", [128, NQB, ((S // 16 - 1 + 127) // 128)])]


def build_test_nsa(S, NQB):
    P = Prog()
    d = {}
    for nm, shp in NSA_IN(S, NQB):
        d[nm] = P.dram(nm, shp, F32, "ExternalInput")
    d["y"] = P.dram("y", [NQB, 128, 256], F32, "ExternalOutput")
    nsa_phase(P, S, NQB, d)
    print("streams", P.stats(), "sems", P.n_sems)
    return P.finish()


import math
import numpy as np

NT = 2048
NTILE = NT // 128

A_OPS = ([("copy", 1)] * 4 + [("prod", 2)] * 4 + [("silu", 1)] * 4 + [("copy", 1)] * 12 +
         [("rope", 2)] * 4 + [("rope", 2)] * 3 + [("copy", 1)] * 3 + [("gate", 1)])
A_NOUT = len(A_OPS)
A_NIN = sum(n for _, n in A_OPS)
A_COLS = (A_NIN - 1) * 128 + 24
A_ROWS = (A_NOUT - 1) * 128 + 24


def a_col_perm():
    BR = 512
    o = {}
    names = ["ab", "ac", "ax", "hq", "hf", "hi", "hg", "nq", "nkc", "nvc", "nks", "nvs", "nkw", "nvw", "ng", "mg"]
    sizes = [512] * 3 + [512] * 4 + [512] + [128] * 6 + [24, 3072]
    off = 0
    for n, s in zip(names, sizes):
        o[n] = np.arange(off, off + s)
        off += s

    def swap64(ix):
        ix = ix.reshape(-1, 2, 32)
        return ix[:, ::-1, :].reshape(-1)
    cols = []
    for c in range(4):
        cols.append(o["ab"][c * 128:(c + 1) * 128])
    for c in range(4):
        cols.append(o["ac"][c * 128:(c + 1) * 128])
        cols.append(o["ax"][c * 128:(c + 1) * 128])
    for n in ("hq", "hf", "hi", "hg"):
        for c in range(4):
            cols.append(o[n][c * 128:(c + 1) * 128])
    for c in range(4):
        ix = o["nq"][c * 128:(c + 1) * 128]
        cols.append(ix)
        cols.append(swap64(ix))
    for n in ("nkc", "nks", "nkw"):
        cols.append(o[n])
        cols.append(swap64(o[n]))
    for n in ("nvc", "nvs", "nvw"):
        cols.append(o[n])
    cols.append(o["ng"])
    return np.concatenate(cols), o["mg"]


def rope_tables(P, pos_d, inv_d, sgn_d, n):
    posi = P.sb([128, n], I32, "posi")
    P.dma("sp", posi[:, :], pos_d[:, :], dst=posi)
    inv = P.sb([128, 1], F32, "inv")
    sgn = P.sb([128, 1], F32, "sgn")
    P.dma("sp", inv[:, :], inv_d[:, :], dst=inv)
    P.dma("sp", sgn[:, :], sgn_d[:, :], dst=sgn)
    ang = P.sb([128, n], F32, "ang")
    P.op("dve", lambda e: e.tensor_copy(out=ang[:, :], in_=posi[:, :]), reads=[posi], writes=[ang])
    P.op("dve", lambda e: e.tensor_scalar(out=ang[:, :], in0=ang[:, :], scalar1=inv[:, :], scalar2=None, op0=ALU.mult),
         reads=[ang, inv], writes=[ang])
    qf = P.sb([128, n], F32, "qf")
    P.op("dve", lambda e: e.tensor_scalar(out=qf[:, :], in0=ang[:, :], scalar1=1.0 / (2 * math.pi), scalar2=None,
                                          op0=ALU.mult), reads=[ang], writes=[qf])
    qi = P.sb([128, n], I32, "qi")
    P.op("dve", lambda e: e.tensor_copy(out=qi[:, :], in_=qf[:, :]), reads=[qf], writes=[qi])
    P.op("dve", lambda e: e.tensor_copy(out=qf[:, :], in_=qi[:, :]), reads=[qi], writes=[qf])
    C1, C2 = 6.28125, 2 * math.pi - 6.28125
    C2a = float(np.float32(C2))
    C3 = C2 - C2a
    for Cc in (C1, C2a, C3):
        P.op("dve", lambda e, Cc=Cc: e.scalar_tensor_tensor(out=ang[:, :], in0=qf[:, :], scalar=-Cc, in1=ang[:, :],
                                                            op0=ALU.mult, op1=ALU.add), reads=[qf, ang], writes=[ang])
    m = qf
    P.op("dve", lambda e: e.tensor_scalar(out=m[:, :], in0=ang[:, :], scalar1=math.pi, scalar2=-2 * math.pi,
                                          op0=ALU.is_gt, op1=ALU.mult), reads=[ang], writes=[m])
    P.op("dve", lambda e: e.tensor_tensor(out=ang[:, :], in0=ang[:, :], in1=m[:, :], op=ALU.add), reads=[ang, m], writes=[ang])
    P.op("dve", lambda e: e.tensor_scalar(out=m[:, :], in0=ang[:, :], scalar1=-math.pi, scalar2=2 * math.pi,
                                          op0=ALU.is_lt, op1=ALU.mult), reads=[ang], writes=[m])
    P.op("dve", lambda e: e.tensor_tensor(out=ang[:, :], in0=ang[:, :], in1=m[:, :], op=ALU.add), reads=[ang, m], writes=[ang])
    P.op("dve", lambda e: e.tensor_scalar(out=ang[:, :], in0=ang[:, :], scalar1=3.14159, scalar2=-3.14159,
                                          op0=ALU.min, op1=ALU.max), reads=[ang], writes=[ang])
    SINS = P.sb([128, n], F32, "SINS")
    COS = P.sb([128, n], F32, "COS")
    P.op("act", lambda e: e.activation(out=SINS[:, :], in_=ang[:, :], func=AF.Sin), reads=[ang], writes=[SINS])
    P.op("dve", lambda e: e.tensor_scalar(out=SINS[:, :], in0=SINS[:, :], scalar1=sgn[:, :], scalar2=None, op0=ALU.mult),
         reads=[SINS, sgn], writes=[SINS])
    P.op("dve", lambda e: e.tensor_scalar(out=m[:, :], in0=ang[:, :], scalar1=-1.0, scalar2=None, op0=ALU.mult),
         reads=[ang], writes=[m])
    P.op("dve", lambda e: e.tensor_tensor(out=ang[:, :], in0=ang[:, :], in1=m[:, :], op=ALU.max), reads=[ang, m], writes=[ang])
    P.op("dve", lambda e: e.tensor_scalar(out=ang[:, :], in0=ang[:, :], scalar1=-1.0, scalar2=math.pi / 2, op0=ALU.mult,
                                          op1=ALU.add), reads=[ang], writes=[ang])
    P.op("act", lambda e: e.activation(out=COS[:, :], in_=ang[:, :], func=AF.Sin), reads=[ang], writes=[COS])
    return COS, SINS


def make_h2T(P, C, x_src_d, g2, h2T, tp, ntile):
    xt = [P.sb([128, D], F32, "x1t") for _ in range(2)]
    hb = [P.sb([128, D], BF16, "h2") for _ in range(2)]
    for i in range(ntile):
        xb = xt[i % 2]
        P.dma("sp", xb[:, :], x_src_d[i * 128:(i + 1) * 128, :], dst=xb)
        rstd = rms_stats(P, C, xb, xb[:, :], False)
        h = hb[i % 2]
        P.op("dve", lambda e, h=h, xb=xb, rstd=rstd: e.scalar_tensor_tensor(
            out=h[:, :], in0=xb[:, :], scalar=rstd[:, :], in1=g2[:, :], op0=ALU.mult, op1=ALU.mult),
            reads=[xb, rstd, g2], writes=[h])
        for k in range(8):
            P.op("pe", lambda e, h=h, k=k: e.transpose(out=tp[:, k * 128:(k + 1) * 128],
                                                      in_=h[:, k * 128:(k + 1) * 128], identity=C.ident[:, :]),
                 reads=[h, C.ident], writes=[tp])
        P.op("act", lambda e, i=i: e.activation(out=h2T[:, :, i * 128:(i + 1) * 128],
                                                in_=tp[:, :].rearrange("p (k n) -> p k n", k=8), func=AF.Copy),
             reads=[tp], writes=[h2T])


def body_A(P, C, x_d, pfx):
    gains_d = P.dram(pfx + "gains", [3, 128, D], F32, "ExternalInput")
    wg_d = P.dram(pfx + "wg", [D, DFF], F32, "ExternalInput")
    wu_d = P.dram(pfx + "wu", [D, DFF], F32, "ExternalInput")
    wd_d = P.dram(pfx + "wd", [DFF, D], F32, "ExternalInput")
    wa_d = P.dram(pfx + "wa", [D, A_COLS], F32, "ExternalInput")
    pos_d = P.dram(pfx + "pos", [128, NT], I32, "ExternalInput")
    inv_d = P.dram(pfx + "inv", [128, 1], F32, "ExternalInput")
    sgn_d = P.dram(pfx + "sgn", [128, 1], F32, "ExternalInput")
    x1_d = P.dram(pfx + "x1", [NT, D], F32, "ExternalOutput")
    fm_d = P.dram(pfx + "fm", [A_ROWS, NT], F32, "ExternalOutput")
    g = []
    for i in range(3):
        b = P.sb([128, D], F32, "gain")
        P.dma("sp", b[:, :], gains_d[i], dst=b)
        g.append(b)
    with P.scope():
        wg, wu, wd = load_ffn_weights(P, wg_d, wu_d, wd_d)
        xpool = [P.sb([128, D], F32, "x") for _ in range(4)]

        def get_x(i):
            b = xpool[i % 4]
            P.dma("sp", b[:, :], x_d[i * 128:(i + 1) * 128, :], dst=b)
            return b

        def put_x(i, b):
            P.dma("sp", x1_d[i * 128:(i + 1) * 128, :], b[:, :], src=b)

        ffn_phase(P, C, NTILE, get_x, put_x, g[0], g[1], wg, wu, wd)
    with P.scope():
        COS, SINS = rope_tables(P, pos_d, inv_d, sgn_d, NT)
        h2T = P.sb([128, 8, NT], BF16, "h2T")
        tp = P.ps([128, 8 * 128], BF16, "tp")
        make_h2T(P, C, x1_d, g[2], h2T, tp, NTILE)
        wsrc = wa_d.rearrange("(c p) n -> p c n", p=128)
        wbufs = [P.sb([128, 8, 512], BF16, "wa") for _ in range(3)]
        pp = [P.ps([128, 512], F32, "pp") for _ in range(4)]
        stg = [P.sb([128, 512], F32, "stg") for _ in range(4)]
        tmp = [P.sb([128, 512], F32, "tmp") for _ in range(2)]
        nblk = (A_NIN + 3) // 4
        ops = []
        ic = 0
        for oc, (kind, nin) in enumerate(A_OPS):
            ops.append((oc, kind, list(range(ic, ic + nin))))
            ic += nin
        loaded = {}
        state = {"pp": 0, "stg": 0, "tmp": 0}

        def get_w(chunk):
            blk = chunk // 4
            _load(blk)
            if blk + 1 < nblk:
                _load(blk + 1)
            wb = loaded[blk]
            off = (chunk % 4) * 128
            return wb, off

        def _load(blk):
            if blk not in loaded:
                wb = wbufs[blk % 3]
                c0 = blk * 512
                c1 = min(A_COLS, c0 + 512)
                P.dma("pool", wb[:, :, 0:c1 - c0], wsrc[:, :, c0:c1], dst=wb)
                loaded[blk] = wb

        for oc, kind, chunks in ops:
            rows = 24 if kind == "gate" else 128
            for tg in range(NT // 512):
                tsl = slice(tg * 512, (tg + 1) * 512)
                pts = []
                for ch in chunks:
                    wb, off = get_w(ch)
                    pt = pp[state["pp"] % 4]
                    state["pp"] += 1
                    for k in range(8):
                        P.op("pe", lambda e, pt=pt, wb=wb, off=off, k=k, rows=rows, tsl=tsl: e.matmul(
                            pt[0:rows, :], lhsT=wb[:, k, off:off + rows], rhs=h2T[:, k, tsl], start=(k == 0), stop=(k == 7)),
                            reads=[wb, h2T], writes=[pt])
                    pts.append(pt)
                sb = stg[state["stg"] % 4]
                state["stg"] += 1
                if kind in ("copy", "gate"):
                    if oc % 2 == 0:
                        P.op("act", lambda e, sb=sb, pt=pts[0], rows=rows: e.activation(out=sb[0:rows, :], in_=pt[0:rows, :],
                                                                                        func=AF.Copy), reads=[pts[0]], writes=[sb])
                    else:
                        P.op("dve", lambda e, sb=sb, pt=pts[0], rows=rows: e.tensor_copy(out=sb[0:rows, :], in_=pt[0:rows, :]),
                             reads=[pts[0]], writes=[sb])
                elif kind == "silu":
                    P.op("act", lambda e, sb=sb, pt=pts[0]: e.activation(out=sb[:, :], in_=pt[:, :], func=AF.Silu),
                         reads=[pts[0]], writes=[sb])
                elif kind == "prod":
                    t = tmp[state["tmp"] % 2]
                    state["tmp"] += 1
                    P.op("act", lambda e, t=t, pt=pts[0]: e.activation(out=t[:, :], in_=pt[:, :], func=AF.Copy),
                         reads=[pts[0]], writes=[t])
                    P.op("dve", lambda e, sb=sb, t=t, pt=pts[1]: e.tensor_tensor(out=sb[:, :], in0=t[:, :], in1=pt[:, :],
                                                                                 op=ALU.mult), reads=[t, pts[1]], writes=[sb])
                elif kind == "rope":
                    t = tmp[state["tmp"] % 2]
                    state["tmp"] += 1
                    P.op("dve", lambda e, t=t, pt=pts[0], tsl=tsl: e.tensor_tensor(out=t[:, :], in0=pt[:, :], in1=COS[:, tsl],
                                                                                   op=ALU.mult), reads=[pts[0], COS], writes=[t])
                    P.op("dve", lambda e, sb=sb, pt=pts[1], tsl=tsl: e.tensor_tensor(out=sb[:, :], in0=pt[:, :], in1=SINS[:, tsl],
                                                                                     op=ALU.mult), reads=[pts[1], SINS], writes=[sb])
                    P.op("pool", lambda e, sb=sb, t=t: e.tensor_tensor(out=sb[:, :], in0=sb[:, :], in1=t[:, :], op=ALU.add),
                         reads=[sb, t], writes=[sb])
                P.dma("sp", fm_d[oc * 128:oc * 128 + rows, tsl], sb[0:rows, :], src=sb)


def build_A():
    P = Prog()
    ident_d = P.dram("ident", [128, 128], F32, "ExternalInput")
    x_d = P.dram("x", [NT, D], F32, "ExternalInput")
    C = build_consts(P, ident_d)
    with P.scope():
        body_A(P, C, x_d, "")
    print("A streams", P.stats(), "sems", P.n_sems)
    return P.finish()


def body_C(P, C, out_d, pfx):
    x1_d = P.dram(pfx + "x1", [NT, D], F32, "ExternalInput")
    gains_d = P.dram(pfx + "gains", [4, 128, D], F32, "ExternalInput")
    vT_d = P.dram(pfx + "vT", [512, NT + 2], F32, "ExternalInput")
    bT_d = P.dram(pfx + "bT", [512, NT], F32, "ExternalInput")
    ybT_d = P.dram(pfx + "ybT", [512, NT], F32, "ExternalInput")
    ycT_d = P.dram(pfx + "ycT", [512, NT], F32, "ExternalInput")
    cw_d = P.dram(pfx + "cw", [128, 4, 3], F32, "ExternalInput")
    wmg_d = P.dram(pfx + "wmg", [D, 3 * D], F32, "ExternalInput")
    wbr_d = P.dram(pfx + "wbr", [3 * 512, D], F32, "ExternalInput")
    wout_d = P.dram(pfx + "wout", [D, D], F32, "ExternalInput")
    wg_d = P.dram(pfx + "wg", [D, DFF], F32, "ExternalInput")
    wu_d = P.dram(pfx + "wu", [D, DFF], F32, "ExternalInput")
    wd_d = P.dram(pfx + "wd", [DFF, D], F32, "ExternalInput")
    x2_d = P.dram(pfx + "x2s", [NT, D], F32, "Internal")
    g = []
    for i in range(4):
        b = P.sb([128, D], F32, "gain")
        P.dma("sp", b[:, :], gains_d[i], dst=b)
        g.append(b)
    with P.scope():
        cw = P.sb([128, 4, 3], F32, "cw")
        P.dma("sp", cw[:, :, :], cw_d[:, :, :], dst=cw)
        wbr, ibr = load_w_cast(P, wbr_d, 3 * 512, D, 512, "wbr", defer=True)
        wmg, img = load_w_cast(P, wmg_d, D, 3 * D, 512, "wmg", defer=True)
        wout, iout = load_w_cast(P, wout_d, D, D, 512, "wout", defer=True)
        for f in [ibr[0], img[0], img[2], img[4], ibr[1], img[1], img[3], img[5]] + iout:
            f()
        tp = P.ps([128, 8 * 128], BF16, "tp")
        h2T = P.sb([128, 8, 512], BF16, "h2T")
        xt = [P.sb([128, D], F32, "x1t") for _ in range(4)]
        hb = [P.sb([128, D], BF16, "h2") for _ in range(2)]
        vin = [P.sb([128, 514], F32, "vin") for _ in range(2)]
        bin_ = [P.sb([128, 512], F32, "bin") for _ in range(2)]
        acc = [P.sb([128, 512], F32, "acc") for _ in range(2)]
        ybrs = [P.sb([128, 4, 512], BF16, "ybr%d" % i) for i in range(3)]
        mT = P.sb([128, 8, 512], BF16, "mT")
        macc = [P.sb([128, 512], F32, "macc") for _ in range(2)]
        gsb = [P.sb([128, 512], F32, "gsb") for _ in range(2)]
        tt = [P.sb([128, 512], F32, "tt") for _ in range(2)]
        pg = [P.ps([128, 512], F32, "pg") for _ in range(2)]
        ppj = [P.ps([128, 512], F32, "ppj") for _ in range(2)]
        py = P.ps([128, D], F32, "py")
        t1 = P.sb([128, D], F32, "t1")
        for tg in range(NT // 512):
            tsl = slice(tg * 512, (tg + 1) * 512)
            for t in range(4):
                i = tg * 4 + t
                xb = xt[t]
                P.dma("sp", xb[:, :], x1_d[i * 128:(i + 1) * 128, :], dst=xb)
                rstd = rms_stats(P, C, xb, xb[:, :], False)
                h = hb[t % 2]
                P.op("dve", lambda e, h=h, xb=xb, rstd=rstd: e.scalar_tensor_tensor(
                    out=h[:, :], in0=xb[:, :], scalar=rstd[:, :], in1=g[0][:, :], op0=ALU.mult, op1=ALU.mult),
                    reads=[xb, rstd, g[0]], writes=[h])
                for k in range(8):
                    P.op("pe", lambda e, h=h, k=k: e.transpose(out=tp[:, k * 128:(k + 1) * 128],
                                                              in_=h[:, k * 128:(k + 1) * 128], identity=C.ident[:, :]),
                         reads=[h, C.ident], writes=[tp])
                P.op("act", lambda e, t=t: e.activation(out=h2T[:, :, t * 128:(t + 1) * 128],
                                                        in_=tp[:, :].rearrange("p (k n) -> p k n", k=8), func=AF.Copy),
                     reads=[tp], writes=[h2T])
            for c in range(4):
                v = vin[c % 2]
                bb = bin_[c % 2]
                a = acc[c % 2]
                P.dma("sp", v[:, :], vT_d[c * 128:(c + 1) * 128, tg * 512:tg * 512 + 514], dst=v)
                P.dma("sp", bb[:, :], bT_d[c * 128:(c + 1) * 128, tsl], dst=bb)
                P.op("dve", lambda e, a=a, v=v, c=c: e.tensor_scalar(out=a[:, :], in0=v[:, 2:514], scalar1=cw[:, c, 2:3],
                                                                     scalar2=None, op0=ALU.mult), reads=[v, cw], writes=[a])
                P.op("dve", lambda e, a=a, v=v, c=c: e.scalar_tensor_tensor(out=a[:, :], in0=v[:, 1:513], scalar=cw[:, c, 1:2],
                                                                            in1=a[:, :], op0=ALU.mult, op1=ALU.add),
                     reads=[v, cw, a], writes=[a])
                P.op("dve", lambda e, a=a, v=v, c=c: e.scalar_tensor_tensor(out=a[:, :], in0=v[:, 0:512], scalar=cw[:, c, 0:1],
                                                                            in1=a[:, :], op0=ALU.mult, op1=ALU.add),
                     reads=[v, cw, a], writes=[a])
                P.op("pool", lambda e, a=a, bb=bb, c=c: e.tensor_tensor(out=ybrs[0][:, c, :], in0=a[:, :], in1=bb[:, :], op=ALU.mult),
                     reads=[a, bb], writes=[ybrs[0]])
            P.dma("pool", ybrs[1][:, :, :], ybT_d[:, tsl].rearrange("(c p) n -> p c n", p=128), dst=ybrs[1])
            P.dma("pool", ybrs[2][:, :, :], ycT_d[:, tsl].rearrange("(c p) n -> p c n", p=128), dst=ybrs[2])
            for mc in range(8):
                ma = macc[mc % 2]
                for br in range(3):
                    pj = ppj[br % 2]
                    for k in range(4):
                        wb, wap = wslice(wbr, br * 4 + k, mc * 128, (mc + 1) * 128)
                        P.op("pe", lambda e, pj=pj, wap=wap, br=br, k=k: e.matmul(
                            pj[:, :], lhsT=wap, rhs=ybrs[br][:, k, :], start=(k == 0), stop=(k == 3)),
                            reads=[wb, ybrs[br]], writes=[pj])
                    pgt = pg[br % 2]
                    for k in range(8):
                        wb, wap = wslice(wmg, k, br * D + mc * 128, br * D + (mc + 1) * 128)
                        P.op("pe", lambda e, pgt=pgt, wap=wap, k=k: e.matmul(
                            pgt[:, :], lhsT=wap, rhs=h2T[:, k, :], start=(k == 0), stop=(k == 7)),
                            reads=[wb, h2T], writes=[pgt])
                    gs = gsb[br % 2]
                    P.op("act", lambda e, gs=gs, pgt=pgt: e.activation(out=gs[:, :], in_=pgt[:, :], func=AF.Sigmoid),
                         reads=[pgt], writes=[gs])
                    if br == 0:
                        P.op("dve", lambda e, ma=ma, gs=gs, pj=pj: e.tensor_tensor(out=ma[:, :], in0=gs[:, :], in1=pj[:, :],
                                                                                   op=ALU.mult), reads=[gs, pj], writes=[ma])
                    else:
                        t_ = tt[br % 2]
                        P.op("dve", lambda e, t_=t_, gs=gs, pj=pj: e.tensor_tensor(out=t_[:, :], in0=gs[:, :], in1=pj[:, :],
                                                                                   op=ALU.mult), reads=[gs, pj], writes=[t_])
                        if br == 1:
                            P.op("pool", lambda e, ma=ma, t_=t_: e.tensor_tensor(out=ma[:, :], in0=ma[:, :], in1=t_[:, :],
                                                                                 op=ALU.add), reads=[ma, t_], writes=[ma])
                        else:
                            P.op("pool", lambda e, ma=ma, t_=t_, mc=mc: e.tensor_tensor(out=mT[:, mc, :], in0=ma[:, :], in1=t_[:, :],
                                                                                        op=ALU.add), reads=[ma, t_], writes=[mT])
            for t in range(4):
                i = tg * 4 + t
                for half in range(2):
                    for k in range(8):
                        wb, wap = wslice(wout, k, half * 512, (half + 1) * 512)
                        P.op("pe", lambda e, wap=wap, k=k, half=half, t=t: e.matmul(
                            py[:, half * 512:(half + 1) * 512], lhsT=mT[:, k, t * 128:(t + 1) * 128], rhs=wap,
                            start=(k == 0), stop=(k == 7)), reads=[wb, mT], writes=[py])
                rstd = rms_stats(P, C, py, py[:, :], True)
                P.op("dve", lambda e, rstd=rstd: e.scalar_tensor_tensor(
                    out=t1[:, :], in0=py[:, :], scalar=rstd[:, :], in1=g[1][:, :], op0=ALU.mult, op1=ALU.mult),
                    reads=[py, rstd, g[1]], writes=[t1])
                xb = xt[t]
                P.op("pool", lambda e, xb=xb: e.tensor_tensor(out=xb[:, :], in0=xb[:, :], in1=t1[:, :], op=ALU.add),
                     reads=[xb, t1], writes=[xb])
                P.dma("sp", x2_d[i * 128:(i + 1) * 128, :], xb[:, :], src=xb)
    with P.scope():
        wg, wu, wd = load_ffn_weights(P, wg_d, wu_d, wd_d)
        xpool = [P.sb([128, D], F32, "x") for _ in range(4)]

        def get_x(i):
            b = xpool[i % 4]
            P.dma("sp", b[:, :], x2_d[i * 128:(i + 1) * 128, :], dst=b)
            return b

        def put_x(i, b):
            P.dma("sp", out_d[i * 128:(i + 1) * 128, :], b[:, :], src=b)

        ffn_phase(P, C, NTILE, get_x, put_x, g[2], g[3], wg, wu, wd)


def build_C():
    P = Prog()
    ident_d = P.dram("ident", [128, 128], F32, "ExternalInput")
    out_d = P.dram("xo", [NT, D], F32, "ExternalOutput")
    C = build_consts(P, ident_d)
    with P.scope():
        body_C(P, C, out_d, "")
    print("C streams", P.stats(), "sems", P.n_sems)
    return P.finish()


def build_CA():
    P = Prog()
    ident_d = P.dram("ident", [128, 128], F32, "ExternalInput")
    x3_d = P.dram("x3s", [NT, D], F32, "Internal")
    C = build_consts(P, ident_d)
    with P.scope():
        body_C(P, C, x3_d, "c_")
    with P.scope():
        body_A(P, C, x3_d, "a_")
    print("CA streams", P.stats(), "sems", P.n_sems)
    return P.finish()


S = 8192
NQB = 32


def build_B():
    P = Prog()
    dh = {}
    for nm, shp in (("qT", [128, S]), ("zfT", [128, S]), ("zf", [S, 128]), ("v", [S, 128]), ("g", [S, 128]),
                    ("lbT", [128, 4]), ("lmask", [128, 4]), ("lbrow", [128, 4, 128]), ("lmaskrow", [128, 4, 128]),
                    ("gnorm", [128, 128]), ("M1", [128, 128]), ("M2", [128, 128])):
        dh[nm] = P.dram("h_" + nm, shp, F32, "ExternalInput")
    dh["y"] = P.dram("yh", [S, 128], F32, "ExternalOutput")
    dn = {}
    for nm, shp in NSA_IN(S, NQB):
        dn[nm] = P.dram("n_" + nm, shp, F32, "ExternalInput")
    dn["y"] = P.dram("yn", [NQB, 128, 256], F32, "ExternalOutput")
    with P.scope():
        hgrn_phase(P, S, dh)
    with P.scope():
        nsa_phase(P, S, NQB, dn)
    print("B streams", P.stats(), "sems", P.n_sems)
    return P.finish()


import numpy as np

S = 8192
NC = 8

def rep(a, n=128):
    return np.ascontiguousarray(np.broadcast_to(a[None], (n,) + a.shape))

IDENT = np.eye(128, dtype=np.float32)
_p = np.arange(128)
INV = (10000.0 ** (-(_p % 32).astype(np.float32) * 2.0 / 64)).astype(np.float32).reshape(128, 1)
SGN = np.where((_p % 64) < 32, -1.0, 1.0).astype(np.float32).reshape(128, 1)
PERM, MG = a_col_perm()


def prep_A(inp, l, x):
    wa = np.ascontiguousarray(inp["w_in"][l][:, PERM])
    gains = np.stack([rep(inp["norm_gains"][l, i]) for i in (0, 1, 2)])
    maps = []
    pos = inp["positions"].reshape(-1)
    for c in range(NC):
        sl = slice(c * NT, (c + 1) * NT)
        maps.append(dict(x=(np.ascontiguousarray(x[sl]) if x is not None else None), gains=gains, wg=inp["w_ffn_gate"][l, 0], wu=inp["w_ffn_up"][l, 0],
                         wd=inp["w_ffn_down"][l, 0], wa=wa, ident=IDENT, pos=rep(pos[sl].astype(np.int32)), inv=INV, sgn=SGN))
    return maps

HG_M1, HG_M2 = hgrn_consts_np()
NSA_K = nsa_consts_np(S)
ONES128 = np.ones((128, 128), np.float32)
ZEROS128 = np.zeros((128, 128), np.float32)
NQB = 32
NCC = 4


def _shift(A, par):
    o = np.empty_like(A)
    if par:
        o[:, 2 * par:] = A[:, :A.shape[1] - 2 * par]
        o[:, :2 * par] = A[:, :1]
    else:
        o[:] = A
    return o


def nsa_core_consts(par):
    K = NSA_K
    if par == 0:
        wm = [K["Mlow"], ONES128, ONES128, ONES128, K["Mdiag"], ZEROS128, K["Mdiag"], ZEROS128]
    else:
        wm = [ZEROS128, K["Mlow"], ONES128, ONES128, ONES128, K["Mdiag"], ONES128, K["Mdiag"]]
    Tt = np.zeros((128, NQB, NCC), np.float32)
    for m in range(NQB):
        for c in range(NCC):
            Tt[:, m, c] = 128 * (2 * m + par) - 2048 * c - 31
    return dict(WM=np.ascontiguousarray(np.stack(wm, 1)), Tt=Tt, Amul=_shift(K["Amul"], par), Aadd=_shift(K["Aadd"], par))


NSA_CC = [nsa_core_consts(0), nsa_core_consts(1)]
OV_PM = np.ascontiguousarray(NSA_K["ov"].reshape(-1, 128, NSA_K["ov"].shape[-1]).transpose(1, 0, 2))


def prep_B(inp, l, FM):
    maps = []
    logits = inp["hgrn_lb_logits"]
    lmask = np.zeros((4,), np.float32)
    lmask[1:l + 1] = 1
    gn = rep(inp["hgrn_gnorm"][l])
    pe = inp["cmp_pe"][l]
    pe2 = np.ascontiguousarray(pe.reshape(2, 16, 2, 64).transpose(0, 2, 3, 1).reshape(2, 128, 16))
    for c in range(NC):
        b, hd = c // 4, c % 4
        kvh, par = (c // 2) % 2, c % 2
        tk = slice(b * S, (b + 1) * S)
        d = {}
        r = lambda base, n: FM[base:base + n, tk]
        d["h_qT"] = np.ascontiguousarray(r(1024 + hd * 128, 128))
        zfT = r(1536 + hd * 128, 128)
        d["h_zfT"] = np.ascontiguousarray(zfT)
        d["h_zf"] = np.ascontiguousarray(zfT.T)
        d["h_v"] = np.ascontiguousarray(r(2048 + hd * 128, 128).T)
        d["h_g"] = np.ascontiguousarray(r(2560 + hd * 128, 128).T)
        lg = logits[:, hd * 128:(hd + 1) * 128]
        d["h_lbT"] = np.ascontiguousarray(lg.T)
        d["h_lmask"] = rep(lmask)
        d["h_lbrow"] = rep(lg)
        d["h_lmaskrow"] = np.ascontiguousarray(np.broadcast_to(lmask[None, :, None], (128, 4, 128)))
        d["h_gnorm"] = gn
        d["h_M1"] = HG_M1
        d["h_M2"] = HG_M2
        q = r(3072 + kvh * 256, 256).reshape(4, 64, 64, 128)[:, :, par::2]
        d["n_QT"] = np.ascontiguousarray(q.transpose(1, 2, 0, 3).reshape(64, NQB, 512))
        gl = r(4352 + kvh * 12, 12).reshape(12, 64, 128)[:, par::2]
        d["n_gl"] = np.ascontiguousarray(gl.transpose(2, 1, 0))
        d["n_ksT"] = np.ascontiguousarray(r(3712 + kvh * 64, 64))
        d["n_kwT"] = np.ascontiguousarray(r(3840 + kvh * 64, 64))

        def stack2(xT):
            o = np.zeros((128, S), np.float32)
            o[:64] = xT
            o[64:, :-1] = xT[:, 1:]
            return o
        d["n_kc2T"] = stack2(r(3584 + kvh * 64, 64))
        d["n_vc2T"] = stack2(r(3968 + kvh * 64, 64))

        def aug(xT):
            o = np.ones((S, 65), np.float32)
            o[:, :64] = xT.T
            return np.ascontiguousarray(o.reshape(S // 128, 128, 65).transpose(1, 0, 2))
        d["n_vs"] = aug(r(4096 + kvh * 64, 64))
        d["n_vw"] = aug(r(4224 + kvh * 64, 64))
        d["n_pe2"] = pe2
        d["n_w1"] = np.ascontiguousarray(inp["cmp_w1"][l].reshape(2, 16, 128, 256).transpose(0, 2, 1, 3))
        d["n_w2"] = np.ascontiguousarray(inp["cmp_w2"][l].reshape(2, 2, 128, 64).transpose(0, 2, 1, 3))
        for k in ("D16", "Mdiag", "Mlow", "E", "identb"):
            d["n_" + k] = NSA_K[k]
        d["n_ov"] = OV_PM
        for k, v in NSA_CC[par].items():
            d["n_" + k] = v
        maps.append(d)
    return maps


def gather_B(results):
    ybT = np.zeros((2, 512, S), np.float32)
    ycT = np.zeros((2, 512, S), np.float32)
    for c in range(NC):
        b, hd = c // 4, c % 4
        kvh, par = (c // 2) % 2, c % 2
        ybT[b, hd * 128:(hd + 1) * 128, :] = results[c]["yh"].T
        y = results[c]["yn"]
        yT = y.transpose(2, 0, 1)
        ycT[b, kvh * 256:(kvh + 1) * 256].reshape(256, 64, 128)[:, par::2, :] = yT
    return ybT, ycT


def prep_C(inp, l, x1, FM, ybT, ycT):
    gains = np.stack([rep(inp["norm_gains"][l, i]) for i in (2, 3, 4, 5)])
    cw = np.ascontiguousarray(inp["conv_w"][l].reshape(3, 4, 128).transpose(2, 1, 0))
    wmg = np.ascontiguousarray(inp["w_in"][l][:, MG])
    wbr = np.ascontiguousarray(inp["w_branch"][l].reshape(1536, 1024))
    maps = []
    for c in range(NC):
        b = c // 4
        t0 = c * NT
        sl = slice(t0, t0 + NT)
        ls = slice((c % 4) * NT, (c % 4 + 1) * NT)
        vT = np.zeros((512, NT + 2), np.float32)
        vT[:, 2:] = FM[512:1024, sl]
        if c % 4 != 0:
            vT[:, :2] = FM[512:1024, t0 - 2:t0]
        maps.append(dict(x1=np.ascontiguousarray(x1[sl]), gains=gains, vT=vT, bT=np.ascontiguousarray(FM[0:512, sl]),
                         ybT=np.ascontiguousarray(ybT[b][:, ls]), ycT=np.ascontiguousarray(ycT[b][:, ls]), cw=cw, wmg=wmg, wbr=wbr,
                         wout=inp["w_out"][l], wg=inp["w_ffn_gate"][l, 1], wu=inp["w_ffn_up"][l, 1], wd=inp["w_ffn_down"][l, 1],
                         ident=IDENT))
    return maps


from concourse.bass_utils import run_bass_kernel_spmd

_PROGS = {}


def _prog(name):
    if name not in _PROGS:
        _PROGS[name] = {"A": build_A, "B": build_B, "C": build_C, "CA": build_CA}[name]()
    return _PROGS[name]


def kernel(**inputs):
    inp = {k: np.asarray(v) for k, v in inputs.items()}
    cores = list(range(NC))
    x = np.ascontiguousarray(inp["x"].reshape(-1, D).astype(np.float32, copy=False))
    resA = run_bass_kernel_spmd(_prog("A"), prep_A(inp, 0, x), core_ids=cores)
    x1 = np.concatenate([r["x1"] for r in resA.results])
    FM = np.concatenate([r["fm"] for r in resA.results], axis=1)
    del resA
    for l in range(4):
        resB = run_bass_kernel_spmd(_prog("B"), prep_B(inp, l, FM), core_ids=cores)
        ybT, ycT = gather_B(resB.results)
        del resB
        mc = prep_C(inp, l, x1, FM, ybT, ycT)
        if l < 3:
            ma = prep_A(inp, l + 1, None)
            maps = []
            for c in range(NC):
                d = {"c_" + k: v for k, v in mc[c].items() if k != "ident"}
                d.update({"a_" + k: v for k, v in ma[c].items() if k not in ("ident", "x")})
                d["ident"] = IDENT
                maps.append(d)
            res = run_bass_kernel_spmd(_prog("CA"), maps, core_ids=cores)
            x1 = np.concatenate([r["a_x1"] for r in res.results])
            FM = np.concatenate([r["a_fm"] for r in res.results], axis=1)
            del res
        else:
            res = run_bass_kernel_spmd(_prog("C"), mc, core_ids=cores)
            x = np.concatenate([r["xo"] for r in res.results])
            del res
    return np.ascontiguousarray(x.reshape(2, S, D).astype(np.float32))
```

```python
import contextlib
import numpy as np
import concourse.bass as bass
import concourse.mybir as mybir

F32 = mybir.dt.float32
BF16 = mybir.dt.bfloat16
I32 = mybir.dt.int32
ALU = mybir.AluOpType
AF = mybir.ActivationFunctionType
AX = mybir.AxisListType

ENGS = ("pe", "act", "dve", "pool", "sp")


class Buf:
    def __init__(self, prog, t, name, tracked=True):
        self.prog = prog
        self.t = t
        self.name = name
        self.tracked = tracked
        self.last_w = None
        self.readers = {}
        self.wsem = None
        self.wcnt = 0
        self.rsem = None
        self.rcnt = 0

    def __getitem__(self, idx):
        return self.t[idx]

    @property
    def ap(self):
        return self.t


class Prog:
    def __init__(self, same_engine_sync=None):
        import os
        if same_engine_sync is None:
            same_engine_sync = os.environ.get('SES', '1') == '1'
        self.nc = bass.Bass("TRN2", target_bir_lowering=False)
        self.es = contextlib.ExitStack()
        self.sem_es = contextlib.ExitStack()
        self.streams = {e: [] for e in ENGS}
        self.cnt = {e: 0 for e in ENGS}
        self.esem = {}
        for e in ENGS:
            self.esem[e] = self.sem_es.enter_context(self.nc.semaphore("s_" + e))
        self.seen = {e: {} for e in ENGS}
        self.same_engine_sync = same_engine_sync
        self.dma_sems = []
        self.nbuf = 0
        self.n_sems = 5
        self.all_bufs = []
        self.free_sems = []

    def dram(self, name, shape, dtype, kind):
        t = self.nc.dram_tensor(name, list(shape), dtype, kind=kind)
        return t.ap()

    def sb(self, shape, dtype, name=None):
        self.nbuf += 1
        name = (name or "b") + "_%d" % self.nbuf
        t = self.es.enter_context(self.nc.sbuf_tensor(name, list(shape), dtype))
        b = Buf(self, t, name)
        self.all_bufs.append(b)
        return b

    def ps(self, shape, dtype=F32, name=None):
        self.nbuf += 1
        name = (name or "p") + "_%d" % self.nbuf
        t = self.es.enter_context(self.nc.psum_tensor(name, list(shape), dtype))
        b = Buf(self, t, name)
        self.all_bufs.append(b)
        return b

    def _sem(self, name):
        if self.free_sems:
            return self.free_sems.pop()
        self.n_sems += 1
        return (self.sem_es.enter_context(self.nc.semaphore(name)), 0)

    def _collect(self, eng, reads, writes, no_waw=False):
        need = {}

        def add(tok):
            if tok is None:
                return
            key, val, e = tok
            if e == eng and (eng == "pe" or not self.same_engine_sync):
                return
            if need.get(key, (0,))[0] < val:
                need[key] = (val, e)

        for b in reads:
            if b is None or not b.tracked:
                continue
            add(b.last_w)
        for b in writes:
            if b is None or not b.tracked:
                continue
            if not no_waw:
                add(b.last_w)
            for key, (val, e) in b.readers.items():
                add((key, val, e))
        out = []
        for key, (val, e) in need.items():
            if self.seen[eng].get(key, 0) >= val:
                continue
            self.seen[eng][key] = val
            out.append((key, val))
        return out

    def _record(self, tok, reads, writes):
        key, val, e = tok
        for b in writes:
            if b is None or not b.tracked:
                continue
            b.last_w = tok
            b.readers = {}
        for b in reads:
            if b is None or not b.tracked:
                continue
            if b in writes:
                continue
            b.readers[key] = (val, e)

    def op(self, eng, fn, reads=(), writes=(), no_waw=False):
        waits = self._collect(eng, reads, writes, no_waw)
        st = self.streams[eng]
        for key, val in waits:
            st.append(("w", key, val))
        self.cnt[eng] += 1
        st.append(("o", fn, self.esem[eng], 1))
        tok = (self.esem[eng], self.cnt[eng], eng)
        self._record(tok, reads, writes)
        return tok

    def dma(self, queue, out_ap, in_ap, dst=None, src=None, no_waw=False, **kw):
        reads = [src] if src is not None else []
        writes = [dst] if dst is not None else []
        waits = self._collect(queue, reads, writes, no_waw)
        st = self.streams[queue]
        for key, val in waits:
            st.append(("w", key, val))
        if dst is not None:
            if dst.wsem is None:
                dst.wsem, dst.wcnt = self._sem("w_" + dst.name)
                self.dma_sems.append(dst)
            dst.wcnt += 16
            sem, val = dst.wsem, dst.wcnt
        else:
            if src.rsem is None:
                src.rsem, src.rcnt = self._sem("r_" + src.name)
                self.dma_sems.append(src)
            src.rcnt += 16
            sem, val = src.rsem, src.rcnt

        def fn(e, out_ap=out_ap, in_ap=in_ap, kw=kw):
            return e.dma_start(out=out_ap, in_=in_ap, **kw)

        st.append(("o", fn, sem, 16))
        tok = (sem, val, "dma")
        self._record(tok, reads, writes)
        return tok

    @contextlib.contextmanager
    def scope(self):
        outer = self.es
        self.es = contextlib.ExitStack()
        n0 = len(self.all_bufs)
        try:
            yield
        finally:
            self.barrier()
            self.flush()
            for b in self.all_bufs[n0:]:
                if b.wsem is not None:
                    self.free_sems.append((b.wsem, b.wcnt))
                if b.rsem is not None:
                    self.free_sems.append((b.rsem, b.rcnt))
                if b in self.dma_sems:
                    self.dma_sems.remove(b)
                b.dead = True
            del self.all_bufs[n0:]
            self.es.close()
            self.es = outer

    def barrier(self):
        targets = [(self.esem[e], self.cnt[e]) for e in ENGS if self.cnt[e] > 0]
        for b in self.dma_sems:
            if b.wsem is not None and b.wcnt:
                targets.append((b.wsem, b.wcnt))
            if b.rsem is not None and b.rcnt:
                targets.append((b.rsem, b.rcnt))
        for e in ENGS:
            for key, val in targets:
                if key is self.esem[e]:
                    continue
                if self.seen[e].get(key, 0) >= val:
                    continue
                self.seen[e][key] = val
                self.streams[e].append(("w", key, val))

    def flush(self):
        nc = self.nc
        streams = self.streams
        self.streams = {e: [] for e in ENGS}

        def run(stream, e):
            for it in stream:
                if it[0] == "w":
                    e.wait_ge(it[1], it[2])
                else:
                    ins = it[1](e)
                    ins.then_inc(it[2], it[3])

        with nc.Block() as block:
            @block.tensor
            def _(e):
                run(streams["pe"], e)

            @block.scalar
            def _(e):
                run(streams["act"], e)

            @block.vector
            def _(e):
                run(streams["dve"], e)

            @block.gpsimd
            def _(e):
                run(streams["pool"], e)

            @block.sync
            def _(e):
                run(streams["sp"], e)

    def finish(self):
        targets = [(self.esem[e], self.cnt[e]) for e in ENGS if self.cnt[e] > 0 and e != "sp"]
        for b in self.dma_sems:
            if b.wsem is not None and b.wcnt:
                targets.append((b.wsem, b.wcnt))
            if b.rsem is not None and b.rcnt:
                targets.append((b.rsem, b.rcnt))
        for key, val in targets:
            self.streams["sp"].append(("w", key, val))
        self.flush()
        self.es.close()
        self.sem_es.close()
        return self.nc

    def stats(self):
        return {e: self.cnt[e] for e in ENGS}


import numpy as np

D = 1024
DFF = 2816
NJ = 22
TG = 256
EPS = 1e-6


def load_w_cast(P, dram_ap, rows, cols, colblk, name, defer=False):
    kc = rows // 128
    src = dram_ap.rearrange("(c p) n -> p c n", p=128)
    blocks = []
    issue = []
    c0 = 0
    while c0 < cols:
        c1 = min(cols, c0 + colblk)
        b = P.sb([128, kc, c1 - c0], BF16, name)
        fn = (lambda b=b, c0=c0, c1=c1: P.dma("pool", b[:, :, :], src[:, :, c0:c1], dst=b))
        if defer:
            issue.append(fn)
        else:
            fn()
        blocks.append((b, c0, c1))
        c0 = c1
    if defer:
        return blocks, issue
    return blocks


def load_ffn_weights(P, wg_d, wu_d, wd_d):
    wg, ig = load_w_cast(P, wg_d, D, DFF, 512, "wg", defer=True)
    wu, iu = load_w_cast(P, wu_d, D, DFF, 512, "wu", defer=True)
    wd, idn = load_w_cast(P, wd_d, DFF, D, 512, "wd", defer=True)
    for a_, b_ in zip(ig, iu):
        a_()
        b_()
    for f in idn:
        f()
    return wg, wu, wd


def wslice(blocks, k, c0, c1):
    for b, b0, b1 in blocks:
        if b0 <= c0 and c1 <= b1:
            return b, b[:, k, c0 - b0:c1 - b0]
    raise ValueError((c0, c1))


class Consts:
    pass


def rms_stats(P, C, src_buf, src_ap, from_psum):
    ss = P.sb([128, 1], F32, "ss")
    if from_psum:
        P.op("act", lambda e: e.activation(out=C.junk[:, :], in_=src_ap, func=AF.Square, accum_out=ss[:, :]),
             reads=[src_buf], writes=[C.junk, ss])
    else:
        P.op("dve", lambda e: e.scalar_tensor_tensor(out=C.junk[:, :], in0=src_ap, scalar=1.0, in1=src_ap,
                                                      op0=ALU.mult, op1=ALU.mult, accum_out=ss[:, :]),
             reads=[src_buf], writes=[C.junk, ss])
    ms = P.sb([128, 1], F32, "ms")
    P.op("dve", lambda e: e.tensor_scalar(out=ms[:, :], in0=ss[:, :], scalar1=1.0 / D, scalar2=EPS,
                                          op0=ALU.mult, op1=ALU.add), reads=[ss], writes=[ms])
    rstd = P.sb([128, 1], F32, "rstd")
    P.op("pool", lambda e: e.tensor_tensor(out=rstd[:, :], in0=ms[:, :], in1=C.mhalf[:, :], op=ALU.pow),
         reads=[ms, C.mhalf], writes=[rstd])
    return rstd


def ffn_phase(P, C, ntiles, get_x, put_x, g_pre, g_post, wg, wu, wd):
    ngroups = ntiles // 2
    hT = [P.sb([128, 8, TG], BF16, "hT") for _ in range(2)]
    aT = P.sb([128, NJ, TG], BF16, "aT")
    tp = P.ps([128, 8 * 128], BF16, "tp")
    gu = [P.ps([128, 2, TG], F32, "gu") for _ in range(2)]
    ys = [P.ps([128, D], F32, "y") for _ in range(2)]
    hb = [P.sb([128, D], BF16, "h") for _ in range(2)]
    sg = [P.sb([128, TG], BF16, "sg") for _ in range(2)]
    xs = {}
    t1 = P.sb([128, D], F32, "t1")

    def prep(g):
        for t in range(2):
            i = 2 * g + t
            xb = get_x(i)
            xs[i] = xb
            rstd = rms_stats(P, C, xb, xb[:, :], False)
            h = hb[t]
            P.op("dve", lambda e, h=h, xb=xb, rstd=rstd: e.scalar_tensor_tensor(
                out=h[:, :], in0=xb[:, :], scalar=rstd[:, :], in1=g_pre[:, :], op0=ALU.mult, op1=ALU.mult),
                reads=[xb, rstd, g_pre], writes=[h])

    def transposes(g):
        for t in range(2):
            h = hb[t]
            for k in range(8):
                P.op("pe", lambda e, h=h, k=k: e.transpose(out=tp[:, k * 128:(k + 1) * 128],
                                                          in_=h[:, k * 128:(k + 1) * 128], identity=C.ident[:, :]),
                     reads=[h, C.ident], writes=[tp])
            dst = hT[g % 2]
            P.op("act", lambda e, dst=dst, t=t: e.activation(
                out=dst[:, :, t * 128:(t + 1) * 128], in_=tp[:, :].rearrange("p (k n) -> p k n", k=8), func=AF.Copy),
                reads=[tp], writes=[dst])

    def phaseA(g):
        h_t = hT[g % 2]
        for j in range(NJ):
            pg = gu[j % 2]
            for which, W in ((0, wg), (1, wu)):
                for k in range(8):
                    wb, wap = wslice(W, k, j * 128, (j + 1) * 128)
                    P.op("pe", lambda e, pg=pg, which=which, wap=wap, k=k: e.matmul(
                        pg[:, which, :], lhsT=wap, rhs=h_t[:, k, :], start=(k == 0), stop=(k == 7)),
                        reads=[wb, h_t], writes=[pg])
            s = sg[j % 2]
            P.op("act", lambda e, s=s, pg=pg: e.activation(out=s[:, :], in_=pg[:, 0, :], func=AF.Silu),
                 reads=[pg], writes=[s])
            P.op("dve", lambda e, s=s, pg=pg, j=j: e.tensor_tensor(out=aT[:, j, :], in0=s[:, :], in1=pg[:, 1, :],
                                                                   op=ALU.mult),
                 reads=[s, pg], writes=[aT])

    def phaseB(g):
        for t in range(2):
            y = ys[t]
            for half in range(2):
                for j in range(NJ):
                    wb, wap = wslice(wd, j, half * 512, (half + 1) * 512)
                    P.op("pe", lambda e, y=y, half=half, wap=wap, j=j, t=t: e.matmul(
                        y[:, half * 512:(half + 1) * 512], lhsT=aT[:, j, t * 128:(t + 1) * 128], rhs=wap,
                        start=(j == 0), stop=(j == NJ - 1)),
                        reads=[wb, aT], writes=[y])

    def post(g):
        for t in range(2):
            i = 2 * g + t
            y = ys[t]
            rstd = rms_stats(P, C, y, y[:, :], True)
            P.op("dve", lambda e, y=y, rstd=rstd: e.scalar_tensor_tensor(
                out=t1[:, :], in0=y[:, :], scalar=rstd[:, :], in1=g_post[:, :], op0=ALU.mult, op1=ALU.mult),
                reads=[y, rstd, g_post], writes=[t1])
            xb = xs.pop(i)
            P.op("dve", lambda e, xb=xb: e.scalar_tensor_tensor(
                out=xb[:, :], in0=t1[:, :], scalar=0.5, in1=xb[:, :], op0=ALU.mult, op1=ALU.add),
                reads=[t1, xb], writes=[xb])
            put_x(i, xb)

    prep(0)
    transposes(0)
    for g in range(ngroups):
        phaseA(g)
        if g + 1 < ngroups:
            prep(g + 1)
            transposes(g + 1)
        phaseB(g)
        post(g)


def build_consts(P, ident_d):
    C = Consts()
    C.ident = P.sb([128, 128], BF16, "ident")
    P.dma("pool", C.ident[:, :], ident_d[:, :], dst=C.ident)
    C.junk = P.sb([128, D], BF16, "junk")
    C.junk.tracked = False
    C.mhalf = P.sb([128, 1], F32, "mhalf")
    P.op("pool", lambda e: e.memset(C.mhalf[:, :], -0.5), writes=[C.mhalf])
    return C


def build_test_ffn(NT):
    P = Prog()
    x_d = P.dram("x", [NT, D], F32, "ExternalInput")
    gains_d = P.dram("gains", [2, 128, D], F32, "ExternalInput")
    wg_d = P.dram("wg", [D, DFF], F32, "ExternalInput")
    wu_d = P.dram("wu", [D, DFF], F32, "ExternalInput")
    wd_d = P.dram("wd", [DFF, D], F32, "ExternalInput")
    ident_d = P.dram("ident", [128, 128], F32, "ExternalInput")
    out_d = P.dram("out", [NT, D], F32, "ExternalOutput")
    C = build_consts(P, ident_d)
    g_pre = P.sb([128, D], F32, "gpre")
    g_post = P.sb([128, D], F32, "gpost")
    P.dma("sp", g_pre[:, :], gains_d[0], dst=g_pre)
    P.dma("sp", g_post[:, :], gains_d[1], dst=g_post)
    wg = load_w_cast(P, wg_d, D, DFF, 512, "wg")
    wu = load_w_cast(P, wu_d, D, DFF, 512, "wu")
    wd = load_w_cast(P, wd_d, DFF, D, 512, "wd")
    xpool = [P.sb([128, D], F32, "x") for _ in range(4)]

    def get_x(i):
        b = xpool[i % 4]
        P.dma("sp", b[:, :], x_d[i * 128:(i + 1) * 128, :], dst=b)
        return b

    def put_x(i, b):
        P.dma("sp", out_d[i * 128:(i + 1) * 128, :], b[:, :], src=b)

    ffn_phase(P, C, NT // 128, get_x, put_x, g_pre, g_post, wg, wu, wd)
    print("streams", P.stats(), "sems", P.n_sems)
    return P.finish()


import numpy as np

EPS = 1e-6


def hgrn_consts_np():
    s = np.arange(128)
    same = (s[:, None] // 64) == (s[None, :] // 64)
    M1 = (same & (s[:, None] <= s[None, :])).astype(np.float32)
    M2 = (same & (s[:, None] > s[None, :])).astype(np.float32)
    return M1, M2


def hgrn_phase(P, S, d):
    nt = S // 128
    M1 = P.sb([128, 128], F32, "M1")
    M2 = P.sb([128, 128], F32, "M2")
    P.dma("sp", M1[:, :], d["M1"][:, :], dst=M1)
    P.dma("sp", M2[:, :], d["M2"][:, :], dst=M2)
    gn = P.sb([128, 128], F32, "gn")
    P.dma("sp", gn[:, :], d["gnorm"][:, :], dst=gn)
    mhalf = P.sb([128, 1], F32, "mhalf")
    P.op("pool", lambda e: e.memset(mhalf[:, :], -0.5), writes=[mhalf])
    lbl = P.sb([128, 4], F32, "lbl")
    lmk = P.sb([128, 4], F32, "lmk")
    P.dma("sp", lbl[:, :], d["lbT"][:, :], dst=lbl)
    P.dma("sp", lmk[:, :], d["lmask"][:, :], dst=lmk)
    ex = P.sb([128, 4], F32, "ex")
    P.op("act", lambda e: e.activation(out=ex[:, :], in_=lbl[:, :], func=AF.Exp), reads=[lbl], writes=[ex])
    den = P.sb([128, 1], F32, "den")
    P.op("dve", lambda e: e.reduce_sum(out=den[:, :], in_=ex[:, :], axis=AX.X), reads=[ex], writes=[den])
    rden = P.sb([128, 1], F32, "rden")
    P.op("dve", lambda e: e.reciprocal(out=rden[:, :], in_=den[:, :]), reads=[den], writes=[rden])
    exm = P.sb([128, 4], F32, "exm")
    P.op("dve", lambda e: e.tensor_tensor(out=exm[:, :], in0=ex[:, :], in1=lmk[:, :], op=ALU.mult),
         reads=[ex, lmk], writes=[exm])
    num = P.sb([128, 1], F32, "num")
    P.op("dve", lambda e: e.reduce_sum(out=num[:, :], in_=exm[:, :], axis=AX.X), reads=[exm], writes=[num])
    lbc = P.sb([128, 1], F32, "lbc")
    P.op("dve", lambda e: e.tensor_tensor(out=lbc[:, :], in0=num[:, :], in1=rden[:, :], op=ALU.mult),
         reads=[num, rden], writes=[lbc])
    omlc = P.sb([128, 1], F32, "omlc")
    P.op("dve", lambda e: e.tensor_scalar(out=omlc[:, :], in0=lbc[:, :], scalar1=-1.0, scalar2=1.0,
                                          op0=ALU.mult, op1=ALU.add), reads=[lbc], writes=[omlc])
    nomlc = P.sb([128, 1], F32, "nomlc")
    P.op("dve", lambda e: e.tensor_scalar(out=nomlc[:, :], in0=omlc[:, :], scalar1=-1.0, scalar2=None,
                                          op0=ALU.mult), reads=[omlc], writes=[nomlc])
    lbr = P.sb([128, 4, 128], F32, "lbr")
    lmr = P.sb([128, 4, 128], F32, "lmr")
    P.dma("sp", lbr[:, :, :], d["lbrow"][:, :, :], dst=lbr)
    P.dma("sp", lmr[:, :, :], d["lmaskrow"][:, :, :], dst=lmr)
    exr = P.sb([128, 4, 128], F32, "exr")
    P.op("act", lambda e: e.activation(out=exr[:, :, :], in_=lbr[:, :, :], func=AF.Exp), reads=[lbr], writes=[exr])
    denr = P.sb([128, 128], F32, "denr")
    P.op("dve", lambda e: e.tensor_tensor(out=denr[:, :], in0=exr[:, 0, :], in1=exr[:, 1, :], op=ALU.add),
         reads=[exr], writes=[denr])
    P.op("dve", lambda e: e.tensor_tensor(out=denr[:, :], in0=denr[:, :], in1=exr[:, 2, :], op=ALU.add),
         reads=[exr, denr], writes=[denr])
    P.op("dve", lambda e: e.tensor_tensor(out=denr[:, :], in0=denr[:, :], in1=exr[:, 3, :], op=ALU.add),
         reads=[exr, denr], writes=[denr])
    P.op("dve", lambda e: e.reciprocal(out=denr[:, :], in_=denr[:, :]), reads=[denr], writes=[denr])
    P.op("dve", lambda e: e.tensor_tensor(out=exr[:, :, :], in0=exr[:, :, :], in1=lmr[:, :, :], op=ALU.mult),
         reads=[exr, lmr], writes=[exr])
    lbrow = P.sb([128, 128], F32, "lbrow")
    P.op("dve", lambda e: e.tensor_tensor(out=lbrow[:, :], in0=exr[:, 0, :], in1=exr[:, 1, :], op=ALU.add),
         reads=[exr], writes=[lbrow])
    P.op("dve", lambda e: e.tensor_tensor(out=lbrow[:, :], in0=lbrow[:, :], in1=exr[:, 2, :], op=ALU.add),
         reads=[exr, lbrow], writes=[lbrow])
    P.op("dve", lambda e: e.tensor_tensor(out=lbrow[:, :], in0=lbrow[:, :], in1=exr[:, 3, :], op=ALU.add),
         reads=[exr, lbrow], writes=[lbrow])
    P.op("dve", lambda e: e.tensor_tensor(out=lbrow[:, :], in0=lbrow[:, :], in1=denr[:, :], op=ALU.mult),
         reads=[lbrow, denr], writes=[lbrow])
    omlrow = P.sb([128, 128], F32, "omlrow")
    P.op("dve", lambda e: e.tensor_scalar(out=omlrow[:, :], in0=lbrow[:, :], scalar1=-1.0, scalar2=1.0,
                                          op0=ALU.mult, op1=ALU.add), reads=[lbrow], writes=[omlrow])

    G = 4
    GT = G * 128
    ng = nt // G
    NB = 2
    qT = [P.sb([128, GT], F32, "qT") for _ in range(NB)]
    zfT = [P.sb([128, GT], F32, "zfT") for _ in range(NB)]
    zft = [P.sb([128, G, 128], F32, "zft") for _ in range(NB)]
    vt = [P.sb([128, G, 128], F32, "vt") for _ in range(NB)]
    gt = [P.sb([128, G, 128], F32, "gt") for _ in range(NB)]
    vb = [P.sb([128, G, 128], BF16, "vb") for _ in range(NB)]
    sigt = [P.sb([128, G, 128], F32, "sigt") for _ in range(NB)]
    logf = [P.sb([128, G, 128], F32, "logf") for _ in range(NB)]
    kt = [P.sb([128, G, 128], F32, "kt") for _ in range(NB)]
    sigf = [P.sb([128, GT], F32, "sigf") for _ in range(NB)]
    kT = [P.sb([128, GT], F32, "kT") for _ in range(NB)]
    bsb = [P.sb([128, GT], F32, "bsb") for _ in range(NB)]
    dd = [P.sb([128, GT], F32, "dd") for _ in range(NB)]
    eq = [P.sb([128, GT], F32, "eq") for _ in range(NB)]
    ek = [P.sb([128, GT], F32, "ek") for _ in range(NB)]
    eb = [P.sb([128, GT], F32, "eb") for _ in range(NB)]
    er = [P.sb([128, G, 128], F32, "er") for _ in range(NB)]
    qtl = [P.sb([128, GT], BF16, "qtl") for _ in range(NB)]
    ktl = [P.sb([128, GT], BF16, "ktl") for _ in range(NB)]
    QP = [P.sb([128, G, 2, 128], BF16, "QP") for _ in range(NB)]
    KP = [P.sb([128, G, 2, 128], BF16, "KP") for _ in range(NB)]
    for i in range(NB):
        for b in (QP[i], KP[i]):
            P.op("pool", lambda e, b=b: e.memset(b[:, :, :, :], 0.0), writes=[b])
    dec = [P.sb([128, 2 * G], F32, "dec") for _ in range(NB)]
    ATm = [P.sb([128, G, 128], BF16, "ATm") for _ in range(NB)]
    Sf = [P.sb([128, 128], F32, "Sf") for _ in range(2)]
    Sb = [P.sb([128, 128], BF16, "Sb") for _ in range(4)]
    P.op("pool", lambda e: e.memset(Sf[0][:, :], 0.0), writes=[Sf[0]])
    P.op("pool", lambda e: e.memset(Sb[0][:, :], 0.0), writes=[Sb[0]])
    sq = [P.sb([128, G, 128], F32, "sq") for _ in range(NB)]
    sg = [P.sb([128, G, 128], F32, "sg") for _ in range(NB)]
    yo = [P.sb([128, G, 128], F32, "yo") for _ in range(NB)]
    ss4 = [P.sb([128, G], F32, "ss4") for _ in range(NB)]
    rs4 = [P.sb([128, G], F32, "rs4") for _ in range(NB)]
    mh4 = P.sb([128, G], F32, "mh4")
    P.op("pool", lambda e: e.memset(mh4[:, :], -0.5), writes=[mh4])
    pbT = P.ps([128, GT], F32, "pbT")
    pR = P.ps([128, GT], F32, "pR")
    pAT = P.ps([128, GT], F32, "pAT")
    p_o = [P.ps([128, GT], F32, "p_o") for _ in range(2)]
    p_S = [P.ps([128, 512], F32, "p_S") for _ in range(2)]
    st_ = {"sidx": 0}
    bc = lambda t: t[:, :].unsqueeze(1).to_broadcast([128, G, 128])

    def h1(gi):
        n = gi % NB
        gs = slice(gi * GT, (gi + 1) * GT)
        P.dma("sp", qT[n][:, :], d["qT"][:, gs], dst=qT[n])
        P.dma("sp", zfT[n][:, :], d["zfT"][:, gs], dst=zfT[n])
        P.dma("sp", zft[n][:, :, :], d["zf"][gs, :].rearrange("(t p) k -> p t k", p=128), dst=zft[n])
        P.dma("sp", vt[n][:, :, :], d["v"][gs, :].rearrange("(t p) k -> p t k", p=128), dst=vt[n])
        P.dma("sp", gt[n][:, :, :], d["g"][gs, :].rearrange("(t p) k -> p t k", p=128), dst=gt[n])
        s_, lf, k_ = sigt[n], logf[n], kt[n]
        P.op("act", lambda e: e.activation(out=s_[:, :, :], in_=zft[n][:, :, :], func=AF.Sigmoid), reads=[zft[n]], writes=[s_])
        P.op("act", lambda e: e.activation(out=sigf[n][:, :], in_=zfT[n][:, :], func=AF.Sigmoid), reads=[zfT[n]], writes=[sigf[n]])
        P.op("act", lambda e: e.activation(out=sg[n][:, :, :], in_=gt[n][:, :, :], func=AF.Sigmoid), reads=[gt[n]], writes=[sg[n]])
        P.op("pool", lambda e: e.tensor_tensor(out=sg[n][:, :, :], in0=sg[n][:, :, :], in1=gt[n][:, :, :], op=ALU.mult),
             reads=[sg[n], gt[n]], writes=[sg[n]])
        P.op("pool", lambda e: e.tensor_tensor(out=sg[n][:, :, :], in0=sg[n][:, :, :], in1=bc(gn), op=ALU.mult),
             reads=[sg[n], gn], writes=[sg[n]])
        P.op("dve", lambda e: e.tensor_tensor(out=lf[:, :, :], in0=s_[:, :, :], in1=bc(omlrow), op=ALU.mult),
             reads=[s_, omlrow], writes=[lf])
        P.op("dve", lambda e: e.tensor_tensor(out=k_[:, :, :], in0=bc(omlrow), in1=lf[:, :, :], op=ALU.subtract),
             reads=[lf, omlrow], writes=[k_])
        P.op("dve", lambda e: e.tensor_tensor(out=lf[:, :, :], in0=lf[:, :, :], in1=bc(lbrow), op=ALU.add),
             reads=[lf, lbrow], writes=[lf])
        P.op("dve", lambda e: e.tensor_scalar(out=lf[:, :, :], in0=lf[:, :, :], scalar1=1e-30, scalar2=None, op0=ALU.max),
             reads=[lf], writes=[lf])
        P.op("act", lambda e: e.activation(out=lf[:, :, :], in_=lf[:, :, :], func=AF.Ln), reads=[lf], writes=[lf])
        for t in range(G):
            P.op("pe", lambda e, t=t: e.matmul(pbT[:, t * 128:(t + 1) * 128], lhsT=lf[:, t, :], rhs=M1[:, :], start=True, stop=True),
                 reads=[lf, M1], writes=[pbT])
        for t in range(G):
            P.op("pe", lambda e, t=t: e.matmul(pR[:, t * 128:(t + 1) * 128], lhsT=M2[:, :], rhs=lf[:, t, :], start=True, stop=True),
                 reads=[lf, M2], writes=[pR])
        yield
        sf, kT_ = sigf[n], kT[n]
        P.op("dve", lambda e: e.tensor_scalar(out=kT_[:, :], in0=sf[:, :], scalar1=nomlc[:, :], scalar2=omlc[:, :],
                                              op0=ALU.mult, op1=ALU.add), reads=[sf, nomlc, omlc], writes=[kT_])
        b_, d_, eq_, ek_, eb_, er_ = bsb[n], dd[n], eq[n], ek[n], eb[n], er[n]
        P.op("act", lambda e: e.activation(out=b_[:, :], in_=pbT[:, :], func=AF.Copy), reads=[pbT], writes=[b_])
        b3 = b_[:, :].rearrange("p (c s) -> p c s", s=64)
        P.op("dve", lambda e: e.tensor_tensor(out=d_[:, :].rearrange("p (c s) -> p c s", s=64), in0=b3,
                                              in1=b3[:, :, 31:32].to_broadcast([128, 2 * G, 64]), op=ALU.subtract),
             reads=[b_], writes=[d_])
        P.op("dve", lambda e: e.tensor_scalar(out=eq_[:, :], in0=d_[:, :], scalar1=43.0, scalar2=None, op0=ALU.min),
             reads=[d_], writes=[eq_])
        P.op("dve", lambda e: e.tensor_scalar(out=ek_[:, :], in0=d_[:, :], scalar1=-1.0, scalar2=43.0, op0=ALU.mult, op1=ALU.min),
             reads=[d_], writes=[ek_])
        P.op("act", lambda e: e.activation(out=eq_[:, :], in_=eq_[:, :], func=AF.Exp), reads=[eq_], writes=[eq_])
        P.op("act", lambda e: e.activation(out=ek_[:, :], in_=ek_[:, :], func=AF.Exp), reads=[ek_], writes=[ek_])
        P.op("act", lambda e: e.activation(out=eb_[:, :], in_=b_[:, :], func=AF.Exp), reads=[b_], writes=[eb_])
        P.op("act", lambda e: e.activation(out=er_[:, :, :], in_=pR[:, :].rearrange("p (t k) -> p t k", k=128), func=AF.Exp),
             reads=[pR], writes=[er_])
        yield
        dc = dec[n]
        P.op("dve", lambda e: e.tensor_copy(out=dc[:, :], in_=eb_[:, :].rearrange("p (c s) -> p c s", s=64)[:, :, 63]),
             reads=[eb_], writes=[dc])
        P.op("dve", lambda e: e.tensor_tensor(out=qtl[n][:, :], in0=qT[n][:, :], in1=eq_[:, :], op=ALU.mult),
             reads=[qT[n], eq_], writes=[qtl[n]])
        P.op("dve", lambda e: e.tensor_tensor(out=ktl[n][:, :], in0=kT_[:, :], in1=ek_[:, :], op=ALU.mult),
             reads=[kT_, ek_], writes=[ktl[n]])
        q4 = qT[n][:, :].rearrange("p (t c s) -> p t c s", c=2, s=64)
        e4 = eb_[:, :].rearrange("p (t c s) -> p t c s", c=2, s=64)
        P.op("dve", lambda e: e.tensor_tensor(out=QP[n][:, :, 0, 0:64], in0=q4[:, :, 0, :], in1=e4[:, :, 0, :], op=ALU.mult),
             reads=[qT[n], eb_], writes=[QP[n]])
        P.op("dve", lambda e: e.tensor_tensor(out=QP[n][:, :, 1, 64:128], in0=q4[:, :, 1, :], in1=e4[:, :, 1, :], op=ALU.mult),
             reads=[qT[n], eb_], writes=[QP[n]])
        P.op("dve", lambda e: e.tensor_tensor(out=KP[n][0:64, :, 0, :], in0=k_[0:64, :, :], in1=er_[0:64, :, :], op=ALU.mult),
             reads=[k_, er_], writes=[KP[n]])
        P.op("dve", lambda e: e.tensor_tensor(out=KP[n][64:128, :, 1, :], in0=k_[64:128, :, :], in1=er_[64:128, :, :], op=ALU.mult),
             reads=[k_, er_], writes=[KP[n]])
        P.op("pool", lambda e: e.tensor_copy(out=vb[n][:, :, :], in_=vt[n][:, :, :]), reads=[vt[n]], writes=[vb[n]])
        for t in range(G):
            P.op("pe", lambda e, t=t: e.matmul(pAT[:, t * 128:(t + 1) * 128], lhsT=ktl[n][:, t * 128:(t + 1) * 128],
                                               rhs=qtl[n][:, t * 128:(t + 1) * 128], start=True, stop=True),
                 reads=[ktl[n], qtl[n]], writes=[pAT])
        P.op("dve", lambda e: e.tensor_tensor(out=ATm[n][:, :, :], in0=pAT[:, :].rearrange("p (t k) -> p t k", k=128),
                                              in1=bc(M1), op=ALU.mult), reads=[pAT, M1], writes=[ATm[n]])

    def h2(gi):
        n = gi % NB
        gs = slice(gi * GT, (gi + 1) * GT)
        dc = dec[n]
        po = p_o[gi % 2]
        sidx = st_["sidx"]
        for t in range(G):
            osl = slice(t * 128, (t + 1) * 128)
            s0 = sidx
            P.op("pe", lambda e, t=t, osl=osl: e.matmul(po[:, osl], lhsT=ATm[n][:, t, :], rhs=vb[n][:, t, :], start=True, stop=False),
                 reads=[ATm[n], vb[n]], writes=[po])
            P.op("pe", lambda e, t=t, osl=osl, s0=s0: e.matmul(po[:, osl], lhsT=QP[n][:, t, 0, :], rhs=Sb[s0 % 4][:, :],
                                                               start=False, stop=False), reads=[QP[n], Sb[s0 % 4]], writes=[po])
            for c in range(2):
                pS = p_S[c]
                P.op("pe", lambda e, t=t, c=c, pS=pS: e.matmul(pS[:, 0:128], lhsT=KP[n][:, t, c, :], rhs=vb[n][:, t, :],
                                                              start=True, stop=True), reads=[KP[n], vb[n]], writes=[pS])
                so, sn = Sf[sidx % 2], Sf[(sidx + 1) % 2]
                P.op("dve", lambda e, so=so, sn=sn, pS=pS, t=t, c=c: e.scalar_tensor_tensor(
                    out=sn[:, :], in0=so[:, :], scalar=dc[:, 2 * t + c:2 * t + c + 1], in1=pS[:, 0:128], op0=ALU.mult, op1=ALU.add),
                    reads=[so, dc, pS], writes=[sn])
                sbn = Sb[(sidx + 1) % 4]
                P.op("act", lambda e, sbn=sbn, sn=sn: e.activation(out=sbn[:, :], in_=sn[:, :], func=AF.Copy),
                     reads=[sn], writes=[sbn])
                sidx += 1
            P.op("pe", lambda e, t=t, osl=osl, s0=s0: e.matmul(po[:, osl], lhsT=QP[n][:, t, 1, :], rhs=Sb[(s0 + 1) % 4][:, :],
                                                               start=False, stop=True), reads=[QP[n], Sb[(s0 + 1) % 4]], writes=[po])
            st_["sidx"] = sidx
            yield
        st_["sidx"] = sidx
        po3 = po[:, :].rearrange("p (t k) -> p t k", k=128)
        P.op("act", lambda e: e.activation(out=sq[n][:, :, :], in_=po3, func=AF.Square), reads=[po], writes=[sq[n]])
        P.op("dve", lambda e: e.reduce_sum(out=ss4[n][:, :], in_=sq[n][:, :, :], axis=AX.X), reads=[sq[n]], writes=[ss4[n]])
        P.op("dve", lambda e: e.tensor_scalar(out=ss4[n][:, :], in0=ss4[n][:, :], scalar1=1.0 / 128, scalar2=EPS,
                                              op0=ALU.mult, op1=ALU.add), reads=[ss4[n]], writes=[ss4[n]])
        P.op("pool", lambda e: e.tensor_tensor(out=rs4[n][:, :], in0=ss4[n][:, :], in1=mh4[:, :], op=ALU.pow),
             reads=[ss4[n], mh4], writes=[rs4[n]])
        y = yo[n]
        P.op("dve", lambda e: e.tensor_tensor(out=y[:, :, :], in0=po3, in1=rs4[n][:, :].unsqueeze(2).to_broadcast([128, G, 128]),
                                              op=ALU.mult), reads=[po, rs4[n]], writes=[y])
        P.op("dve", lambda e: e.tensor_tensor(out=y[:, :, :], in0=y[:, :, :], in1=sg[n][:, :, :], op=ALU.mult),
             reads=[y, sg[n]], writes=[y])
        P.dma("sp", d["y"][gs, :].rearrange("(t p) k -> p t k", p=128), y[:, :, :], src=y)

    for _ in h1(0):
        pass
    for gi in range(ng):
        a = h1(gi + 1) if gi + 1 < ng else iter(())
        b = h2(gi)
        done_a = done_b = False
        while not (done_a and done_b):
            if not done_b:
                try:
                    next(b)
                except StopIteration:
                    done_b = True
            if not done_a:
                try:
                    next(a)
                except StopIteration:
                    done_a = True


def build_test_hgrn(S):
    P = Prog()
    d = {}
    for nm, shp in (("qT", [128, S]), ("zfT", [128, S]), ("zf", [S, 128]), ("v", [S, 128]), ("g", [S, 128]),
                    ("lbT", [128, 4]), ("lmask", [128, 4]), ("lbrow", [128, 4, 128]), ("lmaskrow", [128, 4, 128]),
                    ("gnorm", [128, 128]), ("M1", [128, 128]), ("M2", [128, 128])):
        d[nm] = P.dram(nm, shp, F32, "ExternalInput")
    d["y"] = P.dram("y", [S, 128], F32, "ExternalOutput")
    hgrn_phase(P, S, d)
    print("streams", P.stats(), "sems", P.n_sems)
    return P.finish()


import numpy as np

BIGF = 1.0e6


def nsa_consts_np(S):
    nsel = S // 64
    ncmp = S // 16 - 1
    ncp = ((ncmp + 127) // 128) * 128
    nch = S // 128
    p = np.arange(128)
    D1 = (p[:, None] - p[None, :]).astype(np.float32)
    D16 = (16 * p[:, None] - p[None, :]).astype(np.float32)
    Mdiag = (D1 <= 0).astype(np.float32)
    Mlow = (D1 > 0).astype(np.float32)
    j = np.arange(nsel)
    E = np.zeros((nsel, nch, 128), np.float32)
    for c in range(nch):
        E[:, c, :] = (j[:, None] == (2 * c + p[None, :] // 64))
    n = np.arange(ncp)
    cs = n * 16
    ss = j * 64
    ov = ((cs[:, None] < ss[None, :] + 64) & (cs[:, None] + 32 > ss[None, :]) & (n[:, None] < ncmp)).astype(np.float32)
    x = np.arange(2 * nsel) - nsel
    cur_rel = (p >= 64).astype(np.int64)
    forced = (x[None, :] == cur_rel[:, None]) | (x[None, :] == cur_rel[:, None] - 1)
    invalid = x[None, :] > cur_rel[:, None]
    Amul = (~forced & ~invalid).astype(np.float32)
    Aadd = np.where(forced, BIGF, np.where(invalid, -BIGF, 0.0)).astype(np.float32)
    ident = np.eye(128, dtype=np.float32)
    return dict(D16=D16, Mdiag=Mdiag, Mlow=Mlow, E=E, ov=ov, Amul=Amul, Aadd=Aadd, identb=ident)


def nsa_phase(P, S, NQB, d):
    nsel = S // 64
    ncmp = S // 16 - 1
    ncp = ((ncmp + 127) // 128) * 128
    ncc = ncp // 128
    nch = S // 128
    W = 65 + nsel

    def cload(name, shape, dt, src, q="pool"):
        b = P.sb(shape, dt, name)
        P.dma(q, b[tuple(slice(None) for _ in shape)], src, dst=b)
        return b

    D16 = cload("D16", [128, 128], F32, d["D16"][:, :], "sp")
    Mdiag = cload("Mdiag", [128, 128], BF16, d["Mdiag"][:, :])
    Mlow = cload("Mlow", [128, 128], BF16, d["Mlow"][:, :])
    Ec = cload("E", [nsel, nch, 128], BF16, d["E"][:, :, :])
    Amul = cload("Amul", [128, 2 * nsel], F32, d["Amul"][:, :], "sp")
    Aadd = cload("Aadd", [128, 2 * nsel], F32, d["Aadd"][:, :], "sp")
    identb = cload("identb", [128, 128], BF16, d["identb"][:, :])
    ksT = P.sb([128, S], BF16, "ksT")
    kwT = P.sb([128, S], BF16, "kwT")
    for kb, nm in ((ksT, "ksT"), (kwT, "kwT")):
        P.op("pool", lambda e, kb=kb: e.memset(kb[64:128, :], 0.0), writes=[kb])
        P.dma("pool", kb[0:64, :], d[nm][:, :], dst=kb)
    vs = cload("vs", [128, nch, 65], BF16, d["vs"][:, :, :])
    vw = cload("vw", [128, nch, 65], BF16, d["vw"][:, :, :])
    KcT = P.sb([128, ncp], BF16, "KcT")
    Vc = P.sb([128, ncc, W], BF16, "Vc")
    P.op("pool", lambda e: e.memset(KcT[:, :], 0.0), writes=[KcT])
    P.op("pool", lambda e: e.memset(Vc[:, :, :], 0.0), writes=[Vc])
    P.op("pool", lambda e: e.memset(Vc[:, :, 64:65], 1.0), writes=[Vc])
    P.dma("pool", Vc[:, :, 65:W], d["ov"][:, :, :], dst=Vc)

    ps_h = [P.ps([128, 512], F32, "ps_h") for _ in range(2)]
    ps_b = P.ps([128, 512], F32, "ps_b")
    pOs = P.ps([128, 512], F32, "pOs")
    ps_o = pOs
    for X in range(2):
        x2T = cload("x2T", [128, S], BF16, d["kc2T" if X == 0 else "vc2T"][:, :])
        pe2 = cload("pe2", [128, 16], BF16, d["pe2"][X])
        w1 = cload("w1", [128, 16, 256], BF16, d["w1"][X])
        w2 = cload("w2", [128, 2, 64], BF16, d["w2"][X])
        GT = P.sb([128, 2, ncp], BF16, "GT")
        P.op("pool", lambda e, GT=GT: e.memset(GT[:, :, :], 0.0), writes=[GT])
        x2v = x2T[:, :].rearrange("p (n s) -> p n s", s=16)
        for hc in range(2):
            for j in range(16):
                P.op("pe", lambda e, j=j, hc=hc, w1=w1, pe2=pe2: e.matmul(
                    ps_b[:, 0:1], lhsT=w1[:, j, hc * 128:(hc + 1) * 128], rhs=pe2[:, j:j + 1],
                    start=(j == 0), stop=(j == 15)), reads=[w1, pe2], writes=[ps_b])
            c1 = P.sb([128, 1], F32, "c1")
            P.op("dve", lambda e, c1=c1: e.tensor_copy(out=c1[:, :], in_=ps_b[:, 0:1]), reads=[ps_b], writes=[c1])
            ph = ps_h[hc]
            for j in range(16):
                if j < 8:
                    rhs = x2v[:, 0:ncmp, 2 * j]
                else:
                    rhs = x2v[:, 1:ncmp + 1, 2 * j - 16]
                P.op("pe", lambda e, j=j, hc=hc, w1=w1, rhs=rhs, ph=ph: e.matmul(
                    ph[:, 0:ncmp], lhsT=w1[:, j, hc * 128:(hc + 1) * 128], rhs=rhs,
                    start=(j == 0), stop=(j == 15)), reads=[w1, x2T], writes=[ph])
            xs = P.sb([128, ncp], F32, "xs")
            P.op("act", lambda e, xs=xs, ph=ph, c1=c1: e.activation(out=xs[:, 0:ncmp], in_=ph[:, 0:ncmp],
                                                                     func=AF.Identity, bias=c1[:, :]),
                 reads=[ph, c1], writes=[xs])
            t2 = P.sb([128, ncp], F32, "t2")
            P.op("dve", lambda e, xs=xs, t2=t2: e.tensor_tensor(out=t2[:, 0:ncmp], in0=xs[:, 0:ncmp], in1=xs[:, 0:ncmp],
                                                                op=ALU.mult), reads=[xs], writes=[t2])
            P.op("dve", lambda e, t2=t2: e.tensor_scalar(out=t2[:, 0:ncmp], in0=t2[:, 0:ncmp], scalar1=0.044715,
                                                         scalar2=1.0, op0=ALU.mult, op1=ALU.add), reads=[t2], writes=[t2])
            P.op("dve", lambda e, xs=xs, t2=t2: e.tensor_tensor(out=t2[:, 0:ncmp], in0=t2[:, 0:ncmp], in1=xs[:, 0:ncmp],
                                                                op=ALU.mult), reads=[xs, t2], writes=[t2])
            P.op("act", lambda e, t2=t2: e.activation(out=t2[:, 0:ncmp], in_=t2[:, 0:ncmp], func=AF.Sigmoid,
                                                      scale=1.5957691216057308), reads=[t2], writes=[t2])
            P.op("dve", lambda e, xs=xs, t2=t2, GT=GT, hc=hc: e.tensor_tensor(
                out=GT[:, hc, 0:ncmp], in0=t2[:, 0:ncmp], in1=xs[:, 0:ncmp], op=ALU.mult), reads=[xs, t2], writes=[GT])
        if X == 0:
            for hc in range(2):
                P.op("pe", lambda e, hc=hc, w2=w2, GT=GT: e.matmul(ps_o[0:64, 0:ncp], lhsT=w2[:, hc, :], rhs=GT[:, hc, :],
                                                                    start=(hc == 0), stop=(hc == 1)),
                     reads=[w2, GT], writes=[ps_o])
            P.op("act", lambda e: e.activation(out=KcT[0:64, 0:ncmp], in_=ps_o[0:64, 0:ncmp], func=AF.Copy),
                 reads=[ps_o], writes=[KcT])
        else:
            for c in range(ncc):
                for hc in range(2):
                    P.op("pe", lambda e, hc=hc, c=c, w2=w2, GT=GT: e.matmul(
                        ps_o[:, 0:64], lhsT=GT[:, hc, c * 128:(c + 1) * 128], rhs=w2[:, hc, :],
                        start=(hc == 0), stop=(hc == 1)), reads=[w2, GT], writes=[ps_o])
                P.op("act", lambda e, c=c: e.activation(out=Vc[:, c, 0:64], in_=ps_o[:, 0:64], func=AF.Copy),
                     reads=[ps_o], writes=[Vc])

    pS = [ps_h[0], ps_h[1], P.ps([128, 512], F32, "pS2")]
    pM = ps_b
    pOc = [P.ps([128, 512], F32, "pOc") for _ in range(2)]
    pOw = P.ps([128, 512], F32, "pOw")
    WMk = cload("WM", [128, 8, 128], BF16, d["WM"][:, :, :])
    Tt = cload("Tt", [128, NQB, ncc], F32, d["Tt"][:, :, :], "sp")
    identf = cload("identf", [128, 128], F32, d["identb"][:, :], "sp")
    QTb = [P.sb([128, 512], BF16, "QT") for _ in range(2)]
    for qb_ in QTb:
        P.op("pool", lambda e, qb_=qb_: e.memset(qb_[64:128, :], 0.0), writes=[qb_])
    gl_all = P.sb([128, NQB, 12], F32, "gl_all")
    P.dma("sp", gl_all[:, :, :], d["gl"][:, :, :], dst=gl_all)
    P.op("act", lambda e: e.activation(out=gl_all[:, :, :], in_=gl_all[:, :, :], func=AF.Sigmoid), reads=[gl_all], writes=[gl_all])
    NPT = 6
    PT = [P.sb([128, 4, 128], BF16, "PT") for _ in range(NPT)]
    PTm = [P.sb([128, 4, 128], BF16, "PTm") for _ in range(NPT)]
    mk = [P.sb([128, 128], BF16, "mk") for _ in range(3)]
    rd = [P.sb([128, 4], F32, "rd") for _ in range(3)]
    wgt = [P.sb([128, 4], F32, "wgt") for _ in range(3)]
    imp = P.sb([128, nsel], F32, "imp")
    sc = P.sb([128, nsel], F32, "sc")
    sc2 = P.sb([128, nsel], F32, "sc2")
    m8a = P.sb([128, 8], F32, "m8a")
    m8b = P.sb([128, 8], F32, "m8b")
    self_f = P.sb([128, nsel], F32, "self")
    selT = [P.sb([nsel, 128], BF16, "selT") for _ in range(2)]
    yacc = [P.sb([128, 4, 64], F32, "yacc") for _ in range(2)]
    state = {"pt": 0, "ps": 0, "mk": 0, "pm": 0}

    class Task:
        pass

    def chunk_task(kT, c, QT, mask_kind, mask_arg, vaug, vw_cols, pouts, first, last, selTb=None):
        t = Task()
        ps = pS[state["ps"] % 3]
        state["ps"] += 1
        pt = PT[state["pt"] % NPT]
        ptm = PTm[state["pt"] % NPT]
        state["pt"] += 1

        def s1():
            P.op("pe", lambda e: e.matmul(ps[:, :], lhsT=kT[:, c * 128:(c + 1) * 128], rhs=QT[:, :], start=True, stop=True),
                 reads=[kT, QT], writes=[ps])

        def s2():
            P.op("act", lambda e: e.activation(out=pt[:, :, :], in_=ps[:, :].rearrange("p (h q) -> p h q", h=4),
                                               func=AF.Exp, scale=0.125), reads=[ps], writes=[pt])
            src = pt
            mbuf = None
            if mask_kind == "none":
                pass
            elif mask_kind == "sb":
                mbuf, map_ = mask_arg
            elif mask_kind == "cmp":
                m_i, cc = mask_arg
                mbuf = mk[state["mk"] % 3]
                state["mk"] += 1
                P.op("dve", lambda e, mb=mbuf: e.tensor_scalar(out=mb[:, :], in0=D16[:, :], scalar1=Tt[:, m_i, cc:cc + 1],
                                                               scalar2=None, op0=ALU.is_le), reads=[D16, Tt], writes=[mbuf])
                map_ = mbuf[:, :]
            elif mask_kind == "slc":
                tail = mask_arg
                off = 256 * (state["pm"] % 2)
                state["pm"] += 1
                P.op("pe", lambda e: e.matmul(pM[0:128, off:off + 128], lhsT=Ec[:, c, :], rhs=selTb[:, :], start=True, stop=True),
                     reads=[Ec, selTb], writes=[pM])
                if tail is None:
                    mbuf, map_ = pM, pM[:, off:off + 128]
                else:
                    mbuf = mk[state["mk"] % 3]
                    state["mk"] += 1
                    P.op("dve", lambda e, mb=mbuf: e.tensor_tensor(out=mb[:, :], in0=pM[:, off:off + 128], in1=WMk[:, tail, :],
                                                                   op=ALU.mult), reads=[pM, WMk], writes=[mbuf])
                    map_ = mbuf[:, :]
            if mbuf is not None:
                P.op("dve", lambda e: e.tensor_tensor(out=ptm[:, :, :], in0=pt[:, :, :],
                                                      in1=map_.unsqueeze(1).to_broadcast([128, 4, 128]), op=ALU.mult),
                     reads=[pt, mbuf], writes=[ptm])
                src = ptm
            t.src = src

        def s3():
            src = t.src
            for pb, heads in pouts:
                for hi, h in enumerate(heads):
                    st = first and hi == 0
                    P.op("pe", lambda e, pb=pb, hi=hi, h=h, st=st: e.matmul(
                        pb[:, hi * vw_cols:(hi + 1) * vw_cols], lhsT=src[:, h, :], rhs=vaug[:, c, 0:vw_cols],
                        start=st, stop=last, skip_group_check=True), reads=[src, vaug], writes=[pb])
        t.s1, t.s2, t.s3 = s1, s2, s3
        return t

    def marker(fn):
        t = Task()
        t.s1 = lambda: None
        t.s2 = lambda: None
        t.s3 = fn
        return t

    tasks = []
    for m_i in range(NQB):
        qbm = 2 * m_i + 1
        QT = QTb[m_i % 2]
        gl = gl_all
        glv = gl_all[:, m_i, :].rearrange("p (h t) -> p h t", t=3)
        ya = yacc[m_i % 2]
        selTb = selT[m_i % 2]

        def load(m_i=m_i, QT=QT, gl=gl):
            P.dma("pool", QT[0:64, :], d["QT"][:, m_i, :], dst=QT)
        ld = marker(lambda: None)
        ld.s1 = load
        tasks.append(ld)
        nvalid = min(8 * qbm + 7, ncmp)
        nchunks = (nvalid + 127) // 128
        for c in range(nchunks):
            Tmin = 128 * (qbm - 1) - 2048 * c - 31
            kind, arg = ("none", None) if Tmin >= 16 * 127 else ("cmp", (m_i, c))
            tasks.append(chunk_task(KcT, c, QT, kind, arg, Vc, W, [(pOc[0], [0, 1]), (pOc[1], [2, 3])], c == 0, c == nchunks - 1))

        def after_cmp(m_i=m_i, glv=glv, gl=gl, ya=ya, selTb=selTb):
            r_, w_ = rd[0], wgt[0]
            for g2 in range(2):
                ov_ = pOc[g2][:, 0:2 * W].rearrange("p (h w) -> p h w", h=2)
                P.op("dve", lambda e, ov_=ov_, g2=g2: e.tensor_scalar(out=r_[:, 2 * g2:2 * g2 + 2], in0=ov_[:, :, 64],
                                                                      scalar1=1e-30, scalar2=None, op0=ALU.max),
                     reads=[pOc[g2]], writes=[r_])
            P.op("dve", lambda e: e.reciprocal(out=r_[:, :], in_=r_[:, :]), reads=[r_], writes=[r_])
            for h in range(4):
                pb = pOc[h // 2]
                base = (h % 2) * W
                if h == 0:
                    P.op("dve", lambda e, pb=pb, base=base, h=h: e.tensor_scalar(
                        out=imp[:, :], in0=pb[:, base + 65:base + W], scalar1=r_[:, h:h + 1], scalar2=None, op0=ALU.mult),
                        reads=[pb, r_], writes=[imp])
                else:
                    P.op("dve", lambda e, pb=pb, base=base, h=h: e.scalar_tensor_tensor(
                        out=imp[:, :], in0=pb[:, base + 65:base + W], scalar=r_[:, h:h + 1], in1=imp[:, :],
                        op0=ALU.mult, op1=ALU.add), reads=[pb, r_, imp], writes=[imp])
            x0 = nsel - 4 * m_i
            P.op("dve", lambda e: e.tensor_tensor(out=sc[:, :], in0=imp[:, :], in1=Amul[:, x0:x0 + nsel], op=ALU.mult),
                 reads=[imp, Amul], writes=[sc])
            P.op("dve", lambda e: e.tensor_tensor(out=sc[:, :], in0=sc[:, :], in1=Aadd[:, x0:x0 + nsel], op=ALU.add),
                 reads=[sc, Aadd], writes=[sc])
            P.op("dve", lambda e: e.memset(sc[:, 0:1], BIGF), writes=[sc])
            P.op("dve", lambda e: e.max(out=m8a[:, :], in_=sc[:, :]), reads=[sc], writes=[m8a])
            P.op("dve", lambda e: e.match_replace(out=sc2[:, :], in_to_replace=m8a[:, :], in_values=sc[:, :],
                                                  imm_value=-3.0 * BIGF), reads=[sc, m8a], writes=[sc2])
            P.op("dve", lambda e: e.max(out=m8b[:, :], in_=sc2[:, :]), reads=[sc2], writes=[m8b])
            P.op("dve", lambda e: e.tensor_scalar(out=self_f[:, :], in0=sc[:, :], scalar1=m8b[:, 7:8], scalar2=None,
                                                  op0=ALU.is_ge), reads=[sc, m8b], writes=[self_f])
            P.op("pe", lambda e: e.transpose(out=pM[0:nsel, 128:256], in_=self_f[:, :], identity=identf[:, :]),
                 reads=[self_f, identf], writes=[pM])
            P.op("act", lambda e: e.activation(out=selTb[:, :], in_=pM[0:nsel, 128:256], func=AF.Copy), reads=[pM], writes=[selTb])
            P.op("dve", lambda e: e.tensor_tensor(out=w_[:, :], in0=r_[:, :], in1=glv[:, :, 0], op=ALU.mult),
                 reads=[r_, gl], writes=[w_])
            for h in range(4):
                pb = pOc[h // 2]
                base = (h % 2) * W
                P.op("dve", lambda e, pb=pb, base=base, h=h: e.tensor_scalar(
                    out=ya[:, h, :], in0=pb[:, base:base + 64], scalar1=w_[:, h:h + 1], scalar2=None, op0=ALU.mult),
                    reads=[pb, w_], writes=[ya])
        tasks.append(marker(after_cmp))
        wl = [c for c in range(2 * m_i - 4, 2 * m_i + 2) if c >= 0]
        for c in wl:
            r = c - (2 * m_i - 4)
            tasks.append(chunk_task(kwT, c, QT, "sb", (WMk, WMk[:, r, :]), vw, 65, [(pOw, [0, 1, 2, 3])], c == wl[0], c == wl[-1]))
        for c in range(0, 2 * m_i + 2):
            tail = (6 + c - 2 * m_i) if c >= 2 * m_i else None
            tasks.append(chunk_task(ksT, c, QT, "slc", tail, vs, 65, [(pOs, [0, 1, 2, 3])], c == 0, c == 2 * m_i + 1, selTb=selTb))

        def combine(m_i=m_i, glv=glv, gl=gl, ya=ya):
            for bi, pb in ((2, pOw), (1, pOs)):
                r_, w_ = rd[bi], wgt[bi]
                pv = pb[:, 0:260].rearrange("p (h w) -> p h w", h=4)
                P.op("dve", lambda e, pv=pv, r_=r_: e.tensor_scalar(out=r_[:, :], in0=pv[:, :, 64], scalar1=1e-30, scalar2=None,
                                                                    op0=ALU.max), reads=[pb], writes=[r_])
                P.op("dve", lambda e, r_=r_: e.reciprocal(out=r_[:, :], in_=r_[:, :]), reads=[r_], writes=[r_])
                P.op("dve", lambda e, r_=r_, w_=w_, bi=bi: e.tensor_tensor(out=w_[:, :], in0=r_[:, :], in1=glv[:, :, bi], op=ALU.mult),
                     reads=[r_, gl], writes=[w_])
                for h in range(4):
                    P.op("dve", lambda e, pb=pb, h=h, w_=w_: e.scalar_tensor_tensor(
                        out=ya[:, h, :], in0=pb[:, h * 65:h * 65 + 64], scalar=w_[:, h:h + 1], in1=ya[:, h, :],
                        op0=ALU.mult, op1=ALU.add), reads=[pb, w_, ya], writes=[ya])
            P.dma("sp", d["y"][m_i], ya[:, :, :].rearrange("p h w -> p (h w)"), src=ya)
        tasks.append(marker(combine))

    DEPTH = 3
    n = len(tasks)
    for i in range(min(DEPTH, n)):
        tasks[i].s1()
    for i in range(n):
        tasks[i].s2()
        if i + DEPTH < n:
            tasks[i + DEPTH].s1()
        tasks[i].s3()


NSA_IN = lambda S, NQB: [
    ("QT", [64, NQB, 512]), ("gl", [128, NQB, 12]), ("ksT", [64, S]), ("kwT", [64, S]), ("vs", [128, S // 128, 65]), ("vw", [128, S // 128, 65]),
    ("kc2T", [128, S]), ("vc2T", [128, S]), ("pe2", [2, 128, 16]), ("w1", [2, 128, 16, 256]), ("w2", [2, 128, 2, 64]),
    ("D16", [128, 128]), ("Mdiag", [128, 128]), ("Mlow", [128, 128]), ("E", [S // 64, S // 128, 128]),
    ("ov", [128, ((S // 16 - 1 + 127) // 128), S // 64]), ("Amul", [128, 2 * (S // 64)]), ("Aadd", [128, 2 * (S // 64)]),
    ("identb", [128, 128]), ("WM", [128, 8, 128]), ("Tt", [128, NQB, ((S // 16 - 1 + 127) // 128)])]


def build_test_nsa(S, NQB):
    P = Prog()
    d = {}
    for nm, shp in NSA_IN(S, NQB):
        d[nm] = P.dram(nm, shp, F32, "ExternalInput")
    d["y"] = P.dram("y", [NQB, 128, 256], F32, "ExternalOutput")
    nsa_phase(P, S, NQB, d)
    print("streams", P.stats(), "sems", P.n_sems)
    return P.finish()


import math
import numpy as np

NT = 2048
NTILE = NT // 128

A_OPS = ([("copy", 1)] * 4 + [("prod", 2)] * 4 + [("silu", 1)] * 4 + [("copy", 1)] * 12 +
         [("rope", 2)] * 4 + [("rope", 2)] * 3 + [("copy", 1)] * 3 + [("gate", 1)])
A_NOUT = len(A_OPS)
A_NIN = sum(n for _, n in A_OPS)
A_COLS = (A_NIN - 1) * 128 + 24
A_ROWS = (A_NOUT - 1) * 128 + 24


def a_col_perm():
    BR = 512
    o = {}
    names = ["ab", "ac", "ax", "hq", "hf", "hi", "hg", "nq", "nkc", "nvc", "nks", "nvs", "nkw", "nvw", "ng", "mg"]
    sizes = [512] * 3 + [512] * 4 + [512] + [128] * 6 + [24, 3072]
    off = 0
    for n, s in zip(names, sizes):
        o[n] = np.arange(off, off + s)
        off += s

    def swap64(ix):
        ix = ix.reshape(-1, 2, 32)
        return ix[:, ::-1, :].reshape(-1)
    cols = []
    for c in range(4):
        cols.append(o["ab"][c * 128:(c + 1) * 128])
    for c in range(4):
        cols.append(o["ac"][c * 128:(c + 1) * 128])
        cols.append(o["ax"][c * 128:(c + 1) * 128])
    for n in ("hq", "hf", "hi", "hg"):
        for c in range(4):
            cols.append(o[n][c * 128:(c + 1) * 128])
    for c in range(4):
        ix = o["nq"][c * 128:(c + 1) * 128]
        cols.append(ix)
        cols.append(swap64(ix))
    for n in ("nkc", "nks", "nkw"):
        cols.append(o[n])
        cols.append(swap64(o[n]))
    for n in ("nvc", "nvs", "nvw"):
        cols.append(o[n])
    cols.append(o["ng"])
    return np.concatenate(cols), o["mg"]


def rope_tables(P, pos_d, inv_d, sgn_d, n):
    posi = P.sb([128, n], I32, "posi")
    P.dma("sp", posi[:, :], pos_d[:, :], dst=posi)
    inv = P.sb([128, 1], F32, "inv")
    sgn = P.sb([128, 1], F32, "sgn")
    P.dma("sp", inv[:, :], inv_d[:, :], dst=inv)
    P.dma("sp", sgn[:, :], sgn_d[:, :], dst=sgn)
    ang = P.sb([128, n], F32, "ang")
    P.op("dve", lambda e: e.tensor_copy(out=ang[:, :], in_=posi[:, :]), reads=[posi], writes=[ang])
    P.op("dve", lambda e: e.tensor_scalar(out=ang[:, :], in0=ang[:, :], scalar1=inv[:, :], scalar2=None, op0=ALU.mult),
         reads=[ang, inv], writes=[ang])
    qf = P.sb([128, n], F32, "qf")
    P.op("dve", lambda e: e.tensor_scalar(out=qf[:, :], in0=ang[:, :], scalar1=1.0 / (2 * math.pi), scalar2=None,
                                          op0=ALU.mult), reads=[ang], writes=[qf])
    qi = P.sb([128, n], I32, "qi")
    P.op("dve", lambda e: e.tensor_copy(out=qi[:, :], in_=qf[:, :]), reads=[qf], writes=[qi])
    P.op("dve", lambda e: e.tensor_copy(out=qf[:, :], in_=qi[:, :]), reads=[qi], writes=[qf])
    C1, C2 = 6.28125, 2 * math.pi - 6.28125
    C2a = float(np.float32(C2))
    C3 = C2 - C2a
    for Cc in (C1, C2a, C3):
        P.op("dve", lambda e, Cc=Cc: e.scalar_tensor_tensor(out=ang[:, :], in0=qf[:, :], scalar=-Cc, in1=ang[:, :],
                                                            op0=ALU.mult, op1=ALU.add), reads=[qf, ang], writes=[ang])
    m = qf
    P.op("dve", lambda e: e.tensor_scalar(out=m[:, :], in0=ang[:, :], scalar1=math.pi, scalar2=-2 * math.pi,
                                          op0=ALU.is_gt, op1=ALU.mult), reads=[ang], writes=[m])
    P.op("dve", lambda e: e.tensor_tensor(out=ang[:, :], in0=ang[:, :], in1=m[:, :], op=ALU.add), reads=[ang, m], writes=[ang])
    P.op("dve", lambda e: e.tensor_scalar(out=m[:, :], in0=ang[:, :], scalar1=-math.pi, scalar2=2 * math.pi,
                                          op0=ALU.is_lt, op1=ALU.mult), reads=[ang], writes=[m])
    P.op("dve", lambda e: e.tensor_tensor(out=ang[:, :], in0=ang[:, :], in1=m[:, :], op=ALU.add), reads=[ang, m], writes=[ang])
    P.op("dve", lambda e: e.tensor_scalar(out=ang[:, :], in0=ang[:, :], scalar1=3.14159, scalar2=-3.14159,
                                          op0=ALU.min, op1=ALU.max), reads=[ang], writes=[ang])
    SINS = P.sb([128, n], F32, "SINS")
    COS = P.sb([128, n], F32, "COS")
    P.op("act", lambda e: e.activation(out=SINS[:, :], in_=ang[:, :], func=AF.Sin), reads=[ang], writes=[SINS])
    P.op("dve", lambda e: e.tensor_scalar(out=SINS[:, :], in0=SINS[:, :], scalar1=sgn[:, :], scalar2=None, op0=ALU.mult),
         reads=[SINS, sgn], writes=[SINS])
    P.op("dve", lambda e: e.tensor_scalar(out=m[:, :], in0=ang[:, :], scalar1=-1.0, scalar2=None, op0=ALU.mult),
         reads=[ang], writes=[m])
    P.op("dve", lambda e: e.tensor_tensor(out=ang[:, :], in0=ang[:, :], in1=m[:, :], op=ALU.max), reads=[ang, m], writes=[ang])
    P.op("dve", lambda e: e.tensor_scalar(out=ang[:, :], in0=ang[:, :], scalar1=-1.0, scalar2=math.pi / 2, op0=ALU.mult,
                                          op1=ALU.add), reads=[ang], writes=[ang])
    P.op("act", lambda e: e.activation(out=COS[:, :], in_=ang[:, :], func=AF.Sin), reads=[ang], writes=[COS])
    return COS, SINS


def make_h2T(P, C, x_src_d, g2, h2T, tp, ntile):
    xt = [P.sb([128, D], F32, "x1t") for _ in range(2)]
    hb = [P.sb([128, D], BF16, "h2") for _ in range(2)]
    for i in range(ntile):
        xb = xt[i % 2]
        P.dma("sp", xb[:, :], x_src_d[i * 128:(i + 1) * 128, :], dst=xb)
        rstd = rms_stats(P, C, xb, xb[:, :], False)
        h = hb[i % 2]
        P.op("dve", lambda e, h=h, xb=xb, rstd=rstd: e.scalar_tensor_tensor(
            out=h[:, :], in0=xb[:, :], scalar=rstd[:, :], in1=g2[:, :], op0=ALU.mult, op1=ALU.mult),
            reads=[xb, rstd, g2], writes=[h])
        for k in range(8):
            P.op("pe", lambda e, h=h, k=k: e.transpose(out=tp[:, k * 128:(k + 1) * 128],
                                                      in_=h[:, k * 128:(k + 1) * 128], identity=C.ident[:, :]),
                 reads=[h, C.ident], writes=[tp])
        P.op("act", lambda e, i=i: e.activation(out=h2T[:, :, i * 128:(i + 1) * 128],
                                                in_=tp[:, :].rearrange("p (k n) -> p k n", k=8), func=AF.Copy),
             reads=[tp], writes=[h2T])


def body_A(P, C, x_d, pfx):
    gains_d = P.dram(pfx + "gains", [3, 128, D], F32, "ExternalInput")
    wg_d = P.dram(pfx + "wg", [D, DFF], F32, "ExternalInput")
    wu_d = P.dram(pfx + "wu", [D, DFF], F32, "ExternalInput")
    wd_d = P.dram(pfx + "wd", [DFF, D], F32, "ExternalInput")
    wa_d = P.dram(pfx + "wa", [D, A_COLS], F32, "ExternalInput")
    pos_d = P.dram(pfx + "pos", [128, NT], I32, "ExternalInput")
    inv_d = P.dram(pfx + "inv", [128, 1], F32, "ExternalInput")
    sgn_d = P.dram(pfx + "sgn", [128, 1], F32, "ExternalInput")
    x1_d = P.dram(pfx + "x1", [NT, D], F32, "ExternalOutput")
    fm_d = P.dram(pfx + "fm", [A_ROWS, NT], F32, "ExternalOutput")
    g = []
    for i in range(3):
        b = P.sb([128, D], F32, "gain")
        P.dma("sp", b[:, :], gains_d[i], dst=b)
        g.append(b)
    with P.scope():
        wg, wu, wd = load_ffn_weights(P, wg_d, wu_d, wd_d)
        xpool = [P.sb([128, D], F32, "x") for _ in range(4)]

        def get_x(i):
            b = xpool[i % 4]
            P.dma("sp", b[:, :], x_d[i * 128:(i + 1) * 128, :], dst=b)
            return b

        def put_x(i, b):
            P.dma("sp", x1_d[i * 128:(i + 1) * 128, :], b[:, :], src=b)

        ffn_phase(P, C, NTILE, get_x, put_x, g[0], g[1], wg, wu, wd)
    with P.scope():
        COS, SINS = rope_tables(P, pos_d, inv_d, sgn_d, NT)
        h2T = P.sb([128, 8, NT], BF16, "h2T")
        tp = P.ps([128, 8 * 128], BF16, "tp")
        make_h2T(P, C, x1_d, g[2], h2T, tp, NTILE)
        wsrc = wa_d.rearrange("(c p) n -> p c n", p=128)
        wbufs = [P.sb([128, 8, 512], BF16, "wa") for _ in range(3)]
        pp = [P.ps([128, 512], F32, "pp") for _ in range(4)]
        stg = [P.sb([128, 512], F32, "stg") for _ in range(4)]
        tmp = [P.sb([128, 512], F32, "tmp") for _ in range(2)]
        nblk = (A_NIN + 3) // 4
        ops = []
        ic = 0
        for oc, (kind, nin) in enumerate(A_OPS):
            ops.append((oc, kind, list(range(ic, ic + nin))))
            ic += nin
        loaded = {}
        state = {"pp": 0, "stg": 0, "tmp": 0}

        def get_w(chunk):
            blk = chunk // 4
            _load(blk)
            if blk + 1 < nblk:
                _load(blk + 1)
            wb = loaded[blk]
            off = (chunk % 4) * 128
            return wb, off

        def _load(blk):
            if blk not in loaded:
                wb = wbufs[blk % 3]
                c0 = blk * 512
                c1 = min(A_COLS, c0 + 512)
                P.dma("pool", wb[:, :, 0:c1 - c0], wsrc[:, :, c0:c1], dst=wb)
                loaded[blk] = wb

        for oc, kind, chunks in ops:
            rows = 24 if kind == "gate" else 128
            for tg in range(NT // 512):
                tsl = slice(tg * 512, (tg + 1) * 512)
                pts = []
                for ch in chunks:
                    wb, off = get_w(ch)
                    pt = pp[state["pp"] % 4]
                    state["pp"] += 1
                    for k in range(8):
                        P.op("pe", lambda e, pt=pt, wb=wb, off=off, k=k, rows=rows, tsl=tsl: e.matmul(
                            pt[0:rows, :], lhsT=wb[:, k, off:off + rows], rhs=h2T[:, k, tsl], start=(k == 0), stop=(k == 7)),
                            reads=[wb, h2T], writes=[pt])
                    pts.append(pt)
                sb = stg[state["stg"] % 4]
                state["stg"] += 1
                if kind in ("copy", "gate"):
                    if oc % 2 == 0:
                        P.op("act", lambda e, sb=sb, pt=pts[0], rows=rows: e.activation(out=sb[0:rows, :], in_=pt[0:rows, :],
                                                                                        func=AF.Copy), reads=[pts[0]], writes=[sb])
                    else:
                        P.op("dve", lambda e, sb=sb, pt=pts[0], rows=rows: e.tensor_copy(out=sb[0:rows, :], in_=pt[0:rows, :]),
                             reads=[pts[0]], writes=[sb])
                elif kind == "silu":
                    P.op("act", lambda e, sb=sb, pt=pts[0]: e.activation(out=sb[:, :], in_=pt[:, :], func=AF.Silu),
                         reads=[pts[0]], writes=[sb])
                elif kind == "prod":
                    t = tmp[state["tmp"] % 2]
                    state["tmp"] += 1
                    P.op("act", lambda e, t=t, pt=pts[0]: e.activation(out=t[:, :], in_=pt[:, :], func=AF.Copy),
                         reads=[pts[0]], writes=[t])
                    P.op("dve", lambda e, sb=sb, t=t, pt=pts[1]: e.tensor_tensor(out=sb[:, :], in0=t[:, :], in1=pt[:, :],
                                                                                 op=ALU.mult), reads=[t, pts[1]], writes=[sb])
                elif kind == "rope":
                    t = tmp[state["tmp"] % 2]
                    state["tmp"] += 1
                    P.op("dve", lambda e, t=t, pt=pts[0], tsl=tsl: e.tensor_tensor(out=t[:, :], in0=pt[:, :], in1=COS[:, tsl],
                                                                                   op=ALU.mult), reads=[pts[0], COS], writes=[t])
                    P.op("dve", lambda e, sb=sb, pt=pts[1], tsl=tsl: e.tensor_tensor(out=sb[:, :], in0=pt[:, :], in1=SINS[:, tsl],
                                                                                     op=ALU.mult), reads=[pts[1], SINS], writes=[sb])
                    P.op("pool", lambda e, sb=sb, t=t: e.tensor_tensor(out=sb[:, :], in0=sb[:, :], in1=t[:, :], op=ALU.add),
                         reads=[sb, t], writes=[sb])
                P.dma("sp", fm_d[oc * 128:oc * 128 + rows, tsl], sb[0:rows, :], src=sb)


def build_A():
    P = Prog()
    ident_d = P.dram("ident", [128, 128], F32, "ExternalInput")
    x_d = P.dram("x", [NT, D], F32, "ExternalInput")
    C = build_consts(P, ident_d)
    with P.scope():
        body_A(P, C, x_d, "")
    print("A streams", P.stats(), "sems", P.n_sems)
    return P.finish()


def body_C(P, C, out_d, pfx):
    x1_d = P.dram(pfx + "x1", [NT, D], F32, "ExternalInput")
    gains_d = P.dram(pfx + "gains", [4, 128, D], F32, "ExternalInput")
    vT_d = P.dram(pfx + "vT", [512, NT + 2], F32, "ExternalInput")
    bT_d = P.dram(pfx + "bT", [512, NT], F32, "ExternalInput")
    ybT_d = P.dram(pfx + "ybT", [512, NT], F32, "ExternalInput")
    ycT_d = P.dram(pfx + "ycT", [512, NT], F32, "ExternalInput")
    cw_d = P.dram(pfx + "cw", [128, 4, 3], F32, "ExternalInput")
    wmg_d = P.dram(pfx + "wmg", [D, 3 * D], F32, "ExternalInput")
    wbr_d = P.dram(pfx + "wbr", [3 * 512, D], F32, "ExternalInput")
    wout_d = P.dram(pfx + "wout", [D, D], F32, "ExternalInput")
    wg_d = P.dram(pfx + "wg", [D, DFF], F32, "ExternalInput")
    wu_d = P.dram(pfx + "wu", [D, DFF], F32, "ExternalInput")
    wd_d = P.dram(pfx + "wd", [DFF, D], F32, "ExternalInput")
    x2_d = P.dram(pfx + "x2s", [NT, D], F32, "Internal")
    g = []
    for i in range(4):
        b = P.sb([128, D], F32, "gain")
        P.dma("sp", b[:, :], gains_d[i], dst=b)
        g.append(b)
    with P.scope():
        cw = P.sb([128, 4, 3], F32, "cw")
        P.dma("sp", cw[:, :, :], cw_d[:, :, :], dst=cw)
        wbr, ibr = load_w_cast(P, wbr_d, 3 * 512, D, 512, "wbr", defer=True)
        wmg, img = load_w_cast(P, wmg_d, D, 3 * D, 512, "wmg", defer=True)
        wout, iout = load_w_cast(P, wout_d, D, D, 512, "wout", defer=True)
        for f in [ibr[0], img[0], img[2], img[4], ibr[1], img[1], img[3], img[5]] + iout:
            f()
        tp = P.ps([128, 8 * 128], BF16, "tp")
        h2T = P.sb([128, 8, 512], BF16, "h2T")
        xt = [P.sb([128, D], F32, "x1t") for _ in range(4)]
        hb = [P.sb([128, D], BF16, "h2") for _ in range(2)]
        vin = [P.sb([128, 514], F32, "vin") for _ in range(2)]
        bin_ = [P.sb([128, 512], F32, "bin") for _ in range(2)]
        acc = [P.sb([128, 512], F32, "acc") for _ in range(2)]
        ybrs = [P.sb([128, 4, 512], BF16, "ybr%d" % i) for i in range(3)]
        mT = P.sb([128, 8, 512], BF16, "mT")
        macc = [P.sb([128, 512], F32, "macc") for _ in range(2)]
        gsb = [P.sb([128, 512], F32, "gsb") for _ in range(2)]
        tt = [P.sb([128, 512], F32, "tt") for _ in range(2)]
        pg = [P.ps([128, 512], F32, "pg") for _ in range(2)]
        ppj = [P.ps([128, 512], F32, "ppj") for _ in range(2)]
        py = P.ps([128, D], F32, "py")
        t1 = P.sb([128, D], F32, "t1")
        for tg in range(NT // 512):
            tsl = slice(tg * 512, (tg + 1) * 512)
            for t in range(4):
                i = tg * 4 + t
                xb = xt[t]
                P.dma("sp", xb[:, :], x1_d[i * 128:(i + 1) * 128, :], dst=xb)
                rstd = rms_stats(P, C, xb, xb[:, :], False)
                h = hb[t % 2]
                P.op("dve", lambda e, h=h, xb=xb, rstd=rstd: e.scalar_tensor_tensor(
                    out=h[:, :], in0=xb[:, :], scalar=rstd[:, :], in1=g[0][:, :], op0=ALU.mult, op1=ALU.mult),
                    reads=[xb, rstd, g[0]], writes=[h])
                for k in range(8):
                    P.op("pe", lambda e, h=h, k=k: e.transpose(out=tp[:, k * 128:(k + 1) * 128],
                                                              in_=h[:, k * 128:(k + 1) * 128], identity=C.ident[:, :]),
                         reads=[h, C.ident], writes=[tp])
                P.op("act", lambda e, t=t: e.activation(out=h2T[:, :, t * 128:(t + 1) * 128],
                                                        in_=tp[:, :].rearrange("p (k n) -> p k n", k=8), func=AF.Copy),
                     reads=[tp], writes=[h2T])
            for c in range(4):
                v = vin[c % 2]
                bb = bin_[c % 2]
                a = acc[c % 2]
                P.dma("sp", v[:, :], vT_d[c * 128:(c + 1) * 128, tg * 512:tg * 512 + 514], dst=v)
                P.dma("sp", bb[:, :], bT_d[c * 128:(c + 1) * 128, tsl], dst=bb)
                P.op("dve", lambda e, a=a, v=v, c=c: e.tensor_scalar(out=a[:, :], in0=v[:, 2:514], scalar1=cw[:, c, 2:3],
                                                                     scalar2=None, op0=ALU.mult), reads=[v, cw], writes=[a])
                P.op("dve", lambda e, a=a, v=v, c=c: e.scalar_tensor_tensor(out=a[:, :], in0=v[:, 1:513], scalar=cw[:, c, 1:2],
                                                                            in1=a[:, :], op0=ALU.mult, op1=ALU.add),
                     reads=[v, cw, a], writes=[a])
                P.op("dve", lambda e, a=a, v=v, c=c: e.scalar_tensor_tensor(out=a[:, :], in0=v[:, 0:512], scalar=cw[:, c, 0:1],
                                                                            in1=a[:, :], op0=ALU.mult, op1=ALU.add),
                     reads=[v, cw, a], writes=[a])
                P.op("pool", lambda e, a=a, bb=bb, c=c: e.tensor_tensor(out=ybrs[0][:, c, :], in0=a[:, :], in1=bb[:, :], op=ALU.mult),
                     reads=[a, bb], writes=[ybrs[0]])
            P.dma("pool", ybrs[1][:, :, :], ybT_d[:, tsl].rearrange("(c p) n -> p c n", p=128), dst=ybrs[1])
            P.dma("pool", ybrs[2][:, :, :], ycT_d[:, tsl].rearrange("(c p) n -> p c n", p=128), dst=ybrs[2])
            for mc in range(8):
                ma = macc[mc % 2]
                for br in range(3):
                    pj = ppj[br % 2]
                    for k in range(4):
                        wb, wap = wslice(wbr, br * 4 + k, mc * 128, (mc + 1) * 128)
                        P.op("pe", lambda e, pj=pj, wap=wap, br=br, k=k: e.matmul(
                            pj[:, :], lhsT=wap, rhs=ybrs[br][:, k, :], start=(k == 0), stop=(k == 3)),
                            reads=[wb, ybrs[br]], writes=[pj])
                    pgt = pg[br % 2]
                    for k in range(8):
                        wb, wap = wslice(wmg, k, br * D + mc * 128, br * D + (mc + 1) * 128)
                        P.op("pe", lambda e, pgt=pgt, wap=wap, k=k: e.matmul(
                            pgt[:, :], lhsT=wap, rhs=h2T[:, k, :], start=(k == 0), stop=(k == 7)),
                            reads=[wb, h2T], writes=[pgt])
                    gs = gsb[br % 2]
                    P.op("act", lambda e, gs=gs, pgt=pgt: e.activation(out=gs[:, :], in_=pgt[:, :], func=AF.Sigmoid),
                         reads=[pgt], writes=[gs])
                    if br == 0:
                        P.op("dve", lambda e, ma=ma, gs=gs, pj=pj: e.tensor_tensor(out=ma[:, :], in0=gs[:, :], in1=pj[:, :],
                                                                                   op=ALU.mult), reads=[gs, pj], writes=[ma])
                    else:
                        t_ = tt[br % 2]
                        P.op("dve", lambda e, t_=t_, gs=gs, pj=pj: e.tensor_tensor(out=t_[:, :], in0=gs[:, :], in1=pj[:, :],
                                                                                   op=ALU.mult), reads=[gs, pj], writes=[t_])
                        if br == 1:
                            P.op("pool", lambda e, ma=ma, t_=t_: e.tensor_tensor(out=ma[:, :], in0=ma[:, :], in1=t_[:, :],
                                                                                 op=ALU.add), reads=[ma, t_], writes=[ma])
                        else:
                            P.op("pool", lambda e, ma=ma, t_=t_, mc=mc: e.tensor_tensor(out=mT[:, mc, :], in0=ma[:, :], in1=t_[:, :],
                                                                                        op=ALU.add), reads=[ma, t_], writes=[mT])
            for t in range(4):
                i = tg * 4 + t
                for half in range(2):
                    for k in range(8):
                        wb, wap = wslice(wout, k, half * 512, (half + 1) * 512)
                        P.op("pe", lambda e, wap=wap, k=k, half=half, t=t: e.matmul(
                            py[:, half * 512:(half + 1) * 512], lhsT=mT[:, k, t * 128:(t + 1) * 128], rhs=wap,
                            start=(k == 0), stop=(k == 7)), reads=[wb, mT], writes=[py])
                rstd = rms_stats(P, C, py, py[:, :], True)
                P.op("dve", lambda e, rstd=rstd: e.scalar_tensor_tensor(
                    out=t1[:, :], in0=py[:, :], scalar=rstd[:, :], in1=g[1][:, :], op0=ALU.mult, op1=ALU.mult),
                    reads=[py, rstd, g[1]], writes=[t1])
                xb = xt[t]
                P.op("pool", lambda e, xb=xb: e.tensor_tensor(out=xb[:, :], in0=xb[:, :], in1=t1[:, :], op=ALU.add),
                     reads=[xb, t1], writes=[xb])
                P.dma("sp", x2_d[i * 128:(i + 1) * 128, :], xb[:, :], src=xb)
    with P.scope():
        wg, wu, wd = load_ffn_weights(P, wg_d, wu_d, wd_d)
        xpool = [P.sb([128, D], F32, "x") for _ in range(4)]

        def get_x(i):
            b = xpool[i % 4]
            P.dma("sp", b[:, :], x2_d[i * 128:(i + 1) * 128, :], dst=b)
            return b

        def put_x(i, b):
            P.dma("sp", out_d[i * 128:(i + 1) * 128, :], b[:, :], src=b)

        ffn_phase(P, C, NTILE, get_x, put_x, g[2], g[3], wg, wu, wd)


def build_C():
    P = Prog()
    ident_d = P.dram("ident", [128, 128], F32, "ExternalInput")
    out_d = P.dram("xo", [NT, D], F32, "ExternalOutput")
    C = build_consts(P, ident_d)
    with P.scope():
        body_C(P, C, out_d, "")
    print("C streams", P.stats(), "sems", P.n_sems)
    return P.finish()


def build_CA():
    P = Prog()
    ident_d = P.dram("ident", [128, 128], F32, "ExternalInput")
    x3_d = P.dram("x3s", [NT, D], F32, "Internal")
    C = build_consts(P, ident_d)
    with P.scope():
        body_C(P, C, x3_d, "c_")
    with P.scope():
        body_A(P, C, x3_d, "a_")
    print("CA streams", P.stats(), "sems", P.n_sems)
    return P.finish()


S = 8192
NQB = 32


def build_B():
    P = Prog()
    dh = {}
    for nm, shp in (("qT", [128, S]), ("zfT", [128, S]), ("zf", [S, 128]), ("v", [S, 128]), ("g", [S, 128]),
                    ("lbT", [128, 4]), ("lmask", [128, 4]), ("lbrow", [128, 4, 128]), ("lmaskrow", [128, 4, 128]),
                    ("gnorm", [128, 128]), ("M1", [128, 128]), ("M2", [128, 128])):
        dh[nm] = P.dram("h_" + nm, shp, F32, "ExternalInput")
    dh["y"] = P.dram("yh", [S, 128], F32, "ExternalOutput")
    dn = {}
    for nm, shp in NSA_IN(S, NQB):
        dn[nm] = P.dram("n_" + nm, shp, F32, "ExternalInput")
    dn["y"] = P.dram("yn", [NQB, 128, 256], F32, "ExternalOutput")
    with P.scope():
        hgrn_phase(P, S, dh)
    with P.scope():
        nsa_phase(P, S, NQB, dn)
    print("B streams", P.stats(), "sems", P.n_sems)
    return P.finish()


import numpy as np

S = 8192
NC = 8

def rep(a, n=128):
    return np.ascontiguousarray(np.broadcast_to(a[None], (n,) + a.shape))

IDENT = np.eye(128, dtype=np.float32)
_p = np.arange(128)
INV = (10000.0 ** (-(_p % 32).astype(np.float32) * 2.0 / 64)).astype(np.float32).reshape(128, 1)
SGN = np.where((_p % 64) < 32, -1.0, 1.0).astype(np.float32).reshape(128, 1)
PERM, MG = a_col_perm()


def prep_A(inp, l, x):
    wa = np.ascontiguousarray(inp["w_in"][l][:, PERM])
    gains = np.stack([rep(inp["norm_gains"][l, i]) for i in (0, 1, 2)])
    maps = []
    pos = inp["positions"].reshape(-1)
    for c in range(NC):
        sl = slice(c * NT, (c + 1) * NT)
        maps.append(dict(x=(np.ascontiguousarray(x[sl]) if x is not None else None), gains=gains, wg=inp["w_ffn_gate"][l, 0], wu=inp["w_ffn_up"][l, 0],
                         wd=inp["w_ffn_down"][l, 0], wa=wa, ident=IDENT, pos=rep(pos[sl].astype(np.int32)), inv=INV, sgn=SGN))
    return maps

HG_M1, HG_M2 = hgrn_consts_np()
NSA_K = nsa_consts_np(S)
ONES128 = np.ones((128, 128), np.float32)
ZEROS128 = np.zeros((128, 128), np.float32)
NQB = 32
NCC = 4


def _shift(A, par):
    o = np.empty_like(A)
    if par:
        o[:, 2 * par:] = A[:, :A.shape[1] - 2 * par]
        o[:, :2 * par] = A[:, :1]
    else:
        o[:] = A
    return o


def nsa_core_consts(par):
    K = NSA_K
    if par == 0:
        wm = [K["Mlow"], ONES128, ONES128, ONES128, K["Mdiag"], ZEROS128, K["Mdiag"], ZEROS128]
    else:
        wm = [ZEROS128, K["Mlow"], ONES128, ONES128, ONES128, K["Mdiag"], ONES128, K["Mdiag"]]
    Tt = np.zeros((128, NQB, NCC), np.float32)
    for m in range(NQB):
        for c in range(NCC):
            Tt[:, m, c] = 128 * (2 * m + par) - 2048 * c - 31
    return dict(WM=np.ascontiguousarray(np.stack(wm, 1)), Tt=Tt, Amul=_shift(K["Amul"], par), Aadd=_shift(K["Aadd"], par))


NSA_CC = [nsa_core_consts(0), nsa_core_consts(1)]
OV_PM = np.ascontiguousarray(NSA_K["ov"].reshape(-1, 128, NSA_K["ov"].shape[-1]).transpose(1, 0, 2))


def prep_B(inp, l, FM):
    maps = []
    logits = inp["hgrn_lb_logits"]
    lmask = np.zeros((4,), np.float32)
    lmask[1:l + 1] = 1
    gn = rep(inp["hgrn_gnorm"][l])
    pe = inp["cmp_pe"][l]
    pe2 = np.ascontiguousarray(pe.reshape(2, 16, 2, 64).transpose(0, 2, 3, 1).reshape(2, 128, 16))
    for c in range(NC):
        b, hd = c // 4, c % 4
        kvh, par = (c // 2) % 2, c % 2
        tk = slice(b * S, (b + 1) * S)
        d = {}
        r = lambda base, n: FM[base:base + n, tk]
        d["h_qT"] = np.ascontiguousarray(r(1024 + hd * 128, 128))
        zfT = r(1536 + hd * 128, 128)
        d["h_zfT"] = np.ascontiguousarray(zfT)
        d["h_zf"] = np.ascontiguousarray(zfT.T)
        d["h_v"] = np.ascontiguousarray(r(2048 + hd * 128, 128).T)
        d["h_g"] = np.ascontiguousarray(r(2560 + hd * 128, 128).T)
        lg = logits[:, hd * 128:(hd + 1) * 128]
        d["h_lbT"] = np.ascontiguousarray(lg.T)
        d["h_lmask"] = rep(lmask)
        d["h_lbrow"] = rep(lg)
        d["h_lmaskrow"] = np.ascontiguousarray(np.broadcast_to(lmask[None, :, None], (128, 4, 128)))
        d["h_gnorm"] = gn
        d["h_M1"] = HG_M1
        d["h_M2"] = HG_M2
        q = r(3072 + kvh * 256, 256).reshape(4, 64, 64, 128)[:, :, par::2]
        d["n_QT"] = np.ascontiguousarray(q.transpose(1, 2, 0, 3).reshape(64, NQB, 512))
        gl = r(4352 + kvh * 12, 12).reshape(12, 64, 128)[:, par::2]
        d["n_gl"] = np.ascontiguousarray(gl.transpose(2, 1, 0))
        d["n_ksT"] = np.ascontiguousarray(r(3712 + kvh * 64, 64))
        d["n_kwT"] = np.ascontiguousarray(r(3840 + kvh * 64, 64))

        def stack2(xT):
            o = np.zeros((128, S), np.float32)
            o[:64] = xT
            o[64:, :-1] = xT[:, 1:]
            return o
        d["n_kc2T"] = stack2(r(3584 + kvh * 64, 64))
        d["n_vc2T"] = stack2(r(3968 + kvh * 64, 64))

        def aug(xT):
            o = np.ones((S, 65), np.float32)
            o[:, :64] = xT.T
            return np.ascontiguousarray(o.reshape(S // 128, 128, 65).transpose(1, 0, 2))
        d["n_vs"] = aug(r(4096 + kvh * 64, 64))
        d["n_vw"] = aug(r(4224 + kvh * 64, 64))
        d["n_pe2"] = pe2
        d["n_w1"] = np.ascontiguousarray(inp["cmp_w1"][l].reshape(2, 16, 128, 256).transpose(0, 2, 1, 3))
        d["n_w2"] = np.ascontiguousarray(inp["cmp_w2"][l].reshape(2, 2, 128, 64).transpose(0, 2, 1, 3))
        for k in ("D16", "Mdiag", "Mlow", "E", "identb"):
            d["n_" + k] = NSA_K[k]
        d["n_ov"] = OV_PM
        for k, v in NSA_CC[par].items():
            d["n_" + k] = v
        maps.append(d)
    return maps


def gather_B(results):
    ybT = np.zeros((2, 512, S), np.float32)
    ycT = np.zeros((2, 512, S), np.float32)
    for c in range(NC):
        b, hd = c // 4, c % 4
        kvh, par = (c // 2) % 2, c % 2
        ybT[b, hd * 128:(hd + 1) * 128, :] = results[c]["yh"].T
        y = results[c]["yn"]
        yT = y.transpose(2, 0, 1)
        ycT[b, kvh * 256:(kvh + 1) * 256].reshape(256, 64, 128)[:, par::2, :] = yT
    return ybT, ycT


def prep_C(inp, l, x1, FM, ybT, ycT):
    gains = np.stack([rep(inp["norm_gains"][l, i]) for i in (2, 3, 4, 5)])
    cw = np.ascontiguousarray(inp["conv_w"][l].reshape(3, 4, 128).transpose(2, 1, 0))
    wmg = np.ascontiguousarray(inp["w_in"][l][:, MG])
    wbr = np.ascontiguousarray(inp["w_branch"][l].reshape(1536, 1024))
    maps = []
    for c in range(NC):
        b = c // 4
        t0 = c * NT
        sl = slice(t0, t0 + NT)
        ls = slice((c % 4) * NT, (c % 4 + 1) * NT)
        vT = np.zeros((512, NT + 2), np.float32)
        vT[:, 2:] = FM[512:1024, sl]
        if c % 4 != 0:
            vT[:, :2] = FM[512:1024, t0 - 2:t0]
        maps.append(dict(x1=np.ascontiguousarray(x1[sl]), gains=gains, vT=vT, bT=np.ascontiguousarray(FM[0:512, sl]),
                         ybT=np.ascontiguousarray(ybT[b][:, ls]), ycT=np.ascontiguousarray(ycT[b][:, ls]), cw=cw, wmg=wmg, wbr=wbr,
                         wout=inp["w_out"][l], wg=inp["w_ffn_gate"][l, 1], wu=inp["w_ffn_up"][l, 1], wd=inp["w_ffn_down"][l, 1],
                         ident=IDENT))
    return maps


from concourse.bass_utils import run_bass_kernel_spmd

_PROGS = {}


def _prog(name):
    if name not in _PROGS:
        _PROGS[name] = {"A": build_A, "B": build_B, "C": build_C, "CA": build_CA}[name]()
    return _PROGS[name]


def kernel(**inputs):
    inp = {k: np.asarray(v) for k, v in inputs.items()}
    cores = list(range(NC))
    x = np.ascontiguousarray(inp["x"].reshape(-1, D).astype(np.float32, copy=False))
    resA = run_bass_kernel_spmd(_prog("A"), prep_A(inp, 0, x), core_ids=cores)
    x1 = np.concatenate([r["x1"] for r in resA.results])
    FM = np.concatenate([r["fm"] for r in resA.results], axis=1)
    del resA
    for l in range(4):
        resB = run_bass_kernel_spmd(_prog("B"), prep_B(inp, l, FM), core_ids=cores)
        ybT, ycT = gather_B(resB.results)
        del resB
        mc = prep_C(inp, l, x1, FM, ybT, ycT)
        if l < 3:
            ma = prep_A(inp, l + 1, None)
            maps = []
            for c in range(NC):
                d = {"c_" + k: v for k, v in mc[c].items() if k != "ident"}
                d.update({"a_" + k: v for k, v in ma[c].items() if k not in ("ident", "x")})
                d["ident"] = IDENT
                maps.append(d)
            res = run_bass_kernel_spmd(_prog("CA"), maps, core_ids=cores)
            x1 = np.concatenate([r["a_x1"] for r in res.results])
            FM = np.concatenate([r["a_fm"] for r in res.results], axis=1)
            del res
        else:
            res = run_bass_kernel_spmd(_prog("C"), mc, core_ids=cores)
            x = np.concatenate([r["xo"] for r in res.results])
            del res
    return np.ascontiguousarray(x.reshape(2, S, D).astype(np.float32))
```

```python
import contextlib
import numpy as np
import concourse.bass as bass
import concourse.mybir as mybir

F32 = mybir.dt.float32
BF16 = mybir.dt.bfloat16
I32 = mybir.dt.int32
ALU = mybir.AluOpType
AF = mybir.ActivationFunctionType
AX = mybir.AxisListType

ENGS = ("pe", "act", "dve", "pool", "sp")


class Buf:
    def __init__(self, prog, t, name, tracked=True):
        self.prog = prog
        self.t = t
        self.name = name
        self.tracked = tracked
        self.last_w = None
        self.readers = {}
        self.wsem = None
        self.wcnt = 0
        self.rsem = None
        self.rcnt = 0

    def __getitem__(self, idx):
        return self.t[idx]

    @property
    def ap(self):
        return self.t


class Prog:
    def __init__(self, same_engine_sync=None):
        import os
        if same_engine_sync is None:
            same_engine_sync = os.environ.get('SES', '1') == '1'
        self.nc = bass.Bass("TRN2", target_bir_lowering=False)
        self.es = contextlib.ExitStack()
        self.sem_es = contextlib.ExitStack()
        self.streams = {e: [] for e in ENGS}
        self.cnt = {e: 0 for e in ENGS}
        self.esem = {}
        for e in ENGS:
            self.esem[e] = self.sem_es.enter_context(self.nc.semaphore("s_" + e))
        self.seen = {e: {} for e in ENGS}
        self.same_engine_sync = same_engine_sync
        self.dma_sems = []
        self.nbuf = 0
        self.n_sems = 5
        self.all_bufs = []
        self.free_sems = []

    def dram(self, name, shape, dtype, kind):
        t = self.nc.dram_tensor(name, list(shape), dtype, kind=kind)
        return t.ap()

    def sb(self, shape, dtype, name=None):
        self.nbuf += 1
        name = (name or "b") + "_%d" % self.nbuf
        t = self.es.enter_context(self.nc.sbuf_tensor(name, list(shape), dtype))
        b = Buf(self, t, name)
        self.all_bufs.append(b)
        return b

    def ps(self, shape, dtype=F32, name=None):
        self.nbuf += 1
        name = (name or "p") + "_%d" % self.nbuf
        t = self.es.enter_context(self.nc.psum_tensor(name, list(shape), dtype))
        b = Buf(self, t, name)
        self.all_bufs.append(b)
        return b

    def _sem(self, name):
        if self.free_sems:
            return self.free_sems.pop()
        self.n_sems += 1
        return (self.sem_es.enter_context(self.nc.semaphore(name)), 0)

    def _collect(self, eng, reads, writes, no_waw=False):
        need = {}

        def add(tok):
            if tok is None:
                return
            key, val, e = tok
            if e == eng and (eng == "pe" or not self.same_engine_sync):
                return
            if need.get(key, (0,))[0] < val:
                need[key] = (val, e)

        for b in reads:
            if b is None or not b.tracked:
                continue
            add(b.last_w)
        for b in writes:
            if b is None or not b.tracked:
                continue
            if not no_waw:
                add(b.last_w)
            for key, (val, e) in b.readers.items():
                add((key, val, e))
        out = []
        for key, (val, e) in need.items():
            if self.seen[eng].get(key, 0) >= val:
                continue
            self.seen[eng][key] = val
            out.append((key, val))
        return out

    def _record(self, tok, reads, writes):
        key, val, e = tok
        for b in writes:
            if b is None or not b.tracked:
                continue
            b.last_w = tok
            b.readers = {}
        for b in reads:
            if b is None or not b.tracked:
                continue
            if b in writes:
                continue
            b.readers[key] = (val, e)

    def op(self, eng, fn, reads=(), writes=(), no_waw=False):
        waits = self._collect(eng, reads, writes, no_waw)
        st = self.streams[eng]
        for key, val in waits:
            st.append(("w", key, val))
        self.cnt[eng] += 1
        st.append(("o", fn, self.esem[eng], 1))
        tok = (self.esem[eng], self.cnt[eng], eng)
        self._record(tok, reads, writes)
        return tok

    def dma(self, queue, out_ap, in_ap, dst=None, src=None, no_waw=False, **kw):
        reads = [src] if src is not None else []
        writes = [dst] if dst is not None else []
        waits = self._collect(queue, reads, writes, no_waw)
        st = self.streams[queue]
        for key, val in waits:
            st.append(("w", key, val))
        if dst is not None:
            if dst.wsem is None:
                dst.wsem, dst.wcnt = self._sem("w_" + dst.name)
                self.dma_sems.append(dst)
            dst.wcnt += 16
            sem, val = dst.wsem, dst.wcnt
        else:
            if src.rsem is None:
                src.rsem, src.rcnt = self._sem("r_" + src.name)
                self.dma_sems.append(src)
            src.rcnt += 16
            sem, val = src.rsem, src.rcnt

        def fn(e, out_ap=out_ap, in_ap=in_ap, kw=kw):
            return e.dma_start(out=out_ap, in_=in_ap, **kw)

        st.append(("o", fn, sem, 16))
        tok = (sem, val, "dma")
        self._record(tok, reads, writes)
        return tok

    @contextlib.contextmanager
    def scope(self):
        outer = self.es
        self.es = contextlib.ExitStack()
        n0 = len(self.all_bufs)
        try:
            yield
        finally:
            self.barrier()
            self.flush()
            for b in self.all_bufs[n0:]:
                if b.wsem is not None:
                    self.free_sems.append((b.wsem, b.wcnt))
                if b.rsem is not None:
                    self.free_sems.append((b.rsem, b.rcnt))
                if b in self.dma_sems:
                    self.dma_sems.remove(b)
                b.dead = True
            del self.all_bufs[n0:]
            self.es.close()
            self.es = outer

    def barrier(self):
        targets = [(self.esem[e], self.cnt[e]) for e in ENGS if self.cnt[e] > 0]
        for b in self.dma_sems:
            if b.wsem is not None and b.wcnt:
                targets.append((b.wsem, b.wcnt))
            if b.rsem is not None and b.rcnt:
                targets.append((b.rsem, b.rcnt))
        for e in ENGS:
            for key, val in targets:
                if key is self.esem[e]:
                    continue
                if self.seen[e].get(key, 0) >= val:
                    continue
                self.seen[e][key] = val
                self.streams[e].append(("w", key, val))

    def flush(self):
        nc = self.nc
        streams = self.streams
        self.streams = {e: [] for e in ENGS}

        def run(stream, e):
            for it in stream:
                if it[0] == "w":
                    e.wait_ge(it[1], it[2])
                else:
                    ins = it[1](e)
                    ins.then_inc(it[2], it[3])

        with nc.Block() as block:
            @block.tensor
            def _(e):
                run(streams["pe"], e)

            @block.scalar
            def _(e):
                run(streams["act"], e)

            @block.vector
            def _(e):
                run(streams["dve"], e)

            @block.gpsimd
            def _(e):
                run(streams["pool"], e)

            @block.sync
            def _(e):
                run(streams["sp"], e)

    def finish(self):
        targets = [(self.esem[e], self.cnt[e]) for e in ENGS if self.cnt[e] > 0 and e != "sp"]
        for b in self.dma_sems:
            if b.wsem is not None and b.wcnt:
                targets.append((b.wsem, b.wcnt))
            if b.rsem is not None and b.rcnt:
                targets.append((b.rsem, b.rcnt))
        for key, val in targets:
            self.streams["sp"].append(("w", key, val))
        self.flush()
        self.es.close()
        self.sem_es.close()
        return self.nc

    def stats(self):
        return {e: self.cnt[e] for e in ENGS}


import numpy as np

D = 1024
DFF = 2816
NJ = 22
TG = 256
EPS = 1e-6


def load_w_cast(P, dram_ap, rows, cols, colblk, name, defer=False):
    kc = rows // 128
    src = dram_ap.rearrange("(c p) n -> p c n", p=128)
    blocks = []
    issue = []
    c0 = 0
    while c0 < cols:
        c1 = min(cols, c0 + colblk)
        b = P.sb([128, kc, c1 - c0], BF16, name)
        fn = (lambda b=b, c0=c0, c1=c1: P.dma("pool", b[:, :, :], src[:, :, c0:c1], dst=b))
        if defer:
            issue.append(fn)
        else:
            fn()
        blocks.append((b, c0, c1))
        c0 = c1
    if defer:
        return blocks, issue
    return blocks


def load_ffn_weights(P, wg_d, wu_d, wd_d):
    wg, ig = load_w_cast(P, wg_d, D, DFF, 512, "wg", defer=True)
    wu, iu = load_w_cast(P, wu_d, D, DFF, 512, "wu", defer=True)
    wd, idn = load_w_cast(P, wd_d, DFF, D, 512, "wd", defer=True)
    for a_, b_ in zip(ig, iu):
        a_()
        b_()
    for f in idn:
        f()
    return wg, wu, wd


def wslice(blocks, k, c0, c1):
    for b, b0, b1 in blocks:
        if b0 <= c0 and c1 <= b1:
            return b, b[:, k, c0 - b0:c1 - b0]
    raise ValueError((c0, c1))


class Consts:
    pass


def rms_stats(P, C, src_buf, src_ap, from_psum):
    ss = P.sb([128, 1], F32, "ss")
    if from_psum:
        P.op("act", lambda e: e.activation(out=C.junk[:, :], in_=src_ap, func=AF.Square, accum_out=ss[:, :]),
             reads=[src_buf], writes=[C.junk, ss])
    else:
        P.op("dve", lambda e: e.scalar_tensor_tensor(out=C.junk[:, :], in0=src_ap, scalar=1.0, in1=src_ap,
                                                      op0=ALU.mult, op1=ALU.mult, accum_out=ss[:, :]),
             reads=[src_buf], writes=[C.junk, ss])
    ms = P.sb([128, 1], F32, "ms")
    P.op("dve", lambda e: e.tensor_scalar(out=ms[:, :], in0=ss[:, :], scalar1=1.0 / D, scalar2=EPS,
                                          op0=ALU.mult, op1=ALU.add), reads=[ss], writes=[ms])
    rstd = P.sb([128, 1], F32, "rstd")
    P.op("pool", lambda e: e.tensor_tensor(out=rstd[:, :], in0=ms[:, :], in1=C.mhalf[:, :], op=ALU.pow),
         reads=[ms, C.mhalf], writes=[rstd])
    return rstd


def ffn_phase(P, C, ntiles, get_x, put_x, g_pre, g_post, wg, wu, wd):
    ngroups = ntiles // 2
    hT = [P.sb([128, 8, TG], BF16, "hT") for _ in range(2)]
    aT = P.sb([128, NJ, TG], BF16, "aT")
    tp = P.ps([128, 8 * 128], BF16, "tp")
    gu = [P.ps([128, 2, TG], F32, "gu") for _ in range(2)]
    ys = [P.ps([128, D], F32, "y") for _ in range(2)]
    hb = [P.sb([128, D], BF16, "h") for _ in range(2)]
    sg = [P.sb([128, TG], BF16, "sg") for _ in range(2)]
    xs = {}
    t1 = P.sb([128, D], F32, "t1")

    def prep(g):
        for t in range(2):
            i = 2 * g + t
            xb = get_x(i)
            xs[i] = xb
            rstd = rms_stats(P, C, xb, xb[:, :], False)
            h = hb[t]
            P.op("dve", lambda e, h=h, xb=xb, rstd=rstd: e.scalar_tensor_tensor(
                out=h[:, :], in0=xb[:, :], scalar=rstd[:, :], in1=g_pre[:, :], op0=ALU.mult, op1=ALU.mult),
                reads=[xb, rstd, g_pre], writes=[h])

    def transposes(g):
        for t in range(2):
            h = hb[t]
            for k in range(8):
                P.op("pe", lambda e, h=h, k=k: e.transpose(out=tp[:, k * 128:(k + 1) * 128],
                                                          in_=h[:, k * 128:(k + 1) * 128], identity=C.ident[:, :]),
                     reads=[h, C.ident], writes=[tp])
            dst = hT[g % 2]
            P.op("act", lambda e, dst=dst, t=t: e.activation(
                out=dst[:, :, t * 128:(t + 1) * 128], in_=tp[:, :].rearrange("p (k n) -> p k n", k=8), func=AF.Copy),
                reads=[tp], writes=[dst])

    def phaseA(g):
        h_t = hT[g % 2]
        for j in range(NJ):
            pg = gu[j % 2]
            for which, W in ((0, wg), (1, wu)):
                for k in range(8):
                    wb, wap = wslice(W, k, j * 128, (j + 1) * 128)
                    P.op("pe", lambda e, pg=pg, which=which, wap=wap, k=k: e.matmul(
                        pg[:, which, :], lhsT=wap, rhs=h_t[:, k, :], start=(k == 0), stop=(k == 7)),
                        reads=[wb, h_t], writes=[pg])
            s = sg[j % 2]
            P.op("act", lambda e, s=s, pg=pg: e.activation(out=s[:, :], in_=pg[:, 0, :], func=AF.Silu),
                 reads=[pg], writes=[s])
            P.op("dve", lambda e, s=s, pg=pg, j=j: e.tensor_tensor(out=aT[:, j, :], in0=s[:, :], in1=pg[:, 1, :],
                                                                   op=ALU.mult),
                 reads=[s, pg], writes=[aT])

    def phaseB(g):
        for t in range(2):
            y = ys[t]
            for half in range(2):
                for j in range(NJ):
                    wb, wap = wslice(wd, j, half * 512, (half + 1) * 512)
                    P.op("pe", lambda e, y=y, half=half, wap=wap, j=j, t=t: e.matmul(
                        y[:, half * 512:(half + 1) * 512], lhsT=aT[:, j, t * 128:(t + 1) * 128], rhs=wap,
                        start=(j == 0), stop=(j == NJ - 1)),
                        reads=[wb, aT], writes=[y])

    def post(g):
        for t in range(2):
            i = 2 * g + t
            y = ys[t]
            rstd = rms_stats(P, C, y, y[:, :], True)
            P.op("dve", lambda e, y=y, rstd=rstd: e.scalar_tensor_tensor(
                out=t1[:, :], in0=y[:, :], scalar=rstd[:, :], in1=g_post[:, :], op0=ALU.mult, op1=ALU.mult),
                reads=[y, rstd, g_post], writes=[t1])
            xb = xs.pop(i)
            P.op("dve", lambda e, xb=xb: e.scalar_tensor_tensor(
                out=xb[:, :], in0=t1[:, :], scalar=0.5, in1=xb[:, :], op0=ALU.mult, op1=ALU.add),
                reads=[t1, xb], writes=[xb])
            put_x(i, xb)

    prep(0)
    transposes(0)
    for g in range(ngroups):
        phaseA(g)
        if g + 1 < ngroups:
            prep(g + 1)
            transposes(g + 1)
        phaseB(g)
        post(g)


def build_consts(P, ident_d):
    C = Consts()
    C.ident = P.sb([128, 128], BF16, "ident")
    P.dma("pool", C.ident[:, :], ident_d[:, :], dst=C.ident)
    C.junk = P.sb([128, D], BF16, "junk")
    C.junk.tracked = False
    C.mhalf = P.sb([128, 1], F32, "mhalf")
    P.op("pool", lambda e: e.memset(C.mhalf[:, :], -0.5), writes=[C.mhalf])
    return C


def build_test_ffn(NT):
    P = Prog()
    x_d = P.dram("x", [NT, D], F32, "ExternalInput")
    gains_d = P.dram("gains", [2, 128, D], F32, "ExternalInput")
    wg_d = P.dram("wg", [D, DFF], F32, "ExternalInput")
    wu_d = P.dram("wu", [D, DFF], F32, "ExternalInput")
    wd_d = P.dram("wd", [DFF, D], F32, "ExternalInput")
    ident_d = P.dram("ident", [128, 128], F32, "ExternalInput")
    out_d = P.dram("out", [NT, D], F32, "ExternalOutput")
    C = build_consts(P, ident_d)
    g_pre = P.sb([128, D], F32, "gpre")
    g_post = P.sb([128, D], F32, "gpost")
    P.dma("sp", g_pre[:, :], gains_d[0], dst=g_pre)
    P.dma("sp", g_post[:, :], gains_d[1], dst=g_post)
    wg = load_w_cast(P, wg_d, D, DFF, 512, "wg")
    wu = load_w_cast(P, wu_d, D, DFF, 512, "wu")
    wd = load_w_cast(P, wd_d, DFF, D, 512, "wd")
    xpool = [P.sb([128, D], F32, "x") for _ in range(4)]

    def get_x(i):
        b = xpool[i % 4]
        P.dma("sp", b[:, :], x_d[i * 128:(i + 1) * 128, :], dst=b)
        return b

    def put_x(i, b):
        P.dma("sp", out_d[i * 128:(i + 1) * 128, :], b[:, :], src=b)

    ffn_phase(P, C, NT // 128, get_x, put_x, g_pre, g_post, wg, wu, wd)
    print("streams", P.stats(), "sems", P.n_sems)
    return P.finish()


import numpy as np

EPS = 1e-6


def hgrn_consts_np():
    s = np.arange(128)
    same = (s[:, None] // 64) == (s[None, :] // 64)
    M1 = (same & (s[:, None] <= s[None, :])).astype(np.float32)
    M2 = (same & (s[:, None] > s[None, :])).astype(np.float32)
    return M1, M2


def hgrn_phase(P, S, d):
    nt = S // 128
    M1 = P.sb([128, 128], F32, "M1")
    M2 = P.sb([128, 128], F32, "M2")
    P.dma("sp", M1[:, :], d["M1"][:, :], dst=M1)
    P.dma("sp", M2[:, :], d["M2"][:, :], dst=M2)
    gn = P.sb([128, 128], F32, "gn")
    P.dma("sp", gn[:, :], d["gnorm"][:, :], dst=gn)
    mhalf = P.sb([128, 1], F32, "mhalf")
    P.op("pool", lambda e: e.memset(mhalf[:, :], -0.5), writes=[mhalf])
    lbl = P.sb([128, 4], F32, "lbl")
    lmk = P.sb([128, 4], F32, "lmk")
    P.dma("sp", lbl[:, :], d["lbT"][:, :], dst=lbl)
    P.dma("sp", lmk[:, :], d["lmask"][:, :], dst=lmk)
    ex = P.sb([128, 4], F32, "ex")
    P.op("act", lambda e: e.activation(out=ex[:, :], in_=lbl[:, :], func=AF.Exp), reads=[lbl], writes=[ex])
    den = P.sb([128, 1], F32, "den")
    P.op("dve", lambda e: e.reduce_sum(out=den[:, :], in_=ex[:, :], axis=AX.X), reads=[ex], writes=[den])
    rden = P.sb([128, 1], F32, "rden")
    P.op("dve", lambda e: e.reciprocal(out=rden[:, :], in_=den[:, :]), reads=[den], writes=[rden])
    exm = P.sb([128, 4], F32, "exm")
    P.op("dve", lambda e: e.tensor_tensor(out=exm[:, :], in0=ex[:, :], in1=lmk[:, :], op=ALU.mult),
         reads=[ex, lmk], writes=[exm])
    num = P.sb([128, 1], F32, "num")
    P.op("dve", lambda e: e.reduce_sum(out=num[:, :], in_=exm[:, :], axis=AX.X), reads=[exm], writes=[num])
    lbc = P.sb([128, 1], F32, "lbc")
    P.op("dve", lambda e: e.tensor_tensor(out=lbc[:, :], in0=num[:, :], in1=rden[:, :], op=ALU.mult),
         reads=[num, rden], writes=[lbc])
    omlc = P.sb([128, 1], F32, "omlc")
    P.op("dve", lambda e: e.tensor_scalar(out=omlc[:, :], in0=lbc[:, :], scalar1=-1.0, scalar2=1.0,
                                          op0=ALU.mult, op1=ALU.add), reads=[lbc], writes=[omlc])
    nomlc = P.sb([128, 1], F32, "nomlc")
    P.op("dve", lambda e: e.tensor_scalar(out=nomlc[:, :], in0=omlc[:, :], scalar1=-1.0, scalar2=None,
                                          op0=ALU.mult), reads=[omlc], writes=[nomlc])
    lbr = P.sb([128, 4, 128], F32, "lbr")
    lmr = P.sb([128, 4, 128], F32, "lmr")
    P.dma("sp", lbr[:, :, :], d["lbrow"][:, :, :], dst=lbr)
    P.dma("sp", lmr[:, :, :], d["lmaskrow"][:, :, :], dst=lmr)
    exr = P.sb([128, 4, 128], F32, "exr")
    P.op("act", lambda e: e.activation(out=exr[:, :, :], in_=lbr[:, :, :], func=AF.Exp), reads=[lbr], writes=[exr])
    denr = P.sb([128, 128], F32, "denr")
    P.op("dve", lambda e: e.tensor_tensor(out=denr[:, :], in0=exr[:, 0, :], in1=exr[:, 1, :], op=ALU.add),
         reads=[exr], writes=[denr])
    P.op("dve", lambda e: e.tensor_tensor(out=denr[:, :], in0=denr[:, :], in1=exr[:, 2, :], op=ALU.add),
         reads=[exr, denr], writes=[denr])
    P.op("dve", lambda e: e.tensor_tensor(out=denr[:, :], in0=denr[:, :], in1=exr[:, 3, :], op=ALU.add),
         reads=[exr, denr], writes=[denr])
    P.op("dve", lambda e: e.reciprocal(out=denr[:, :], in_=denr[:, :]), reads=[denr], writes=[denr])
    P.op("dve", lambda e: e.tensor_tensor(out=exr[:, :, :], in0=exr[:, :, :], in1=lmr[:, :, :], op=ALU.mult),
         reads=[exr, lmr], writes=[exr])
    lbrow = P.sb([128, 128], F32, "lbrow")
    P.op("dve", lambda e: e.tensor_tensor(out=lbrow[:, :], in0=exr[:, 0, :], in1=exr[:, 1, :], op=ALU.add),
         reads=[exr], writes=[lbrow])
    P.op("dve", lambda e: e.tensor_tensor(out=lbrow[:, :], in0=lbrow[:, :], in1=exr[:, 2, :], op=ALU.add),
         reads=[exr, lbrow], writes=[lbrow])
    P.op("dve", lambda e: e.tensor_tensor(out=lbrow[:, :], in0=lbrow[:, :], in1=exr[:, 3, :], op=ALU.add),
         reads=[exr, lbrow], writes=[lbrow])
    P.op("dve", lambda e: e.tensor_tensor(out=lbrow[:, :], in0=lbrow[:, :], in1=denr[:, :], op=ALU.mult),
         reads=[lbrow, denr], writes=[lbrow])
    omlrow = P.sb([128, 128], F32, "omlrow")
    P.op("dve", lambda e: e.tensor_scalar(out=omlrow[:, :], in0=lbrow[:, :], scalar1=-1.0, scalar2=1.0,
                                          op0=ALU.mult, op1=ALU.add), reads=[lbrow], writes=[omlrow])

    G = 4
    GT = G * 128
    ng = nt // G
    NB = 2
    qT = [P.sb([128, GT], F32, "qT") for _ in range(NB)]
    zfT = [P.sb([128, GT], F32, "zfT") for _ in range(NB)]
    zft = [P.sb([128, G, 128], F32, "zft") for _ in range(NB)]
    vt = [P.sb([128, G, 128], F32, "vt") for _ in range(NB)]
    gt = [P.sb([128, G, 128], F32, "gt") for _ in range(NB)]
    vb = [P.sb([128, G, 128], BF16, "vb") for _ in range(NB)]
    sigt = [P.sb([128, G, 128], F32, "sigt") for _ in range(NB)]
    logf = [P.sb([128, G, 128], F32, "logf") for _ in range(NB)]
    kt = [P.sb([128, G, 128], F32, "kt") for _ in range(NB)]
    sigf = [P.sb([128, GT], F32, "sigf") for _ in range(NB)]
    kT = [P.sb([128, GT], F32, "kT") for _ in range(NB)]
    bsb = [P.sb([128, GT], F32, "bsb") for _ in range(NB)]
    dd = [P.sb([128, GT], F32, "dd") for _ in range(NB)]
    eq = [P.sb([128, GT], F32, "eq") for _ in range(NB)]
    ek = [P.sb([128, GT], F32, "ek") for _ in range(NB)]
    eb = [P.sb([128, GT], F32, "eb") for _ in range(NB)]
    er = [P.sb([128, G, 128], F32, "er") for _ in range(NB)]
    qtl = [P.sb([128, GT], BF16, "qtl") for _ in range(NB)]
    ktl = [P.sb([128, GT], BF16, "ktl") for _ in range(NB)]
    QP = [P.sb([128, G, 2, 128], BF16, "QP") for _ in range(NB)]
    KP = [P.sb([128, G, 2, 128], BF16, "KP") for _ in range(NB)]
    for i in range(NB):
        for b in (QP[i], KP[i]):
            P.op("pool", lambda e, b=b: e.memset(b[:, :, :, :], 0.0), writes=[b])
    dec = [P.sb([128, 2 * G], F32, "dec") for _ in range(NB)]
    ATm = [P.sb([128, G, 128], BF16, "ATm") for _ in range(NB)]
    Sf = [P.sb([128, 128], F32, "Sf") for _ in range(2)]
    Sb = [P.sb([128, 128], BF16, "Sb") for _ in range(4)]
    P.op("pool", lambda e: e.memset(Sf[0][:, :], 0.0), writes=[Sf[0]])
    P.op("pool", lambda e: e.memset(Sb[0][:, :], 0.0), writes=[Sb[0]])
    sq = [P.sb([128, G, 128], F32, "sq") for _ in range(NB)]
    sg = [P.sb([128, G, 128], F32, "sg") for _ in range(NB)]
    yo = [P.sb([128, G, 128], F32, "yo") for _ in range(NB)]
    ss4 = [P.sb([128, G], F32, "ss4") for _ in range(NB)]
    rs4 = [P.sb([128, G], F32, "rs4") for _ in range(NB)]
    mh4 = P.sb([128, G], F32, "mh4")
    P.op("pool", lambda e: e.memset(mh4[:, :], -0.5), writes=[mh4])
    pbT = P.ps([128, GT], F32, "pbT")
    pR = P.ps([128, GT], F32, "pR")
    pAT = P.ps([128, GT], F32, "pAT")
    p_o = [P.ps([128, GT], F32, "p_o") for _ in range(2)]
    p_S = [P.ps([128, 512], F32, "p_S") for _ in range(2)]
    st_ = {"sidx": 0}
    bc = lambda t: t[:, :].unsqueeze(1).to_broadcast([128, G, 128])

    def h1(gi):
        n = gi % NB
        gs = slice(gi * GT, (gi + 1) * GT)
        P.dma("sp", qT[n][:, :], d["qT"][:, gs], dst=qT[n])
        P.dma("sp", zfT[n][:, :], d["zfT"][:, gs], dst=zfT[n])
        P.dma("sp", zft[n][:, :, :], d["zf"][gs, :].rearrange("(t p) k -> p t k", p=128), dst=zft[n])
        P.dma("sp", vt[n][:, :, :], d["v"][gs, :].rearrange("(t p) k -> p t k", p=128), dst=vt[n])
        P.dma("sp", gt[n][:, :, :], d["g"][gs, :].rearrange("(t p) k -> p t k", p=128), dst=gt[n])
        s_, lf, k_ = sigt[n], logf[n], kt[n]
        P.op("act", lambda e: e.activation(out=s_[:, :, :], in_=zft[n][:, :, :], func=AF.Sigmoid), reads=[zft[n]], writes=[s_])
        P.op("act", lambda e: e.activation(out=sigf[n][:, :], in_=zfT[n][:, :], func=AF.Sigmoid), reads=[zfT[n]], writes=[sigf[n]])
        P.op("act", lambda e: e.activation(out=sg[n][:, :, :], in_=gt[n][:, :, :], func=AF.Sigmoid), reads=[gt[n]], writes=[sg[n]])
        P.op("pool", lambda e: e.tensor_tensor(out=sg[n][:, :, :], in0=sg[n][:, :, :], in1=gt[n][:, :, :], op=ALU.mult),
             reads=[sg[n], gt[n]], writes=[sg[n]])
        P.op("pool", lambda e: e.tensor_tensor(out=sg[n][:, :, :], in0=sg[n][:, :, :], in1=bc(gn), op=ALU.mult),
             reads=[sg[n], gn], writes=[sg[n]])
        P.op("dve", lambda e: e.tensor_tensor(out=lf[:, :, :], in0=s_[:, :, :], in1=bc(omlrow), op=ALU.mult),
             reads=[s_, omlrow], writes=[lf])
        P.op("dve", lambda e: e.tensor_tensor(out=k_[:, :, :], in0=bc(omlrow), in1=lf[:, :, :], op=ALU.subtract),
             reads=[lf, omlrow], writes=[k_])
        P.op("dve", lambda e: e.tensor_tensor(out=lf[:, :, :], in0=lf[:, :, :], in1=bc(lbrow), op=ALU.add),
             reads=[lf, lbrow], writes=[lf])
        P.op("dve", lambda e: e.tensor_scalar(out=lf[:, :, :], in0=lf[:, :, :], scalar1=1e-30, scalar2=None, op0=ALU.max),
             reads=[lf], writes=[lf])
        P.op("act", lambda e: e.activation(out=lf[:, :, :], in_=lf[:, :, :], func=AF.Ln), reads=[lf], writes=[lf])
        for t in range(G):
            P.op("pe", lambda e, t=t: e.matmul(pbT[:, t * 128:(t + 1) * 128], lhsT=lf[:, t, :], rhs=M1[:, :], start=True, stop=True),
                 reads=[lf, M1], writes=[pbT])
        for t in range(G):
            P.op("pe", lambda e, t=t: e.matmul(pR[:, t * 128:(t + 1) * 128], lhsT=M2[:, :], rhs=lf[:, t, :], start=True, stop=True),
                 reads=[lf, M2], writes=[pR])
        yield
        sf, kT_ = sigf[n], kT[n]
        P.op("dve", lambda e: e.tensor_scalar(out=kT_[:, :], in0=sf[:, :], scalar1=nomlc[:, :], scalar2=omlc[:, :],
                                              op0=ALU.mult, op1=ALU.add), reads=[sf, nomlc, omlc], writes=[kT_])
        b_, d_, eq_, ek_, eb_, er_ = bsb[n], dd[n], eq[n], ek[n], eb[n], er[n]
        P.op("act", lambda e: e.activation(out=b_[:, :], in_=pbT[:, :], func=AF.Copy), reads=[pbT], writes=[b_])
        b3 = b_[:, :].rearrange("p (c s) -> p c s", s=64)
        P.op("dve", lambda e: e.tensor_tensor(out=d_[:, :].rearrange("p (c s) -> p c s", s=64), in0=b3,
                                              in1=b3[:, :, 31:32].to_broadcast([128, 2 * G, 64]), op=ALU.subtract),
             reads=[b_], writes=[d_])
        P.op("dve", lambda e: e.tensor_scalar(out=eq_[:, :], in0=d_[:, :], scalar1=43.0, scalar2=None, op0=ALU.min),
             reads=[d_], writes=[eq_])
        P.op("dve", lambda e: e.tensor_scalar(out=ek_[:, :], in0=d_[:, :], scalar1=-1.0, scalar2=43.0, op0=ALU.mult, op1=ALU.min),
             reads=[d_], writes=[ek_])
        P.op("act", lambda e: e.activation(out=eq_[:, :], in_=eq_[:, :], func=AF.Exp), reads=[eq_], writes=[eq_])
        P.op("act", lambda e: e.activation(out=ek_[:, :], in_=ek_[:, :], func=AF.Exp), reads=[ek_], writes=[ek_])
        P.op("act", lambda e: e.activation(out=eb_[:, :], in_=b_[:, :], func=AF.Exp), reads=[b_], writes=[eb_])
        P.op("act", lambda e: e.activation(out=er_[:, :, :], in_=pR[:, :].rearrange("p (t k) -> p t k", k=128), func=AF.Exp),
             reads=[pR], writes=[er_])
        yield
        dc = dec[n]
        P.op("dve", lambda e: e.tensor_copy(out=dc[:, :], in_=eb_[:, :].rearrange("p (c s) -> p c s", s=64)[:, :, 63]),
             reads=[eb_], writes=[dc])
        P.op("dve", lambda e: e.tensor_tensor(out=qtl[n][:, :], in0=qT[n][:, :], in1=eq_[:, :], op=ALU.mult),
             reads=[qT[n], eq_], writes=[qtl[n]])
        P.op("dve", lambda e: e.tensor_tensor(out=ktl[n][:, :], in0=kT_[:, :], in1=ek_[:, :], op=ALU.mult),
             reads=[kT_, ek_], writes=[ktl[n]])
        q4 = qT[n][:, :].rearrange("p (t c s) -> p t c s", c=2, s=64)
        e4 = eb_[:, :].rearrange("p (t c s) -> p t c s", c=2, s=64)
        P.op("dve", lambda e: e.tensor_tensor(out=QP[n][:, :, 0, 0:64], in0=q4[:, :, 0, :], in1=e4[:, :, 0, :], op=ALU.mult),
             reads=[qT[n], eb_], writes=[QP[n]])
        P.op("dve", lambda e: e.tensor_tensor(out=QP[n][:, :, 1, 64:128], in0=q4[:, :, 1, :], in1=e4[:, :, 1, :], op=ALU.mult),
             reads=[qT[n], eb_], writes=[QP[n]])
        P.op("dve", lambda e: e.tensor_tensor(out=KP[n][0:64, :, 0, :], in0=k_[0:64, :, :], in1=er_[0:64, :, :], op=ALU.mult),
             reads=[k_, er_], writes=[KP[n]])
        P.op("dve", lambda e: e.tensor_tensor(out=KP[n][64:128, :, 1, :], in0=k_[64:128, :, :], in1=er_[64:128, :, :], op=ALU.mult),
             reads=[k_, er_], writes=[KP[n]])
        P.op("pool", lambda e: e.tensor_copy(out=vb[n][:, :, :], in_=vt[n][:, :, :]), reads=[vt[n]], writes=[vb[n]])
        for t in range(G):
            P.op("pe", lambda e, t=t: e.matmul(pAT[:, t * 128:(t + 1) * 128], lhsT=ktl[n][:, t * 128:(t + 1) * 128],
                                               rhs=qtl[n][:, t * 128:(t + 1) * 128], start=True, stop=True),
                 reads=[ktl[n], qtl[n]], writes=[pAT])
        P.op("dve", lambda e: e.tensor_tensor(out=ATm[n][:, :, :], in0=pAT[:, :].rearrange("p (t k) -> p t k", k=128),
                                              in1=bc(M1), op=ALU.mult), reads=[pAT, M1], writes=[ATm[n]])

    def h2(gi):
        n = gi % NB
        gs = slice(gi * GT, (gi + 1) * GT)
        dc = dec[n]
        po = p_o[gi % 2]
        sidx = st_["sidx"]
        for t in range(G):
            osl = slice(t * 128, (t + 1) * 128)
            s0 = sidx
            P.op("pe", lambda e, t=t, osl=osl: e.matmul(po[:, osl], lhsT=ATm[n][:, t, :], rhs=vb[n][:, t, :], start=True, stop=False),
                 reads=[ATm[n], vb[n]], writes=[po])
            P.op("pe", lambda e, t=t, osl=osl, s0=s0: e.matmul(po[:, osl], lhsT=QP[n][:, t, 0, :], rhs=Sb[s0 % 4][:, :],
                                                               start=False, stop=False), reads=[QP[n], Sb[s0 % 4]], writes=[po])
            for c in range(2):
                pS = p_S[c]
                P.op("pe", lambda e, t=t, c=c, pS=pS: e.matmul(pS[:, 0:128], lhsT=KP[n][:, t, c, :], rhs=vb[n][:, t, :],
                                                              start=True, stop=True), reads=[KP[n], vb[n]], writes=[pS])
                so, sn = Sf[sidx % 2], Sf[(sidx + 1) % 2]
                P.op("dve", lambda e, so=so, sn=sn, pS=pS, t=t, c=c: e.scalar_tensor_tensor(
                    out=sn[:, :], in0=so[:, :], scalar=dc[:, 2 * t + c:2 * t + c + 1], in1=pS[:, 0:128], op0=ALU.mult, op1=ALU.add),
                    reads=[so, dc, pS], writes=[sn])
                sbn = Sb[(sidx + 1) % 4]
                P.op("act", lambda e, sbn=sbn, sn=sn: e.activation(out=sbn[:, :], in_=sn[:, :], func=AF.Copy),
                     reads=[sn], writes=[sbn])
                sidx += 1
            P.op("pe", lambda e, t=t, osl=osl, s0=s0: e.matmul(po[:, osl], lhsT=QP[n][:, t, 1, :], rhs=Sb[(s0 + 1) % 4][:, :],
                                                               start=False, stop=True), reads=[QP[n], Sb[(s0 + 1) % 4]], writes=[po])
            st_["sidx"] = sidx
            yield
        st_["sidx"] = sidx
        po3 = po[:, :].rearrange("p (t k) -> p t k", k=128)
        P.op("act", lambda e: e.activation(out=sq[n][:, :, :], in_=po3, func=AF.Square), reads=[po], writes=[sq[n]])
        P.op("dve", lambda e: e.reduce_sum(out=ss4[n][:, :], in_=sq[n][:, :, :], axis=AX.X), reads=[sq[n]], writes=[ss4[n]])
        P.op("dve", lambda e: e.tensor_scalar(out=ss4[n][:, :], in0=ss4[n][:, :], scalar1=1.0 / 128, scalar2=EPS,
                                              op0=ALU.mult, op1=ALU.add), reads=[ss4[n]], writes=[ss4[n]])
        P.op("pool", lambda e: e.tensor_tensor(out=rs4[n][:, :], in0=ss4[n][:, :], in1=mh4[:, :], op=ALU.pow),
             reads=[ss4[n], mh4], writes=[rs4[n]])
        y = yo[n]
        P.op("dve", lambda e: e.tensor_tensor(out=y[:, :, :], in0=po3, in1=rs4[n][:, :].unsqueeze(2).to_broadcast([128, G, 128]),
                                              op=ALU.mult), reads=[po, rs4[n]], writes=[y])
        P.op("dve", lambda e: e.tensor_tensor(out=y[:, :, :], in0=y[:, :, :], in1=sg[n][:, :, :], op=ALU.mult),
             reads=[y, sg[n]], writes=[y])
        P.dma("sp", d["y"][gs, :].rearrange("(t p) k -> p t k", p=128), y[:, :, :], src=y)

    for _ in h1(0):
        pass
    for gi in range(ng):
        a = h1(gi + 1) if gi + 1 < ng else iter(())
        b = h2(gi)
        done_a = done_b = False
        while not (done_a and done_b):
            if not done_b:
                try:
                    next(b)
                except StopIteration:
                    done_b = True
            if not done_a:
                try:
                    next(a)
                except StopIteration:
                    done_a = True


def build_test_hgrn(S):
    P = Prog()
    d = {}
    for nm, shp in (("qT", [128, S]), ("zfT", [128, S]), ("zf", [S, 128]), ("v", [S, 128]), ("g", [S, 128]),
                    ("lbT", [128, 4]), ("lmask", [128, 4]), ("lbrow", [128, 4, 128]), ("lmaskrow", [128, 4, 128]),
                    ("gnorm", [128, 128]), ("M1", [128, 128]), ("M2", [128, 128])):
        d[nm] = P.dram(nm, shp, F32, "ExternalInput")
    d["y"] = P.dram("y", [S, 128], F32, "ExternalOutput")
    hgrn_phase(P, S, d)
    print("streams", P.stats(), "sems", P.n_sems)
    return P.finish()


import numpy as np

BIGF = 1.0e6


def nsa_consts_np(S):
    nsel = S // 64
    ncmp = S // 16 - 1
    ncp = ((ncmp + 127) // 128) * 128
    nch = S // 128
    p = np.arange(128)
    D1 = (p[:, None] - p[None, :]).astype(np.float32)
    D16 = (16 * p[:, None] - p[None, :]).astype(np.float32)
    Mdiag = (D1 <= 0).astype(np.float32)
    Mlow = (D1 > 0).astype(np.float32)
    j = np.arange(nsel)
    E = np.zeros((nsel, nch, 128), np.float32)
    for c in range(nch):
        E[:, c, :] = (j[:, None] == (2 * c + p[None, :] // 64))
    n = np.arange(ncp)
    cs = n * 16
    ss = j * 64
    ov = ((cs[:, None] < ss[None, :] + 64) & (cs[:, None] + 32 > ss[None, :]) & (n[:, None] < ncmp)).astype(np.float32)
    x = np.arange(2 * nsel) - nsel
    cur_rel = (p >= 64).astype(np.int64)
    forced = (x[None, :] == cur_rel[:, None]) | (x[None, :] == cur_rel[:, None] - 1)
    invalid = x[None, :] > cur_rel[:, None]
    Amul = (~forced & ~invalid).astype(np.float32)
    Aadd = np.where(forced, BIGF, np.where(invalid, -BIGF, 0.0)).astype(np.float32)
    ident = np.eye(128, dtype=np.float32)
    return dict(D16=D16, Mdiag=Mdiag, Mlow=Mlow, E=E, ov=ov, Amul=Amul, Aadd=Aadd, identb=ident)


def nsa_phase(P, S, NQB, d):
    nsel = S // 64
    ncmp = S // 16 - 1
    ncp = ((ncmp + 127) // 128) * 128
    ncc = ncp // 128
    nch = S // 128
    W = 65 + nsel

    def cload(name, shape, dt, src, q="pool"):
        b = P.sb(shape, dt, name)
        P.dma(q, b[tuple(slice(None) for _ in shape)], src, dst=b)
        return b

    D16 = cload("D16", [128, 128], F32, d["D16"][:, :], "sp")
    Mdiag = cload("Mdiag", [128, 128], BF16, d["Mdiag"][:, :])
    Mlow = cload("Mlow", [128, 128], BF16, d["Mlow"][:, :])
    Ec = cload("E", [nsel, nch, 128], BF16, d["E"][:, :, :])
    Amul = cload("Amul", [128, 2 * nsel], F32, d["Amul"][:, :], "sp")
    Aadd = cload("Aadd", [128, 2 * nsel], F32, d["Aadd"][:, :], "sp")
    identb = cload("identb", [128, 128], BF16, d["identb"][:, :])
    ksT = P.sb([128, S], BF16, "ksT")
    kwT = P.sb([128, S], BF16, "kwT")
    for kb, nm in ((ksT, "ksT"), (kwT, "kwT")):
        P.op("pool", lambda e, kb=kb: e.memset(kb[64:128, :], 0.0), writes=[kb])
        P.dma("pool", kb[0:64, :], d[nm][:, :], dst=kb)
    vs = cload("vs", [128, nch, 65], BF16, d["vs"][:, :, :])
    vw = cload("vw", [128, nch, 65], BF16, d["vw"][:, :, :])
    KcT = P.sb([128, ncp], BF16, "KcT")
    Vc = P.sb([128, ncc, W], BF16, "Vc")
    P.op("pool", lambda e: e.memset(KcT[:, :], 0.0), writes=[KcT])
    P.op("pool", lambda e: e.memset(Vc[:, :, :], 0.0), writes=[Vc])
    P.op("pool", lambda e: e.memset(Vc[:, :, 64:65], 1.0), writes=[Vc])
    P.dma("pool", Vc[:, :, 65:W], d["ov"][:, :, :], dst=Vc)

    ps_h = [P.ps([128, 512], F32, "ps_h") for _ in range(2)]
    ps_b = P.ps([128, 512], F32, "ps_b")
    pOs = P.ps([128, 512], F32, "pOs")
    ps_o = pOs
    for X in range(2):
        x2T = cload("x2T", [128, S], BF16, d["kc2T" if X == 0 else "vc2T"][:, :])
        pe2 = cload("pe2", [128, 16], BF16, d["pe2"][X])
        w1 = cload("w1", [128, 16, 256], BF16, d["w1"][X])
        w2 = cload("w2", [128, 2, 64], BF16, d["w2"][X])
        GT = P.sb([128, 2, ncp], BF16, "GT")
        P.op("pool", lambda e, GT=GT: e.memset(GT[:, :, :], 0.0), writes=[GT])
        x2v = x2T[:, :].rearrange("p (n s) -> p n s", s=16)
        for hc in range(2):
            for j in range(16):
                P.op("pe", lambda e, j=j, hc=hc, w1=w1, pe2=pe2: e.matmul(
                    ps_b[:, 0:1], lhsT=w1[:, j, hc * 128:(hc + 1) * 128], rhs=pe2[:, j:j + 1],
                    start=(j == 0), stop=(j == 15)), reads=[w1, pe2], writes=[ps_b])
            c1 = P.sb([128, 1], F32, "c1")
            P.op("dve", lambda e, c1=c1: e.tensor_copy(out=c1[:, :], in_=ps_b[:, 0:1]), reads=[ps_b], writes=[c1])
            ph = ps_h[hc]
            for j in range(16):
                if j < 8:
                    rhs = x2v[:, 0:ncmp, 2 * j]
                else:
                    rhs = x2v[:, 1:ncmp + 1, 2 * j - 16]
                P.op("pe", lambda e, j=j, hc=hc, w1=w1, rhs=rhs, ph=ph: e.matmul(
                    ph[:, 0:ncmp], lhsT=w1[:, j, hc * 128:(hc + 1) * 128], rhs=rhs,
                    start=(j == 0), stop=(j == 15)), reads=[w1, x2T], writes=[ph])
            xs = P.sb([128, ncp], F32, "xs")
            P.op("act", lambda e, xs=xs, ph=ph, c1=c1: e.activation(out=xs[:, 0:ncmp], in_=ph[:, 0:ncmp],
                                                                     func=AF.Identity, bias=c1[:, :]),
                 reads=[ph, c1], writes=[xs])
            t2 = P.sb([128, ncp], F32, "t2")
            P.op("dve", lambda e, xs=xs, t2=t2: e.tensor_tensor(out=t2[:, 0:ncmp], in0=xs[:, 0:ncmp], in1=xs[:, 0:ncmp],
                                                                op=ALU.mult), reads=[xs], writes=[t2])
            P.op("dve", lambda e, t2=t2: e.tensor_scalar(out=t2[:, 0:ncmp], in0=t2[:, 0:ncmp], scalar1=0.044715,
                                                         scalar2=1.0, op0=ALU.mult, op1=ALU.add), reads=[t2], writes=[t2])
            P.op("dve", lambda e, xs=xs, t2=t2: e.tensor_tensor(out=t2[:, 0:ncmp], in0=t2[:, 0:ncmp], in1=xs[:, 0:ncmp],
                                                                op=ALU.mult), reads=[xs, t2], writes=[t2])
            P.op("act", lambda e, t2=t2: e.activation(out=t2[:, 0:ncmp], in_=t2[:, 0:ncmp], func=AF.Sigmoid,
                                                      scale=1.5957691216057308), reads=[t2], writes=[t2])
            P.op("dve", lambda e, xs=xs, t2=t2, GT=GT, hc=hc: e.tensor_tensor(
                out=GT[:, hc, 0:ncmp], in0=t2[:, 0:ncmp], in1=xs[:, 0:ncmp], op=ALU.mult), reads=[xs, t2], writes=[GT])
        if X == 0:
            for hc in range(2):
                P.op("pe", lambda e, hc=hc, w2=w2, GT=GT: e.matmul(ps_o[0:64, 0:ncp], lhsT=w2[:, hc, :], rhs=GT[:, hc, :],
                                                                    start=(hc == 0), stop=(hc == 1)),
                     reads=[w2, GT], writes=[ps_o])
            P.op("act", lambda e: e.activation(out=KcT[0:64, 0:ncmp], in_=ps_o[0:64, 0:ncmp], func=AF.Copy),
                 reads=[ps_o], writes=[KcT])
        else:
            for c in range(ncc):
                for hc in range(2):
                    P.op("pe", lambda e, hc=hc, c=c, w2=w2, GT=GT: e.matmul(
                        ps_o[:, 0:64], lhsT=GT[:, hc, c * 128:(c + 1) * 128], rhs=w2[:, hc, :],
                        start=(hc == 0), stop=(hc == 1)), reads=[w2, GT], writes=[ps_o])
                P.op("act", lambda e, c=c: e.activation(out=Vc[:, c, 0:64], in_=ps_o[:, 0:64], func=AF.Copy),
                     reads=[ps_o], writes=[Vc])

    pS = [ps_h[0], ps_h[1], P.ps([128, 512], F32, "pS2")]
    pM = ps_b
    pOc = [P.ps([128, 512], F32, "pOc") for _ in range(2)]
    pOw = P.ps([128, 512], F32, "pOw")
    WMk = cload("WM", [128, 8, 128], BF16, d["WM"][:, :, :])
    Tt = cload("Tt", [128, NQB, ncc], F32, d["Tt"][:, :, :], "sp")
    identf = cload("identf", [128, 128], F32, d["identb"][:, :], "sp")
    QTb = [P.sb([128, 512], BF16, "QT") for _ in range(2)]
    for qb_ in QTb:
        P.op("pool", lambda e, qb_=qb_: e.memset(qb_[64:128, :], 0.0), writes=[qb_])
    gl_all = P.sb([128, NQB, 12], F32, "gl_all")
    P.dma("sp", gl_all[:, :, :], d["gl"][:, :, :], dst=gl_all)
    P.op("act", lambda e: e.activation(out=gl_all[:, :, :], in_=gl_all[:, :, :], func=AF.Sigmoid), reads=[gl_all], writes=[gl_all])
    NPT = 6
    PT = [P.sb([128, 4, 128], BF16, "PT") for _ in range(NPT)]
    PTm = [P.sb([128, 4, 128], BF16, "PTm") for _ in range(NPT)]
    mk = [P.sb([128, 128], BF16, "mk") for _ in range(3)]
    rd = [P.sb([128, 4], F32, "rd") for _ in range(6)]
    wgt = [P.sb([128, 4], F32, "wgt") for _ in range(6)]
    osb = [P.sb([128, 260], F32, "osb") for _ in range(4)]
    import collections
    q_cmp = collections.deque()
    q_comb = collections.deque()
    imp = P.sb([128, nsel], F32, "imp")
    sc = P.sb([128, nsel], F32, "sc")
    sc2 = P.sb([128, nsel], F32, "sc2")
    m8a = P.sb([128, 8], F32, "m8a")
    m8b = P.sb([128, 8], F32, "m8b")
    self_f = P.sb([128, nsel], F32, "self")
    selT = [P.sb([nsel, 128], BF16, "selT") for _ in range(2)]
    yacc = [P.sb([128, 4, 64], F32, "yacc") for _ in range(2)]
    state = {"pt": 0, "ps": 0, "mk": 0, "pm": 0}

    class Task:
        pass

    def chunk_task(kT, c, QT, mask_kind, mask_arg, vaug, vw_cols, pouts, first, last, selTb=None):
        t = Task()
        ps = pS[state["ps"] % 3]
        state["ps"] += 1
        pt = PT[state["pt"] % NPT]
        ptm = PTm[state["pt"] % NPT]
        state["pt"] += 1

        def s1():
            P.op("pe", lambda e: e.matmul(ps[:, :], lhsT=kT[:, c * 128:(c + 1) * 128], rhs=QT[:, :], start=True, stop=True),
                 reads=[kT, QT], writes=[ps])

        def s2():
            P.op("act", lambda e: e.activation(out=pt[:, :, :], in_=ps[:, :].rearrange("p (h q) -> p h q", h=4),
                                               func=AF.Exp, scale=0.125), reads=[ps], writes=[pt])
            src = pt
            mbuf = None
            if mask_kind == "none":
                pass
            elif mask_kind == "sb":
                mbuf, map_ = mask_arg
            elif mask_kind == "cmp":
                m_i, cc = mask_arg
                mbuf = mk[state["mk"] % 3]
                state["mk"] += 1
                P.op("dve", lambda e, mb=mbuf: e.tensor_scalar(out=mb[:, :], in0=D16[:, :], scalar1=Tt[:, m_i, cc:cc + 1],
                                                               scalar2=None, op0=ALU.is_le), reads=[D16, Tt], writes=[mbuf])
                map_ = mbuf[:, :]
            elif mask_kind == "slc":
                tail = mask_arg
                off = 256 * (state["pm"] % 2)
                state["pm"] += 1
                P.op("pe", lambda e: e.matmul(pM[0:128, off:off + 128], lhsT=Ec[:, c, :], rhs=selTb[:, :], start=True, stop=True),
                     reads=[Ec, selTb], writes=[pM])
                if tail is None:
                    mbuf, map_ = pM, pM[:, off:off + 128]
                else:
                    mbuf = mk[state["mk"] % 3]
                    state["mk"] += 1
                    P.op("dve", lambda e, mb=mbuf: e.tensor_tensor(out=mb[:, :], in0=pM[:, off:off + 128], in1=WMk[:, tail, :],
                                                                   op=ALU.mult), reads=[pM, WMk], writes=[mbuf])
                    map_ = mbuf[:, :]
            if mbuf is not None:
                P.op("dve", lambda e: e.tensor_tensor(out=ptm[:, :, :], in0=pt[:, :, :],
                                                      in1=map_.unsqueeze(1).to_broadcast([128, 4, 128]), op=ALU.mult),
                     reads=[pt, mbuf], writes=[ptm])
                src = ptm
            t.src = src
            if q_cmp:
                q_cmp.popleft()()
            elif q_comb:
                q_comb.popleft()[1]()

        def s3():
            src = t.src
            for pb, heads in pouts:
                for hi, h in enumerate(heads):
                    st = first and hi == 0
                    P.op("pe", lambda e, pb=pb, hi=hi, h=h, st=st: e.matmul(
                        pb[:, hi * vw_cols:(hi + 1) * vw_cols], lhsT=src[:, h, :], rhs=vaug[:, c, 0:vw_cols],
                        start=st, stop=last, skip_group_check=True), reads=[src, vaug], writes=[pb])
        t.s1, t.s2, t.s3 = s1, s2, s3
        return t

    def marker(fn):
        t = Task()
        t.s1 = lambda: None
        t.s2 = lambda: None
        t.s3 = fn
        return t

    tasks = []
    for m_i in range(NQB):
        qbm = 2 * m_i + 1
        QT = QTb[m_i % 2]
        gl = gl_all
        glv = gl_all[:, m_i, :].rearrange("p (h t) -> p h t", t=3)
        ya = yacc[m_i % 2]
        selTb = selT[m_i % 2]

        def load(m_i=m_i, QT=QT, gl=gl):
            P.dma("pool", QT[0:64, :], d["QT"][:, m_i, :], dst=QT)
        def flush_old(m_i=m_i):
            while q_comb and q_comb[0][0] <= m_i - 2:
                q_comb.popleft()[1]()
        ld = marker(flush_old)
        ld.s1 = load
        tasks.append(ld)
        nvalid = min(8 * qbm + 7, ncmp)
        nchunks = (nvalid + 127) // 128
        for c in range(nchunks):
            Tmin = 128 * (qbm - 1) - 2048 * c - 31
            kind, arg = ("none", None) if Tmin >= 16 * 127 else ("cmp", (m_i, c))
            tasks.append(chunk_task(KcT, c, QT, kind, arg, Vc, W, [(pOc[0], [0, 1]), (pOc[1], [2, 3])], c == 0, c == nchunks - 1))

        def after_cmp(m_i=m_i, glv=glv, gl=gl, ya=ya, selTb=selTb):
            r_, w_ = rd[(m_i % 2) * 3], wgt[(m_i % 2) * 3]
            for g2 in range(2):
                ov_ = pOc[g2][:, 0:2 * W].rearrange("p (h w) -> p h w", h=2)
                P.op("dve", lambda e, ov_=ov_, g2=g2: e.tensor_scalar(out=r_[:, 2 * g2:2 * g2 + 2], in0=ov_[:, :, 64],
                                                                      scalar1=1e-30, scalar2=None, op0=ALU.max),
                     reads=[pOc[g2]], writes=[r_])
            P.op("dve", lambda e: e.reciprocal(out=r_[:, :], in_=r_[:, :]), reads=[r_], writes=[r_])
            for h in range(4):
                pb = pOc[h // 2]
                base = (h % 2) * W
                if h == 0:
                    P.op("dve", lambda e, pb=pb, base=base, h=h: e.tensor_scalar(
                        out=imp[:, :], in0=pb[:, base + 65:base + W], scalar1=r_[:, h:h + 1], scalar2=None, op0=ALU.mult),
                        reads=[pb, r_], writes=[imp])
                else:
                    P.op("dve", lambda e, pb=pb, base=base, h=h: e.scalar_tensor_tensor(
                        out=imp[:, :], in0=pb[:, base + 65:base + W], scalar=r_[:, h:h + 1], in1=imp[:, :],
                        op0=ALU.mult, op1=ALU.add), reads=[pb, r_, imp], writes=[imp])
            x0 = nsel - 4 * m_i
            P.op("dve", lambda e: e.tensor_tensor(out=sc[:, :], in0=imp[:, :], in1=Amul[:, x0:x0 + nsel], op=ALU.mult),
                 reads=[imp, Amul], writes=[sc])
            P.op("dve", lambda e: e.tensor_tensor(out=sc[:, :], in0=sc[:, :], in1=Aadd[:, x0:x0 + nsel], op=ALU.add),
                 reads=[sc, Aadd], writes=[sc])
            P.op("dve", lambda e: e.memset(sc[:, 0:1], BIGF), writes=[sc])
            P.op("dve", lambda e: e.max(out=m8a[:, :], in_=sc[:, :]), reads=[sc], writes=[m8a])
            P.op("dve", lambda e: e.match_replace(out=sc2[:, :], in_to_replace=m8a[:, :], in_values=sc[:, :],
                                                  imm_value=-3.0 * BIGF), reads=[sc, m8a], writes=[sc2])
            P.op("dve", lambda e: e.max(out=m8b[:, :], in_=sc2[:, :]), reads=[sc2], writes=[m8b])
            P.op("dve", lambda e: e.tensor_scalar(out=self_f[:, :], in0=sc[:, :], scalar1=m8b[:, 7:8], scalar2=None,
                                                  op0=ALU.is_ge), reads=[sc, m8b], writes=[self_f])
            P.op("pe", lambda e: e.transpose(out=pM[0:nsel, 128:256], in_=self_f[:, :], identity=identf[:, :]),
                 reads=[self_f, identf], writes=[pM])
            P.op("act", lambda e: e.activation(out=selTb[:, :], in_=pM[0:nsel, 128:256], func=AF.Copy), reads=[pM], writes=[selTb])
            P.op("dve", lambda e: e.tensor_tensor(out=w_[:, :], in0=r_[:, :], in1=glv[:, :, 0], op=ALU.mult),
                 reads=[r_, gl], writes=[w_])
            for h in range(4):
                pb = pOc[h // 2]
                base = (h % 2) * W
                q_cmp.append(lambda pb=pb, base=base, h=h: P.op("dve", lambda e: e.tensor_scalar(
                    out=ya[:, h, :], in0=pb[:, base:base + 64], scalar1=w_[:, h:h + 1], scalar2=None, op0=ALU.mult),
                    reads=[pb, w_], writes=[ya]))
        tasks.append(marker(after_cmp))
        wl = [c for c in range(2 * m_i - 4, 2 * m_i + 2) if c >= 0]
        for c in wl:
            r = c - (2 * m_i - 4)
            tasks.append(chunk_task(kwT, c, QT, "sb", (WMk, WMk[:, r, :]), vw, 65, [(pOw, [0, 1, 2, 3])], c == wl[0], c == wl[-1]))
        for c in range(0, 2 * m_i + 2):
            tail = (6 + c - 2 * m_i) if c >= 2 * m_i else None
            tasks.append(chunk_task(ksT, c, QT, "slc", tail, vs, 65, [(pOs, [0, 1, 2, 3])], c == 0, c == 2 * m_i + 1, selTb=selTb))

        def combine(m_i=m_i, glv=glv, gl=gl, ya=ya):
            while q_cmp:
                q_cmp.popleft()()
            for bi, pb in ((2, pOw), (1, pOs)):
                r_, w_ = rd[(m_i % 2) * 3 + bi], wgt[(m_i % 2) * 3 + bi]
                ob = osb[(m_i % 2) * 2 + (bi - 1)]
                P.op("act", lambda e, ob=ob, pb=pb: e.activation(out=ob[:, :], in_=pb[:, 0:260], func=AF.Copy), reads=[pb], writes=[ob])
                pv = ob[:, :].rearrange("p (h w) -> p h w", h=4)
                q_comb.append((m_i, lambda pv=pv, r_=r_, ob=ob: P.op("dve", lambda e: e.tensor_scalar(
                    out=r_[:, :], in0=pv[:, :, 64], scalar1=1e-30, scalar2=None, op0=ALU.max), reads=[ob], writes=[r_])))
                q_comb.append((m_i, lambda r_=r_: P.op("dve", lambda e: e.reciprocal(out=r_[:, :], in_=r_[:, :]), reads=[r_], writes=[r_])))
                q_comb.append((m_i, lambda r_=r_, w_=w_, bi=bi: P.op("dve", lambda e: e.tensor_tensor(
                    out=w_[:, :], in0=r_[:, :], in1=glv[:, :, bi], op=ALU.mult), reads=[r_, gl], writes=[w_])))
                for h in range(4):
                    q_comb.append((m_i, lambda ob=ob, h=h, w_=w_: P.op("dve", lambda e: e.scalar_tensor_tensor(
                        out=ya[:, h, :], in0=ob[:, h * 65:h * 65 + 64], scalar=w_[:, h:h + 1], in1=ya[:, h, :],
                        op0=ALU.mult, op1=ALU.add), reads=[ob, w_, ya], writes=[ya])))
            q_comb.append((m_i, lambda: P.dma("sp", d["y"][m_i], ya[:, :, :].rearrange("p h w -> p (h w)"), src=ya)))
        tasks.append(marker(combine))

    DEPTH = 3
    n = len(tasks)
    for i in range(min(DEPTH, n)):
        tasks[i].s1()
    for i in range(n):
        tasks[i].s2()
        if i + DEPTH < n:
            tasks[i + DEPTH].s1()
        tasks[i].s3()
    while q_cmp:
        q_cmp.popleft()()
    while q_comb:
        q_comb.popleft()[1]()


NSA_IN = lambda S, NQB: [
    ("QT", [64, NQB, 512]), ("gl", [128, NQB, 12]), ("ksT", [64, S]), ("kwT", [64, S]), ("vs", [128, S // 128, 65]), ("vw", [128, S // 128, 65]),
    ("kc2T", [128, S]), ("vc2T", [128, S]), ("pe2", [2, 128, 16]), ("w1", [2, 128, 16, 256]), ("w2", [2, 128, 2, 64]),
    ("D16", [128, 128]), ("Mdiag", [128, 128]), ("Mlow", [128, 128]), ("E", [S // 64, S // 128, 128]),
    ("ov", [128, ((S // 16 - 1 + 127) // 128), S // 64]), ("Amul", [128, 2 * (S // 64)]), ("Aadd", [128, 2 * (S // 64)]),
    ("identb", [128, 128]), ("WM", [128, 8, 128]), ("Tt", [128, NQB, ((S // 16 - 1 + 127) // 128)])]


def build_test_nsa(S, NQB):
    P = Prog()
    d = {}
    for nm, shp in NSA_IN(S, NQB):
        d[nm] = P.dram(nm, shp, F32, "ExternalInput")
    d["y"] = P.dram("y", [NQB, 128, 256], F32, "ExternalOutput")
    nsa_phase(P, S, NQB, d)
    print("streams", P.stats(), "sems", P.n_sems)
    return P.finish()


import math
import numpy as np

NT = 2048
NTILE = NT // 128

A_OPS = ([("copy", 1)] * 4 + [("prod", 2)] * 4 + [("silu", 1)] * 4 + [("copy", 1)] * 12 +
         [("rope", 2)] * 4 + [("rope", 2)] * 3 + [("copy", 1)] * 3 + [("gate", 1)])
A_NOUT = len(A_OPS)
A_NIN = sum(n for _, n in A_OPS)
A_COLS = (A_NIN - 1) * 128 + 24
A_ROWS = (A_NOUT - 1) * 128 + 24


def a_col_perm():
    BR = 512
    o = {}
    names = ["ab", "ac", "ax", "hq", "hf", "hi", "hg", "nq", "nkc", "nvc", "nks", "nvs", "nkw", "nvw", "ng", "mg"]
    sizes = [512] * 3 + [512] * 4 + [512] + [128] * 6 + [24, 3072]
    off = 0
    for n, s in zip(names, sizes):
        o[n] = np.arange(off, off + s)
        off += s

    def swap64(ix):
        ix = ix.reshape(-1, 2, 32)
        return ix[:, ::-1, :].reshape(-1)
    cols = []
    for c in range(4):
        cols.append(o["ab"][c * 128:(c + 1) * 128])
    for c in range(4):
        cols.append(o["ac"][c * 128:(c + 1) * 128])
        cols.append(o["ax"][c * 128:(c + 1) * 128])
    for n in ("hq", "hf", "hi", "hg"):
        for c in range(4):
            cols.append(o[n][c * 128:(c + 1) * 128])
    for c in range(4):
        ix = o["nq"][c * 128:(c + 1) * 128]
        cols.append(ix)
        cols.append(swap64(ix))
    for n in ("nkc", "nks", "nkw"):
        cols.append(o[n])
        cols.append(swap64(o[n]))
    for n in ("nvc", "nvs", "nvw"):
        cols.append(o[n])
    cols.append(o["ng"])
    return np.concatenate(cols), o["mg"]


def rope_tables(P, pos_d, inv_d, sgn_d, n):
    posi = P.sb([128, n], I32, "posi")
    P.dma("sp", posi[:, :], pos_d[:, :], dst=posi)
    inv = P.sb([128, 1], F32, "inv")
    sgn = P.sb([128, 1], F32, "sgn")
    P.dma("sp", inv[:, :], inv_d[:, :], dst=inv)
    P.dma("sp", sgn[:, :], sgn_d[:, :], dst=sgn)
    ang = P.sb([128, n], F32, "ang")
    P.op("dve", lambda e: e.tensor_copy(out=ang[:, :], in_=posi[:, :]), reads=[posi], writes=[ang])
    P.op("dve", lambda e: e.tensor_scalar(out=ang[:, :], in0=ang[:, :], scalar1=inv[:, :], scalar2=None, op0=ALU.mult),
         reads=[ang, inv], writes=[ang])
    qf = P.sb([128, n], F32, "qf")
    P.op("dve", lambda e: e.tensor_scalar(out=qf[:, :], in0=ang[:, :], scalar1=1.0 / (2 * math.pi), scalar2=None,
                                          op0=ALU.mult), reads=[ang], writes=[qf])
    qi = P.sb([128, n], I32, "qi")
    P.op("dve", lambda e: e.tensor_copy(out=qi[:, :], in_=qf[:, :]), reads=[qf], writes=[qi])
    P.op("dve", lambda e: e.tensor_copy(out=qf[:, :], in_=qi[:, :]), reads=[qi], writes=[qf])
    C1, C2 = 6.28125, 2 * math.pi - 6.28125
    C2a = float(np.float32(C2))
    C3 = C2 - C2a
    for Cc in (C1, C2a, C3):
        P.op("dve", lambda e, Cc=Cc: e.scalar_tensor_tensor(out=ang[:, :], in0=qf[:, :], scalar=-Cc, in1=ang[:, :],
                                                            op0=ALU.mult, op1=ALU.add), reads=[qf, ang], writes=[ang])
    m = qf
    P.op("dve", lambda e: e.tensor_scalar(out=m[:, :], in0=ang[:, :], scalar1=math.pi, scalar2=-2 * math.pi,
                                          op0=ALU.is_gt, op1=ALU.mult), reads=[ang], writes=[m])
    P.op("dve", lambda e: e.tensor_tensor(out=ang[:, :], in0=ang[:, :], in1=m[:, :], op=ALU.add), reads=[ang, m], writes=[ang])
    P.op("dve", lambda e: e.tensor_scalar(out=m[:, :], in0=ang[:, :], scalar1=-math.pi, scalar2=2 * math.pi,
                                          op0=ALU.is_lt, op1=ALU.mult), reads=[ang], writes=[m])
    P.op("dve", lambda e: e.tensor_tensor(out=ang[:, :], in0=ang[:, :], in1=m[:, :], op=ALU.add), reads=[ang, m], writes=[ang])
    P.op("dve", lambda e: e.tensor_scalar(out=ang[:, :], in0=ang[:, :], scalar1=3.14159, scalar2=-3.14159,
                                          op0=ALU.min, op1=ALU.max), reads=[ang], writes=[ang])
    SINS = P.sb([128, n], F32, "SINS")
    COS = P.sb([128, n], F32, "COS")
    P.op("act", lambda e: e.activation(out=SINS[:, :], in_=ang[:, :], func=AF.Sin), reads=[ang], writes=[SINS])
    P.op("dve", lambda e: e.tensor_scalar(out=SINS[:, :], in0=SINS[:, :], scalar1=sgn[:, :], scalar2=None, op0=ALU.mult),
         reads=[SINS, sgn], writes=[SINS])
    P.op("dve", lambda e: e.tensor_scalar(out=m[:, :], in0=ang[:, :], scalar1=-1.0, scalar2=None, op0=ALU.mult),
         reads=[ang], writes=[m])
    P.op("dve", lambda e: e.tensor_tensor(out=ang[:, :], in0=ang[:, :], in1=m[:, :], op=ALU.max), reads=[ang, m], writes=[ang])
    P.op("dve", lambda e: e.tensor_scalar(out=ang[:, :], in0=ang[:, :], scalar1=-1.0, scalar2=math.pi / 2, op0=ALU.mult,
                                          op1=ALU.add), reads=[ang], writes=[ang])
    P.op("act", lambda e: e.activation(out=COS[:, :], in_=ang[:, :], func=AF.Sin), reads=[ang], writes=[COS])
    return COS, SINS


def make_h2T(P, C, x_src_d, g2, h2T, tp, ntile):
    xt = [P.sb([128, D], F32, "x1t") for _ in range(2)]
    hb = [P.sb([128, D], BF16, "h2") for _ in range(2)]
    for i in range(ntile):
        xb = xt[i % 2]
        P.dma("sp", xb[:, :], x_src_d[i * 128:(i + 1) * 128, :], dst=xb)
        rstd = rms_stats(P, C, xb, xb[:, :], False)
        h = hb[i % 2]
        P.op("dve", lambda e, h=h, xb=xb, rstd=rstd: e.scalar_tensor_tensor(
            out=h[:, :], in0=xb[:, :], scalar=rstd[:, :], in1=g2[:, :], op0=ALU.mult, op1=ALU.mult),
            reads=[xb, rstd, g2], writes=[h])
        for k in range(8):
            P.op("pe", lambda e, h=h, k=k: e.transpose(out=tp[:, k * 128:(k + 1) * 128],
                                                      in_=h[:, k * 128:(k + 1) * 128], identity=C.ident[:, :]),
                 reads=[h, C.ident], writes=[tp])
        P.op("act", lambda e, i=i: e.activation(out=h2T[:, :, i * 128:(i + 1) * 128],
                                                in_=tp[:, :].rearrange("p (k n) -> p k n", k=8), func=AF.Copy),
             reads=[tp], writes=[h2T])


def body_A(P, C, x_d, pfx):
    gains_d = P.dram(pfx + "gains", [3, 128, D], F32, "ExternalInput")
    wg_d = P.dram(pfx + "wg", [D, DFF], F32, "ExternalInput")
    wu_d = P.dram(pfx + "wu", [D, DFF], F32, "ExternalInput")
    wd_d = P.dram(pfx + "wd", [DFF, D], F32, "ExternalInput")
    wa_d = P.dram(pfx + "wa", [D, A_COLS], F32, "ExternalInput")
    pos_d = P.dram(pfx + "pos", [128, NT], I32, "ExternalInput")
    inv_d = P.dram(pfx + "inv", [128, 1], F32, "ExternalInput")
    sgn_d = P.dram(pfx + "sgn", [128, 1], F32, "ExternalInput")
    x1_d = P.dram(pfx + "x1", [NT, D], F32, "ExternalOutput")
    fm_d = P.dram(pfx + "fm", [A_ROWS, NT], F32, "ExternalOutput")
    g = []
    for i in range(3):
        b = P.sb([128, D], F32, "gain")
        P.dma("sp", b[:, :], gains_d[i], dst=b)
        g.append(b)
    with P.scope():
        wg, wu, wd = load_ffn_weights(P, wg_d, wu_d, wd_d)
        xpool = [P.sb([128, D], F32, "x") for _ in range(4)]

        def get_x(i):
            b = xpool[i % 4]
            P.dma("sp", b[:, :], x_d[i * 128:(i + 1) * 128, :], dst=b)
            return b

        def put_x(i, b):
            P.dma("sp", x1_d[i * 128:(i + 1) * 128, :], b[:, :], src=b)

        ffn_phase(P, C, NTILE, get_x, put_x, g[0], g[1], wg, wu, wd)
    with P.scope():
        COS, SINS = rope_tables(P, pos_d, inv_d, sgn_d, NT)
        h2T = P.sb([128, 8, NT], BF16, "h2T")
        tp = P.ps([128, 8 * 128], BF16, "tp")
        make_h2T(P, C, x1_d, g[2], h2T, tp, NTILE)
        wsrc = wa_d.rearrange("(c p) n -> p c n", p=128)
        wbufs = [P.sb([128, 8, 512], BF16, "wa") for _ in range(3)]
        pp = [P.ps([128, 512], F32, "pp") for _ in range(4)]
        stg = [P.sb([128, 512], F32, "stg") for _ in range(4)]
        tmp = [P.sb([128, 512], F32, "tmp") for _ in range(2)]
        nblk = (A_NIN + 3) // 4
        ops = []
        ic = 0
        for oc, (kind, nin) in enumerate(A_OPS):
            ops.append((oc, kind, list(range(ic, ic + nin))))
            ic += nin
        loaded = {}
        state = {"pp": 0, "stg": 0, "tmp": 0}

        def get_w(chunk):
            blk = chunk // 4
            _load(blk)
            if blk + 1 < nblk:
                _load(blk + 1)
            wb = loaded[blk]
            off = (chunk % 4) * 128
            return wb, off

        def _load(blk):
            if blk not in loaded:
                wb = wbufs[blk % 3]
                c0 = blk * 512
                c1 = min(A_COLS, c0 + 512)
                P.dma("pool", wb[:, :, 0:c1 - c0], wsrc[:, :, c0:c1], dst=wb)
                loaded[blk] = wb

        for oc, kind, chunks in ops:
            rows = 24 if kind == "gate" else 128
            for tg in range(NT // 512):
                tsl = slice(tg * 512, (tg + 1) * 512)
                pts = []
                for ch in chunks:
                    wb, off = get_w(ch)
                    pt = pp[state["pp"] % 4]
                    state["pp"] += 1
                    for k in range(8):
                        P.op("pe", lambda e, pt=pt, wb=wb, off=off, k=k, rows=rows, tsl=tsl: e.matmul(
                            pt[0:rows, :], lhsT=wb[:, k, off:off + rows], rhs=h2T[:, k, tsl], start=(k == 0), stop=(k == 7)),
                            reads=[wb, h2T], writes=[pt])
                    pts.append(pt)
                sb = stg[state["stg"] % 4]
                state["stg"] += 1
                if kind in ("copy", "gate"):
                    if oc % 2 == 0:
                        P.op("act", lambda e, sb=sb, pt=pts[0], rows=rows: e.activation(out=sb[0:rows, :], in_=pt[0:rows, :],
                                                                                        func=AF.Copy), reads=[pts[0]], writes=[sb])
                    else:
                        P.op("dve", lambda e, sb=sb, pt=pts[0], rows=rows: e.tensor_copy(out=sb[0:rows, :], in_=pt[0:rows, :]),
                             reads=[pts[0]], writes=[sb])
                elif kind == "silu":
                    P.op("act", lambda e, sb=sb, pt=pts[0]: e.activation(out=sb[:, :], in_=pt[:, :], func=AF.Silu),
                         reads=[pts[0]], writes=[sb])
                elif kind == "prod":
                    t = tmp[state["tmp"] % 2]
                    state["tmp"] += 1
                    P.op("act", lambda e, t=t, pt=pts[0]: e.activation(out=t[:, :], in_=pt[:, :], func=AF.Copy),
                         reads=[pts[0]], writes=[t])
                    P.op("dve", lambda e, sb=sb, t=t, pt=pts[1]: e.tensor_tensor(out=sb[:, :], in0=t[:, :], in1=pt[:, :],
                                                                                 op=ALU.mult), reads=[t, pts[1]], writes=[sb])
                elif kind == "rope":
                    t = tmp[state["tmp"] % 2]
                    state["tmp"] += 1
                    P.op("dve", lambda e, t=t, pt=pts[0], tsl=tsl: e.tensor_tensor(out=t[:, :], in0=pt[:, :], in1=COS[:, tsl],
                                                                                   op=ALU.mult), reads=[pts[0], COS], writes=[t])
                    P.op("dve", lambda e, sb=sb, pt=pts[1], tsl=tsl: e.tensor_tensor(out=sb[:, :], in0=pt[:, :], in1=SINS[:, tsl],
                                                                                     op=ALU.mult), reads=[pts[1], SINS], writes=[sb])
                    P.op("pool", lambda e, sb=sb, t=t: e.tensor_tensor(out=sb[:, :], in0=sb[:, :], in1=t[:, :], op=ALU.add),
                         reads=[sb, t], writes=[sb])
                P.dma("sp", fm_d[oc * 128:oc * 128 + rows, tsl], sb[0:rows, :], src=sb)


def build_A():
    P = Prog()
    ident_d = P.dram("ident", [128, 128], F32, "ExternalInput")
    x_d = P.dram("x", [NT, D], F32, "ExternalInput")
    C = build_consts(P, ident_d)
    with P.scope():
        body_A(P, C, x_d, "")
    print("A streams", P.stats(), "sems", P.n_sems)
    return P.finish()


def body_C(P, C, out_d, pfx):
    x1_d = P.dram(pfx + "x1", [NT, D], F32, "ExternalInput")
    gains_d = P.dram(pfx + "gains", [4, 128, D], F32, "ExternalInput")
    vT_d = P.dram(pfx + "vT", [512, NT + 2], F32, "ExternalInput")
    bT_d = P.dram(pfx + "bT", [512, NT], F32, "ExternalInput")
    ybT_d = P.dram(pfx + "ybT", [512, NT], F32, "ExternalInput")
    ycT_d = P.dram(pfx + "ycT", [512, NT], F32, "ExternalInput")
    cw_d = P.dram(pfx + "cw", [128, 4, 3], F32, "ExternalInput")
    wmg_d = P.dram(pfx + "wmg", [D, 3 * D], F32, "ExternalInput")
    wbr_d = P.dram(pfx + "wbr", [3 * 512, D], F32, "ExternalInput")
    wout_d = P.dram(pfx + "wout", [D, D], F32, "ExternalInput")
    wg_d = P.dram(pfx + "wg", [D, DFF], F32, "ExternalInput")
    wu_d = P.dram(pfx + "wu", [D, DFF], F32, "ExternalInput")
    wd_d = P.dram(pfx + "wd", [DFF, D], F32, "ExternalInput")
    x2_d = P.dram(pfx + "x2s", [NT, D], F32, "Internal")
    g = []
    for i in range(4):
        b = P.sb([128, D], F32, "gain")
        P.dma("sp", b[:, :], gains_d[i], dst=b)
        g.append(b)
    with P.scope():
        cw = P.sb([128, 4, 3], F32, "cw")
        P.dma("sp", cw[:, :, :], cw_d[:, :, :], dst=cw)
        wbr, ibr = load_w_cast(P, wbr_d, 3 * 512, D, 512, "wbr", defer=True)
        wmg, img = load_w_cast(P, wmg_d, D, 3 * D, 512, "wmg", defer=True)
        wout, iout = load_w_cast(P, wout_d, D, D, 512, "wout", defer=True)
        for f in [ibr[0], img[0], img[2], img[4], ibr[1], img[1], img[3], img[5]] + iout:
            f()
        tp = P.ps([128, 8 * 128], BF16, "tp")
        h2T = P.sb([128, 8, 512], BF16, "h2T")
        xt = [P.sb([128, D], F32, "x1t") for _ in range(4)]
        hb = [P.sb([128, D], BF16, "h2") for _ in range(2)]
        vin = [P.sb([128, 514], F32, "vin") for _ in range(2)]
        bin_ = [P.sb([128, 512], F32, "bin") for _ in range(2)]
        acc = [P.sb([128, 512], F32, "acc") for _ in range(2)]
        ybrs = [P.sb([128, 4, 512], BF16, "ybr%d" % i) for i in range(3)]
        mT = P.sb([128, 8, 512], BF16, "mT")
        macc = [P.sb([128, 512], F32, "macc") for _ in range(2)]
        gsb = [P.sb([128, 512], F32, "gsb") for _ in range(2)]
        tt = [P.sb([128, 512], F32, "tt") for _ in range(2)]
        pg = [P.ps([128, 512], F32, "pg") for _ in range(2)]
        ppj = [P.ps([128, 512], F32, "ppj") for _ in range(2)]
        py = P.ps([128, D], F32, "py")
        t1 = P.sb([128, D], F32, "t1")
        for tg in range(NT // 512):
            tsl = slice(tg * 512, (tg + 1) * 512)
            for t in range(4):
                i = tg * 4 + t
                xb = xt[t]
                P.dma("sp", xb[:, :], x1_d[i * 128:(i + 1) * 128, :], dst=xb)
                rstd = rms_stats(P, C, xb, xb[:, :], False)
                h = hb[t % 2]
                P.op("dve", lambda e, h=h, xb=xb, rstd=rstd: e.scalar_tensor_tensor(
                    out=h[:, :], in0=xb[:, :], scalar=rstd[:, :], in1=g[0][:, :], op0=ALU.mult, op1=ALU.mult),
                    reads=[xb, rstd, g[0]], writes=[h])
                for k in range(8):
                    P.op("pe", lambda e, h=h, k=k: e.transpose(out=tp[:, k * 128:(k + 1) * 128],
                                                              in_=h[:, k * 128:(k + 1) * 128], identity=C.ident[:, :]),
                         reads=[h, C.ident], writes=[tp])
                P.op("act", lambda e, t=t: e.activation(out=h2T[:, :, t * 128:(t + 1) * 128],
                                                        in_=tp[:, :].rearrange("p (k n) -> p k n", k=8), func=AF.Copy),
                     reads=[tp], writes=[h2T])
            for c in range(4):
                v = vin[c % 2]
                bb = bin_[c % 2]
                a = acc[c % 2]
                P.dma("sp", v[:, :], vT_d[c * 128:(c + 1) * 128, tg * 512:tg * 512 + 514], dst=v)
                P.dma("sp", bb[:, :], bT_d[c * 128:(c + 1) * 128, tsl], dst=bb)
                P.op("dve", lambda e, a=a, v=v, c=c: e.tensor_scalar(out=a[:, :], in0=v[:, 2:514], scalar1=cw[:, c, 2:3],
                                                                     scalar2=None, op0=ALU.mult), reads=[v, cw], writes=[a])
                P.op("dve", lambda e, a=a, v=v, c=c: e.scalar_tensor_tensor(out=a[:, :], in0=v[:, 1:513], scalar=cw[:, c, 1:2],
                                                                            in1=a[:, :], op0=ALU.mult, op1=ALU.add),
                     reads=[v, cw, a], writes=[a])
                P.op("dve", lambda e, a=a, v=v, c=c: e.scalar_tensor_tensor(out=a[:, :], in0=v[:, 0:512], scalar=cw[:, c, 0:1],
                                                                            in1=a[:, :], op0=ALU.mult, op1=ALU.add),
                     reads=[v, cw, a], writes=[a])
                P.op("pool", lambda e, a=a, bb=bb, c=c: e.tensor_tensor(out=ybrs[0][:, c, :], in0=a[:, :], in1=bb[:, :], op=ALU.mult),
                     reads=[a, bb], writes=[ybrs[0]])
            P.dma("pool", ybrs[1][:, :, :], ybT_d[:, tsl].rearrange("(c p) n -> p c n", p=128), dst=ybrs[1])
            P.dma("pool", ybrs[2][:, :, :], ycT_d[:, tsl].rearrange("(c p) n -> p c n", p=128), dst=ybrs[2])
            for mc in range(8):
                ma = macc[mc % 2]
                for br in range(3):
                    pj = ppj[br % 2]
                    for k in range(4):
                        wb, wap = wslice(wbr, br * 4 + k, mc * 128, (mc + 1) * 128)
                        P.op("pe", lambda e, pj=pj, wap=wap, br=br, k=k: e.matmul(
                            pj[:, :], lhsT=wap, rhs=ybrs[br][:, k, :], start=(k == 0), stop=(k == 3)),
                            reads=[wb, ybrs[br]], writes=[pj])
                    pgt = pg[br % 2]
                    for k in range(8):
                        wb, wap = wslice(wmg, k, br * D + mc * 128, br * D + (mc + 1) * 128)
                        P.op("pe", lambda e, pgt=pgt, wap=wap, k=k: e.matmul(
                            pgt[:, :], lhsT=wap, rhs=h2T[:, k, :], start=(k == 0), stop=(k == 7)),
                            reads=[wb, h2T], writes=[pgt])
                    gs = gsb[br % 2]
                    P.op("act", lambda e, gs=gs, pgt=pgt: e.activation(out=gs[:, :], in_=pgt[:, :], func=AF.Sigmoid),
                         reads=[pgt], writes=[gs])
                    if br == 0:
                        P.op("dve", lambda e, ma=ma, gs=gs, pj=pj: e.tensor_tensor(out=ma[:, :], in0=gs[:, :], in1=pj[:, :],
                                                                                   op=ALU.mult), reads=[gs, pj], writes=[ma])
                    else:
                        t_ = tt[br % 2]
                        P.op("dve", lambda e, t_=t_, gs=gs, pj=pj: e.tensor_tensor(out=t_[:, :], in0=gs[:, :], in1=pj[:, :],
                                                                                   op=ALU.mult), reads=[gs, pj], writes=[t_])
                        if br == 1:
                            P.op("pool", lambda e, ma=ma, t_=t_: e.tensor_tensor(out=ma[:, :], in0=ma[:, :], in1=t_[:, :],
                                                                                 op=ALU.add), reads=[ma, t_], writes=[ma])
                        else:
                            P.op("pool", lambda e, ma=ma, t_=t_, mc=mc: e.tensor_tensor(out=mT[:, mc, :], in0=ma[:, :], in1=t_[:, :],
                                                                                        op=ALU.add), reads=[ma, t_], writes=[mT])
            for t in range(4):
                i = tg * 4 + t
                for half in range(2):
                    for k in range(8):
                        wb, wap = wslice(wout, k, half * 512, (half + 1) * 512)
                        P.op("pe", lambda e, wap=wap, k=k, half=half, t=t: e.matmul(
                            py[:, half * 512:(half + 1) * 512], lhsT=mT[:, k, t * 128:(t + 1) * 128], rhs=wap,
                            start=(k == 0), stop=(k == 7)), reads=[wb, mT], writes=[py])
                rstd = rms_stats(P, C, py, py[:, :], True)
                P.op("dve", lambda e, rstd=rstd: e.scalar_tensor_tensor(
                    out=t1[:, :], in0=py[:, :], scalar=rstd[:, :], in1=g[1][:, :], op0=ALU.mult, op1=ALU.mult),
                    reads=[py, rstd, g[1]], writes=[t1])
                xb = xt[t]
                P.op("pool", lambda e, xb=xb: e.tensor_tensor(out=xb[:, :], in0=xb[:, :], in1=t1[:, :], op=ALU.add),
                     reads=[xb, t1], writes=[xb])
                P.dma("sp", x2_d[i * 128:(i + 1) * 128, :], xb[:, :], src=xb)
    with P.scope():
        wg, wu, wd = load_ffn_weights(P, wg_d, wu_d, wd_d)
        xpool = [P.sb([128, D], F32, "x") for _ in range(4)]

        def get_x(i):
            b = xpool[i % 4]
            P.dma("sp", b[:, :], x2_d[i * 128:(i + 1) * 128, :], dst=b)
            return b

        def put_x(i, b):
            P.dma("sp", out_d[i * 128:(i + 1) * 128, :], b[:, :], src=b)

        ffn_phase(P, C, NTILE, get_x, put_x, g[2], g[3], wg, wu, wd)


def build_C():
    P = Prog()
    ident_d = P.dram("ident", [128, 128], F32, "ExternalInput")
    out_d = P.dram("xo", [NT, D], F32, "ExternalOutput")
    C = build_consts(P, ident_d)
    with P.scope():
        body_C(P, C, out_d, "")
    print("C streams", P.stats(), "sems", P.n_sems)
    return P.finish()


def build_CA():
    P = Prog()
    ident_d = P.dram("ident", [128, 128], F32, "ExternalInput")
    x3_d = P.dram("x3s", [NT, D], F32, "Internal")
    C = build_consts(P, ident_d)
    with P.scope():
        body_C(P, C, x3_d, "c_")
    with P.scope():
        body_A(P, C, x3_d, "a_")
    print("CA streams", P.stats(), "sems", P.n_sems)
    return P.finish()


S = 8192
NQB = 32


def build_B():
    P = Prog()
    dh = {}
    for nm, shp in (("qT", [128, S]), ("zfT", [128, S]), ("zf", [S, 128]), ("v", [S, 128]), ("g", [S, 128]),
                    ("lbT", [128, 4]), ("lmask", [128, 4]), ("lbrow", [128, 4, 128]), ("lmaskrow", [128, 4, 128]),
                    ("gnorm", [128, 128]), ("M1", [128, 128]), ("M2", [128, 128])):
        dh[nm] = P.dram("h_" + nm, shp, F32, "ExternalInput")
    dh["y"] = P.dram("yh", [S, 128], F32, "ExternalOutput")
    dn = {}
    for nm, shp in NSA_IN(S, NQB):
        dn[nm] = P.dram("n_" + nm, shp, F32, "ExternalInput")
    dn["y"] = P.dram("yn", [NQB, 128, 256], F32, "ExternalOutput")
    with P.scope():
        hgrn_phase(P, S, dh)
    with P.scope():
        nsa_phase(P, S, NQB, dn)
    print("B streams", P.stats(), "sems", P.n_sems)
    return P.finish()


import numpy as np

S = 8192
NC = 8

def rep(a, n=128):
    return np.ascontiguousarray(np.broadcast_to(a[None], (n,) + a.shape))

IDENT = np.eye(128, dtype=np.float32)
_p = np.arange(128)
INV = (10000.0 ** (-(_p % 32).astype(np.float32) * 2.0 / 64)).astype(np.float32).reshape(128, 1)
SGN = np.where((_p % 64) < 32, -1.0, 1.0).astype(np.float32).reshape(128, 1)
PERM, MG = a_col_perm()


def prep_A(inp, l, x):
    wa = np.ascontiguousarray(inp["w_in"][l][:, PERM])
    gains = np.stack([rep(inp["norm_gains"][l, i]) for i in (0, 1, 2)])
    maps = []
    pos = inp["positions"].reshape(-1)
    for c in range(NC):
        sl = slice(c * NT, (c + 1) * NT)
        maps.append(dict(x=(np.ascontiguousarray(x[sl]) if x is not None else None), gains=gains, wg=inp["w_ffn_gate"][l, 0], wu=inp["w_ffn_up"][l, 0],
                         wd=inp["w_ffn_down"][l, 0], wa=wa, ident=IDENT, pos=rep(pos[sl].astype(np.int32)), inv=INV, sgn=SGN))
    return maps

HG_M1, HG_M2 = hgrn_consts_np()
NSA_K = nsa_consts_np(S)
ONES128 = np.ones((128, 128), np.float32)
ZEROS128 = np.zeros((128, 128), np.float32)
NQB = 32
NCC = 4


def _shift(A, par):
    o = np.empty_like(A)
    if par:
        o[:, 2 * par:] = A[:, :A.shape[1] - 2 * par]
        o[:, :2 * par] = A[:, :1]
    else:
        o[:] = A
    return o


def nsa_core_consts(par):
    K = NSA_K
    if par == 0:
        wm = [K["Mlow"], ONES128, ONES128, ONES128, K["Mdiag"], ZEROS128, K["Mdiag"], ZEROS128]
    else:
        wm = [ZEROS128, K["Mlow"], ONES128, ONES128, ONES128, K["Mdiag"], ONES128, K["Mdiag"]]
    Tt = np.zeros((128, NQB, NCC), np.float32)
    for m in range(NQB):
        for c in range(NCC):
            Tt[:, m, c] = 128 * (2 * m + par) - 2048 * c - 31
    return dict(WM=np.ascontiguousarray(np.stack(wm, 1)), Tt=Tt, Amul=_shift(K["Amul"], par), Aadd=_shift(K["Aadd"], par))


NSA_CC = [nsa_core_consts(0), nsa_core_consts(1)]
OV_PM = np.ascontiguousarray(NSA_K["ov"].reshape(-1, 128, NSA_K["ov"].shape[-1]).transpose(1, 0, 2))


def prep_B(inp, l, FM):
    maps = []
    logits = inp["hgrn_lb_logits"]
    lmask = np.zeros((4,), np.float32)
    lmask[1:l + 1] = 1
    gn = rep(inp["hgrn_gnorm"][l])
    pe = inp["cmp_pe"][l]
    pe2 = np.ascontiguousarray(pe.reshape(2, 16, 2, 64).transpose(0, 2, 3, 1).reshape(2, 128, 16))
    for c in range(NC):
        b, hd = c // 4, c % 4
        kvh, par = (c // 2) % 2, c % 2
        tk = slice(b * S, (b + 1) * S)
        d = {}
        r = lambda base, n: FM[base:base + n, tk]
        d["h_qT"] = np.ascontiguousarray(r(1024 + hd * 128, 128))
        zfT = r(1536 + hd * 128, 128)
        d["h_zfT"] = np.ascontiguousarray(zfT)
        d["h_zf"] = np.ascontiguousarray(zfT.T)
        d["h_v"] = np.ascontiguousarray(r(2048 + hd * 128, 128).T)
        d["h_g"] = np.ascontiguousarray(r(2560 + hd * 128, 128).T)
        lg = logits[:, hd * 128:(hd + 1) * 128]
        d["h_lbT"] = np.ascontiguousarray(lg.T)
        d["h_lmask"] = rep(lmask)
        d["h_lbrow"] = rep(lg)
        d["h_lmaskrow"] = np.ascontiguousarray(np.broadcast_to(lmask[None, :, None], (128, 4, 128)))
        d["h_gnorm"] = gn
        d["h_M1"] = HG_M1
        d["h_M2"] = HG_M2
        q = r(3072 + kvh * 256, 256).reshape(4, 64, 64, 128)[:, :, par::2]
        d["n_QT"] = np.ascontiguousarray(q.transpose(1, 2, 0, 3).reshape(64, NQB, 512))
        gl = r(4352 + kvh * 12, 12).reshape(12, 64, 128)[:, par::2]
        d["n_gl"] = np.ascontiguousarray(gl.transpose(2, 1, 0))
        d["n_ksT"] = np.ascontiguousarray(r(3712 + kvh * 64, 64))
        d["n_kwT"] = np.ascontiguousarray(r(3840 + kvh * 64, 64))

        def stack2(xT):
            o = np.zeros((128, S), np.float32)
            o[:64] = xT
            o[64:, :-1] = xT[:, 1:]
            return o
        d["n_kc2T"] = stack2(r(3584 + kvh * 64, 64))
        d["n_vc2T"] = stack2(r(3968 + kvh * 64, 64))

        def aug(xT):
            o = np.ones((S, 65), np.float32)
            o[:, :64] = xT.T
            return np.ascontiguousarray(o.reshape(S // 128, 128, 65).transpose(1, 0, 2))
        d["n_vs"] = aug(r(4096 + kvh * 64, 64))
        d["n_vw"] = aug(r(4224 + kvh * 64, 64))
        d["n_pe2"] = pe2
        d["n_w1"] = np.ascontiguousarray(inp["cmp_w1"][l].reshape(2, 16, 128, 256).transpose(0, 2, 1, 3))
        d["n_w2"] = np.ascontiguousarray(inp["cmp_w2"][l].reshape(2, 2, 128, 64).transpose(0, 2, 1, 3))
        for k in ("D16", "Mdiag", "Mlow", "E", "identb"):
            d["n_" + k] = NSA_K[k]
        d["n_ov"] = OV_PM
        for k, v in NSA_CC[par].items():
            d["n_" + k] = v
        maps.append(d)
    return maps


def gather_B(results):
    ybT = np.zeros((2, 512, S), np.float32)
    ycT = np.zeros((2, 512, S), np.float32)
    for c in range(NC):
        b, hd = c // 4, c % 4
        kvh, par = (c // 2) % 2, c % 2
        ybT[b, hd * 128:(hd + 1) * 128, :] = results[c]["yh"].T
        y = results[c]["yn"]
        yT = y.transpose(2, 0, 1)
        ycT[b, kvh * 256:(kvh + 1) * 256].reshape(256, 64, 128)[:, par::2, :] = yT
    return ybT, ycT


def prep_C(inp, l, x1, FM, ybT, ycT):
    gains = np.stack([rep(inp["norm_gains"][l, i]) for i in (2, 3, 4, 5)])
    cw = np.ascontiguousarray(inp["conv_w"][l].reshape(3, 4, 128).transpose(2, 1, 0))
    wmg = np.ascontiguousarray(inp["w_in"][l][:, MG])
    wbr = np.ascontiguousarray(inp["w_branch"][l].reshape(1536, 1024))
    maps = []
    for c in range(NC):
        b = c // 4
        t0 = c * NT
        sl = slice(t0, t0 + NT)
        ls = slice((c % 4) * NT, (c % 4 + 1) * NT)
        vT = np.zeros((512, NT + 2), np.float32)
        vT[:, 2:] = FM[512:1024, sl]
        if c % 4 != 0:
            vT[:, :2] = FM[512:1024, t0 - 2:t0]
        maps.append(dict(x1=np.ascontiguousarray(x1[sl]), gains=gains, vT=vT, bT=np.ascontiguousarray(FM[0:512, sl]),
                         ybT=np.ascontiguousarray(ybT[b][:, ls]), ycT=np.ascontiguousarray(ycT[b][:, ls]), cw=cw, wmg=wmg, wbr=wbr,
                         wout=inp["w_out"][l], wg=inp["w_ffn_gate"][l, 1], wu=inp["w_ffn_up"][l, 1], wd=inp["w_ffn_down"][l, 1],
                         ident=IDENT))
    return maps


from concourse.bass_utils import run_bass_kernel_spmd

_PROGS = {}


def _prog(name):
    if name not in _PROGS:
        _PROGS[name] = {"A": build_A, "B": build_B, "C": build_C, "CA": build_CA}[name]()
    return _PROGS[name]


def kernel(**inputs):
    inp = {k: np.asarray(v) for k, v in inputs.items()}
    cores = list(range(NC))
    x = np.ascontiguousarray(inp["x"].reshape(-1, D).astype(np.float32, copy=False))
    resA = run_bass_kernel_spmd(_prog("A"), prep_A(inp, 0, x), core_ids=cores)
    x1 = np.concatenate([r["x1"] for r in resA.results])
    FM = np.concatenate([r["fm"] for r in resA.results], axis=1)
    del resA
    for l in range(4):
        resB = run_bass_kernel_spmd(_prog("B"), prep_B(inp, l, FM), core_ids=cores)
        ybT, ycT = gather_B(resB.results)
        del resB
        mc = prep_C(inp, l, x1, FM, ybT, ycT)
        if l < 3:
            ma = prep_A(inp, l + 1, None)
            maps = []
            for c in range(NC):
                d = {"c_" + k: v for k, v in mc[c].items() if k != "ident"}
                d.update({"a_" + k: v for k, v in ma[c].items() if k not in ("ident", "x")})
                d["ident"] = IDENT
                maps.append(d)
            res = run_bass_kernel_spmd(_prog("CA"), maps, core_ids=cores)
            x1 = np.concatenate([r["a_x1"] for r in res.results])
            FM = np.concatenate([r["a_fm"] for r in res.results], axis=1)
            del res
        else:
            res = run_bass_kernel_spmd(_prog("C"), mc, core_ids=cores)
            x = np.concatenate([r["xo"] for r in res.results])
            del res
    return np.ascontiguousarray(x.reshape(2, S, D).astype(np.float32))
```
